# Optimizing a Trainium2 kernel written in Bass

```python
import jax, jax.numpy as jnp
from jax import lax
import numpy as np

D_MODEL = 1024
BATCH = 2
SEQ = 16384
DEPTH = 2
DEC_BATCH = 8
DEC_SEQ = 4096
PAST_LEN = 128

N_EVEN = (DEPTH + 1) // 2
N_ODD = DEPTH // 2
EPS = 1e-6

D_RNN = D_MODEL // 2
RG_HEADS = 8
RG_HEAD_DIM = D_RNN // RG_HEADS
RG_CONV = 4
RG_C = 8.0
MLA_HEADS = 8
Q_LORA = D_MODEL // 4
KV_LORA = D_MODEL // 8
QK_NOPE = 64
QK_ROPE = 32
V_DIM = 64
QK_DIM = QK_NOPE + QK_ROPE
ROPE_THETA = 10000.0
Q_BLOCK = 128
AB_IN = 2 * D_RNN + Q_LORA + KV_LORA + QK_ROPE
AB_OUT = D_RNN + MLA_HEADS * V_DIM
D_CONV = D_MODEL
C_CONV = 3
PEER_HEADS = 8
N_KEYS = 128
N_EXPERTS = N_KEYS * N_KEYS
PEER_TOPK = 16
D_KEY = 256
PEER_CHUNK = 128

kernel_name = 'hybrid_bidir_rglru_mla_shortconv_peer'


def rms_norm(x, g):
    xf = x.astype(jnp.float32)
    y = xf * lax.rsqrt(jnp.mean(xf * xf, axis=-1, keepdims=True) + EPS)
    return (y * g.astype(jnp.float32)).astype(x.dtype)


def depthwise_conv(x, w, pad):
    return lax.conv_general_dilated(x, w[:, None, :].astype(x.dtype), window_strides=(1,), padding=[pad],
                                    dimension_numbers=('NWC', 'WIO', 'NWC'), feature_group_count=x.shape[-1])


def _lin_combine(e1, e2):
    a1, b1 = e1
    a2, b2 = e2
    return a1 * a2, a2 * b1 + b2


def rg_lru_scan(xc, wa, ba, wx, bx, lam):
    b_, s_, _ = xc.shape
    xh = xc.reshape(b_, s_, RG_HEADS, RG_HEAD_DIM)
    r = jax.nn.sigmoid(jnp.einsum('bshi,hij->bshj', xh, wa.astype(jnp.float32)).reshape(b_, s_, D_RNN) + ba.astype(jnp.float32))
    i = jax.nn.sigmoid(jnp.einsum('bshi,hij->bshj', xh, wx.astype(jnp.float32)).reshape(b_, s_, D_RNN) + bx.astype(jnp.float32))
    log_a = -RG_C * r * jax.nn.softplus(-lam.astype(jnp.float32))
    a = jnp.exp(log_a)
    b = jnp.sqrt(-jnp.expm1(2.0 * log_a)) * (i * xc)
    _, h = lax.associative_scan(_lin_combine, (a, b), axis=1)
    return h


def rope_tables(s_):
    inv = 1.0 / (ROPE_THETA ** (jnp.arange(0, QK_ROPE, 2, dtype=jnp.float32) / QK_ROPE))
    ang = jnp.arange(s_, dtype=jnp.float32)[:, None] * inv[None, :]
    return jnp.cos(ang), jnp.sin(ang)


def apply_rope(t, cos, sin):
    half = QK_ROPE // 2
    nope = t[..., :QK_NOPE]
    rot = t[..., QK_NOPE:].astype(jnp.float32)
    r1, r2 = rot[..., :half], rot[..., half:]
    c = cos[None, :, None, :]
    s = sin[None, :, None, :]
    rot = jnp.concatenate([r1 * c - r2 * s, r2 * c + r1 * s], axis=-1)
    return jnp.concatenate([nope, rot.astype(t.dtype)], axis=-1)


def block_attention(q, k, v):
    b_, s_, h_, dq = q.shape
    nb = s_ // Q_BLOCK
    qb = q.reshape(b_, nb, Q_BLOCK, h_, dq).transpose(1, 0, 2, 3, 4)
    scale = QK_DIM ** -0.5

    def one(qblk):
        s = jnp.einsum('bqhd,bkhd->bhqk', qblk, k).astype(jnp.float32) * scale
        p = jax.nn.softmax(s, axis=-1).astype(v.dtype)
        return jnp.einsum('bhqk,bkhd->bqhd', p, v)

    o = lax.map(one, qb)
    return o.transpose(1, 0, 2, 3, 4).reshape(b_, s_, h_ * V_DIM)


def mixer_ab(h, w_in, conv_w, conv_b, rg_wa, rg_ba, rg_wx, rg_bx, rg_lam,
             q_norm, w_uq, kv_norm, w_ukv, qn_q, qn_k, w_out):
    b_, s_, _ = h.shape
    z = h @ w_in
    cuts = [D_RNN, 2 * D_RNN, 2 * D_RNN + Q_LORA, 2 * D_RNN + Q_LORA + KV_LORA]
    xr, yr, q_lat, kv_lat, k_rope = jnp.split(z, cuts, axis=-1)
    xr = depthwise_conv(xr, conv_w, (2, 1)) + conv_b
    xf = xr.astype(jnp.float32)
    h_fw = rg_lru_scan(xf, rg_wa[0], rg_ba[0], rg_wx[0], rg_bx[0], rg_lam[0])
    h_bw = jnp.flip(rg_lru_scan(jnp.flip(xf, axis=1), rg_wa[1], rg_ba[1], rg_wx[1], rg_bx[1], rg_lam[1]), axis=1)
    rg_out = (h_fw + h_bw).astype(h.dtype) * jax.nn.gelu(yr)
    q = (rms_norm(q_lat, q_norm) @ w_uq).reshape(b_, s_, MLA_HEADS, QK_DIM)
    kv = (rms_norm(kv_lat, kv_norm) @ w_ukv).reshape(b_, s_, MLA_HEADS, QK_NOPE + V_DIM)
    k_nope, v = kv[..., :QK_NOPE], kv[..., QK_NOPE:]
    k = jnp.concatenate([k_nope, jnp.broadcast_to(k_rope[:, :, None, :], (b_, s_, MLA_HEADS, QK_ROPE))], axis=-1)
    q = rms_norm(q, qn_q)
    k = rms_norm(k, qn_k)
    cos, sin = rope_tables(s_)
    q = apply_rope(q, cos, sin)
    k = apply_rope(k, cos, sin)
    attn = block_attention(q, k, v)
    return jnp.concatenate([rg_out, attn], axis=-1) @ w_out


def mixer_c(h, w_in, conv_w, w_out):
    z = h @ w_in
    bg, cg, xin = jnp.split(z, 3, axis=-1)
    y = bg * depthwise_conv(cg * xin, conv_w, (1, 1))
    return y @ w_out


def peer(h, wq, k1, k2, u, v):
    b_, s_, d = h.shape
    t = b_ * s_
    hf = h.reshape(t, d)
    q = (hf @ wq).reshape(t, PEER_HEADS, D_KEY).astype(jnp.float32)
    half = D_KEY // 2
    s1 = jnp.einsum('thd,nd->thn', q[..., :half], k1.astype(jnp.float32))
    s2 = jnp.einsum('thd,nd->thn', q[..., half:], k2.astype(jnp.float32))
    v1, i1 = lax.top_k(s1, PEER_TOPK)
    v2, i2 = lax.top_k(s2, PEER_TOPK)
    cand = (v1[..., :, None] + v2[..., None, :]).reshape(t, PEER_HEADS, PEER_TOPK * PEER_TOPK)
    vs, j = lax.top_k(cand, PEER_TOPK)
    e1 = jnp.take_along_axis(i1, j // PEER_TOPK, axis=-1)
    e2 = jnp.take_along_axis(i2, j % PEER_TOPK, axis=-1)
    idx = e1 * N_KEYS + e2
    g = jax.nn.softmax(vs, axis=-1).astype(h.dtype)
    nc = t // PEER_CHUNK

    def chunk(args):
        xc, ic, gc = args
        act = jax.nn.gelu(jnp.einsum('chkd,cd->chk', u[ic], xc)) * gc
        return jnp.einsum('chk,chkd->cd', act, v[ic])

    out = lax.map(chunk, (hf.reshape(nc, PEER_CHUNK, d),
                          idx.reshape(nc, PEER_CHUNK, PEER_HEADS, PEER_TOPK),
                          g.reshape(nc, PEER_CHUNK, PEER_HEADS, PEER_TOPK)))
    return out.reshape(b_, s_, d)


def trunk(x, c, p):
    for i in range(DEPTH):
        mod = jax.nn.silu(c) @ p['ada_w'][i] + p['ada_b'][i]
        sh1, sc1, g1, sh2, sc2, g2 = jnp.split(mod[:, None, :], 6, axis=-1)
        h = rms_norm(x, p['norm1_g'][i]) * (1.0 + sc1) + sh1
        if i % 2 == 0:
            j = i // 2
            m = mixer_ab(h, p['ab_w_in'][j], p['rg_conv_w'][j], p['rg_conv_b'][j], p['rg_wa'][j], p['rg_ba'][j],
                         p['rg_wx'][j], p['rg_bx'][j], p['rg_lambda'][j], p['mla_q_norm'][j], p['mla_w_uq'][j],
                         p['mla_kv_norm'][j], p['mla_w_ukv'][j], p['mla_qn_q'][j], p['mla_qn_k'][j], p['ab_w_out'][j])
        else:
            j = i // 2
            m = mixer_c(h, p['c_w_in'][j], p['c_conv_w'][j], p['c_w_out'][j])
        x = x + g1 * m
        h = rms_norm(x, p['norm2_g'][i]) * (1.0 + sc2) + sh2
        x = x + g2 * peer(h, p['peer_wq'][i], p['peer_k1'][i], p['peer_k2'][i], p['peer_u'][i], p['peer_v'][i])
    return x


def setup_inputs(seed: int = 0) -> dict:
    key = jax.random.key(seed)
    keys = iter(jax.random.split(key, 64))
    f32 = jnp.float32

    def nrm(shape, scale):
        return jax.random.normal(next(keys), shape, f32) * scale

    def gain(shape):
        return 1.0 + nrm(shape, 0.05)

    u_lam = jax.random.uniform(next(keys), (N_EVEN, 2, D_RNN), f32, 0.9, 0.999)
    a0 = u_lam ** (1.0 / RG_C)
    rg_lambda = jnp.log(a0) - jnp.log1p(-a0)
    return {
        'x_prompt': nrm((BATCH, SEQ, D_MODEL), 1.0),
        'x_sample': nrm((DEC_BATCH, DEC_SEQ, D_MODEL), 1.0),
        'c_prompt': nrm((BATCH, D_MODEL), 1.0),
        'c_sample': nrm((DEC_BATCH, D_MODEL), 1.0),
        'ada_w': nrm((DEPTH, D_MODEL, 6 * D_MODEL), 0.02),
        'ada_b': nrm((DEPTH, 6 * D_MODEL), 0.01),
        'norm1_g': gain((DEPTH, D_MODEL)),
        'norm2_g': gain((DEPTH, D_MODEL)),
        'ab_w_in': nrm((N_EVEN, D_MODEL, AB_IN), D_MODEL ** -0.5),
        'rg_conv_w': nrm((N_EVEN, RG_CONV, D_RNN), RG_CONV ** -0.5),
        'rg_conv_b': nrm((N_EVEN, D_RNN), 0.01),
        'rg_wa': nrm((N_EVEN, 2, RG_HEADS, RG_HEAD_DIM, RG_HEAD_DIM), RG_HEAD_DIM ** -0.5),
        'rg_ba': nrm((N_EVEN, 2, D_RNN), 0.01),
        'rg_wx': nrm((N_EVEN, 2, RG_HEADS, RG_HEAD_DIM, RG_HEAD_DIM), RG_HEAD_DIM ** -0.5),
        'rg_bx': nrm((N_EVEN, 2, D_RNN), 0.01),
        'rg_lambda': rg_lambda,
        'mla_q_norm': gain((N_EVEN, Q_LORA)),
        'mla_w_uq': nrm((N_EVEN, Q_LORA, MLA_HEADS * QK_DIM), Q_LORA ** -0.5),
        'mla_kv_norm': gain((N_EVEN, KV_LORA)),
        'mla_w_ukv': nrm((N_EVEN, KV_LORA, MLA_HEADS * (QK_NOPE + V_DIM)), KV_LORA ** -0.5),
        'mla_qn_q': gain((N_EVEN, QK_DIM)),
        'mla_qn_k': gain((N_EVEN, QK_DIM)),
        'ab_w_out': nrm((N_EVEN, AB_OUT, D_MODEL), AB_OUT ** -0.5),
        'c_w_in': nrm((N_ODD, D_MODEL, 3 * D_CONV), D_MODEL ** -0.5),
        'c_conv_w': nrm((N_ODD, C_CONV, D_CONV), C_CONV ** -0.5),
        'c_w_out': nrm((N_ODD, D_CONV, D_MODEL), D_CONV ** -0.5),
        'peer_wq': nrm((DEPTH, D_MODEL, PEER_HEADS * D_KEY), D_MODEL ** -0.5),
        'peer_k1': nrm((DEPTH, N_KEYS, D_KEY // 2), (D_KEY // 2) ** -0.5),
        'peer_k2': nrm((DEPTH, N_KEYS, D_KEY // 2), (D_KEY // 2) ** -0.5),
        'peer_u': nrm((DEPTH, N_EXPERTS, D_MODEL), D_MODEL ** -0.5),
        'peer_v': nrm((DEPTH, N_EXPERTS, D_MODEL), PEER_HEADS ** -0.5),
    }


def reference(x_prompt, x_sample, c_prompt, c_sample, ada_w, ada_b, norm1_g, norm2_g, ab_w_in, rg_conv_w, rg_conv_b,
              rg_wa, rg_ba, rg_wx, rg_bx, rg_lambda, mla_q_norm, mla_w_uq, mla_kv_norm, mla_w_ukv, mla_qn_q, mla_qn_k,
              ab_w_out, c_w_in, c_conv_w, c_w_out, peer_wq, peer_k1, peer_k2, peer_u, peer_v):
    p = dict(ada_w=ada_w, ada_b=ada_b, norm1_g=norm1_g, norm2_g=norm2_g, ab_w_in=ab_w_in, rg_conv_w=rg_conv_w,
             rg_conv_b=rg_conv_b, rg_wa=rg_wa, rg_ba=rg_ba, rg_wx=rg_wx, rg_bx=rg_bx, rg_lambda=rg_lambda,
             mla_q_norm=mla_q_norm, mla_w_uq=mla_w_uq, mla_kv_norm=mla_kv_norm, mla_w_ukv=mla_w_ukv,
             mla_qn_q=mla_qn_q, mla_qn_k=mla_qn_k, ab_w_out=ab_w_out, c_w_in=c_w_in, c_conv_w=c_conv_w,
             c_w_out=c_w_out, peer_wq=peer_wq, peer_k1=peer_k1, peer_k2=peer_k2, peer_u=peer_u, peer_v=peer_v)
    y_prompt = trunk(x_prompt, c_prompt, p)
    y_sample = trunk(x_sample, c_sample, p)
    return (y_prompt, y_sample)
```

```python
import os
import contextlib
import numpy as np
import concourse.bass as bass
import concourse.mybir as mybir
from concourse.bass_utils import run_bass_kernel_spmd

F32 = mybir.dt.float32
BF16 = mybir.dt.bfloat16
AF = mybir.ActivationFunctionType
ALU = mybir.AluOpType
AX = mybir.AxisListType
ENGS = ["pe", "act", "dve", "pool", "sp"]
EPS = 1e-6
BIG = 1.0e30

S_S = 4096
S_P = 16384
NOWN = 4098
ROT0 = 12286
TA = 256
NT = 256


class Buf:
    __slots__ = ("name", "last_w", "readers", "dsem", "dval")

    def __init__(self, name):
        self.name = name
        self.last_w = None
        self.readers = []
        self.dsem = None
        self.dval = 0


class T:
    __slots__ = ("t", "b")

    def __init__(self, t, b):
        self.t = t
        self.b = b

    def __getitem__(self, k):
        return self.t[k]


def _b(x):
    return x.b if isinstance(x, T) else x


class Sched:
    def __init__(self, nc, stack):
        self.nc = nc
        self.stack = stack
        self.ops = {e: [] for e in ENGS}
        self.cnt = {e: 0 for e in ENGS}
        self.sem = {e: stack.enter_context(nc.semaphore("sem_" + e)) for e in ENGS if e != "sp"}
        self.seen = {e: {} for e in ENGS}
        self.dma_bufs = []
        self.free_sems = []
        self.nsem_alloc = 0
        self.ninstr = 0

    def _deps(self, eng, reads, writes):
        deps = []
        own = self.sem.get(eng)
        for b in reads:
            if b.last_w is not None:
                deps.append(b.last_w)
        for b in writes:
            if b.last_w is not None and b.last_w[0] is not own:
                deps.append(b.last_w)
            deps.extend(r for r in b.readers if r[0] is not own)
        seen = self.seen[eng]
        best = {}
        pe_sem = self.sem["pe"]
        for (s, v) in deps:
            if eng == "pe" and s is pe_sem:
                continue
            k = id(s)
            if seen.get(k, 0) >= v:
                continue
            if k not in best or best[k][1] < v:
                best[k] = (s, v)
        for k, (s, v) in best.items():
            seen[k] = v
        return list(best.values())

    def op(self, eng, fn, reads=(), writes=()):
        reads = [_b(x) for x in reads]
        writes = [_b(x) for x in writes]
        deps = self._deps(eng, reads, writes)
        self.cnt[eng] += 1
        s = self.sem[eng]
        tok = (s, self.cnt[eng])
        self.ops[eng].append((deps, fn, s, 1))
        for b in writes:
            b.last_w = tok
            b.readers = []
        for b in reads:
            if b in writes:
                continue
            b.readers = [r for r in b.readers if r[0] is not s] + [tok]
        self.ninstr += 1

    def dma(self, out_ap, in_ap, reads=(), writes=(), eng="sp"):
        reads = [_b(x) for x in reads]
        writes = [_b(x) for x in writes]
        deps = self._deps(eng, reads, writes)
        tb = writes[0] if writes else reads[0]
        if tb.dsem is None:
            if self.free_sems:
                tb.dsem, tb.dval = self.free_sems.pop()
            else:
                self.nsem_alloc += 1
                tb.dsem = self.stack.enter_context(self.nc.semaphore("dsem%d" % self.nsem_alloc))
                tb.dval = 0
            self.dma_bufs.append(tb)
        tb.dval += 16
        tok = (tb.dsem, tb.dval)
        self.ops[eng].append((deps, lambda e: e.dma_start(out=out_ap, in_=in_ap, allow_slow_non_contiguous=True), tb.dsem, 16))
        for b in writes:
            b.last_w = tok
            b.readers = []
        for b in reads:
            b.readers = b.readers + [tok]
        self.ninstr += 1
        return tok

    def barrier(self):
        toks = [(self.sem[e], self.cnt[e]) for e in self.sem if self.cnt[e] > 0]
        toks += [(b.dsem, b.dval) for b in self.dma_bufs]
        for e in ENGS:
            deps = []
            for (s, v) in toks:
                if e in self.sem and s is self.sem[e]:
                    continue
                if self.seen[e].get(id(s), 0) >= v:
                    continue
                self.seen[e][id(s)] = v
                deps.append((s, v))
            self.ops[e].append((deps, None, None, 0))
        for b in self.dma_bufs:
            self.free_sems.append((b.dsem, b.dval))
            b.dsem = None
        self.dma_bufs = []

    def flush(self, final_waits=()):
        nc = self.nc
        engmap = {"pe": "tensor", "act": "scalar", "dve": "vector", "pool": "gpsimd", "sp": "sync"}
        with nc.Block() as block:
            for e in ENGS:
                ops = self.ops[e]
                fw = list(final_waits) if e == "sp" else []

                def body(eng, ops=ops, fw=fw):
                    for (deps, fn, s, inc) in ops:
                        for (ds, dv) in deps:
                            eng.wait_ge(ds, dv)
                        if fn is not None:
                            fn(eng).then_inc(s, inc)
                    for (ds, dv) in fw:
                        eng.wait_ge(ds, dv)
                getattr(block, engmap[e])(body)
        self.ops = {e: [] for e in ENGS}


class KB:
    def __init__(self, nc, st, debug):
        self.nc = nc
        self.st = st
        self.S = Sched(nc, st)
        self.ph = None
        self.uid = 0
        self.debug = debug
        self.inputs = {}
        self.outputs = {}
        self.psr = 0

    def sb(self, shape, dt=F32, name="t", buf=None):
        self.uid += 1
        nm = "%s_%d" % (name, self.uid)
        t = (self.ph or self.st).enter_context(self.nc.sbuf_tensor(nm, list(shape), dt))
        return T(t, buf if buf is not None else Buf(nm))

    def ring(self, n, shape, dt=F32, name="r"):
        return [self.sb(shape, dt, name) for _ in range(n)]

    def inp(self, name, shape, dt=F32):
        t = T(self.nc.dram_tensor(name, list(shape), dt, kind="ExternalInput").ap(), Buf(name))
        self.inputs[name] = t
        return t

    def outp(self, name, shape, dt=F32):
        t = T(self.nc.dram_tensor(name, list(shape), dt, kind="ExternalOutput").ap(), Buf(name))
        self.outputs[name] = t
        return t

    def scratch(self, name, shape, dt):
        kind = "ExternalOutput" if (self.debug and name in self.debug) else "Internal"
        t = T(self.nc.dram_tensor(name, list(shape), dt, kind=kind).ap(), Buf(name))
        if kind == "ExternalOutput":
            self.outputs[name] = t
        return t

    @contextlib.contextmanager
    def phase(self):
        with contextlib.ExitStack() as ph:
            old = self.ph
            self.ph = ph
            yield
            self.S.barrier()
            self.S.flush()
            self.ph = old

    def ps(self):
        b = self.psf[self.psr % len(self.psf)]
        self.psr += 1
        return b

    def mm(self, out, lhsT, rhs, start, stop, reads, writes):
        self.S.op("pe", lambda e: e.matmul(out, lhsT=lhsT, rhs=rhs, start=start, stop=stop), reads, writes)

    def tr(self, out, in_, ident, reads, writes):
        self.S.op("pe", lambda e: e.transpose(out=out, in_=in_, identity=ident), reads, writes)

    def act(self, out, in_, func, reads, writes, scale=None, bias=None, accum=None):
        kw = {}
        if scale is not None:
            kw["scale"] = scale
        if bias is not None:
            kw["bias"] = bias
        if accum is not None:
            kw["accum_out"] = accum
        self.S.op("act", lambda e: e.activation(out=out, in_=in_, func=func, **kw), reads, writes)

    def ts(self, eng, out, in0, s1, s2, op0, op1, reads, writes):
        if op1 is None:
            self.S.op(eng, lambda e: e.tensor_scalar(out=out, in0=in0, scalar1=s1, scalar2=None, op0=op0), reads, writes)
        else:
            self.S.op(eng, lambda e: e.tensor_scalar(out=out, in0=in0, scalar1=s1, scalar2=s2, op0=op0, op1=op1),
                      reads, writes)

    def tt(self, eng, out, in0, in1, op, reads, writes):
        self.S.op(eng, lambda e: e.tensor_tensor(out=out, in0=in0, in1=in1, op=op), reads, writes)

    def stt(self, out, in0, scalar, in1, op0, op1, reads, writes):
        self.S.op("dve", lambda e: e.scalar_tensor_tensor(out=out, in0=in0, scalar=scalar, in1=in1, op0=op0, op1=op1),
                  reads, writes)

    def cp(self, eng, out, in_, reads, writes):
        if eng == "act":
            self.S.op("act", lambda e: e.copy(out=out, in_=in_), reads, writes)
        else:
            self.S.op(eng, lambda e: e.tensor_copy(out=out, in_=in_), reads, writes)

    def memset(self, eng, out, val, writes):
        self.S.op(eng, lambda e: e.memset(out, val), (), writes)

    def dma(self, out, in_, reads, writes):
        self.S.dma(out, in_, reads, writes)

    def load_bf16(self, dst, dst_ap, src, src_ap, shape):
        stg = self.stage[self.stage_i % len(self.stage)]
        self.stage_i += 1
        n = int(np.prod(shape[1:]))
        sv = stg.t[0:shape[0], 0:n]
        if len(shape) == 3:
            sv = sv.rearrange("p (a b) -> p a b", b=shape[2])
        self.dma(sv, src_ap, [src], [stg])
        eng = ["act", "dve", "pool"][self.stage_i % 3]
        self.cp(eng, dst_ap, sv, [stg], [dst])

    def rstd_from_ps(self, ps, ps_ap, n, nfeat, parts=128):
        r = self.rs_ring[self.rs_i % len(self.rs_ring)]
        self.rs_i += 1
        rv = r.t[0:parts, 0:n]
        self.act(rv, ps_ap, AF.Ln, [ps], [r], scale=1.0 / nfeat, bias=self.eps_t.t[0:parts, 0:1])
        self.act(rv, rv, AF.Exp, [r], [r], scale=-0.5)
        return r, rv

    def norm_mod(self, x, n, gmod, sh, h, inplace=False):
        sq = self.nm_sq
        self.act(sq.t[:, :, 0:n], x.t[:, :, 0:n], AF.Square, [x], [sq])
        ps = self.ps()
        for c in range(8):
            self.mm(ps.t[:, 0:n], self.ones_b.t[:, :], sq.t[:, c, 0:n], c == 0, c == 7, [self.ones_b, sq], [ps])
        r, rv = self.rstd_from_ps(ps, ps.t[:, 0:n], n, 1024.0)
        tmp = x if inplace else self.nm_tmp
        self.tt("dve", tmp.t[:, :, 0:n], x.t[:, :, 0:n], rv.unsqueeze(1).to_broadcast([128, 8, n]), ALU.mult,
                [x, r], [tmp])
        for c in range(8):
            if c % 2 == 0:
                self.act(h.t[:, c, 0:n], tmp.t[:, c, 0:n], AF.Identity, [tmp, self.modv], [h],
                         scale=gmod[:, c:c + 1], bias=sh[:, c:c + 1])
            else:
                self.ts("pool", h.t[:, c, 0:n], tmp.t[:, c, 0:n], gmod[:, c:c + 1], sh[:, c:c + 1], ALU.mult, ALU.add,
                        [tmp, self.modv], [h])


def seq_desc(kind):
    d = {}
    if kind == "p":
        S = S_P
        segs = [("ctx", 0, 4095), ("ctx", 4095, 8191), ("ctx", 8191, 12286), ("halo", 12286, 12287),
                ("own", 12287, 16383), ("halo", 16383, 16384)]
        links = {4095: "a", 8191: "b", 12287: "c", 16383: "d"}
        own0 = ROT0
    else:
        S = S_S
        segs = [("own", 0, 4096)]
        links = {0: "zero"}
        own0 = -1
    tiles = []
    for (role, a, b) in segs:
        s = a
        while s < b:
            n = min(TA, b - s)
            tiles.append(dict(s=s, n=n, role=role, own=(s - own0) if role != "ctx" else None))
            s += n
    d.update(S=S, tiles=tiles, links=links, own0=own0, kind=kind)
    return d


def crossed(links, S, c_from, c_to):
    keys = []
    for p in range(c_from + 1, c_to + 1):
        k = links.get(p % S)
        if k is not None:
            keys.append(k)
    return keys


def build(debug=None, limit=None):
    debug = debug or {}
    limit = limit or {}
    nc = bass.Bass("TRN2", target_bir_lowering=False)
    with contextlib.ExitStack() as st:
        K = KB(nc, st, debug)
        _program(K, limit)
        finals = [t.b.last_w for t in K.outputs.values() if t.b.last_w is not None]
        K.S.flush(finals)
        print("kernel instrs:", K.S.ninstr, "dma sems:", K.S.nsem_alloc, flush=True)
    return nc, K


def _program(K, limit):
    nc = K.nc
    xs = K.inp("xs", [1024, S_S])
    xp = K.inp("xp", [1024, S_P])
    cvec = K.inp("cvec", [128, 8, 2])
    links_d = K.inp("links", [128, 4])
    rope_s = K.inp("rope_s", [32, 2, S_S])
    rope_p = K.inp("rope_p", [32, 2, S_P])
    ident_d = K.inp("ident", [128, 128])
    p96_d = K.inp("p96", [96, 96])
    adaw = K.inp("adaw", [2, 128, 8, 6144])
    adab = K.inp("adab", [128, 2, 48])
    n1g = K.inp("n1g", [128, 2, 8])
    n2g = K.inp("n2g", [128, 2, 8])
    w_in = K.inp("w_in", [128, 8, 1440])
    wkr = K.inp("wkr", [128, 8, 96])
    rgdiag = K.inp("rgdiag", [128, 16, 128])
    rgcb = K.inp("rgcb", [128, 4])
    rgbd = K.inp("rgbd", [128, 16, 128])
    rgb = K.inp("rgb", [128, 16])
    rglam = K.inp("rglam", [128, 8])
    qnorm = K.inp("qnorm", [128, 2])
    w_uq = K.inp("w_uq", [128, 2, 768])
    kvnorm = K.inp("kvnorm", [128, 1])
    wk = K.inp("wk", [128, 512])
    wv = K.inp("wv", [128, 512])
    qng = K.inp("qng", [96, 2])
    wo_rg = K.inp("wo_rg", [128, 4, 1024])
    wo_at = K.inp("wo_at", [64, 8, 1024])
    c_w_in = K.inp("c_w_in", [128, 8, 3072])
    cdiag = K.inp("cdiag", [128, 24, 128])
    c_w_out = K.inp("c_w_out", [128, 8, 1024])
    wq_d = K.inp("wq", [2, 128, 8, 2048])
    k12T = K.inp("k12T", [128, 4, 128])
    ut_d = K.inp("ut", [2, 128, 128, 1024])
    v_d = K.inp("pv", [2, 128, 128, 1024])
    ys = K.outp("ys", [1024, S_S])
    yp = K.outp("yp", [1024, S_S])

    kt_s = K.scratch("kt_s", [8, 96, S_P], BF16)
    v_s = K.scratch("v_s", [S_P, 512], BF16)
    qt_s = K.scratch("qt_s", [8, 96, NOWN], BF16)
    x1_s = {k: K.scratch("x1_" + k, [1024, NOWN], F32) for k in "sp"}
    x2_s = {k: K.scratch("x2_" + k, [1024, NOWN], F32) for k in "sp"}
    x3_s = {k: K.scratch("x3_" + k, [1024, S_S], F32) for k in "sp"}
    wq_b = K.scratch("wq_b", [2, 128, 8, 2048], BF16)
    uv_b = K.scratch("uv_b", [2, 128, 128, 2048], BF16)

    K.psf = [T(st_enter(K, nc.psum_tensor("psf%d" % i, [128, 512], F32)), Buf("psf%d" % i)) for i in range(6)]
    K.psb = [T(st_enter(K, nc.psum_tensor("psb%d" % i, [128, 1024], BF16)), Buf("psb%d" % i)) for i in range(2)]

    ident_f = K.sb([128, 128], F32, "identf")
    ident_b = K.sb([128, 128], BF16, "identb")
    K.ones_b = K.sb([128, 128], BF16, "onesb")
    ones_f = K.sb([128, 128], F32, "onesf")
    K.eps_t = K.sb([128, 1], F32, "eps")
    K.modv = K.sb([128, 2, 2, 6, 8], F32, "modv")
    gm = K.sb([128, 2, 2, 2, 8], F32, "gm")
    links = K.sb([128, 4], F32, "links")
    K.rs_ring = K.ring(3, [128, 264], F32, "rstd")
    K.rs_i = 0
    K.stage_i = 0
    K.one_t = K.sb([128, 1], F32, "one")
    K.memset("pool", K.one_t.t[:], 1.0, [K.one_t])
    modv = K.modv
    K.memset("pool", K.ones_b.t[:], 1.0, [K.ones_b])
    K.memset("pool", ones_f.t[:], 1.0, [ones_f])
    K.memset("pool", K.eps_t.t[:], EPS, [K.eps_t])
    K.dma(ident_f.t[:], ident_d.t[:, :], [ident_d], [ident_f])
    K.cp("dve", ident_b.t[:], ident_f.t[:], [ident_f], [ident_b])
    K.dma(links.t[:], links_d.t[:, :], [links_d], [links])

    LK = {"a": links.t[:, 0:1], "b": links.t[:, 1:2], "c": links.t[:, 2:3], "d": links.t[:, 3:4]}

    def apply_links(eng, ap, keys, t, parts=128):
        for k in keys:
            if k == "zero":
                K.ts(eng, ap, ap, 0.0, None, ALU.mult, None, [t], [t])
            else:
                K.ts(eng, ap, ap, LK[k][0:parts], None, ALU.mult, None, [t, links], [t])

    with K.phase():
        K.stage = K.ring(2, [128, 4096], F32, "stage")
        K.castb = K.ring(2, [128, 4096], BF16, "castb")
        zt = K.sb([128, 8, 1], F32, "zt")
        K.memset("pool", zt.t[:], 0.0, [zt])
        for cc in (0, NOWN - 1):
            K.dma(x2_s["s"].t[:, cc:cc + 1].rearrange("(c p) t -> p c t", p=128), zt.t[:], [zt], [x2_s["s"]])
        cv = K.sb([128, 8, 2], F32, "cv")
        K.dma(cv.t[:], cvec.t[:, :, :], [cvec], [cv])
        scv = K.sb([128, 8, 2], F32, "scv")
        K.act(scv.t[:], cv.t[:], AF.Silu, [cv], [scv])
        adb = K.sb([128, 2, 48], F32, "adb")
        K.dma(adb.t[:], adab.t[:, :, :], [adab], [adb])
        g12 = K.sb([128, 2, 2, 8], F32, "g12")
        K.dma(g12.t[:, 0], n1g.t[:, :, :], [n1g], [g12])
        K.dma(g12.t[:, 1], n2g.t[:, :, :], [n2g], [g12])
        awr = K.ring(2, [128, 8, 512], F32, "adaw")
        for l in range(2):
            psm = K.ps()
            for g in range(12):
                aw = awr[g % 2]
                K.dma(aw.t[:], adaw.t[l, :, :, g * 512:(g + 1) * 512], [adaw], [aw])
                for jj in range(4):
                    j = g * 4 + jj
                    for kc in range(8):
                        K.mm(psm.t[:, 2 * j:2 * j + 2], aw.t[:, kc, jj * 128:(jj + 1) * 128], scv.t[:, kc, :],
                             kc == 0, kc == 7, [aw, scv], [psm])
            for s in range(2):
                K.tt("dve", modv.t[:, l, s].rearrange("p w c -> p (w c)"),
                     psm.t[:, 0:96].rearrange("p (j s) -> p j s", s=2)[:, :, s], adb.t[:, l, :], ALU.add,
                     [psm, adb], [modv])
        for l in range(2):
            for s in range(2):
                for k in range(2):
                    K.stt(gm.t[:, l, s, k, :], modv.t[:, l, s, 1 + 3 * k, :], 1.0, g12.t[:, k, l, :], ALU.add, ALU.mult,
                          [modv, g12], [gm])
        if not limit.get("skip_prep"):
            cast_i = 0
            for l in range(2):
                jobs = [(wq_d.t[l], wq_b.t[l], wq_d, wq_b, 4)]
                for i0 in range(0, 128, 4):
                    jobs.append((ut_d.t[l, i0:i0 + 4].rearrange("i p f -> p i f"),
                                 uv_b.t[l, i0:i0 + 4, :, 0:1024].rearrange("i p f -> p i f"), ut_d, uv_b, 1))
                    jobs.append((v_d.t[l, i0:i0 + 4].rearrange("i p f -> p i f"),
                                 uv_b.t[l, i0:i0 + 4, :, 1024:2048].rearrange("i p f -> p i f"), v_d, uv_b, 1))
                for (src, dst, srcT, dstT, nsplit) in jobs:
                    for sp_ in range(nsplit):
                        if nsplit == 1:
                            s_ap, d_ap = src, dst
                            shp = [128, 4, 1024]
                        else:
                            s_ap = src[:, 2 * sp_:2 * sp_ + 2, :]
                            d_ap = dst[:, 2 * sp_:2 * sp_ + 2, :]
                            shp = [128, 2, 2048]
                        stg = K.stage[cast_i % 2]
                        sv = stg.t[:, :].rearrange("p (a b) -> p a b", b=shp[2])
                        K.dma(sv, s_ap, [srcT], [stg])
                        cb = K.castb[cast_i % 2]
                        cbv = cb.t[:, :].rearrange("p (a b) -> p a b", b=shp[2])
                        K.cp(["act", "dve"][cast_i % 2], cbv, sv, [stg], [cb])
                        K.dma(d_ap, cbv, [cb], [dstT])
                        cast_i += 1

    def MV(l, s, which):
        return modv.t[:, l, s, which, :]

    def GM(l, s, k):
        return gm.t[:, l, s, k, :]

    seqs = [("s", 0, xs, rope_s, ys), ("p", 1, xp, rope_p, yp)]
    if limit.get("seqs"):
        seqs = [q for q in seqs if q[0] in limit["seqs"]]

    for (kind, si, xT, rope_d, yout) in seqs:
        sd = seq_desc(kind)
        S = sd["S"]
        with K.phase():
            _layer0_mixer(K, sd, si, xT, rope_d, dict(
                w_in=w_in, wkr=wkr, rgdiag=rgdiag, rgcb=rgcb, rgbd=rgbd, rgb=rgb, rglam=rglam, qnorm=qnorm, w_uq=w_uq,
                kvnorm=kvnorm, wk=wk, wv=wv, qng=qng, wo_rg=wo_rg, wo_at=wo_at, p96=p96_d, ident_b=ident_b,
                ones_f=ones_f, kt_s=kt_s, v_s=v_s, qt_s=qt_s, x1=x1_s[kind], MV=MV, GM=GM, apply_links=apply_links,
                links=links, LK=LK), limit)
        if limit.get("stop") in ("stageA", "mixer0"):
            continue
        with K.phase():
            tl = [[(1 + NT * k, NT)] for k in range(S_S // NT)]
            if kind == "p":
                tl.append([(0, 1), (NOWN - 1, 1)])
            if limit.get("ptiles"):
                tl = tl[:limit["ptiles"]]
            _peer(K, 0, si, x1_s[kind], x2_s[kind], tl, dict(wq_b=wq_b, uv_b=uv_b, k12T=k12T, ident_b=ident_b,
                                                             ident_f=ident_f, MV=MV, GM=GM), limit)
        if limit.get("stop") == "peer0":
            continue
        with K.phase():
            _mixer_c(K, si, kind, x2_s[kind], x3_s[kind], dict(c_w_in=c_w_in, cdiag=cdiag, c_w_out=c_w_out, MV=MV, GM=GM,
                                                                LK=LK, links=links), limit)
        if limit.get("stop") == "mixer1":
            continue
        with K.phase():
            tl = [[(NT * k, NT)] for k in range(S_S // NT)]
            if limit.get("ptiles"):
                tl = tl[:limit["ptiles"]]
            _peer(K, 1, si, x3_s[kind], yout, tl, dict(wq_b=wq_b, uv_b=uv_b, k12T=k12T, ident_b=ident_b,
                                                       ident_f=ident_f, MV=MV, GM=GM), limit)


def st_enter(K, cm):
    return K.st.enter_context(cm)


def _layer0_mixer(K, sd, si, xT, rope_d, W, limit):
    S = sd["S"]
    kind = sd["kind"]
    tiles = sd["tiles"]
    MV, GM = W["MV"], W["GM"]
    apply_links = W["apply_links"]
    LK = W["LK"]
    links = W["links"]
    NE = TA + 3
    GY = K.sb([128, 4, NOWN], BF16, "GY")
    _stageA(K, sd, si, xT, rope_d, W, limit, GY)
    if limit.get("stop") == "stageA":
        dbg = K.outp("dbg_rg_" + kind, [128, 4, NOWN], BF16)
        K.dma(dbg.t[:, :, :], GY.t[:], [GY], [dbg])
        K.S.barrier()
        return
    with K.phase():
        _stageB(K, sd, si, xT, W, limit, GY)


def _stageA(K, sd, si, xT, rope_d, W, limit, GY):
  with K.phase():
    S = sd["S"]
    kind = sd["kind"]
    tiles = sd["tiles"]
    MV, GM = W["MV"], W["GM"]
    apply_links = W["apply_links"]
    LK = W["LK"]
    links = W["links"]
    NE = TA + 3
    K.stage = K.ring(1, [128, 2048], F32, "stage")
    w_in_b = K.sb([128, 8, 1440], BF16, "w_in_b")
    for kc in range(8):
        K.load_bf16(w_in_b, w_in_b.t[:, kc, :], W["w_in"], W["w_in"].t[:, kc, :], [128, 1440])
    wkr_b = K.sb([128, 8, 96], BF16, "wkr_b")
    K.load_bf16(wkr_b, wkr_b.t[:], W["wkr"], W["wkr"].t[:, :, :], [128, 8, 96])
    dg_b = K.sb([128, 16, 128], BF16, "dg_b")
    K.load_bf16(dg_b, dg_b.t[:], W["rgdiag"], W["rgdiag"].t[:, :, :], [128, 16, 128])
    bd_b = K.sb([128, 16, 128], BF16, "bd_b")
    K.load_bf16(bd_b, bd_b.t[:], W["rgbd"], W["rgbd"].t[:, :, :], [128, 16, 128])
    wuq_b = K.sb([128, 2, 768], BF16, "wuq_b")
    K.load_bf16(wuq_b, wuq_b.t[:], W["w_uq"], W["w_uq"].t[:, :, :], [128, 2, 768])
    wk_b = K.sb([128, 512], BF16, "wk_b")
    K.load_bf16(wk_b, wk_b.t[:], W["wk"], W["wk"].t[:, :], [128, 512])
    wv_b = K.sb([128, 512], BF16, "wv_b")
    K.load_bf16(wv_b, wv_b.t[:], W["wv"], W["wv"].t[:, :], [128, 512])
    p96_b = K.sb([96, 96], BF16, "p96_b")
    K.load_bf16(p96_b, p96_b.t[:], W["p96"], W["p96"].t[:, :], [96, 96])
    small = K.sb([128, 64], F32, "small")
    K.dma(small.t[:, 0:4], W["rgcb"].t[:, :], [W["rgcb"]], [small])
    K.dma(small.t[:, 4:20], W["rgb"].t[:, :], [W["rgb"]], [small])
    K.dma(small.t[:, 20:28], W["rglam"].t[:, :], [W["rglam"]], [small])
    K.dma(small.t[:, 28:30], W["qnorm"].t[:, :], [W["qnorm"]], [small])
    K.dma(small.t[:, 30:31], W["kvnorm"].t[:, :], [W["kvnorm"]], [small])
    K.dma(small.t[0:96, 32:34], W["qng"].t[:, :], [W["qng"]], [small])
    K.act(small.t[:, 52:60], small.t[:, 20:28], AF.Exp, [small], [small], scale=-1.0)
    K.act(small.t[:, 52:60], small.t[:, 52:60], AF.Ln, [small], [small], bias=K.one_t.t[:, 0:1])
    K.ts("dve", small.t[:, 36:44], small.t[:, 52:60], -8.0, None, ALU.mult, None, [small], [small])
    K.ts("dve", small.t[:, 44:52], small.t[:, 52:60], -16.0, None, ALU.mult, None, [small], [small])

    def SA(d, ch):
        return small.t[:, 36 + d * 4 + ch:37 + d * 4 + ch]

    def SA2(d, ch):
        return small.t[:, 44 + d * 4 + ch:45 + d * 4 + ch]

    def GB(d, which, ch):
        j = 4 + (d * 2 + which) * 4 + ch
        return small.t[:, j:j + 1]

    gmod1, sh1, g1 = GM(0, si, 0), MV(0, si, 0), MV(0, si, 2)

    XC = K.sb([128, 4, NOWN], BF16, "XC")
    HF = K.sb([128, 4, NOWN], BF16, "HF")
    nctx = sum(1 for t in tiles if t["role"] != "own")
    AB = K.sb([128, 2, max(nctx, 1), 2, 4], F32, "AB")
    carry = K.sb([128, 2, 4], F32, "carry")
    K.memset("pool", carry.t[:], 0.0, [carry])

    xe_r = K.ring(1, [128, 8, NE], F32, "xe")
    he_r = K.ring(2, [128, 8, NE], BF16, "he")
    K.nm_sq = K.sb([128, 8, NE], BF16, "nmsq")
    xr_b = K.sb([128, 4, NE], BF16, "xr_b")
    xc_t = K.ring(2, [128, 4, TA], BF16, "xc_t")
    g_r = K.ring(2, [128, TA], F32, "g_r")
    g_i = K.ring(2, [128, TA], F32, "g_i")
    g_a = K.ring(2, [128, TA], F32, "g_a")
    g_q = K.ring(2, [128, TA], F32, "g_q")
    g_b = K.ring(2, [128, TA], F32, "g_b")
    g_h = K.ring(2, [128, TA], F32, "g_h")
    sumr = K.sb([128, 64], F32, "sumr")
    sqb = K.ring(2, [128, TA], BF16, "sqb")
    ckv = K.ring(2, [128, TA], BF16, "ckv")
    qn = K.ring(2, [128, 2, TA], BF16, "qn")
    vt = K.ring(2, [128, 512], BF16, "vt")
    krope = K.sb([96, TA], F32, "krope")
    kh = K.ring(2, [96, TA], F32, "kh")
    khn = K.ring(2, [96, TA], F32, "khn")
    khb = K.ring(3, [96, TA], BF16, "khb")
    rt1 = K.ring(2, [96, TA], F32, "rt1")
    rope_t = K.ring(2, [96, 2, TA], F32, "rope")
    gi = [0]

    def rnext(r):
        gi[0] += 1
        return r[gi[0] % len(r)]

    W_p96 = p96_b
    ctx_i = [0]

    def load_ext(t):
        s, n = t["s"], t["n"]
        xe = rnext(xe_r)
        lo, hi = s - 2, s + n + 1
        pieces = []
        c = lo
        while c < hi:
            cm = c % S
            ln = min(hi - c, S - cm)
            pieces.append((c - lo, cm, ln))
            c += ln
        for (o, cm, ln) in pieces:
            K.dma(xe.t[:, :, o:o + ln], xT.t[:, cm:cm + ln].rearrange("(c p) t -> p c t", p=128), [xT], [xe])
        return xe

    def load_rope(t):
        s, n = t["s"], t["n"]
        rp = rnext(rope_t)
        K.dma(rp.t[64:96, :, 0:n], rope_d.t[:, :, s:s + n], [rope_d], [rp])
        return rp

    def tileA(t, mode):
        s, n = t["s"], t["n"]
        ne = n + 3
        if mode in ("ctx", "ownF"):
            xe = load_ext(t)
            he = rnext(he_r)
            K.norm_mod(xe, ne, gmod1, sh1, he, inplace=True)
            for ch in range(4):
                ps = K.ps()
                for kc in range(8):
                    K.mm(ps.t[:, 0:ne], w_in_b.t[:, kc, ch * 128:(ch + 1) * 128], he.t[:, kc, 0:ne], kc == 0, kc == 7,
                         [w_in_b, he], [ps])
                K.cp("act", xr_b.t[:, ch, 0:ne], ps.t[:, 0:ne], [ps], [xr_b])
            for (col, cf, ct) in ((0, s - 2, s), (1, s - 1, s), (ne - 1, s + n - 1, s + n)):
                ks = crossed(sd["links"], S, cf, ct)
                if ks:
                    apply_links("pool", xr_b.t[:, :, col:col + 1], ks, xr_b)
            if mode == "ctx":
                xc = rnext(xc_t)
                xcv = lambda ch: xc.t[:, ch, 0:n]
            else:
                xc = XC
                xcv = lambda ch: XC.t[:, ch, t["own"]:t["own"] + n]
            for ch in range(4):
                ps = K.ps()
                for k in range(4):
                    K.mm(ps.t[:, 0:n], dg_b.t[:, ch * 4 + k, :], xr_b.t[:, ch, k:k + n], k == 0, k == 3, [dg_b, xr_b],
                         [ps])
                K.act(xcv(ch), ps.t[:, 0:n], AF.Identity, [ps, small], [xc], bias=small.t[:, ch:ch + 1])
            rp = load_rope(t)
            ps_kv = K.ps()
            for kc in range(8):
                K.mm(ps_kv.t[:, 0:n], w_in_b.t[:, kc, 1280:1408], he.t[:, kc, 2:2 + n], kc == 0, kc == 7, [w_in_b, he],
                     [ps_kv])
            sq = rnext(sqb)
            K.act(sq.t[:, 0:n], ps_kv.t[:, 0:n], AF.Square, [ps_kv], [sq])
            ps2 = K.ps()
            K.mm(ps2.t[:, 0:n], K.ones_b.t[:, :], sq.t[:, 0:n], True, True, [K.ones_b, sq], [ps2])
            r, rv = K.rstd_from_ps(ps2, ps2.t[:, 0:n], n, 128.0)
            ck = rnext(ckv)
            K.stt(ck.t[:, 0:n], ps_kv.t[:, 0:n], small.t[:, 30:31], rv, ALU.mult, ALU.mult, [ps_kv, small, r], [ck])
            for sub in range(0, n, 128):
                ns = min(128, n - sub)
                psv = K.ps()
                K.mm(psv.t[0:ns, :], ck.t[:, sub:sub + ns], wv_b.t[:, :], True, True, [ck, wv_b], [psv])
                v1 = rnext(vt)
                K.cp("act", v1.t[0:ns, :], psv.t[0:ns, :], [psv], [v1])
                K.dma(W["v_s"].t[s + sub:s + sub + ns, :], v1.t[0:ns, :], [v1], [W["v_s"]])
            pskr = K.ps()
            for kc in range(8):
                K.mm(pskr.t[0:96, 0:n], wkr_b.t[:, kc, :], he.t[:, kc, 2:2 + n], kc == 0, kc == 7, [wkr_b, he], [pskr])
            K.cp("act", krope.t[64:96, 0:n], pskr.t[64:96, 0:n], [pskr], [krope])
            for h in range(8):
                psk = K.ps()
                K.mm(psk.t[0:64, 0:n], wk_b.t[:, h * 64:(h + 1) * 64], ck.t[:, 0:n], True, True, [wk_b, ck], [psk])
                khf = rnext(kh)
                K.cp("act", khf.t[0:64, 0:n], psk.t[0:64, 0:n], [psk], [khf])
                K.cp("pool", khf.t[64:96, 0:n], krope.t[64:96, 0:n], [krope], [khf])
                _norm_rope_from_sb(K, khf, n, small, 33, rp, W_p96, sqb, khn, khb, rt1, rnext,
                                   W["kt_s"], W["kt_s"].t[h, :, s:s + n])
        if mode == "ownF":
            o = t["own"]
            for ch in range(4):
                ps = K.ps()
                for kc in range(8):
                    K.mm(ps.t[:, 0:n], w_in_b.t[:, kc, 512 + ch * 128:512 + (ch + 1) * 128], he.t[:, kc, 2:2 + n],
                         kc == 0, kc == 7, [w_in_b, he], [ps])
                K.act(GY.t[:, ch, o:o + n], ps.t[:, 0:n], AF.Gelu_apprx_tanh, [ps], [GY])
            psq = [K.ps(), K.ps()]
            sq2 = [rnext(sqb), rnext(sqb)]
            for c in range(2):
                for kc in range(8):
                    K.mm(psq[c].t[:, 0:n], w_in_b.t[:, kc, 1024 + c * 128:1024 + (c + 1) * 128], he.t[:, kc, 2:2 + n],
                         kc == 0, kc == 7, [w_in_b, he], [psq[c]])
                K.act(sq2[c].t[:, 0:n], psq[c].t[:, 0:n], AF.Square, [psq[c]], [sq2[c]])
            ps2 = K.ps()
            for c in range(2):
                K.mm(ps2.t[:, 0:n], K.ones_b.t[:, :], sq2[c].t[:, 0:n], c == 0, c == 1, [K.ones_b, sq2[c]], [ps2])
            r, rv = K.rstd_from_ps(ps2, ps2.t[:, 0:n], n, 256.0)
            qq = rnext(qn)
            for c in range(2):
                K.stt(qq.t[:, c, 0:n], psq[c].t[:, 0:n], small.t[:, 28 + c:29 + c], rv, ALU.mult, ALU.mult,
                      [psq[c], small, r], [qq])
            for h in range(8):
                psh = K.ps()
                for c in range(2):
                    K.mm(psh.t[0:96, 0:n], wuq_b.t[:, c, h * 96:(h + 1) * 96], qq.t[:, c, 0:n], c == 0, c == 1,
                         [wuq_b, qq], [psh])
                khf = rnext(kh)
                K.cp("act", khf.t[:, 0:n], psh.t[0:96, 0:n], [psh], [khf])
                _norm_rope_from_sb(K, khf, n, small, 32, rp, W_p96, sqb, khn, khb, rt1, rnext,
                                   W["qt_s"], W["qt_s"].t[h, :, o:o + n])
        dirs = (0, 1) if mode == "ctx" else ((0,) if mode == "ownF" else (1,))
        if mode == "ctx":
            ti = ctx_i[0]
            ctx_i[0] += 1
            t["ctx_idx"] = ti
        for d in dirs:
            for ch in range(4):
                if mode == "ctx":
                    xcs, xca = xc, xc.t[:, ch, 0:n]
                else:
                    xcs, xca = XC, XC.t[:, ch, t["own"]:t["own"] + n]
                psr_ = K.ps()
                K.mm(psr_.t[:, 0:n], bd_b.t[:, (d * 2 + 0) * 4 + ch, :], xca, True, True, [bd_b, xcs], [psr_])
                psi_ = K.ps()
                K.mm(psi_.t[:, 0:n], bd_b.t[:, (d * 2 + 1) * 4 + ch, :], xca, True, True, [bd_b, xcs], [psi_])
                rr = rnext(g_r)
                sc = sumr.t[:, (d * 4 + ch):(d * 4 + ch) + 1]
                K.act(rr.t[:, 0:n], psr_.t[:, 0:n], AF.Sigmoid, [psr_, small], [rr, sumr], bias=GB(d, 0, ch),
                      accum=sc if mode == "ctx" else None)
                ii = rnext(g_i)
                K.act(ii.t[:, 0:n], psi_.t[:, 0:n], AF.Sigmoid, [psi_, small], [ii], bias=GB(d, 1, ch))
                aa = rnext(g_a)
                K.act(aa.t[:, 0:n], rr.t[:, 0:n], AF.Exp, [rr, small], [aa], scale=SA(d, ch))
                qq_ = rnext(g_q)
                K.act(qq_.t[:, 0:n], rr.t[:, 0:n], AF.Exp, [rr, small], [qq_], scale=SA2(d, ch))
                K.act(qq_.t[:, 0:n], qq_.t[:, 0:n], AF.Ln, [qq_], [qq_], scale=-1.0, bias=K.one_t.t[:, 0:1])
                K.act(qq_.t[:, 0:n], qq_.t[:, 0:n], AF.Exp, [qq_], [qq_], scale=0.5)
                K.tt("dve", ii.t[:, 0:n], ii.t[:, 0:n], xca, ALU.mult, [ii, xcs], [ii])
                bb = rnext(g_b)
                K.tt("pool", bb.t[:, 0:n], qq_.t[:, 0:n], ii.t[:, 0:n], ALU.mult, [qq_, ii], [bb])
                hh = rnext(g_h)
                if mode == "ctx":
                    init = 0.0
                    rd = [aa, bb]
                else:
                    init = carry.t[:, d, ch:ch + 1]
                    rd = [aa, bb, carry]
                if d == 0:
                    K.S.op("dve", lambda e, hh=hh, aa=aa, bb=bb, init=init, n=n: e.tensor_tensor_scan(
                        out=hh.t[:, 0:n], data0=aa.t[:, 0:n], data1=bb.t[:, 0:n], initial=init, op0=ALU.mult,
                        op1=ALU.add), rd, [hh])
                    last = hh.t[:, n - 1:n]
                else:
                    K.S.op("dve", lambda e, hh=hh, aa=aa, bb=bb, init=init, n=n: e.tensor_tensor_scan(
                        out=hh.t[:, 0:n][:, ::-1], data0=aa.t[:, 0:n][:, ::-1], data1=bb.t[:, 0:n][:, ::-1],
                        initial=init, op0=ALU.mult, op1=ALU.add), rd, [hh])
                    last = hh.t[:, 0:1]
                if mode == "ctx":
                    K.cp("pool", AB.t[:, d, ti, 1, ch:ch + 1], last, [hh], [AB])
                    K.act(AB.t[:, d, ti, 0, ch:ch + 1], sc, AF.Exp, [sumr, small], [AB], scale=SA(d, ch))
                else:
                    o = t["own"]
                    K.cp("pool", carry.t[:, d, ch:ch + 1], last, [hh], [carry])
                    if d == 0:
                        K.cp("act", HF.t[:, ch, o:o + n], hh.t[:, 0:n], [hh], [HF])
                    else:
                        K.tt("dve", hh.t[:, 0:n], hh.t[:, 0:n], HF.t[:, ch, o:o + n], ALU.add, [hh, HF], [hh])
                        K.tt("pool", GY.t[:, ch, o:o + n], hh.t[:, 0:n], GY.t[:, ch, o:o + n], ALU.mult, [hh, GY], [GY])

    ctx_tiles = [t for t in tiles if t["role"] != "own"]
    own_tiles = [t for t in tiles if t["role"] != "ctx"]
    if limit.get("atiles"):
        own_tiles = own_tiles[:limit["atiles"]]
        ctx_tiles = ctx_tiles[:limit["atiles"]] if ctx_tiles else ctx_tiles
    for t in ctx_tiles:
        tileA(t, "ctx")

    def link_between(prev, t, d):
        if prev is None:
            return
        if d == 0:
            a = prev["s"] + prev["n"] - 1
            b_ = t["s"]
            if b_ <= a:
                b_ += S
            ks = crossed(sd["links"], S, a, b_)
        else:
            a = t["s"] + t["n"] - 1
            b_ = prev["s"]
            if b_ <= a:
                b_ += S
            ks = crossed(sd["links"], S, a, b_)
        apply_links("dve", carry.t[:, d, :], ks, carry)

    def chain(order, d):
        K.memset("pool", carry.t[:, d, :], 0.0, [carry])
        prev = None
        for t in order:
            link_between(prev, t, d)
            ti = t["ctx_idx"]
            K.tt("dve", carry.t[:, d, :], carry.t[:, d, :], AB.t[:, d, ti, 0, :], ALU.mult, [carry, AB], [carry])
            K.tt("dve", carry.t[:, d, :], carry.t[:, d, :], AB.t[:, d, ti, 1, :], ALU.add, [carry, AB], [carry])
            prev = t
        return prev

    halo = [t for t in tiles if t["role"] == "halo"]
    ctxo = [t for t in tiles if t["role"] == "ctx"]
    if kind == "p" and not limit.get("atiles"):
        fwd_order = [halo[1]] + ctxo
        prev = chain(fwd_order, 0)
    else:
        prev = None
    first = True
    for t in own_tiles:
        if kind == "p" or not first:
            link_between(prev, t, 0)
        first = False
        tileA(t, "ownF")
        prev = t
    if kind == "p" and not limit.get("atiles"):
        bwd_order = [halo[0]] + ctxo[::-1]
        prev = chain(bwd_order, 1)
    else:
        prev = None
        K.memset("pool", carry.t[:, 1, :], 0.0, [carry])
    first = True
    for t in own_tiles[::-1]:
        if kind == "p" or not first:
            link_between(prev, t, 1)
        first = False
        tileA(t, "ownB")
        prev = t


def _stageB(K, sd, si, xT, W, limit, GY):
    S = sd["S"]
    kind = sd["kind"]
    MV, GM = W["MV"], W["GM"]
    g1 = MV(0, si, 2)
    K.stage = K.ring(1, [128, 2048], F32, "stage")
    wo_rg_b = K.sb([128, 4, 1024], BF16, "wo_rg_b")
    for c in range(4):
        K.load_bf16(wo_rg_b, wo_rg_b.t[:, c, :], W["wo_rg"], W["wo_rg"].t[:, c, :], [128, 1024])
    wo_at_b = K.sb([64, 8, 1024], BF16, "wo_at_b")
    for c in range(0, 8, 2):
        K.load_bf16(wo_at_b, wo_at_b.t[:, c:c + 2, :], W["wo_at"], W["wo_at"].t[:, c:c + 2, :], [64, 2, 1024])
    QB = 512
    KBLK = 2048
    nkb = S // KBLK
    qtile = K.ring(2, [96, QB], BF16, "qtile")
    kblk = K.ring(2, [96, KBLK], BF16, "kblk")
    vblk = K.ring(2, [128, KBLK // 128, 65], BF16, "vblk")
    for vb in vblk:
        K.memset("pool", vb.t[:, :, 64:65], 1.0, [vb])
    pbuf = K.ring(3, [128, QB], BF16, "pbuf")
    at = [K.sb([64, QB], BF16, "at%d" % h) for h in range(8)]
    rd_t = K.sb([128, QB], F32, "rd")
    bcs = K.sb([64, QB], F32, "bcs")
    xq = K.ring(2, [128, 8, QB], F32, "xq")
    x1t = K.ring(2, [128, 8, QB], F32, "x1t")
    ps_s = [T(K.psf[i].t, K.psf[i].b) for i in range(3)]
    ps_o = [K.psf[3], K.psf[4]]
    ps_m = K.psf[5]
    scale = 96.0 ** -0.5
    own0 = sd["own0"]
    qtl = [[(1 + QB * k, QB)] for k in range(S_S // QB)]
    if kind == "p":
        qtl.append([(0, 1), (NOWN - 1, 1)])
    if limit.get("qtiles"):
        qtl = qtl[:limit["qtiles"]]
    it = 0
    for pieces in qtl:
        nq = sum(p[1] for p in pieces)
        xt_ = xq[it % 2]
        o = 0
        for (c0, ln) in pieces:
            cm = (own0 + c0) % S
            K.dma(xt_.t[:, :, o:o + ln], xT.t[:, cm:cm + ln].rearrange("(c p) t -> p c t", p=128), [xT], [xt_])
            o += ln
        for h in range(8):
            qt_ = qtile[(it * 8 + h) % 2]
            o = 0
            for (c0, ln) in pieces:
                K.dma(qt_.t[:, o:o + ln], W["qt_s"].t[h, :, c0:c0 + ln], [W["qt_s"]], [qt_])
                o += ln
            pso = ps_o[h % 2]
            nkt = S // 128
            jobs = []
            for kb_ in range(nkb):
                jobs.append(kb_)
            kcur = None
            vcur = None
            pend = None
            bi = 0
            for kt in range(nkt):
                if kt % (KBLK // 128) == 0:
                    kb_ = kt // (KBLK // 128)
                    kcur = kblk[(it * 8 * nkb + h * nkb + kb_) % 2]
                    vcur = vblk[(it * 8 * nkb + h * nkb + kb_) % 2]
                    K.dma(kcur.t[:, :], W["kt_s"].t[h, :, kb_ * KBLK:(kb_ + 1) * KBLK], [W["kt_s"]], [kcur])
                    K.dma(vcur.t[:, :, 0:64],
                          W["v_s"].t[kb_ * KBLK:(kb_ + 1) * KBLK, h * 64:(h + 1) * 64].rearrange("(k p) d -> p k d", p=128),
                          [W["v_s"]], [vcur])
                kk = kt % (KBLK // 128)
                pss = ps_s[kt % 3]
                K.mm(pss.t[:, 0:nq], kcur.t[:, kk * 128:(kk + 1) * 128], qt_.t[:, 0:nq], True, True, [kcur, qt_], [pss])
                if pend is not None:
                    (pb, pkt, pv, pkk) = pend
                    K.mm(pso.t[0:65, 0:nq], pv.t[:, pkk, :], pb.t[:, 0:nq], pkt == 0, False, [pv, pb], [pso])
                pb = pbuf[kt % 3]
                K.act(pb.t[:, 0:nq], pss.t[:, 0:nq], AF.Exp, [pss], [pb], scale=scale)
                pend = (pb, kt, vcur, kk)
            (pb, pkt, pv, pkk) = pend
            K.mm(pso.t[0:65, 0:nq], pv.t[:, pkk, :], pb.t[:, 0:nq], pkt == 0, True, [pv, pb], [pso])
            K.S.op("dve", lambda e, pso=pso, nq=nq: e.reciprocal(out=rd_t.t[64:65, 0:nq], in_=pso.t[64:65, 0:nq]),
                   [pso], [rd_t])
            psb_ = ps_m
            K.mm(psb_.t[0:64, 0:nq], W["ones_f"].t[64:65, 0:64], rd_t.t[64:65, 0:nq], True, True, [W["ones_f"], rd_t],
                 [psb_])
            K.cp("act", bcs.t[:, 0:nq], psb_.t[0:64, 0:nq], [psb_], [bcs])
            K.tt("dve", at[h].t[:, 0:nq], pso.t[0:64, 0:nq], bcs.t[:, 0:nq], ALU.mult, [pso, bcs], [at[h]])
        x1 = x1t[it % 2]
        for dc in range(8):
            psm = K.ps() if False else ps_m
            o = 0
            for (c0, ln) in pieces:
                for c in range(4):
                    K.mm(psm.t[:, o:o + ln], wo_rg_b.t[:, c, dc * 128:(dc + 1) * 128], GY.t[:, c, c0:c0 + ln],
                         c == 0, False, [wo_rg_b, GY], [psm])
                for h in range(8):
                    K.mm(psm.t[:, o:o + ln], wo_at_b.t[:, h, dc * 128:(dc + 1) * 128], at[h].t[:, o:o + ln],
                         False, h == 7, [wo_at_b, at[h]], [psm])
                o += ln
            K.stt(x1.t[:, dc, 0:nq], psm.t[:, 0:nq], g1[:, dc:dc + 1], xt_.t[:, dc, 0:nq], ALU.mult, ALU.add,
                  [psm, K.modv, xt_], [x1])
        o = 0
        for (c0, ln) in pieces:
            K.dma(W["x1"].t[:, c0:c0 + ln].rearrange("(c p) t -> p c t", p=128), x1.t[:, :, o:o + ln], [x1], [W["x1"]])
            o += ln
        it += 1


def _norm_rope_from_sb(K, khf, n, small, gcol, rp, p96_b, sqb, khn, khb, rt1, rnext, dst, dst_ap):
    sq = rnext(sqb)
    K.act(sq.t[0:96, 0:n], khf.t[:, 0:n], AF.Square, [khf], [sq])
    ps2 = K.ps()
    K.mm(ps2.t[0:96, 0:n], K.ones_b.t[0:96, 0:96], sq.t[0:96, 0:n], True, True, [K.ones_b, sq], [ps2])
    r, rv = K.rstd_from_ps(ps2, ps2.t[0:96, 0:n], n, 96.0, parts=96)
    kb = rnext(khb)
    K.stt(kb.t[:, 0:n], khf.t[:, 0:n], small.t[0:96, gcol:gcol + 1], rv, ALU.mult, ALU.mult, [khf, small, r], [kb])
    ps3 = K.ps()
    K.mm(ps3.t[0:96, 0:n], p96_b.t[:, :], kb.t[:, 0:n], True, True, [p96_b, kb], [ps3])
    t1 = rnext(rt1)
    K.tt("pool", t1.t[64:96, 0:n], kb.t[64:96, 0:n], rp.t[64:96, 0, 0:n], ALU.mult, [kb, rp], [t1])
    t2 = rnext(rt1)
    K.tt("dve", t2.t[64:96, 0:n], ps3.t[64:96, 0:n], rp.t[64:96, 1, 0:n], ALU.mult, [ps3, rp], [t2])
    K.tt("dve", kb.t[64:96, 0:n], t1.t[64:96, 0:n], t2.t[64:96, 0:n], ALU.add, [t1, t2, kb], [kb])
    K.dma(dst_ap, kb.t[:, 0:n], [kb], [dst])


def _mixer_c(K, si, kind, x2, x3, W, limit):
    MV, GM = W["MV"], W["GM"]
    LK, links = W["LK"], W["links"]
    gmod, sh, g1 = GM(1, si, 0), MV(1, si, 0), MV(1, si, 2)
    K.stage = K.ring(2, [128, 2048], F32, "stage")
    cw_b = K.sb([128, 8, 3072], BF16, "cw_b")
    for kc in range(8):
        for hf in range(2):
            K.load_bf16(cw_b, cw_b.t[:, kc, hf * 1536:(hf + 1) * 1536], W["c_w_in"],
                        W["c_w_in"].t[:, kc, hf * 1536:(hf + 1) * 1536], [128, 1536])
    cd_b = K.sb([128, 24, 128], BF16, "cd_b")
    for hf in range(2):
        K.load_bf16(cd_b, cd_b.t[:, hf * 12:(hf + 1) * 12, :], W["cdiag"], W["cdiag"].t[:, hf * 12:(hf + 1) * 12, :],
                    [128, 12, 128])
    co_b = K.sb([128, 8, 1024], BF16, "co_b")
    for kc in range(0, 8, 2):
        K.load_bf16(co_b, co_b.t[:, kc:kc + 2, :], W["c_w_out"], W["c_w_out"].t[:, kc:kc + 2, :], [128, 2, 1024])
    NE = NT + 2
    xe_r = K.ring(2, [128, 8, NE], F32, "cxe")
    he_r = K.ring(2, [128, 8, NE], BF16, "che")
    K.nm_sq = K.sb([128, 8, NE], BF16, "cnmsq")
    K.nm_tmp = K.sb([128, 8, NE], F32, "cnmtmp")
    cgs = K.ring(2, [128, NE], F32, "cgs")
    u_b = K.sb([128, 8, NE], BF16, "u_b")
    bg = K.sb([128, 8, NT], F32, "bg")
    y_b = K.ring(2, [128, 8, NT], BF16, "y_b")
    x3t = K.ring(2, [128, 8, NT], F32, "x3t")
    ntl = S_S // NT
    if limit.get("ctiles"):
        ntl = limit["ctiles"]
    for k in range(ntl):
        xe = xe_r[k % 2]
        K.dma(xe.t[:, :, :], x2.t[:, NT * k:NT * k + NE].rearrange("(c p) t -> p c t", p=128), [x2], [xe])
        he = he_r[k % 2]
        K.norm_mod(xe, NE, gmod, sh, he)
        for ch in range(8):
            psc = K.ps()
            for kc in range(8):
                K.mm(psc.t[:, 0:NE], cw_b.t[:, kc, 1024 + ch * 128:1024 + (ch + 1) * 128], he.t[:, kc, :], kc == 0,
                     kc == 7, [cw_b, he], [psc])
            psx = K.ps()
            for kc in range(8):
                K.mm(psx.t[:, 0:NE], cw_b.t[:, kc, 2048 + ch * 128:2048 + (ch + 1) * 128], he.t[:, kc, :], kc == 0,
                     kc == 7, [cw_b, he], [psx])
            cg = cgs[ch % 2]
            K.cp("act", cg.t[:, :], psc.t[:, 0:NE], [psc], [cg])
            K.tt("dve", u_b.t[:, ch, :], psx.t[:, 0:NE], cg.t[:, :], ALU.mult, [psx, cg], [u_b])
            psb_ = K.ps()
            for kc in range(8):
                K.mm(psb_.t[:, 0:NT], cw_b.t[:, kc, ch * 128:(ch + 1) * 128], he.t[:, kc, 1:1 + NT], kc == 0, kc == 7,
                     [cw_b, he], [psb_])
            K.cp("act", bg.t[:, ch, :], psb_.t[:, 0:NT], [psb_], [bg])
        if k == 0:
            key = "c" if kind == "p" else "zero"
            _lk(K, u_b, u_b.t[:, :, 0:1], key, LK, links)
        if k == S_S // NT - 1:
            key = "d" if kind == "p" else "zero"
            _lk(K, u_b, u_b.t[:, :, NE - 1:NE], key, LK, links)
        yb = y_b[k % 2]
        for ch in range(8):
            psv = K.ps()
            for tp in range(3):
                K.mm(psv.t[:, 0:NT], cd_b.t[:, ch * 3 + tp, :], u_b.t[:, ch, tp:tp + NT], tp == 0, tp == 2, [cd_b, u_b],
                     [psv])
            K.tt("dve", yb.t[:, ch, :], psv.t[:, 0:NT], bg.t[:, ch, :], ALU.mult, [psv, bg], [yb])
        x3_ = x3t[k % 2]
        for dc in range(8):
            psm = K.ps()
            for c in range(8):
                K.mm(psm.t[:, 0:NT], co_b.t[:, c, dc * 128:(dc + 1) * 128], yb.t[:, c, :], c == 0, c == 7, [co_b, yb],
                     [psm])
            K.stt(x3_.t[:, dc, :], psm.t[:, 0:NT], g1[:, dc:dc + 1], xe.t[:, dc, 1:1 + NT], ALU.mult, ALU.add,
                  [psm, K.modv, xe], [x3_])
        K.dma(x3.t[:, NT * k:NT * (k + 1)].rearrange("(c p) t -> p c t", p=128), x3_.t[:, :, :], [x3_], [x3])


def _lk(K, t, ap, key, LK, links):
    if key == "zero":
        K.ts("pool", ap, ap, 0.0, None, ALU.mult, None, [t], [t])
    else:
        K.ts("pool", ap, ap, LK[key], None, ALU.mult, None, [t, links], [t])


def _peer(K, l, si, xsrc, xdst, tlist, W, limit):
    MV, GM = W["MV"], W["GM"]
    gmod, sh, g2 = GM(l, si, 1), MV(l, si, 3), MV(l, si, 5)
    ident_b, ident_f = W["ident_b"], W["ident_f"]
    wq_b, uv_b = W["wq_b"], W["uv_b"]
    K.stage = K.ring(1, [128, 256], F32, "pstage")
    kT = K.sb([128, 2, 128], BF16, "kT")
    K.load_bf16(kT, kT.t[:], W["k12T"], W["k12T"].t[:, 2 * l:2 * l + 2, :], [128, 2, 128])
    GT = K.sb([128, NT, 128], BF16, "GT")
    RT = K.sb([128, 128, 128], BF16, "RT")
    OHT = K.sb([128, 128, 128], BF16, "OHT")
    ROr = K.ring(2, [128, 64, 128], BF16, "RO")
    s12 = K.sb([128, 16, 128], F32, "s12")
    e2 = K.sb([128, 8, 128], BF16, "e2")
    vv = K.sb([128, 16, 16], F32, "vv")
    vs = K.sb([128, 8, 16], F32, "vs")
    sm = K.sb([128, 8, 64], F32, "sm")
    h2 = K.sb([128, 8, NT], BF16, "h2")
    wqs = K.ring(2, [128, 8, 128], BF16, "wqs")
    NSB = 4
    strm = K.sb([128, NSB, 2048], BF16, "strm")
    sbuf_ = [Buf("strm%d" % i) for i in range(NSB)]
    uvs = [T(strm.t[:, i, :], sbuf_[i]) for i in range(NSB)]
    qTv = strm.t[:, :, :].rearrange("p a b -> p (a b)")[:, 0:16 * NT].rearrange("p (c t) -> p c t", t=NT)
    qTb = sbuf_[0:2]
    gel = K.ring(3, [128, NT], BF16, "gel")
    atb = K.ring(3, [128, NT], BF16, "atb")
    rtb = RT.t[:, :, :].rearrange("p a b -> p (a b)")
    rtf = rtb.bitcast(F32)
    ohf = OHT.t[:, :, :].rearrange("p a b -> p (a b)").bitcast(F32)
    K.nm_sq = T(rtb[:, 0:8 * NT].rearrange("p (c t) -> p c t", t=NT), RT.b)
    K.nm_tmp = T(ohf[:, 0:8 * NT].rearrange("p (c t) -> p c t", t=NT), OHT.b)
    GTB = [GT.b, Buf("GTb")]
    gtf = GT.t[:, 128:256, :].rearrange("p a b -> p (a b)").bitcast(F32)
    tmpT = T(gtf[:, 0:2048].rearrange("p (c j) -> p c j", j=128), GTB[1])
    e2fT = T(gtf[:, 2048:3072].rearrange("p (h j) -> p h j", j=128), GTB[1])
    tmp = tmpT.t
    e2f = e2fT.t
    xt = T(rtf[:, 4096:4096 + 8 * NT].rearrange("p (c t) -> p c t", t=NT), RT.b)
    cand = ohf[:, 0:2048].rearrange("p (h a b) -> p h a b", a=16, b=16)
    ctmp = ohf[:, 2048:4096].rearrange("p (h a b) -> p h a b", a=16, b=16)
    sel = ohf[:, 4096:6144].rearrange("p (h a b) -> p h a b", a=16, b=16)
    t1v = ohf[:, 6144:8192].rearrange("p (h a b) -> p h a b", a=16, b=16)
    osb = T(ohf[:, 0:1024], OHT.b)
    nchunk = limit.get("pchunks", 128)

    def load_x(pieces):
        o = 0
        for (c0, ln) in pieces:
            K.dma(xt.t[:, :, o:o + ln], xsrc.t[:, c0:c0 + ln].rearrange("(c p) t -> p c t", p=128), [xsrc], [xt])
            o += ln

    for pieces in tlist:
        nt = sum(p[1] for p in pieces)
        load_x(pieces)
        K.norm_mod(xt, nt, gmod, sh, h2)
        for c in range(16):
            wq_ = wqs[c % 2]
            K.dma(wq_.t[:, :, :], wq_b.t[l, :, :, c * 128:(c + 1) * 128], [wq_b], [wq_])
            ps = K.ps()
            for kc in range(8):
                K.mm(ps.t[:, 0:nt], wq_.t[:, kc, :], h2.t[:, kc, 0:nt], kc == 0, kc == 7, [wq_, h2], [ps])
            K.cp("act" if c % 2 == 0 else "dve", qTv[:, c, 0:nt], ps.t[:, 0:nt], [ps], qTb)
        for t0 in range(0, nt, 128):
            ns = min(128, nt - t0)
            for g in range(4):
                ps = K.ps()
                for m in range(4):
                    c = g * 4 + m
                    K.mm(ps.t[0:ns, m * 128:(m + 1) * 128], qTv[:, c, t0:t0 + ns], kT.t[:, c % 2, :], True, True,
                         qTb + [kT], [ps])
                K.cp("act", s12.t[0:ns, g * 4:(g + 1) * 4, :], ps.t[0:ns, :].rearrange("p (m j) -> p m j", j=128),
                     [ps], [s12])
            for c in range(16):
                K.S.op("dve", lambda e, c=c, ns=ns: e.max(out=vv.t[0:ns, c, 0:8], in_=s12.t[0:ns, c, :]), [s12], [vv])
            for c in range(16):
                K.S.op("dve", lambda e, c=c, ns=ns: e.match_replace(out=tmp[0:ns, c, :], in_to_replace=vv.t[0:ns, c, 0:8],
                                                                    in_values=s12.t[0:ns, c, :], imm_value=-BIG),
                       [s12, vv], [tmpT])
            for c in range(16):
                K.S.op("dve", lambda e, c=c, ns=ns: e.max(out=vv.t[0:ns, c, 8:16], in_=tmp[0:ns, c, :]), [tmpT], [vv])
            v4 = vv.t[:, :, :].rearrange("p (h w) a -> p h w a", w=2)
            v1 = v4[0:ns, :, 0, :]
            v2 = v4[0:ns, :, 1, :]
            s4 = s12.t[:, :, :].rearrange("p (h w) j -> p h w j", w=2)
            s1 = s4[0:ns, :, 0, :]
            s2 = s4[0:ns, :, 1, :]
            K.tt("pool", cand[0:ns], v1.unsqueeze(3).to_broadcast([ns, 8, 16, 16]),
                 v2.unsqueeze(2).to_broadcast([ns, 8, 16, 16]), ALU.add, [vv], [OHT])
            for h in range(8):
                K.S.op("dve", lambda e, h=h, ns=ns: e.max(out=vs.t[0:ns, h, 0:8], in_=cand[0:ns, h]), [OHT], [vs])
            for h in range(8):
                K.S.op("dve", lambda e, h=h, ns=ns: e.match_replace(out=ctmp[0:ns, h], in_to_replace=vs.t[0:ns, h, 0:8],
                                                                    in_values=cand[0:ns, h], imm_value=-BIG),
                       [OHT, vs], [OHT])
            for h in range(8):
                K.S.op("dve", lambda e, h=h, ns=ns: e.max(out=vs.t[0:ns, h, 8:16], in_=ctmp[0:ns, h]), [OHT], [vs])
            ev = sm.t[0:ns, :, 0:16]
            thr = sm.t[0:ns, :, 16:32]
            Z = sm.t[0:ns, :, 32]
            rZ = sm.t[0:ns, :, 33]
            w1 = sm.t[0:ns, :, 40:56]
            K.tt("pool", ev, vs.t[0:ns], vs.t[0:ns, :, 0:1].to_broadcast([ns, 8, 16]), ALU.subtract, [vs], [sm])
            K.act(ev, ev, AF.Exp, [sm], [sm])
            K.S.op("dve", lambda e, ev=ev, Z=Z: e.tensor_reduce(out=Z, in_=ev, op=ALU.add, axis=AX.X), [sm], [sm])
            K.S.op("dve", lambda e, Z=Z, rZ=rZ: e.reciprocal(out=rZ, in_=Z), [sm], [sm])
            K.tt("pool", w1, v1, v1[:, :, 0:1].to_broadcast([ns, 8, 16]), ALU.subtract, [vv], [sm])
            K.act(w1, w1, AF.Exp, [sm], [sm])
            K.tt("pool", w1, w1, rZ.unsqueeze(2).to_broadcast([ns, 8, 16]), ALU.mult, [sm], [sm])
            K.tt("pool", e2f[0:ns], s2, v2[:, :, 0:1].to_broadcast([ns, 8, 128]), ALU.subtract, [s12, vv], [e2fT])
            K.act(e2.t[0:ns], e2f[0:ns], AF.Exp, [e2fT], [e2])
            K.tt("dve", sel[0:ns], cand[0:ns], vs.t[0:ns, :, 15:16].unsqueeze(3).to_broadcast([ns, 8, 16, 16]),
                 ALU.is_ge, [OHT, vs], [OHT])
            K.tt("pool", t1v[0:ns], sel[0:ns], v2.unsqueeze(2).to_broadcast([ns, 8, 16, 16]), ALU.mult, [OHT, vv], [OHT])
            K.ts("pool", sel[0:ns], sel[0:ns], -BIG, BIG, ALU.mult, ALU.add, [OHT], [OHT])
            K.tt("pool", t1v[0:ns], t1v[0:ns], sel[0:ns], ALU.add, [OHT], [OHT])
            K.S.op("dve", lambda e, ns=ns, thr=thr: e.tensor_reduce(out=thr, in_=t1v[0:ns], op=ALU.min, axis=AX.X),
                   [OHT], [sm])
            rnd = 0
            for ih in range(2):
                RO = ROr[rnd % 2]
                rnd += 1
                is_ = slice(ih * 64, (ih + 1) * 64)
                for h in range(8):
                    for a_ in range(16):
                        K.ts("dve", RO.t[0:ns, :, h * 16 + a_], s1[:, h, is_], v1[:, h, a_:a_ + 1], w1[:, h, a_:a_ + 1],
                             ALU.is_equal, ALU.mult, [s12, vv, sm], [RO])
                _transposes(K, RO, OHT, ih, ns, ident_b)
            for jh in range(2):
                RO = ROr[rnd % 2]
                rnd += 1
                js = slice(jh * 64, (jh + 1) * 64)
                for h in range(8):
                    for a_ in range(16):
                        K.stt(RO.t[0:ns, :, h * 16 + a_], s2[:, h, js], thr[:, h, a_:a_ + 1], e2.t[0:ns, h, js],
                              ALU.is_ge, ALU.mult, [s12, sm, e2], [RO])
                _transposes(K, RO, RT, jh, ns, ident_b)
            for tb in range(0, ns, 4):
                nb = min(4, ns - tb)
                ps = K.ps()
                for u in range(nb):
                    K.mm(ps.t[:, u * 128:(u + 1) * 128], RT.t[:, :, tb + u], OHT.t[:, :, tb + u], True, True, [RT, OHT],
                         [ps])
                K.cp("act", GT.t[:, t0 + tb:t0 + tb + nb, :].rearrange("p t i -> p (t i)"), ps.t[:, 0:nb * 128],
                     [ps], [GTB[t0 // 128]])
        nsub = (nt + 127) // 128
        pso = [[K.psf[2 + 2 * s_ + dh] for dh in range(2)] for s_ in range(nsub)]
        pss = [K.psf[0], K.psf[1]]
        PF = NSB - 2

        def issue_dma(i):
            uv_ = uvs[i % NSB]
            K.dma(uv_.t, uv_b.t[l, i], [uv_b], [uv_])

        def front(i):
            uv_ = uvs[i % NSB]
            ps = pss[i % 2]
            for kc in range(8):
                K.mm(ps.t[:, 0:nt], uv_.t[:, kc * 128:(kc + 1) * 128], h2.t[:, kc, 0:nt], kc == 0, kc == 7,
                     [uv_, h2], [ps])
            gl = gel[i % 3]
            K.act(gl.t[:, 0:nt], ps.t[:, 0:nt], AF.Gelu_apprx_tanh, [ps], [gl])
            ab = atb[i % 3]
            K.tt("dve" if i % 3 else "pool", ab.t[:, 0:nt], gl.t[:, 0:nt], GT.t[:, 0:nt, i], ALU.mult, [gl] + GTB, [ab])

        def back(i):
            uv_ = uvs[i % NSB]
            ab = atb[i % 3]
            for s_ in range(nsub):
                ns = min(128, nt - s_ * 128)
                for dh in range(2):
                    K.mm(pso[s_][dh].t[0:ns, :], ab.t[:, s_ * 128:s_ * 128 + ns],
                         uv_.t[:, 1024 + dh * 512:1024 + (dh + 1) * 512], i == 0, i == nchunk - 1, [ab, uv_],
                         [pso[s_][dh]])

        for i in range(min(PF, nchunk)):
            issue_dma(i)
        for i in range(nchunk + 1):
            if i < nchunk:
                front(i)
            if i >= 1:
                back(i - 1)
            if i + PF < nchunk:
                issue_dma(i + PF)
        load_x(pieces)
        for s_ in range(nsub):
            ns = min(128, nt - s_ * 128)
            for dh in range(2):
                K.cp("act", osb.t[0:ns, dh * 512:(dh + 1) * 512], pso[s_][dh].t[0:ns, :], [pso[s_][dh]], [osb])
            for dc in range(8):
                ps = pss[dc % 2]
                K.tr(ps.t[:, 0:ns], osb.t[0:ns, dc * 128:(dc + 1) * 128], ident_f.t[0:ns, 0:ns], [osb, ident_f], [ps])
                K.stt(xt.t[:, dc, s_ * 128:s_ * 128 + ns], ps.t[:, 0:ns], g2[:, dc:dc + 1],
                      xt.t[:, dc, s_ * 128:s_ * 128 + ns], ALU.mult, ALU.add, [ps, K.modv, xt], [xt])
        o = 0
        for (c0, ln) in pieces:
            K.dma(xdst.t[:, c0:c0 + ln].rearrange("(c p) t -> p c t", p=128), xt.t[:, :, o:o + ln], [xt], [xdst])
            o += ln


def _transposes(K, RO, DST, half, ns, ident_b):
    for g in range(8):
        pb = K.psb[g % 2]
        for u in range(8):
            jj = g * 8 + u
            K.tr(pb.t[:, u * 128:u * 128 + ns], RO.t[0:ns, jj, :], ident_b.t[0:ns, 0:ns], [RO, ident_b], [pb])
        j0 = half * 64 + g * 8
        if ns == 128:
            K.cp("act", DST.t[:, j0:j0 + 8, :].rearrange("p j t -> p (j t)"), pb.t[:, :], [pb], [DST])
        else:
            K.cp("act", DST.t[:, j0:j0 + 8, 0:ns], pb.t[:, :].rearrange("p (j t) -> p j t", t=128)[:, :, 0:ns], [pb], [DST])


def _fm(v, n=8):
    return np.ascontiguousarray(np.asarray(v, np.float32).reshape(n, 128).T)


def _wmat(w):
    K_, N = w.shape
    return np.ascontiguousarray(np.asarray(w, np.float32).reshape(K_ // 128, 128, N).transpose(1, 0, 2))


def _rope_tables(pos):
    inv = (1.0 / (10000.0 ** (np.arange(0, 32, 2, dtype=np.float32) / np.float32(32)))).astype(np.float32)
    ang = pos.astype(np.float32)[:, None] * inv[None, :]
    c = np.cos(ang).astype(np.float32).T
    s = np.sin(ang).astype(np.float32).T
    out = np.empty((32, 2, pos.shape[0]), np.float32)
    out[0:16, 0] = c
    out[16:32, 0] = c
    out[0:16, 1] = s
    out[16:32, 1] = s
    return out


def host_inputs(I):
    f = np.float32
    shared = {}
    shared["ident"] = np.eye(128, dtype=f)
    p96 = np.zeros((96, 96), f)
    for m in range(16):
        p96[64 + m + 16, 64 + m] = -1.0
        p96[64 + m, 64 + m + 16] = 1.0
    shared["p96"] = p96
    shared["adaw"] = np.ascontiguousarray(I["ada_w"].reshape(2, 8, 128, 6144).transpose(0, 2, 1, 3))
    shared["adab"] = np.ascontiguousarray(I["ada_b"].reshape(2, 48, 128).transpose(2, 0, 1))
    shared["n1g"] = np.ascontiguousarray(I["norm1_g"].reshape(2, 8, 128).transpose(2, 0, 1))
    shared["n2g"] = np.ascontiguousarray(I["norm2_g"].reshape(2, 8, 128).transpose(2, 0, 1))
    shared["w_in"] = _wmat(I["ab_w_in"][0])
    wkr = np.zeros((128, 8, 96), f)
    wkr[:, :, 64:96] = shared["w_in"][:, :, 1408:1440]
    shared["wkr"] = wkr
    cw = I["rg_conv_w"][0]
    dg = np.zeros((128, 16, 128), f)
    for ch in range(4):
        for k in range(4):
            dg[np.arange(128), ch * 4 + k, np.arange(128)] = cw[k, ch * 128:(ch + 1) * 128]
    shared["rgdiag"] = dg
    shared["rgcb"] = _fm(I["rg_conv_b"][0], 4)
    bd = np.zeros((128, 16, 128), f)
    rgb = np.zeros((128, 16), f)
    for d in range(2):
        for wi, (wn, bn) in enumerate((("rg_wa", "rg_ba"), ("rg_wx", "rg_bx"))):
            for ch in range(4):
                idx = (d * 2 + wi) * 4 + ch
                for hh in range(2):
                    bd[hh * 64:(hh + 1) * 64, idx, hh * 64:(hh + 1) * 64] = I[wn][0, d, ch * 2 + hh]
                rgb[:, idx] = I[bn][0, d, ch * 128:(ch + 1) * 128]
    shared["rgbd"] = bd
    shared["rgb"] = rgb
    lam = np.zeros((128, 8), f)
    for d in range(2):
        lam[:, d * 4:(d + 1) * 4] = _fm(I["rg_lambda"][0, d], 4)
    shared["rglam"] = lam
    shared["qnorm"] = _fm(I["mla_q_norm"][0], 2)
    shared["w_uq"] = _wmat(I["mla_w_uq"][0])
    shared["kvnorm"] = _fm(I["mla_kv_norm"][0], 1)
    wukv = I["mla_w_ukv"][0].reshape(128, 8, 128)
    shared["wk"] = np.ascontiguousarray(wukv[:, :, 0:64].reshape(128, 512))
    shared["wv"] = np.ascontiguousarray(wukv[:, :, 64:128].reshape(128, 512))
    shared["qng"] = np.ascontiguousarray(np.stack([I["mla_qn_q"][0], I["mla_qn_k"][0]], axis=1).astype(f))
    wo = I["ab_w_out"][0]
    shared["wo_rg"] = _wmat(wo[0:512])
    shared["wo_at"] = np.ascontiguousarray(wo[512:1024].reshape(8, 64, 1024).transpose(1, 0, 2))
    shared["c_w_in"] = _wmat(I["c_w_in"][0])
    ccw = I["c_conv_w"][0]
    cd = np.zeros((128, 24, 128), f)
    for ch in range(8):
        for k in range(3):
            cd[np.arange(128), ch * 3 + k, np.arange(128)] = ccw[k, ch * 128:(ch + 1) * 128]
    shared["cdiag"] = cd
    shared["c_w_out"] = _wmat(I["c_w_out"][0])
    shared["wq"] = np.ascontiguousarray(I["peer_wq"].reshape(2, 8, 128, 2048).transpose(0, 2, 1, 3))
    k12 = np.zeros((128, 4, 128), f)
    for l in range(2):
        k12[:, 2 * l + 0, :] = I["peer_k1"][l].T
        k12[:, 2 * l + 1, :] = I["peer_k2"][l].T
    shared["k12T"] = k12
    U = I["peer_u"].reshape(2, 128, 128, 8, 128)
    shared["ut"] = np.ascontiguousarray(U.transpose(0, 1, 4, 3, 2)).reshape(2, 128, 128, 1024)
    shared["pv"] = np.ascontiguousarray(I["peer_v"].reshape(2, 128, 128, 1024))
    shared["rope_s"] = _rope_tables(np.arange(S_S))
    maps = []
    for c in range(8):
        b, q = c // 4, c % 4
        m = dict(shared)
        m["xs"] = np.ascontiguousarray(I["x_sample"][c].T)
        start = ((q + 1) * 4096 + 1) % S_P
        pos = (start + np.arange(S_P)) % S_P
        m["xp"] = np.ascontiguousarray(I["x_prompt"][b][pos].T)
        m["rope_p"] = _rope_tables(pos)
        cv = np.zeros((128, 8, 2), f)
        cv[:, :, 0] = _fm(I["c_sample"][c])
        cv[:, :, 1] = _fm(I["c_prompt"][b])
        m["cvec"] = cv
        lk = np.ones((128, 4), f)
        lk[:, {2: 0, 1: 1, 0: 2, 3: 3}[q]] = 0.0
        m["links"] = lk
        maps.append(m)
    return maps


_CACHE = {}


def kernel(**inputs):
    I = {k: np.asarray(v) for k, v in inputs.items()}
    maps = host_inputs(I)
    if "nc" not in _CACHE:
        _CACHE["nc"] = build()
    nc, K = _CACHE["nc"]
    res = run_bass_kernel_spmd(nc, maps, core_ids=list(range(8)))
    y_prompt = np.empty((2, S_P, 1024), np.float32)
    y_sample = np.empty((8, S_S, 1024), np.float32)
    for c in range(8):
        r = res.results[c]
        b, q = c // 4, c % 4
        y_sample[c] = r["ys"].T
        y_prompt[b, q * 4096:(q + 1) * 4096] = r["yp"].T
    return (y_prompt, y_sample)
```

```python
import os
import contextlib
import numpy as np
import concourse.bass as bass
import concourse.mybir as mybir
from concourse.bass_utils import run_bass_kernel_spmd

F32 = mybir.dt.float32
BF16 = mybir.dt.bfloat16
AF = mybir.ActivationFunctionType
ALU = mybir.AluOpType
AX = mybir.AxisListType
ENGS = ["pe", "act", "dve", "pool", "sp"]
EPS = 1e-6
BIG = 1.0e30

S_S = 4096
S_P = 16384
NOWN = 4098
ROT0 = 12286
TA = 256
NT = 256


class Buf:
    __slots__ = ("name", "last_w", "readers", "dsem", "dval")

    def __init__(self, name):
        self.name = name
        self.last_w = None
        self.readers = []
        self.dsem = None
        self.dval = 0


class T:
    __slots__ = ("t", "b")

    def __init__(self, t, b):
        self.t = t
        self.b = b

    def __getitem__(self, k):
        return self.t[k]


def _b(x):
    return x.b if isinstance(x, T) else x


class Sched:
    def __init__(self, nc, stack):
        self.nc = nc
        self.stack = stack
        self.ops = {e: [] for e in ENGS}
        self.cnt = {e: 0 for e in ENGS}
        self.sem = {e: stack.enter_context(nc.semaphore("sem_" + e)) for e in ENGS if e != "sp"}
        self.seen = {e: {} for e in ENGS}
        self.dma_bufs = []
        self.free_sems = []
        self.nsem_alloc = 0
        self.ninstr = 0

    def _deps(self, eng, reads, writes):
        deps = []
        own = self.sem.get(eng)
        for b in reads:
            if b.last_w is not None:
                deps.append(b.last_w)
        for b in writes:
            if b.last_w is not None and b.last_w[0] is not own:
                deps.append(b.last_w)
            deps.extend(r for r in b.readers if r[0] is not own)
        seen = self.seen[eng]
        best = {}
        pe_sem = self.sem["pe"]
        for (s, v) in deps:
            if eng == "pe" and s is pe_sem:
                continue
            k = id(s)
            if seen.get(k, 0) >= v:
                continue
            if k not in best or best[k][1] < v:
                best[k] = (s, v)
        for k, (s, v) in best.items():
            seen[k] = v
        return list(best.values())

    def op(self, eng, fn, reads=(), writes=()):
        reads = [_b(x) for x in reads]
        writes = [_b(x) for x in writes]
        deps = self._deps(eng, reads, writes)
        self.cnt[eng] += 1
        s = self.sem[eng]
        tok = (s, self.cnt[eng])
        self.ops[eng].append((deps, fn, s, 1))
        for b in writes:
            b.last_w = tok
            b.readers = []
        for b in reads:
            if b in writes:
                continue
            b.readers = [r for r in b.readers if r[0] is not s] + [tok]
        self.ninstr += 1

    def dma(self, out_ap, in_ap, reads=(), writes=(), eng="sp"):
        reads = [_b(x) for x in reads]
        writes = [_b(x) for x in writes]
        deps = self._deps(eng, reads, writes)
        tb = writes[0] if writes else reads[0]
        if tb.dsem is None:
            if self.free_sems:
                tb.dsem, tb.dval = self.free_sems.pop()
            else:
                self.nsem_alloc += 1
                tb.dsem = self.stack.enter_context(self.nc.semaphore("dsem%d" % self.nsem_alloc))
                tb.dval = 0
            self.dma_bufs.append(tb)
        tb.dval += 16
        tok = (tb.dsem, tb.dval)
        self.ops[eng].append((deps, lambda e: e.dma_start(out=out_ap, in_=in_ap, allow_slow_non_contiguous=True), tb.dsem, 16))
        for b in writes:
            b.last_w = tok
            b.readers = []
        for b in reads:
            b.readers = b.readers + [tok]
        self.ninstr += 1
        return tok

    def barrier(self):
        toks = [(self.sem[e], self.cnt[e]) for e in self.sem if self.cnt[e] > 0]
        toks += [(b.dsem, b.dval) for b in self.dma_bufs]
        for e in ENGS:
            deps = []
            for (s, v) in toks:
                if e in self.sem and s is self.sem[e]:
                    continue
                if self.seen[e].get(id(s), 0) >= v:
                    continue
                self.seen[e][id(s)] = v
                deps.append((s, v))
            self.ops[e].append((deps, None, None, 0))
        for b in self.dma_bufs:
            self.free_sems.append((b.dsem, b.dval))
            b.dsem = None
        self.dma_bufs = []

    def flush(self, final_waits=()):
        nc = self.nc
        engmap = {"pe": "tensor", "act": "scalar", "dve": "vector", "pool": "gpsimd", "sp": "sync"}
        with nc.Block() as block:
            for e in ENGS:
                ops = self.ops[e]
                fw = list(final_waits) if e == "sp" else []

                def body(eng, ops=ops, fw=fw):
                    for (deps, fn, s, inc) in ops:
                        for (ds, dv) in deps:
                            eng.wait_ge(ds, dv)
                        if fn is not None:
                            fn(eng).then_inc(s, inc)
                    for (ds, dv) in fw:
                        eng.wait_ge(ds, dv)
                getattr(block, engmap[e])(body)
        self.ops = {e: [] for e in ENGS}


class KB:
    def __init__(self, nc, st, debug):
        self.nc = nc
        self.st = st
        self.S = Sched(nc, st)
        self.ph = None
        self.uid = 0
        self.debug = debug
        self.inputs = {}
        self.outputs = {}
        self.psr = 0

    def sb(self, shape, dt=F32, name="t", buf=None):
        self.uid += 1
        nm = "%s_%d" % (name, self.uid)
        t = (self.ph or self.st).enter_context(self.nc.sbuf_tensor(nm, list(shape), dt))
        return T(t, buf if buf is not None else Buf(nm))

    def ring(self, n, shape, dt=F32, name="r"):
        return [self.sb(shape, dt, name) for _ in range(n)]

    def inp(self, name, shape, dt=F32):
        t = T(self.nc.dram_tensor(name, list(shape), dt, kind="ExternalInput").ap(), Buf(name))
        self.inputs[name] = t
        return t

    def outp(self, name, shape, dt=F32):
        t = T(self.nc.dram_tensor(name, list(shape), dt, kind="ExternalOutput").ap(), Buf(name))
        self.outputs[name] = t
        return t

    def scratch(self, name, shape, dt):
        kind = "ExternalOutput" if (self.debug and name in self.debug) else "Internal"
        t = T(self.nc.dram_tensor(name, list(shape), dt, kind=kind).ap(), Buf(name))
        if kind == "ExternalOutput":
            self.outputs[name] = t
        return t

    @contextlib.contextmanager
    def phase(self):
        with contextlib.ExitStack() as ph:
            old = self.ph
            self.ph = ph
            yield
            self.S.barrier()
            self.S.flush()
            self.ph = old

    def ps(self):
        b = self.psf[self.psr % len(self.psf)]
        self.psr += 1
        return b

    def mm(self, out, lhsT, rhs, start, stop, reads, writes):
        self.S.op("pe", lambda e: e.matmul(out, lhsT=lhsT, rhs=rhs, start=start, stop=stop), reads, writes)

    def tr(self, out, in_, ident, reads, writes):
        self.S.op("pe", lambda e: e.transpose(out=out, in_=in_, identity=ident), reads, writes)

    def act(self, out, in_, func, reads, writes, scale=None, bias=None, accum=None):
        kw = {}
        if scale is not None:
            kw["scale"] = scale
        if bias is not None:
            kw["bias"] = bias
        if accum is not None:
            kw["accum_out"] = accum
        self.S.op("act", lambda e: e.activation(out=out, in_=in_, func=func, **kw), reads, writes)

    def ts(self, eng, out, in0, s1, s2, op0, op1, reads, writes):
        if op1 is None:
            self.S.op(eng, lambda e: e.tensor_scalar(out=out, in0=in0, scalar1=s1, scalar2=None, op0=op0), reads, writes)
        else:
            self.S.op(eng, lambda e: e.tensor_scalar(out=out, in0=in0, scalar1=s1, scalar2=s2, op0=op0, op1=op1),
                      reads, writes)

    def tt(self, eng, out, in0, in1, op, reads, writes):
        self.S.op(eng, lambda e: e.tensor_tensor(out=out, in0=in0, in1=in1, op=op), reads, writes)

    def stt(self, out, in0, scalar, in1, op0, op1, reads, writes):
        self.S.op("dve", lambda e: e.scalar_tensor_tensor(out=out, in0=in0, scalar=scalar, in1=in1, op0=op0, op1=op1),
                  reads, writes)

    def cp(self, eng, out, in_, reads, writes):
        if eng == "act":
            self.S.op("act", lambda e: e.copy(out=out, in_=in_), reads, writes)
        else:
            self.S.op(eng, lambda e: e.tensor_copy(out=out, in_=in_), reads, writes)

    def memset(self, eng, out, val, writes):
        self.S.op(eng, lambda e: e.memset(out, val), (), writes)

    def dma(self, out, in_, reads, writes):
        self.S.dma(out, in_, reads, writes)

    def load_bf16(self, dst, dst_ap, src, src_ap, shape):
        stg = self.stage[self.stage_i % len(self.stage)]
        self.stage_i += 1
        n = int(np.prod(shape[1:]))
        sv = stg.t[0:shape[0], 0:n]
        if len(shape) == 3:
            sv = sv.rearrange("p (a b) -> p a b", b=shape[2])
        self.dma(sv, src_ap, [src], [stg])
        eng = ["act", "dve", "pool"][self.stage_i % 3]
        self.cp(eng, dst_ap, sv, [stg], [dst])

    def rstd_from_ps(self, ps, ps_ap, n, nfeat, parts=128):
        r = self.rs_ring[self.rs_i % len(self.rs_ring)]
        self.rs_i += 1
        rv = r.t[0:parts, 0:n]
        self.act(rv, ps_ap, AF.Ln, [ps], [r], scale=1.0 / nfeat, bias=self.eps_t.t[0:parts, 0:1])
        self.act(rv, rv, AF.Exp, [r], [r], scale=-0.5)
        return r, rv

    def norm_mod(self, x, n, gmod, sh, h, inplace=False):
        sq = self.nm_sq
        self.act(sq.t[:, :, 0:n], x.t[:, :, 0:n], AF.Square, [x], [sq])
        ps = self.ps()
        for c in range(8):
            self.mm(ps.t[:, 0:n], self.ones_b.t[:, :], sq.t[:, c, 0:n], c == 0, c == 7, [self.ones_b, sq], [ps])
        r, rv = self.rstd_from_ps(ps, ps.t[:, 0:n], n, 1024.0)
        tmp = x if inplace else self.nm_tmp
        self.tt("dve", tmp.t[:, :, 0:n], x.t[:, :, 0:n], rv.unsqueeze(1).to_broadcast([128, 8, n]), ALU.mult,
                [x, r], [tmp])
        for c in range(8):
            if c % 2 == 0:
                self.act(h.t[:, c, 0:n], tmp.t[:, c, 0:n], AF.Identity, [tmp, self.modv], [h],
                         scale=gmod[:, c:c + 1], bias=sh[:, c:c + 1])
            else:
                self.ts("pool", h.t[:, c, 0:n], tmp.t[:, c, 0:n], gmod[:, c:c + 1], sh[:, c:c + 1], ALU.mult, ALU.add,
                        [tmp, self.modv], [h])


def seq_desc(kind):
    d = {}
    if kind == "p":
        S = S_P
        segs = [("ctx", 0, 4095), ("ctx", 4095, 8191), ("ctx", 8191, 12286), ("halo", 12286, 12287),
                ("own", 12287, 16383), ("halo", 16383, 16384)]
        links = {4095: "a", 8191: "b", 12287: "c", 16383: "d"}
        own0 = ROT0
    else:
        S = S_S
        segs = [("own", 0, 4096)]
        links = {0: "zero"}
        own0 = -1
    tiles = []
    for (role, a, b) in segs:
        s = a
        while s < b:
            n = min(TA, b - s)
            tiles.append(dict(s=s, n=n, role=role, own=(s - own0) if role != "ctx" else None))
            s += n
    d.update(S=S, tiles=tiles, links=links, own0=own0, kind=kind)
    return d


def crossed(links, S, c_from, c_to):
    keys = []
    for p in range(c_from + 1, c_to + 1):
        k = links.get(p % S)
        if k is not None:
            keys.append(k)
    return keys


def build(debug=None, limit=None):
    debug = debug or {}
    limit = limit or {}
    nc = bass.Bass("TRN2", target_bir_lowering=False)
    with contextlib.ExitStack() as st:
        K = KB(nc, st, debug)
        _program(K, limit)
        finals = [t.b.last_w for t in K.outputs.values() if t.b.last_w is not None]
        K.S.flush(finals)
        print("kernel instrs:", K.S.ninstr, "dma sems:", K.S.nsem_alloc, flush=True)
    return nc, K


def _program(K, limit):
    nc = K.nc
    xs = K.inp("xs", [1024, S_S])
    xp = K.inp("xp", [1024, S_P])
    cvec = K.inp("cvec", [128, 8, 2])
    links_d = K.inp("links", [128, 4])
    rope_s = K.inp("rope_s", [32, 2, S_S])
    rope_p = K.inp("rope_p", [32, 2, S_P])
    ident_d = K.inp("ident", [128, 128])
    p96_d = K.inp("p96", [96, 96])
    adaw = K.inp("adaw", [2, 128, 8, 6144])
    adab = K.inp("adab", [128, 2, 48])
    n1g = K.inp("n1g", [128, 2, 8])
    n2g = K.inp("n2g", [128, 2, 8])
    w_in = K.inp("w_in", [128, 8, 1440])
    wkr = K.inp("wkr", [128, 8, 96])
    rgdiag = K.inp("rgdiag", [128, 16, 128])
    rgcb = K.inp("rgcb", [128, 4])
    rgbd = K.inp("rgbd", [128, 16, 128])
    rgb = K.inp("rgb", [128, 16])
    rglam = K.inp("rglam", [128, 8])
    qnorm = K.inp("qnorm", [128, 2])
    w_uq = K.inp("w_uq", [128, 2, 768])
    kvnorm = K.inp("kvnorm", [128, 1])
    wk = K.inp("wk", [128, 512])
    wv = K.inp("wv", [128, 512])
    qng = K.inp("qng", [96, 2])
    wo_rg = K.inp("wo_rg", [128, 4, 1024])
    wo_at = K.inp("wo_at", [64, 8, 1024])
    c_w_in = K.inp("c_w_in", [128, 8, 3072])
    cdiag = K.inp("cdiag", [128, 24, 128])
    c_w_out = K.inp("c_w_out", [128, 8, 1024])
    wq_d = K.inp("wq", [2, 128, 8, 2048])
    k12T = K.inp("k12T", [128, 4, 128])
    ut_d = K.inp("ut", [2, 128, 128, 1024])
    v_d = K.inp("pv", [2, 128, 128, 1024])
    ys = K.outp("ys", [1024, S_S])
    yp = K.outp("yp", [1024, S_S])

    kt_s = K.scratch("kt_s", [8, 96, S_P], BF16)
    v_s = K.scratch("v_s", [S_P, 512], BF16)
    qt_s = K.scratch("qt_s", [8, 96, NOWN], BF16)
    x1_s = {k: K.scratch("x1_" + k, [1024, NOWN], F32) for k in "sp"}
    x2_s = {k: K.scratch("x2_" + k, [1024, NOWN], F32) for k in "sp"}
    x3_s = {k: K.scratch("x3_" + k, [1024, S_S], F32) for k in "sp"}
    wq_b = K.scratch("wq_b", [2, 16, 128, 8, 128], BF16)
    uv_b = K.scratch("uv_b", [2, 128, 128, 2048], BF16)

    K.psf = [T(st_enter(K, nc.psum_tensor("psf%d" % i, [128, 512], F32)), Buf("psf%d" % i)) for i in range(6)]
    K.psb = [T(st_enter(K, nc.psum_tensor("psb%d" % i, [128, 1024], BF16)), Buf("psb%d" % i)) for i in range(2)]

    ident_f = K.sb([128, 128], F32, "identf")
    ident_b = K.sb([128, 128], BF16, "identb")
    K.ones_b = K.sb([128, 128], BF16, "onesb")
    ones_f = K.sb([128, 128], F32, "onesf")
    K.eps_t = K.sb([128, 1], F32, "eps")
    K.modv = K.sb([128, 2, 2, 6, 8], F32, "modv")
    gm = K.sb([128, 2, 2, 2, 8], F32, "gm")
    links = K.sb([128, 4], F32, "links")
    K.rs_ring = K.ring(3, [128, 264], F32, "rstd")
    K.rs_i = 0
    K.stage_i = 0
    K.one_t = K.sb([128, 1], F32, "one")
    K.memset("pool", K.one_t.t[:], 1.0, [K.one_t])
    modv = K.modv
    K.memset("pool", K.ones_b.t[:], 1.0, [K.ones_b])
    K.memset("pool", ones_f.t[:], 1.0, [ones_f])
    K.memset("pool", K.eps_t.t[:], EPS, [K.eps_t])
    K.dma(ident_f.t[:], ident_d.t[:, :], [ident_d], [ident_f])
    K.cp("dve", ident_b.t[:], ident_f.t[:], [ident_f], [ident_b])
    K.dma(links.t[:], links_d.t[:, :], [links_d], [links])

    LK = {"a": links.t[:, 0:1], "b": links.t[:, 1:2], "c": links.t[:, 2:3], "d": links.t[:, 3:4]}

    def apply_links(eng, ap, keys, t, parts=128):
        for k in keys:
            if k == "zero":
                K.ts(eng, ap, ap, 0.0, None, ALU.mult, None, [t], [t])
            else:
                K.ts(eng, ap, ap, LK[k][0:parts], None, ALU.mult, None, [t, links], [t])

    with K.phase():
        K.stage = K.ring(2, [128, 4096], F32, "stage")
        K.castb = K.ring(2, [128, 4096], BF16, "castb")
        zt = K.sb([128, 8, 1], F32, "zt")
        K.memset("pool", zt.t[:], 0.0, [zt])
        for cc in (0, NOWN - 1):
            K.dma(x2_s["s"].t[:, cc:cc + 1].rearrange("(c p) t -> p c t", p=128), zt.t[:], [zt], [x2_s["s"]])
        cv = K.sb([128, 8, 2], F32, "cv")
        K.dma(cv.t[:], cvec.t[:, :, :], [cvec], [cv])
        scv = K.sb([128, 8, 2], F32, "scv")
        K.act(scv.t[:], cv.t[:], AF.Silu, [cv], [scv])
        adb = K.sb([128, 2, 48], F32, "adb")
        K.dma(adb.t[:], adab.t[:, :, :], [adab], [adb])
        g12 = K.sb([128, 2, 2, 8], F32, "g12")
        K.dma(g12.t[:, 0], n1g.t[:, :, :], [n1g], [g12])
        K.dma(g12.t[:, 1], n2g.t[:, :, :], [n2g], [g12])
        awr = K.ring(2, [128, 8, 512], F32, "adaw")
        for l in range(2):
            psm = K.ps()
            for g in range(12):
                aw = awr[g % 2]
                K.dma(aw.t[:], adaw.t[l, :, :, g * 512:(g + 1) * 512], [adaw], [aw])
                for jj in range(4):
                    j = g * 4 + jj
                    for kc in range(8):
                        K.mm(psm.t[:, 2 * j:2 * j + 2], aw.t[:, kc, jj * 128:(jj + 1) * 128], scv.t[:, kc, :],
                             kc == 0, kc == 7, [aw, scv], [psm])
            for s in range(2):
                K.tt("dve", modv.t[:, l, s].rearrange("p w c -> p (w c)"),
                     psm.t[:, 0:96].rearrange("p (j s) -> p j s", s=2)[:, :, s], adb.t[:, l, :], ALU.add,
                     [psm, adb], [modv])
        for l in range(2):
            for s in range(2):
                for k in range(2):
                    K.stt(gm.t[:, l, s, k, :], modv.t[:, l, s, 1 + 3 * k, :], 1.0, g12.t[:, k, l, :], ALU.add, ALU.mult,
                          [modv, g12], [gm])
        if not limit.get("skip_prep"):
            cast_i = 0
            for l in range(2):
                jobs = [(wq_d.t[l], None, wq_d, wq_b, 4)]
                for i0 in range(0, 128, 4):
                    jobs.append((ut_d.t[l, i0:i0 + 4].rearrange("i p f -> p i f"),
                                 uv_b.t[l, i0:i0 + 4, :, 0:1024].rearrange("i p f -> p i f"), ut_d, uv_b, 1))
                    jobs.append((v_d.t[l, i0:i0 + 4].rearrange("i p f -> p i f"),
                                 uv_b.t[l, i0:i0 + 4, :, 1024:2048].rearrange("i p f -> p i f"), v_d, uv_b, 1))
                for (src, dst, srcT, dstT, nsplit) in jobs:
                    for sp_ in range(nsplit):
                        if nsplit == 1:
                            s_ap, d_ap = src, dst
                            shp = [128, 4, 1024]
                        else:
                            s_ap = src[:, 2 * sp_:2 * sp_ + 2, :]
                            d_ap = wq_b.t[l, :, :, 2 * sp_:2 * sp_ + 2, :].rearrange("c p k j -> p k c j")
                            shp = [128, 2, 2048]
                        stg = K.stage[cast_i % 2]
                        sv = stg.t[:, :].rearrange("p (a b) -> p a b", b=shp[2])
                        K.dma(sv, s_ap, [srcT], [stg])
                        cb = K.castb[cast_i % 2]
                        cbv = cb.t[:, :].rearrange("p (a b) -> p a b", b=shp[2])
                        K.cp(["act", "dve"][cast_i % 2], cbv, sv, [stg], [cb])
                        if nsplit == 1:
                            K.dma(d_ap, cbv, [cb], [dstT])
                        else:
                            for kk in range(2):
                                K.dma(wq_b.t[l, :, :, 2 * sp_ + kk, :].rearrange("c p j -> p c j"),
                                      cbv[:, kk, :].rearrange("p (c j) -> p c j", j=128), [cb], [dstT])
                        cast_i += 1

    def MV(l, s, which):
        return modv.t[:, l, s, which, :]

    def GM(l, s, k):
        return gm.t[:, l, s, k, :]

    seqs = [("s", 0, xs, rope_s, ys), ("p", 1, xp, rope_p, yp)]
    if limit.get("seqs"):
        seqs = [q for q in seqs if q[0] in limit["seqs"]]

    for (kind, si, xT, rope_d, yout) in seqs:
        sd = seq_desc(kind)
        S = sd["S"]
        with K.phase():
            _layer0_mixer(K, sd, si, xT, rope_d, dict(
                w_in=w_in, wkr=wkr, rgdiag=rgdiag, rgcb=rgcb, rgbd=rgbd, rgb=rgb, rglam=rglam, qnorm=qnorm, w_uq=w_uq,
                kvnorm=kvnorm, wk=wk, wv=wv, qng=qng, wo_rg=wo_rg, wo_at=wo_at, p96=p96_d, ident_b=ident_b,
                ones_f=ones_f, kt_s=kt_s, v_s=v_s, qt_s=qt_s, x1=x1_s[kind], MV=MV, GM=GM, apply_links=apply_links,
                links=links, LK=LK), limit)
        if limit.get("stop") in ("stageA", "mixer0"):
            continue
        with K.phase():
            tl = [[(1 + NT * k, NT)] for k in range(S_S // NT)]
            if kind == "p":
                tl.append([(0, 1), (NOWN - 1, 1)])
            if limit.get("ptiles"):
                tl = tl[:limit["ptiles"]]
            _peer(K, 0, si, x1_s[kind], x2_s[kind], tl, dict(wq_b=wq_b, uv_b=uv_b, k12T=k12T, ident_b=ident_b,
                                                             ident_f=ident_f, MV=MV, GM=GM), limit)
        if limit.get("stop") == "peer0":
            continue
        with K.phase():
            _mixer_c(K, si, kind, x2_s[kind], x3_s[kind], dict(c_w_in=c_w_in, cdiag=cdiag, c_w_out=c_w_out, MV=MV, GM=GM,
                                                                LK=LK, links=links), limit)
        if limit.get("stop") == "mixer1":
            continue
        with K.phase():
            tl = [[(NT * k, NT)] for k in range(S_S // NT)]
            if limit.get("ptiles"):
                tl = tl[:limit["ptiles"]]
            _peer(K, 1, si, x3_s[kind], yout, tl, dict(wq_b=wq_b, uv_b=uv_b, k12T=k12T, ident_b=ident_b,
                                                       ident_f=ident_f, MV=MV, GM=GM), limit)


def st_enter(K, cm):
    return K.st.enter_context(cm)


def _layer0_mixer(K, sd, si, xT, rope_d, W, limit):
    S = sd["S"]
    kind = sd["kind"]
    tiles = sd["tiles"]
    MV, GM = W["MV"], W["GM"]
    apply_links = W["apply_links"]
    LK = W["LK"]
    links = W["links"]
    NE = TA + 3
    GY = K.sb([128, 4, NOWN], BF16, "GY")
    _stageA(K, sd, si, xT, rope_d, W, limit, GY)
    if limit.get("stop") == "stageA":
        dbg = K.outp("dbg_rg_" + kind, [128, 4, NOWN], BF16)
        K.dma(dbg.t[:, :, :], GY.t[:], [GY], [dbg])
        K.S.barrier()
        return
    with K.phase():
        _stageB(K, sd, si, xT, W, limit, GY)


def _stageA(K, sd, si, xT, rope_d, W, limit, GY):
  with K.phase():
    S = sd["S"]
    kind = sd["kind"]
    tiles = sd["tiles"]
    MV, GM = W["MV"], W["GM"]
    apply_links = W["apply_links"]
    LK = W["LK"]
    links = W["links"]
    NE = TA + 3
    K.stage = K.ring(1, [128, 2048], F32, "stage")
    w_in_b = K.sb([128, 8, 1440], BF16, "w_in_b")
    for kc in range(8):
        K.load_bf16(w_in_b, w_in_b.t[:, kc, :], W["w_in"], W["w_in"].t[:, kc, :], [128, 1440])
    wkr_b = K.sb([128, 8, 96], BF16, "wkr_b")
    K.load_bf16(wkr_b, wkr_b.t[:], W["wkr"], W["wkr"].t[:, :, :], [128, 8, 96])
    dg_b = K.sb([128, 16, 128], BF16, "dg_b")
    K.load_bf16(dg_b, dg_b.t[:], W["rgdiag"], W["rgdiag"].t[:, :, :], [128, 16, 128])
    bd_b = K.sb([128, 16, 128], BF16, "bd_b")
    K.load_bf16(bd_b, bd_b.t[:], W["rgbd"], W["rgbd"].t[:, :, :], [128, 16, 128])
    wuq_b = K.sb([128, 2, 768], BF16, "wuq_b")
    K.load_bf16(wuq_b, wuq_b.t[:], W["w_uq"], W["w_uq"].t[:, :, :], [128, 2, 768])
    wk_b = K.sb([128, 512], BF16, "wk_b")
    K.load_bf16(wk_b, wk_b.t[:], W["wk"], W["wk"].t[:, :], [128, 512])
    wv_b = K.sb([128, 512], BF16, "wv_b")
    K.load_bf16(wv_b, wv_b.t[:], W["wv"], W["wv"].t[:, :], [128, 512])
    p96_b = K.sb([96, 96], BF16, "p96_b")
    K.load_bf16(p96_b, p96_b.t[:], W["p96"], W["p96"].t[:, :], [96, 96])
    small = K.sb([128, 64], F32, "small")
    K.dma(small.t[:, 0:4], W["rgcb"].t[:, :], [W["rgcb"]], [small])
    K.dma(small.t[:, 4:20], W["rgb"].t[:, :], [W["rgb"]], [small])
    K.dma(small.t[:, 20:28], W["rglam"].t[:, :], [W["rglam"]], [small])
    K.dma(small.t[:, 28:30], W["qnorm"].t[:, :], [W["qnorm"]], [small])
    K.dma(small.t[:, 30:31], W["kvnorm"].t[:, :], [W["kvnorm"]], [small])
    K.dma(small.t[0:96, 32:34], W["qng"].t[:, :], [W["qng"]], [small])
    K.act(small.t[:, 52:60], small.t[:, 20:28], AF.Exp, [small], [small], scale=-1.0)
    K.act(small.t[:, 52:60], small.t[:, 52:60], AF.Ln, [small], [small], bias=K.one_t.t[:, 0:1])
    K.ts("dve", small.t[:, 36:44], small.t[:, 52:60], -8.0, None, ALU.mult, None, [small], [small])
    K.ts("dve", small.t[:, 44:52], small.t[:, 52:60], -16.0, None, ALU.mult, None, [small], [small])

    def SA(d, ch):
        return small.t[:, 36 + d * 4 + ch:37 + d * 4 + ch]

    def SA2(d, ch):
        return small.t[:, 44 + d * 4 + ch:45 + d * 4 + ch]

    def GB(d, which, ch):
        j = 4 + (d * 2 + which) * 4 + ch
        return small.t[:, j:j + 1]

    gmod1, sh1, g1 = GM(0, si, 0), MV(0, si, 0), MV(0, si, 2)

    XC = K.sb([128, 4, NOWN], BF16, "XC")
    HF = K.sb([128, 4, NOWN], BF16, "HF")
    nctx = sum(1 for t in tiles if t["role"] != "own")
    AB = K.sb([128, 2, max(nctx, 1), 2, 4], F32, "AB")
    carry = K.sb([128, 2, 4], F32, "carry")
    K.memset("pool", carry.t[:], 0.0, [carry])

    xe_r = K.ring(1, [128, 8, NE], F32, "xe")
    he_r = K.ring(2, [128, 8, NE], BF16, "he")
    K.nm_sq = K.sb([128, 8, NE], BF16, "nmsq")
    xr_b = K.sb([128, 4, NE], BF16, "xr_b")
    xc_t = K.ring(2, [128, 4, TA], BF16, "xc_t")
    g_r = K.ring(2, [128, TA], F32, "g_r")
    g_i = K.ring(2, [128, TA], F32, "g_i")
    g_a = K.ring(2, [128, TA], F32, "g_a")
    g_q = K.ring(2, [128, TA], F32, "g_q")
    g_b = K.ring(2, [128, TA], F32, "g_b")
    g_h = K.ring(2, [128, TA], F32, "g_h")
    sumr = K.sb([128, 64], F32, "sumr")
    sqb = K.ring(2, [128, TA], BF16, "sqb")
    ckv = K.ring(2, [128, TA], BF16, "ckv")
    qn = K.ring(2, [128, 2, TA], BF16, "qn")
    vt = K.ring(2, [128, 512], BF16, "vt")
    krope = K.sb([96, TA], F32, "krope")
    kh = K.ring(2, [96, TA], F32, "kh")
    khn = K.ring(2, [96, TA], F32, "khn")
    khb = K.ring(3, [96, TA], BF16, "khb")
    rt1 = K.ring(2, [96, TA], F32, "rt1")
    rope_t = K.ring(2, [96, 2, TA], F32, "rope")
    gi = [0]

    def rnext(r):
        gi[0] += 1
        return r[gi[0] % len(r)]

    W_p96 = p96_b
    ctx_i = [0]

    def load_ext(t):
        s, n = t["s"], t["n"]
        xe = rnext(xe_r)
        lo, hi = s - 2, s + n + 1
        pieces = []
        c = lo
        while c < hi:
            cm = c % S
            ln = min(hi - c, S - cm)
            pieces.append((c - lo, cm, ln))
            c += ln
        for (o, cm, ln) in pieces:
            K.dma(xe.t[:, :, o:o + ln], xT.t[:, cm:cm + ln].rearrange("(c p) t -> p c t", p=128), [xT], [xe])
        return xe

    def load_rope(t):
        s, n = t["s"], t["n"]
        rp = rnext(rope_t)
        K.dma(rp.t[64:96, :, 0:n], rope_d.t[:, :, s:s + n], [rope_d], [rp])
        return rp

    def tileA(t, mode):
        s, n = t["s"], t["n"]
        ne = n + 3
        if mode in ("ctx", "ownF"):
            xe = load_ext(t)
            he = rnext(he_r)
            K.norm_mod(xe, ne, gmod1, sh1, he, inplace=True)
            for ch in range(4):
                ps = K.ps()
                for kc in range(8):
                    K.mm(ps.t[:, 0:ne], w_in_b.t[:, kc, ch * 128:(ch + 1) * 128], he.t[:, kc, 0:ne], kc == 0, kc == 7,
                         [w_in_b, he], [ps])
                K.cp("act", xr_b.t[:, ch, 0:ne], ps.t[:, 0:ne], [ps], [xr_b])
            for (col, cf, ct) in ((0, s - 2, s), (1, s - 1, s), (ne - 1, s + n - 1, s + n)):
                ks = crossed(sd["links"], S, cf, ct)
                if ks:
                    apply_links("pool", xr_b.t[:, :, col:col + 1], ks, xr_b)
            if mode == "ctx":
                xc = rnext(xc_t)
                xcv = lambda ch: xc.t[:, ch, 0:n]
            else:
                xc = XC
                xcv = lambda ch: XC.t[:, ch, t["own"]:t["own"] + n]
            for ch in range(4):
                ps = K.ps()
                for k in range(4):
                    K.mm(ps.t[:, 0:n], dg_b.t[:, ch * 4 + k, :], xr_b.t[:, ch, k:k + n], k == 0, k == 3, [dg_b, xr_b],
                         [ps])
                K.act(xcv(ch), ps.t[:, 0:n], AF.Identity, [ps, small], [xc], bias=small.t[:, ch:ch + 1])
            rp = load_rope(t)
            ps_kv = K.ps()
            for kc in range(8):
                K.mm(ps_kv.t[:, 0:n], w_in_b.t[:, kc, 1280:1408], he.t[:, kc, 2:2 + n], kc == 0, kc == 7, [w_in_b, he],
                     [ps_kv])
            sq = rnext(sqb)
            K.act(sq.t[:, 0:n], ps_kv.t[:, 0:n], AF.Square, [ps_kv], [sq])
            ps2 = K.ps()
            K.mm(ps2.t[:, 0:n], K.ones_b.t[:, :], sq.t[:, 0:n], True, True, [K.ones_b, sq], [ps2])
            r, rv = K.rstd_from_ps(ps2, ps2.t[:, 0:n], n, 128.0)
            ck = rnext(ckv)
            K.stt(ck.t[:, 0:n], ps_kv.t[:, 0:n], small.t[:, 30:31], rv, ALU.mult, ALU.mult, [ps_kv, small, r], [ck])
            for sub in range(0, n, 128):
                ns = min(128, n - sub)
                psv = K.ps()
                K.mm(psv.t[0:ns, :], ck.t[:, sub:sub + ns], wv_b.t[:, :], True, True, [ck, wv_b], [psv])
                v1 = rnext(vt)
                K.cp("act", v1.t[0:ns, :], psv.t[0:ns, :], [psv], [v1])
                K.dma(W["v_s"].t[s + sub:s + sub + ns, :], v1.t[0:ns, :], [v1], [W["v_s"]])
            pskr = K.ps()
            for kc in range(8):
                K.mm(pskr.t[0:96, 0:n], wkr_b.t[:, kc, :], he.t[:, kc, 2:2 + n], kc == 0, kc == 7, [wkr_b, he], [pskr])
            K.cp("act", krope.t[64:96, 0:n], pskr.t[64:96, 0:n], [pskr], [krope])
            for h in range(8):
                psk = K.ps()
                K.mm(psk.t[0:64, 0:n], wk_b.t[:, h * 64:(h + 1) * 64], ck.t[:, 0:n], True, True, [wk_b, ck], [psk])
                khf = rnext(kh)
                K.cp("act", khf.t[0:64, 0:n], psk.t[0:64, 0:n], [psk], [khf])
                K.cp("pool", khf.t[64:96, 0:n], krope.t[64:96, 0:n], [krope], [khf])
                _norm_rope_from_sb(K, khf, n, small, 33, rp, W_p96, sqb, khn, khb, rt1, rnext,
                                   W["kt_s"], W["kt_s"].t[h, :, s:s + n])
        if mode == "ownF":
            o = t["own"]
            for ch in range(4):
                ps = K.ps()
                for kc in range(8):
                    K.mm(ps.t[:, 0:n], w_in_b.t[:, kc, 512 + ch * 128:512 + (ch + 1) * 128], he.t[:, kc, 2:2 + n],
                         kc == 0, kc == 7, [w_in_b, he], [ps])
                K.act(GY.t[:, ch, o:o + n], ps.t[:, 0:n], AF.Gelu_apprx_tanh, [ps], [GY])
            psq = [K.ps(), K.ps()]
            sq2 = [rnext(sqb), rnext(sqb)]
            for c in range(2):
                for kc in range(8):
                    K.mm(psq[c].t[:, 0:n], w_in_b.t[:, kc, 1024 + c * 128:1024 + (c + 1) * 128], he.t[:, kc, 2:2 + n],
                         kc == 0, kc == 7, [w_in_b, he], [psq[c]])
                K.act(sq2[c].t[:, 0:n], psq[c].t[:, 0:n], AF.Square, [psq[c]], [sq2[c]])
            ps2 = K.ps()
            for c in range(2):
                K.mm(ps2.t[:, 0:n], K.ones_b.t[:, :], sq2[c].t[:, 0:n], c == 0, c == 1, [K.ones_b, sq2[c]], [ps2])
            r, rv = K.rstd_from_ps(ps2, ps2.t[:, 0:n], n, 256.0)
            qq = rnext(qn)
            for c in range(2):
                K.stt(qq.t[:, c, 0:n], psq[c].t[:, 0:n], small.t[:, 28 + c:29 + c], rv, ALU.mult, ALU.mult,
                      [psq[c], small, r], [qq])
            for h in range(8):
                psh = K.ps()
                for c in range(2):
                    K.mm(psh.t[0:96, 0:n], wuq_b.t[:, c, h * 96:(h + 1) * 96], qq.t[:, c, 0:n], c == 0, c == 1,
                         [wuq_b, qq], [psh])
                khf = rnext(kh)
                K.cp("act", khf.t[:, 0:n], psh.t[0:96, 0:n], [psh], [khf])
                _norm_rope_from_sb(K, khf, n, small, 32, rp, W_p96, sqb, khn, khb, rt1, rnext,
                                   W["qt_s"], W["qt_s"].t[h, :, o:o + n])
        dirs = (0, 1) if mode == "ctx" else ((0,) if mode == "ownF" else (1,))
        if mode == "ctx":
            ti = ctx_i[0]
            ctx_i[0] += 1
            t["ctx_idx"] = ti
        for d in dirs:
            for ch in range(4):
                if mode == "ctx":
                    xcs, xca = xc, xc.t[:, ch, 0:n]
                else:
                    xcs, xca = XC, XC.t[:, ch, t["own"]:t["own"] + n]
                psr_ = K.ps()
                K.mm(psr_.t[:, 0:n], bd_b.t[:, (d * 2 + 0) * 4 + ch, :], xca, True, True, [bd_b, xcs], [psr_])
                psi_ = K.ps()
                K.mm(psi_.t[:, 0:n], bd_b.t[:, (d * 2 + 1) * 4 + ch, :], xca, True, True, [bd_b, xcs], [psi_])
                rr = rnext(g_r)
                sc = sumr.t[:, (d * 4 + ch):(d * 4 + ch) + 1]
                K.act(rr.t[:, 0:n], psr_.t[:, 0:n], AF.Sigmoid, [psr_, small], [rr, sumr], bias=GB(d, 0, ch),
                      accum=sc if mode == "ctx" else None)
                ii = rnext(g_i)
                K.act(ii.t[:, 0:n], psi_.t[:, 0:n], AF.Sigmoid, [psi_, small], [ii], bias=GB(d, 1, ch))
                aa = rnext(g_a)
                K.act(aa.t[:, 0:n], rr.t[:, 0:n], AF.Exp, [rr, small], [aa], scale=SA(d, ch))
                qq_ = rnext(g_q)
                K.act(qq_.t[:, 0:n], rr.t[:, 0:n], AF.Exp, [rr, small], [qq_], scale=SA2(d, ch))
                K.act(qq_.t[:, 0:n], qq_.t[:, 0:n], AF.Ln, [qq_], [qq_], scale=-1.0, bias=K.one_t.t[:, 0:1])
                K.act(qq_.t[:, 0:n], qq_.t[:, 0:n], AF.Exp, [qq_], [qq_], scale=0.5)
                K.tt("dve", ii.t[:, 0:n], ii.t[:, 0:n], xca, ALU.mult, [ii, xcs], [ii])
                bb = rnext(g_b)
                K.tt("pool", bb.t[:, 0:n], qq_.t[:, 0:n], ii.t[:, 0:n], ALU.mult, [qq_, ii], [bb])
                hh = rnext(g_h)
                if mode == "ctx":
                    init = 0.0
                    rd = [aa, bb]
                else:
                    init = carry.t[:, d, ch:ch + 1]
                    rd = [aa, bb, carry]
                if d == 0:
                    K.S.op("dve", lambda e, hh=hh, aa=aa, bb=bb, init=init, n=n: e.tensor_tensor_scan(
                        out=hh.t[:, 0:n], data0=aa.t[:, 0:n], data1=bb.t[:, 0:n], initial=init, op0=ALU.mult,
                        op1=ALU.add), rd, [hh])
                    last = hh.t[:, n - 1:n]
                else:
                    K.S.op("dve", lambda e, hh=hh, aa=aa, bb=bb, init=init, n=n: e.tensor_tensor_scan(
                        out=hh.t[:, 0:n][:, ::-1], data0=aa.t[:, 0:n][:, ::-1], data1=bb.t[:, 0:n][:, ::-1],
                        initial=init, op0=ALU.mult, op1=ALU.add), rd, [hh])
                    last = hh.t[:, 0:1]
                if mode == "ctx":
                    K.cp("pool", AB.t[:, d, ti, 1, ch:ch + 1], last, [hh], [AB])
                    K.act(AB.t[:, d, ti, 0, ch:ch + 1], sc, AF.Exp, [sumr, small], [AB], scale=SA(d, ch))
                else:
                    o = t["own"]
                    K.cp("pool", carry.t[:, d, ch:ch + 1], last, [hh], [carry])
                    if d == 0:
                        K.cp("act", HF.t[:, ch, o:o + n], hh.t[:, 0:n], [hh], [HF])
                    else:
                        K.tt("dve", hh.t[:, 0:n], hh.t[:, 0:n], HF.t[:, ch, o:o + n], ALU.add, [hh, HF], [hh])
                        K.tt("pool", GY.t[:, ch, o:o + n], hh.t[:, 0:n], GY.t[:, ch, o:o + n], ALU.mult, [hh, GY], [GY])

    ctx_tiles = [t for t in tiles if t["role"] != "own"]
    own_tiles = [t for t in tiles if t["role"] != "ctx"]
    if limit.get("atiles"):
        own_tiles = own_tiles[:limit["atiles"]]
        ctx_tiles = ctx_tiles[:limit["atiles"]] if ctx_tiles else ctx_tiles
    for t in ctx_tiles:
        tileA(t, "ctx")

    def link_between(prev, t, d):
        if prev is None:
            return
        if d == 0:
            a = prev["s"] + prev["n"] - 1
            b_ = t["s"]
            if b_ <= a:
                b_ += S
            ks = crossed(sd["links"], S, a, b_)
        else:
            a = t["s"] + t["n"] - 1
            b_ = prev["s"]
            if b_ <= a:
                b_ += S
            ks = crossed(sd["links"], S, a, b_)
        apply_links("dve", carry.t[:, d, :], ks, carry)

    def chain(order, d):
        K.memset("pool", carry.t[:, d, :], 0.0, [carry])
        prev = None
        for t in order:
            link_between(prev, t, d)
            ti = t["ctx_idx"]
            K.tt("dve", carry.t[:, d, :], carry.t[:, d, :], AB.t[:, d, ti, 0, :], ALU.mult, [carry, AB], [carry])
            K.tt("dve", carry.t[:, d, :], carry.t[:, d, :], AB.t[:, d, ti, 1, :], ALU.add, [carry, AB], [carry])
            prev = t
        return prev

    halo = [t for t in tiles if t["role"] == "halo"]
    ctxo = [t for t in tiles if t["role"] == "ctx"]
    if kind == "p" and not limit.get("atiles"):
        fwd_order = [halo[1]] + ctxo
        prev = chain(fwd_order, 0)
    else:
        prev = None
    first = True
    for t in own_tiles:
        if kind == "p" or not first:
            link_between(prev, t, 0)
        first = False
        tileA(t, "ownF")
        prev = t
    if kind == "p" and not limit.get("atiles"):
        bwd_order = [halo[0]] + ctxo[::-1]
        prev = chain(bwd_order, 1)
    else:
        prev = None
        K.memset("pool", carry.t[:, 1, :], 0.0, [carry])
    first = True
    for t in own_tiles[::-1]:
        if kind == "p" or not first:
            link_between(prev, t, 1)
        first = False
        tileA(t, "ownB")
        prev = t


def _stageB(K, sd, si, xT, W, limit, GY):
    S = sd["S"]
    kind = sd["kind"]
    MV, GM = W["MV"], W["GM"]
    g1 = MV(0, si, 2)
    K.stage = K.ring(1, [128, 2048], F32, "stage")
    wo_rg_b = K.sb([128, 4, 1024], BF16, "wo_rg_b")
    for c in range(4):
        K.load_bf16(wo_rg_b, wo_rg_b.t[:, c, :], W["wo_rg"], W["wo_rg"].t[:, c, :], [128, 1024])
    wo_at_b = K.sb([64, 8, 1024], BF16, "wo_at_b")
    for c in range(0, 8, 2):
        K.load_bf16(wo_at_b, wo_at_b.t[:, c:c + 2, :], W["wo_at"], W["wo_at"].t[:, c:c + 2, :], [64, 2, 1024])
    QB = 512
    KBLK = 2048
    nkb = S // KBLK
    qtile = K.ring(2, [96, QB], BF16, "qtile")
    kblk = K.ring(2, [96, KBLK], BF16, "kblk")
    vblk = K.ring(2, [128, KBLK // 128, 65], BF16, "vblk")
    for vb in vblk:
        K.memset("pool", vb.t[:, :, 64:65], 1.0, [vb])
    pbuf = K.ring(3, [128, QB], BF16, "pbuf")
    at = [K.sb([64, QB], BF16, "at%d" % h) for h in range(8)]
    rd_t = K.sb([128, QB], F32, "rd")
    bcs = K.sb([64, QB], F32, "bcs")
    xq = K.ring(2, [128, 8, QB], F32, "xq")
    x1t = K.ring(2, [128, 8, QB], F32, "x1t")
    ps_s = [T(K.psf[i].t, K.psf[i].b) for i in range(3)]
    ps_o = [K.psf[3], K.psf[4]]
    ps_m = K.psf[5]
    scale = 96.0 ** -0.5
    own0 = sd["own0"]
    qtl = [[(1 + QB * k, QB)] for k in range(S_S // QB)]
    if kind == "p":
        qtl.append([(0, 1), (NOWN - 1, 1)])
    if limit.get("qtiles"):
        qtl = qtl[:limit["qtiles"]]
    it = 0
    for pieces in qtl:
        nq = sum(p[1] for p in pieces)
        xt_ = xq[it % 2]
        o = 0
        for (c0, ln) in pieces:
            cm = (own0 + c0) % S
            K.dma(xt_.t[:, :, o:o + ln], xT.t[:, cm:cm + ln].rearrange("(c p) t -> p c t", p=128), [xT], [xt_])
            o += ln
        for h in range(8):
            qt_ = qtile[(it * 8 + h) % 2]
            o = 0
            for (c0, ln) in pieces:
                K.dma(qt_.t[:, o:o + ln], W["qt_s"].t[h, :, c0:c0 + ln], [W["qt_s"]], [qt_])
                o += ln
            pso = ps_o[h % 2]
            nkt = S // 128
            jobs = []
            for kb_ in range(nkb):
                jobs.append(kb_)
            kcur = None
            vcur = None
            pend = None
            bi = 0
            for kt in range(nkt):
                if kt % (KBLK // 128) == 0:
                    kb_ = kt // (KBLK // 128)
                    kcur = kblk[(it * 8 * nkb + h * nkb + kb_) % 2]
                    vcur = vblk[(it * 8 * nkb + h * nkb + kb_) % 2]
                    K.dma(kcur.t[:, :], W["kt_s"].t[h, :, kb_ * KBLK:(kb_ + 1) * KBLK], [W["kt_s"]], [kcur])
                    K.dma(vcur.t[:, :, 0:64],
                          W["v_s"].t[kb_ * KBLK:(kb_ + 1) * KBLK, h * 64:(h + 1) * 64].rearrange("(k p) d -> p k d", p=128),
                          [W["v_s"]], [vcur])
                kk = kt % (KBLK // 128)
                pss = ps_s[kt % 3]
                K.mm(pss.t[:, 0:nq], kcur.t[:, kk * 128:(kk + 1) * 128], qt_.t[:, 0:nq], True, True, [kcur, qt_], [pss])
                if pend is not None:
                    (pb, pkt, pv, pkk) = pend
                    K.mm(pso.t[0:65, 0:nq], pv.t[:, pkk, :], pb.t[:, 0:nq], pkt == 0, False, [pv, pb], [pso])
                pb = pbuf[kt % 3]
                K.act(pb.t[:, 0:nq], pss.t[:, 0:nq], AF.Exp, [pss], [pb], scale=scale)
                pend = (pb, kt, vcur, kk)
            (pb, pkt, pv, pkk) = pend
            K.mm(pso.t[0:65, 0:nq], pv.t[:, pkk, :], pb.t[:, 0:nq], pkt == 0, True, [pv, pb], [pso])
            K.S.op("dve", lambda e, pso=pso, nq=nq: e.reciprocal(out=rd_t.t[64:65, 0:nq], in_=pso.t[64:65, 0:nq]),
                   [pso], [rd_t])
            psb_ = ps_m
            K.mm(psb_.t[0:64, 0:nq], W["ones_f"].t[64:65, 0:64], rd_t.t[64:65, 0:nq], True, True, [W["ones_f"], rd_t],
                 [psb_])
            K.cp("act", bcs.t[:, 0:nq], psb_.t[0:64, 0:nq], [psb_], [bcs])
            K.tt("dve", at[h].t[:, 0:nq], pso.t[0:64, 0:nq], bcs.t[:, 0:nq], ALU.mult, [pso, bcs], [at[h]])
        x1 = x1t[it % 2]
        for dc in range(8):
            psm = K.ps() if False else ps_m
            o = 0
            for (c0, ln) in pieces:
                for c in range(4):
                    K.mm(psm.t[:, o:o + ln], wo_rg_b.t[:, c, dc * 128:(dc + 1) * 128], GY.t[:, c, c0:c0 + ln],
                         c == 0, False, [wo_rg_b, GY], [psm])
                for h in range(8):
                    K.mm(psm.t[:, o:o + ln], wo_at_b.t[:, h, dc * 128:(dc + 1) * 128], at[h].t[:, o:o + ln],
                         False, h == 7, [wo_at_b, at[h]], [psm])
                o += ln
            K.stt(x1.t[:, dc, 0:nq], psm.t[:, 0:nq], g1[:, dc:dc + 1], xt_.t[:, dc, 0:nq], ALU.mult, ALU.add,
                  [psm, K.modv, xt_], [x1])
        o = 0
        for (c0, ln) in pieces:
            K.dma(W["x1"].t[:, c0:c0 + ln].rearrange("(c p) t -> p c t", p=128), x1.t[:, :, o:o + ln], [x1], [W["x1"]])
            o += ln
        it += 1


def _norm_rope_from_sb(K, khf, n, small, gcol, rp, p96_b, sqb, khn, khb, rt1, rnext, dst, dst_ap):
    sq = rnext(sqb)
    K.act(sq.t[0:96, 0:n], khf.t[:, 0:n], AF.Square, [khf], [sq])
    ps2 = K.ps()
    K.mm(ps2.t[0:96, 0:n], K.ones_b.t[0:96, 0:96], sq.t[0:96, 0:n], True, True, [K.ones_b, sq], [ps2])
    r, rv = K.rstd_from_ps(ps2, ps2.t[0:96, 0:n], n, 96.0, parts=96)
    kb = rnext(khb)
    K.stt(kb.t[:, 0:n], khf.t[:, 0:n], small.t[0:96, gcol:gcol + 1], rv, ALU.mult, ALU.mult, [khf, small, r], [kb])
    ps3 = K.ps()
    K.mm(ps3.t[0:96, 0:n], p96_b.t[:, :], kb.t[:, 0:n], True, True, [p96_b, kb], [ps3])
    t1 = rnext(rt1)
    K.tt("pool", t1.t[64:96, 0:n], kb.t[64:96, 0:n], rp.t[64:96, 0, 0:n], ALU.mult, [kb, rp], [t1])
    t2 = rnext(rt1)
    K.tt("dve", t2.t[64:96, 0:n], ps3.t[64:96, 0:n], rp.t[64:96, 1, 0:n], ALU.mult, [ps3, rp], [t2])
    K.tt("dve", kb.t[64:96, 0:n], t1.t[64:96, 0:n], t2.t[64:96, 0:n], ALU.add, [t1, t2, kb], [kb])
    K.dma(dst_ap, kb.t[:, 0:n], [kb], [dst])


def _mixer_c(K, si, kind, x2, x3, W, limit):
    MV, GM = W["MV"], W["GM"]
    LK, links = W["LK"], W["links"]
    gmod, sh, g1 = GM(1, si, 0), MV(1, si, 0), MV(1, si, 2)
    K.stage = K.ring(2, [128, 2048], F32, "stage")
    cw_b = K.sb([128, 8, 3072], BF16, "cw_b")
    for kc in range(8):
        for hf in range(2):
            K.load_bf16(cw_b, cw_b.t[:, kc, hf * 1536:(hf + 1) * 1536], W["c_w_in"],
                        W["c_w_in"].t[:, kc, hf * 1536:(hf + 1) * 1536], [128, 1536])
    cd_b = K.sb([128, 24, 128], BF16, "cd_b")
    for hf in range(2):
        K.load_bf16(cd_b, cd_b.t[:, hf * 12:(hf + 1) * 12, :], W["cdiag"], W["cdiag"].t[:, hf * 12:(hf + 1) * 12, :],
                    [128, 12, 128])
    co_b = K.sb([128, 8, 1024], BF16, "co_b")
    for kc in range(0, 8, 2):
        K.load_bf16(co_b, co_b.t[:, kc:kc + 2, :], W["c_w_out"], W["c_w_out"].t[:, kc:kc + 2, :], [128, 2, 1024])
    NE = NT + 2
    xe_r = K.ring(2, [128, 8, NE], F32, "cxe")
    he_r = K.ring(2, [128, 8, NE], BF16, "che")
    K.nm_sq = K.sb([128, 8, NE], BF16, "cnmsq")
    K.nm_tmp = K.sb([128, 8, NE], F32, "cnmtmp")
    cgs = K.ring(2, [128, NE], F32, "cgs")
    u_b = K.sb([128, 8, NE], BF16, "u_b")
    bg = K.sb([128, 8, NT], F32, "bg")
    y_b = K.ring(2, [128, 8, NT], BF16, "y_b")
    x3t = K.ring(2, [128, 8, NT], F32, "x3t")
    ntl = S_S // NT
    if limit.get("ctiles"):
        ntl = limit["ctiles"]
    for k in range(ntl):
        xe = xe_r[k % 2]
        K.dma(xe.t[:, :, :], x2.t[:, NT * k:NT * k + NE].rearrange("(c p) t -> p c t", p=128), [x2], [xe])
        he = he_r[k % 2]
        K.norm_mod(xe, NE, gmod, sh, he)
        for ch in range(8):
            psc = K.ps()
            for kc in range(8):
                K.mm(psc.t[:, 0:NE], cw_b.t[:, kc, 1024 + ch * 128:1024 + (ch + 1) * 128], he.t[:, kc, :], kc == 0,
                     kc == 7, [cw_b, he], [psc])
            psx = K.ps()
            for kc in range(8):
                K.mm(psx.t[:, 0:NE], cw_b.t[:, kc, 2048 + ch * 128:2048 + (ch + 1) * 128], he.t[:, kc, :], kc == 0,
                     kc == 7, [cw_b, he], [psx])
            cg = cgs[ch % 2]
            K.cp("act", cg.t[:, :], psc.t[:, 0:NE], [psc], [cg])
            K.tt("dve", u_b.t[:, ch, :], psx.t[:, 0:NE], cg.t[:, :], ALU.mult, [psx, cg], [u_b])
            psb_ = K.ps()
            for kc in range(8):
                K.mm(psb_.t[:, 0:NT], cw_b.t[:, kc, ch * 128:(ch + 1) * 128], he.t[:, kc, 1:1 + NT], kc == 0, kc == 7,
                     [cw_b, he], [psb_])
            K.cp("act", bg.t[:, ch, :], psb_.t[:, 0:NT], [psb_], [bg])
        if k == 0:
            key = "c" if kind == "p" else "zero"
            _lk(K, u_b, u_b.t[:, :, 0:1], key, LK, links)
        if k == S_S // NT - 1:
            key = "d" if kind == "p" else "zero"
            _lk(K, u_b, u_b.t[:, :, NE - 1:NE], key, LK, links)
        yb = y_b[k % 2]
        for ch in range(8):
            psv = K.ps()
            for tp in range(3):
                K.mm(psv.t[:, 0:NT], cd_b.t[:, ch * 3 + tp, :], u_b.t[:, ch, tp:tp + NT], tp == 0, tp == 2, [cd_b, u_b],
                     [psv])
            K.tt("dve", yb.t[:, ch, :], psv.t[:, 0:NT], bg.t[:, ch, :], ALU.mult, [psv, bg], [yb])
        x3_ = x3t[k % 2]
        for dc in range(8):
            psm = K.ps()
            for c in range(8):
                K.mm(psm.t[:, 0:NT], co_b.t[:, c, dc * 128:(dc + 1) * 128], yb.t[:, c, :], c == 0, c == 7, [co_b, yb],
                     [psm])
            K.stt(x3_.t[:, dc, :], psm.t[:, 0:NT], g1[:, dc:dc + 1], xe.t[:, dc, 1:1 + NT], ALU.mult, ALU.add,
                  [psm, K.modv, xe], [x3_])
        K.dma(x3.t[:, NT * k:NT * (k + 1)].rearrange("(c p) t -> p c t", p=128), x3_.t[:, :, :], [x3_], [x3])


def _lk(K, t, ap, key, LK, links):
    if key == "zero":
        K.ts("pool", ap, ap, 0.0, None, ALU.mult, None, [t], [t])
    else:
        K.ts("pool", ap, ap, LK[key], None, ALU.mult, None, [t, links], [t])


def _peer(K, l, si, xsrc, xdst, tlist, W, limit):
    MV, GM = W["MV"], W["GM"]
    gmod, sh, g2 = GM(l, si, 1), MV(l, si, 3), MV(l, si, 5)
    ident_b, ident_f = W["ident_b"], W["ident_f"]
    wq_b, uv_b = W["wq_b"], W["uv_b"]
    K.stage = K.ring(1, [128, 256], F32, "pstage")
    kT = K.sb([128, 2, 128], BF16, "kT")
    K.load_bf16(kT, kT.t[:], W["k12T"], W["k12T"].t[:, 2 * l:2 * l + 2, :], [128, 2, 128])
    GT = K.sb([128, NT, 128], BF16, "GT")
    RT = K.sb([128, 128, 128], BF16, "RT")
    OHT = K.sb([128, 128, 128], BF16, "OHT")
    ROr = K.ring(2, [128, 64, 128], BF16, "RO")
    s12 = K.sb([128, 16, 128], F32, "s12")
    e2 = K.sb([128, 8, 128], BF16, "e2")
    vv = K.sb([128, 16, 16], F32, "vv")
    vs = K.sb([128, 8, 16], F32, "vs")
    sm = K.sb([128, 8, 64], F32, "sm")
    h2 = K.sb([128, 8, NT], BF16, "h2")
    wqs = K.ring(2, [128, 8, 128], BF16, "wqs")
    NSB = 4
    strm = K.sb([128, NSB, 2048], BF16, "strm")
    sbuf_ = [Buf("strm%d" % i) for i in range(NSB)]
    uvs = [T(strm.t[:, i, :], sbuf_[i]) for i in range(NSB)]
    qTv = strm.t[:, :, :].rearrange("p a b -> p (a b)")[:, 0:16 * NT].rearrange("p (c t) -> p c t", t=NT)
    qTb = sbuf_[0:2]
    gel = K.ring(3, [128, NT], BF16, "gel")
    atb = K.ring(3, [128, NT], BF16, "atb")
    rtb = RT.t[:, :, :].rearrange("p a b -> p (a b)")
    rtf = rtb.bitcast(F32)
    ohf = OHT.t[:, :, :].rearrange("p a b -> p (a b)").bitcast(F32)
    K.nm_sq = T(rtb[:, 0:8 * NT].rearrange("p (c t) -> p c t", t=NT), RT.b)
    K.nm_tmp = T(ohf[:, 0:8 * NT].rearrange("p (c t) -> p c t", t=NT), OHT.b)
    GTB = [GT.b, Buf("GTb")]
    gtf = GT.t[:, 128:256, :].rearrange("p a b -> p (a b)").bitcast(F32)
    tmpT = T(gtf[:, 0:2048].rearrange("p (c j) -> p c j", j=128), GTB[1])
    e2fT = T(gtf[:, 2048:3072].rearrange("p (h j) -> p h j", j=128), GTB[1])
    tmp = tmpT.t
    e2f = e2fT.t
    xt = T(rtf[:, 4096:4096 + 8 * NT].rearrange("p (c t) -> p c t", t=NT), RT.b)
    cand = ohf[:, 0:2048].rearrange("p (h a b) -> p h a b", a=16, b=16)
    ctmp = ohf[:, 2048:4096].rearrange("p (h a b) -> p h a b", a=16, b=16)
    sel = ohf[:, 4096:6144].rearrange("p (h a b) -> p h a b", a=16, b=16)
    t1v = ohf[:, 6144:8192].rearrange("p (h a b) -> p h a b", a=16, b=16)
    osb = T(ohf[:, 0:1024], OHT.b)
    nchunk = limit.get("pchunks", 128)

    def load_x(pieces):
        o = 0
        for (c0, ln) in pieces:
            K.dma(xt.t[:, :, o:o + ln], xsrc.t[:, c0:c0 + ln].rearrange("(c p) t -> p c t", p=128), [xsrc], [xt])
            o += ln

    for pieces in tlist:
        nt = sum(p[1] for p in pieces)
        load_x(pieces)
        K.norm_mod(xt, nt, gmod, sh, h2)
        for c in range(16):
            wq_ = wqs[c % 2]
            K.dma(wq_.t[:, :, :], wq_b.t[l, c], [wq_b], [wq_])
            ps = K.ps()
            for kc in range(8):
                K.mm(ps.t[:, 0:nt], wq_.t[:, kc, :], h2.t[:, kc, 0:nt], kc == 0, kc == 7, [wq_, h2], [ps])
            K.cp("act" if c % 2 == 0 else "dve", qTv[:, c, 0:nt], ps.t[:, 0:nt], [ps], qTb)
        for t0 in range(0, nt, 128):
            ns = min(128, nt - t0)
            for g in range(4):
                ps = K.ps()
                for m in range(4):
                    c = g * 4 + m
                    K.mm(ps.t[0:ns, m * 128:(m + 1) * 128], qTv[:, c, t0:t0 + ns], kT.t[:, c % 2, :], True, True,
                         qTb + [kT], [ps])
                K.cp("act", s12.t[0:ns, g * 4:(g + 1) * 4, :], ps.t[0:ns, :].rearrange("p (m j) -> p m j", j=128),
                     [ps], [s12])
            for c in range(16):
                K.S.op("dve", lambda e, c=c, ns=ns: e.max(out=vv.t[0:ns, c, 0:8], in_=s12.t[0:ns, c, :]), [s12], [vv])
            for c in range(16):
                K.S.op("dve", lambda e, c=c, ns=ns: e.match_replace(out=tmp[0:ns, c, :], in_to_replace=vv.t[0:ns, c, 0:8],
                                                                    in_values=s12.t[0:ns, c, :], imm_value=-BIG),
                       [s12, vv], [tmpT])
            for c in range(16):
                K.S.op("dve", lambda e, c=c, ns=ns: e.max(out=vv.t[0:ns, c, 8:16], in_=tmp[0:ns, c, :]), [tmpT], [vv])
            v4 = vv.t[:, :, :].rearrange("p (h w) a -> p h w a", w=2)
            v1 = v4[0:ns, :, 0, :]
            v2 = v4[0:ns, :, 1, :]
            s4 = s12.t[:, :, :].rearrange("p (h w) j -> p h w j", w=2)
            s1 = s4[0:ns, :, 0, :]
            s2 = s4[0:ns, :, 1, :]
            K.tt("pool", cand[0:ns], v1.unsqueeze(3).to_broadcast([ns, 8, 16, 16]),
                 v2.unsqueeze(2).to_broadcast([ns, 8, 16, 16]), ALU.add, [vv], [OHT])
            for h in range(8):
                K.S.op("dve", lambda e, h=h, ns=ns: e.max(out=vs.t[0:ns, h, 0:8], in_=cand[0:ns, h]), [OHT], [vs])
            for h in range(8):
                K.S.op("dve", lambda e, h=h, ns=ns: e.match_replace(out=ctmp[0:ns, h], in_to_replace=vs.t[0:ns, h, 0:8],
                                                                    in_values=cand[0:ns, h], imm_value=-BIG),
                       [OHT, vs], [OHT])
            for h in range(8):
                K.S.op("dve", lambda e, h=h, ns=ns: e.max(out=vs.t[0:ns, h, 8:16], in_=ctmp[0:ns, h]), [OHT], [vs])
            ev = sm.t[0:ns, :, 0:16]
            thr = sm.t[0:ns, :, 16:32]
            Z = sm.t[0:ns, :, 32]
            rZ = sm.t[0:ns, :, 33]
            w1 = sm.t[0:ns, :, 40:56]
            K.tt("pool", ev, vs.t[0:ns], vs.t[0:ns, :, 0:1].to_broadcast([ns, 8, 16]), ALU.subtract, [vs], [sm])
            K.act(ev, ev, AF.Exp, [sm], [sm])
            K.S.op("dve", lambda e, ev=ev, Z=Z: e.tensor_reduce(out=Z, in_=ev, op=ALU.add, axis=AX.X), [sm], [sm])
            K.S.op("dve", lambda e, Z=Z, rZ=rZ: e.reciprocal(out=rZ, in_=Z), [sm], [sm])
            K.tt("pool", w1, v1, v1[:, :, 0:1].to_broadcast([ns, 8, 16]), ALU.subtract, [vv], [sm])
            K.act(w1, w1, AF.Exp, [sm], [sm])
            K.tt("pool", w1, w1, rZ.unsqueeze(2).to_broadcast([ns, 8, 16]), ALU.mult, [sm], [sm])
            K.tt("pool", e2f[0:ns], s2, v2[:, :, 0:1].to_broadcast([ns, 8, 128]), ALU.subtract, [s12, vv], [e2fT])
            K.act(e2.t[0:ns], e2f[0:ns], AF.Exp, [e2fT], [e2])
            K.tt("dve", sel[0:ns], cand[0:ns], vs.t[0:ns, :, 15:16].unsqueeze(3).to_broadcast([ns, 8, 16, 16]),
                 ALU.is_ge, [OHT, vs], [OHT])
            K.tt("pool", t1v[0:ns], sel[0:ns], v2.unsqueeze(2).to_broadcast([ns, 8, 16, 16]), ALU.mult, [OHT, vv], [OHT])
            K.ts("pool", sel[0:ns], sel[0:ns], -BIG, BIG, ALU.mult, ALU.add, [OHT], [OHT])
            K.tt("pool", t1v[0:ns], t1v[0:ns], sel[0:ns], ALU.add, [OHT], [OHT])
            K.S.op("dve", lambda e, ns=ns, thr=thr: e.tensor_reduce(out=thr, in_=t1v[0:ns], op=ALU.min, axis=AX.X),
                   [OHT], [sm])
            rnd = 0
            for ih in range(2):
                RO = ROr[rnd % 2]
                rnd += 1
                is_ = slice(ih * 64, (ih + 1) * 64)
                for h in range(8):
                    for a_ in range(16):
                        K.ts("dve", RO.t[0:ns, :, h * 16 + a_], s1[:, h, is_], v1[:, h, a_:a_ + 1], w1[:, h, a_:a_ + 1],
                             ALU.is_equal, ALU.mult, [s12, vv, sm], [RO])
                _transposes(K, RO, OHT, ih, ns, ident_b)
            for jh in range(2):
                RO = ROr[rnd % 2]
                rnd += 1
                js = slice(jh * 64, (jh + 1) * 64)
                for h in range(8):
                    for a_ in range(16):
                        K.stt(RO.t[0:ns, :, h * 16 + a_], s2[:, h, js], thr[:, h, a_:a_ + 1], e2.t[0:ns, h, js],
                              ALU.is_ge, ALU.mult, [s12, sm, e2], [RO])
                _transposes(K, RO, RT, jh, ns, ident_b)
            for tb in range(0, ns, 4):
                nb = min(4, ns - tb)
                ps = K.ps()
                for u in range(nb):
                    K.mm(ps.t[:, u * 128:(u + 1) * 128], RT.t[:, :, tb + u], OHT.t[:, :, tb + u], True, True, [RT, OHT],
                         [ps])
                K.cp("act", GT.t[:, t0 + tb:t0 + tb + nb, :].rearrange("p t i -> p (t i)"), ps.t[:, 0:nb * 128],
                     [ps], [GTB[t0 // 128]])
        nsub = (nt + 127) // 128
        pso = [[K.psf[2 + 2 * s_ + dh] for dh in range(2)] for s_ in range(nsub)]
        pss = [K.psf[0], K.psf[1]]
        PF = NSB - 2

        def issue_dma(i):
            uv_ = uvs[i % NSB]
            K.dma(uv_.t, uv_b.t[l, i], [uv_b], [uv_])

        def front(i):
            uv_ = uvs[i % NSB]
            ps = pss[i % 2]
            for kc in range(8):
                K.mm(ps.t[:, 0:nt], uv_.t[:, kc * 128:(kc + 1) * 128], h2.t[:, kc, 0:nt], kc == 0, kc == 7,
                     [uv_, h2], [ps])
            gl = gel[i % 3]
            K.act(gl.t[:, 0:nt], ps.t[:, 0:nt], AF.Gelu_apprx_tanh, [ps], [gl])
            ab = atb[i % 3]
            K.tt("dve" if i % 3 else "pool", ab.t[:, 0:nt], gl.t[:, 0:nt], GT.t[:, 0:nt, i], ALU.mult, [gl] + GTB, [ab])

        def back(i):
            uv_ = uvs[i % NSB]
            ab = atb[i % 3]
            for s_ in range(nsub):
                ns = min(128, nt - s_ * 128)
                for dh in range(2):
                    K.mm(pso[s_][dh].t[0:ns, :], ab.t[:, s_ * 128:s_ * 128 + ns],
                         uv_.t[:, 1024 + dh * 512:1024 + (dh + 1) * 512], i == 0, i == nchunk - 1, [ab, uv_],
                         [pso[s_][dh]])

        for i in range(min(PF, nchunk)):
            issue_dma(i)
        for i in range(nchunk + 1):
            if i < nchunk:
                front(i)
            if i >= 1:
                back(i - 1)
            if i + PF < nchunk:
                issue_dma(i + PF)
        load_x(pieces)
        for s_ in range(nsub):
            ns = min(128, nt - s_ * 128)
            for dh in range(2):
                K.cp("act", osb.t[0:ns, dh * 512:(dh + 1) * 512], pso[s_][dh].t[0:ns, :], [pso[s_][dh]], [osb])
            for dc in range(8):
                ps = pss[dc % 2]
                K.tr(ps.t[:, 0:ns], osb.t[0:ns, dc * 128:(dc + 1) * 128], ident_f.t[0:ns, 0:ns], [osb, ident_f], [ps])
                K.stt(xt.t[:, dc, s_ * 128:s_ * 128 + ns], ps.t[:, 0:ns], g2[:, dc:dc + 1],
                      xt.t[:, dc, s_ * 128:s_ * 128 + ns], ALU.mult, ALU.add, [ps, K.modv, xt], [xt])
        o = 0
        for (c0, ln) in pieces:
            K.dma(xdst.t[:, c0:c0 + ln].rearrange("(c p) t -> p c t", p=128), xt.t[:, :, o:o + ln], [xt], [xdst])
            o += ln


def _transposes(K, RO, DST, half, ns, ident_b):
    for g in range(8):
        pb = K.psb[g % 2]
        for u in range(8):
            jj = g * 8 + u
            K.tr(pb.t[:, u * 128:u * 128 + ns], RO.t[0:ns, jj, :], ident_b.t[0:ns, 0:ns], [RO, ident_b], [pb])
        j0 = half * 64 + g * 8
        if ns == 128:
            K.cp("act", DST.t[:, j0:j0 + 8, :].rearrange("p j t -> p (j t)"), pb.t[:, :], [pb], [DST])
        else:
            K.cp("act", DST.t[:, j0:j0 + 8, 0:ns], pb.t[:, :].rearrange("p (j t) -> p j t", t=128)[:, :, 0:ns], [pb], [DST])


def _fm(v, n=8):
    return np.ascontiguousarray(np.asarray(v, np.float32).reshape(n, 128).T)


def _wmat(w):
    K_, N = w.shape
    return np.ascontiguousarray(np.asarray(w, np.float32).reshape(K_ // 128, 128, N).transpose(1, 0, 2))


def _rope_tables(pos):
    inv = (1.0 / (10000.0 ** (np.arange(0, 32, 2, dtype=np.float32) / np.float32(32)))).astype(np.float32)
    ang = pos.astype(np.float32)[:, None] * inv[None, :]
    c = np.cos(ang).astype(np.float32).T
    s = np.sin(ang).astype(np.float32).T
    out = np.empty((32, 2, pos.shape[0]), np.float32)
    out[0:16, 0] = c
    out[16:32, 0] = c
    out[0:16, 1] = s
    out[16:32, 1] = s
    return out


def host_inputs(I):
    f = np.float32
    shared = {}
    shared["ident"] = np.eye(128, dtype=f)
    p96 = np.zeros((96, 96), f)
    for m in range(16):
        p96[64 + m + 16, 64 + m] = -1.0
        p96[64 + m, 64 + m + 16] = 1.0
    shared["p96"] = p96
    shared["adaw"] = np.ascontiguousarray(I["ada_w"].reshape(2, 8, 128, 6144).transpose(0, 2, 1, 3))
    shared["adab"] = np.ascontiguousarray(I["ada_b"].reshape(2, 48, 128).transpose(2, 0, 1))
    shared["n1g"] = np.ascontiguousarray(I["norm1_g"].reshape(2, 8, 128).transpose(2, 0, 1))
    shared["n2g"] = np.ascontiguousarray(I["norm2_g"].reshape(2, 8, 128).transpose(2, 0, 1))
    shared["w_in"] = _wmat(I["ab_w_in"][0])
    wkr = np.zeros((128, 8, 96), f)
    wkr[:, :, 64:96] = shared["w_in"][:, :, 1408:1440]
    shared["wkr"] = wkr
    cw = I["rg_conv_w"][0]
    dg = np.zeros((128, 16, 128), f)
    for ch in range(4):
        for k in range(4):
            dg[np.arange(128), ch * 4 + k, np.arange(128)] = cw[k, ch * 128:(ch + 1) * 128]
    shared["rgdiag"] = dg
    shared["rgcb"] = _fm(I["rg_conv_b"][0], 4)
    bd = np.zeros((128, 16, 128), f)
    rgb = np.zeros((128, 16), f)
    for d in range(2):
        for wi, (wn, bn) in enumerate((("rg_wa", "rg_ba"), ("rg_wx", "rg_bx"))):
            for ch in range(4):
                idx = (d * 2 + wi) * 4 + ch
                for hh in range(2):
                    bd[hh * 64:(hh + 1) * 64, idx, hh * 64:(hh + 1) * 64] = I[wn][0, d, ch * 2 + hh]
                rgb[:, idx] = I[bn][0, d, ch * 128:(ch + 1) * 128]
    shared["rgbd"] = bd
    shared["rgb"] = rgb
    lam = np.zeros((128, 8), f)
    for d in range(2):
        lam[:, d * 4:(d + 1) * 4] = _fm(I["rg_lambda"][0, d], 4)
    shared["rglam"] = lam
    shared["qnorm"] = _fm(I["mla_q_norm"][0], 2)
    shared["w_uq"] = _wmat(I["mla_w_uq"][0])
    shared["kvnorm"] = _fm(I["mla_kv_norm"][0], 1)
    wukv = I["mla_w_ukv"][0].reshape(128, 8, 128)
    shared["wk"] = np.ascontiguousarray(wukv[:, :, 0:64].reshape(128, 512))
    shared["wv"] = np.ascontiguousarray(wukv[:, :, 64:128].reshape(128, 512))
    shared["qng"] = np.ascontiguousarray(np.stack([I["mla_qn_q"][0], I["mla_qn_k"][0]], axis=1).astype(f))
    wo = I["ab_w_out"][0]
    shared["wo_rg"] = _wmat(wo[0:512])
    shared["wo_at"] = np.ascontiguousarray(wo[512:1024].reshape(8, 64, 1024).transpose(1, 0, 2))
    shared["c_w_in"] = _wmat(I["c_w_in"][0])
    ccw = I["c_conv_w"][0]
    cd = np.zeros((128, 24, 128), f)
    for ch in range(8):
        for k in range(3):
            cd[np.arange(128), ch * 3 + k, np.arange(128)] = ccw[k, ch * 128:(ch + 1) * 128]
    shared["cdiag"] = cd
    shared["c_w_out"] = _wmat(I["c_w_out"][0])
    shared["wq"] = np.ascontiguousarray(I["peer_wq"].reshape(2, 8, 128, 2048).transpose(0, 2, 1, 3))
    k12 = np.zeros((128, 4, 128), f)
    for l in range(2):
        k12[:, 2 * l + 0, :] = I["peer_k1"][l].T
        k12[:, 2 * l + 1, :] = I["peer_k2"][l].T
    shared["k12T"] = k12
    U = I["peer_u"].reshape(2, 128, 128, 8, 128)
    shared["ut"] = np.ascontiguousarray(U.transpose(0, 1, 4, 3, 2)).reshape(2, 128, 128, 1024)
    shared["pv"] = np.ascontiguousarray(I["peer_v"].reshape(2, 128, 128, 1024))
    shared["rope_s"] = _rope_tables(np.arange(S_S))
    maps = []
    for c in range(8):
        b, q = c // 4, c % 4
        m = dict(shared)
        m["xs"] = np.ascontiguousarray(I["x_sample"][c].T)
        start = ((q + 1) * 4096 + 1) % S_P
        pos = (start + np.arange(S_P)) % S_P
        m["xp"] = np.ascontiguousarray(I["x_prompt"][b][pos].T)
        m["rope_p"] = _rope_tables(pos)
        cv = np.zeros((128, 8, 2), f)
        cv[:, :, 0] = _fm(I["c_sample"][c])
        cv[:, :, 1] = _fm(I["c_prompt"][b])
        m["cvec"] = cv
        lk = np.ones((128, 4), f)
        lk[:, {2: 0, 1: 1, 0: 2, 3: 3}[q]] = 0.0
        m["links"] = lk
        maps.append(m)
    return maps


_CACHE = {}


def kernel(**inputs):
    I = {k: np.asarray(v) for k, v in inputs.items()}
    maps = host_inputs(I)
    if "nc" not in _CACHE:
        _CACHE["nc"] = build()
    nc, K = _CACHE["nc"]
    res = run_bass_kernel_spmd(nc, maps, core_ids=list(range(8)))
    y_prompt = np.empty((2, S_P, 1024), np.float32)
    y_sample = np.empty((8, S_S, 1024), np.float32)
    for c in range(8):
        r = res.results[c]
        b, q = c // 4, c % 4
        y_sample[c] = r["ys"].T
        y_prompt[b, q * 4096:(q + 1) * 4096] = r["yp"].T
    return (y_prompt, y_sample)
```

```python
import os
import contextlib
import numpy as np
import concourse.bass as bass
import concourse.mybir as mybir
from concourse.bass_utils import run_bass_kernel_spmd

F32 = mybir.dt.float32
BF16 = mybir.dt.bfloat16
AF = mybir.ActivationFunctionType
ALU = mybir.AluOpType
AX = mybir.AxisListType
ENGS = ["pe", "act", "dve", "pool", "sp"]
EPS = 1e-6
BIG = 1.0e30

S_S = 4096
S_P = 16384
NOWN = 4098
ROT0 = 12286
TA = 256
NT = 256


class Buf:
    __slots__ = ("name", "last_w", "readers", "dsem", "dval")

    def __init__(self, name):
        self.name = name
        self.last_w = None
        self.readers = []
        self.dsem = None
        self.dval = 0


class T:
    __slots__ = ("t", "b")

    def __init__(self, t, b):
        self.t = t
        self.b = b

    def __getitem__(self, k):
        return self.t[k]


def _b(x):
    return x.b if isinstance(x, T) else x


class Sched:
    def __init__(self, nc, stack):
        self.nc = nc
        self.stack = stack
        self.ops = {e: [] for e in ENGS}
        self.cnt = {e: 0 for e in ENGS}
        self.sem = {e: stack.enter_context(nc.semaphore("sem_" + e)) for e in ENGS if e != "sp"}
        self.seen = {e: {} for e in ENGS}
        self.dma_bufs = []
        self.free_sems = []
        self.nsem_alloc = 0
        self.ninstr = 0

    def _deps(self, eng, reads, writes):
        deps = []
        own = self.sem.get(eng)
        for b in reads:
            if b.last_w is not None:
                deps.append(b.last_w)
        for b in writes:
            if b.last_w is not None and b.last_w[0] is not own:
                deps.append(b.last_w)
            deps.extend(r for r in b.readers if r[0] is not own)
        seen = self.seen[eng]
        best = {}
        pe_sem = self.sem["pe"]
        for (s, v) in deps:
            if eng == "pe" and s is pe_sem:
                continue
            k = id(s)
            if seen.get(k, 0) >= v:
                continue
            if k not in best or best[k][1] < v:
                best[k] = (s, v)
        for k, (s, v) in best.items():
            seen[k] = v
        return list(best.values())

    def op(self, eng, fn, reads=(), writes=()):
        reads = [_b(x) for x in reads]
        writes = [_b(x) for x in writes]
        deps = self._deps(eng, reads, writes)
        self.cnt[eng] += 1
        s = self.sem[eng]
        tok = (s, self.cnt[eng])
        self.ops[eng].append((deps, fn, s, 1))
        for b in writes:
            b.last_w = tok
            b.readers = []
        for b in reads:
            if b in writes:
                continue
            b.readers = [r for r in b.readers if r[0] is not s] + [tok]
        self.ninstr += 1

    def dma(self, out_ap, in_ap, reads=(), writes=(), eng="sp"):
        reads = [_b(x) for x in reads]
        writes = [_b(x) for x in writes]
        deps = self._deps(eng, reads, writes)
        tb = writes[0] if writes else reads[0]
        if tb.dsem is None:
            if self.free_sems:
                tb.dsem, tb.dval = self.free_sems.pop()
            else:
                self.nsem_alloc += 1
                tb.dsem = self.stack.enter_context(self.nc.semaphore("dsem%d" % self.nsem_alloc))
                tb.dval = 0
            self.dma_bufs.append(tb)
        tb.dval += 16
        tok = (tb.dsem, tb.dval)
        self.ops[eng].append((deps, lambda e: e.dma_start(out=out_ap, in_=in_ap, allow_slow_non_contiguous=True), tb.dsem, 16))
        for b in writes:
            b.last_w = tok
            b.readers = []
        for b in reads:
            b.readers = b.readers + [tok]
        self.ninstr += 1
        return tok

    def barrier(self):
        toks = [(self.sem[e], self.cnt[e]) for e in self.sem if self.cnt[e] > 0]
        toks += [(b.dsem, b.dval) for b in self.dma_bufs]
        for e in ENGS:
            deps = []
            for (s, v) in toks:
                if e in self.sem and s is self.sem[e]:
                    continue
                if self.seen[e].get(id(s), 0) >= v:
                    continue
                self.seen[e][id(s)] = v
                deps.append((s, v))
            self.ops[e].append((deps, None, None, 0))
        for b in self.dma_bufs:
            self.free_sems.append((b.dsem, b.dval))
            b.dsem = None
        self.dma_bufs = []

    def flush(self, final_waits=()):
        nc = self.nc
        engmap = {"pe": "tensor", "act": "scalar", "dve": "vector", "pool": "gpsimd", "sp": "sync"}
        with nc.Block() as block:
            for e in ENGS:
                ops = self.ops[e]
                fw = list(final_waits) if e == "sp" else []

                def body(eng, ops=ops, fw=fw):
                    for (deps, fn, s, inc) in ops:
                        for (ds, dv) in deps:
                            eng.wait_ge(ds, dv)
                        if fn is not None:
                            fn(eng).then_inc(s, inc)
                    for (ds, dv) in fw:
                        eng.wait_ge(ds, dv)
                getattr(block, engmap[e])(body)
        self.ops = {e: [] for e in ENGS}


class KB:
    def __init__(self, nc, st, debug):
        self.nc = nc
        self.st = st
        self.S = Sched(nc, st)
        self.ph = None
        self.uid = 0
        self.debug = debug
        self.inputs = {}
        self.outputs = {}
        self.psr = 0

    def sb(self, shape, dt=F32, name="t", buf=None):
        self.uid += 1
        nm = "%s_%d" % (name, self.uid)
        t = (self.ph or self.st).enter_context(self.nc.sbuf_tensor(nm, list(shape), dt))
        return T(t, buf if buf is not None else Buf(nm))

    def ring(self, n, shape, dt=F32, name="r"):
        return [self.sb(shape, dt, name) for _ in range(n)]

    def inp(self, name, shape, dt=F32):
        t = T(self.nc.dram_tensor(name, list(shape), dt, kind="ExternalInput").ap(), Buf(name))
        self.inputs[name] = t
        return t

    def outp(self, name, shape, dt=F32):
        t = T(self.nc.dram_tensor(name, list(shape), dt, kind="ExternalOutput").ap(), Buf(name))
        self.outputs[name] = t
        return t

    def scratch(self, name, shape, dt):
        kind = "ExternalOutput" if (self.debug and name in self.debug) else "Internal"
        t = T(self.nc.dram_tensor(name, list(shape), dt, kind=kind).ap(), Buf(name))
        if kind == "ExternalOutput":
            self.outputs[name] = t
        return t

    @contextlib.contextmanager
    def phase(self):
        with contextlib.ExitStack() as ph:
            old = self.ph
            self.ph = ph
            yield
            self.S.barrier()
            self.S.flush()
            self.ph = old

    def ps(self):
        b = self.psf[self.psr % len(self.psf)]
        self.psr += 1
        return b

    def mm(self, out, lhsT, rhs, start, stop, reads, writes):
        self.S.op("pe", lambda e: e.matmul(out, lhsT=lhsT, rhs=rhs, start=start, stop=stop), reads, writes)

    def tr(self, out, in_, ident, reads, writes):
        self.S.op("pe", lambda e: e.transpose(out=out, in_=in_, identity=ident), reads, writes)

    def act(self, out, in_, func, reads, writes, scale=None, bias=None, accum=None):
        kw = {}
        if scale is not None:
            kw["scale"] = scale
        if bias is not None:
            kw["bias"] = bias
        if accum is not None:
            kw["accum_out"] = accum
        self.S.op("act", lambda e: e.activation(out=out, in_=in_, func=func, **kw), reads, writes)

    def ts(self, eng, out, in0, s1, s2, op0, op1, reads, writes):
        if op1 is None:
            self.S.op(eng, lambda e: e.tensor_scalar(out=out, in0=in0, scalar1=s1, scalar2=None, op0=op0), reads, writes)
        else:
            self.S.op(eng, lambda e: e.tensor_scalar(out=out, in0=in0, scalar1=s1, scalar2=s2, op0=op0, op1=op1),
                      reads, writes)

    def tt(self, eng, out, in0, in1, op, reads, writes):
        self.S.op(eng, lambda e: e.tensor_tensor(out=out, in0=in0, in1=in1, op=op), reads, writes)

    def stt(self, out, in0, scalar, in1, op0, op1, reads, writes):
        self.S.op("dve", lambda e: e.scalar_tensor_tensor(out=out, in0=in0, scalar=scalar, in1=in1, op0=op0, op1=op1),
                  reads, writes)

    def cp(self, eng, out, in_, reads, writes):
        if eng == "act":
            self.S.op("act", lambda e: e.copy(out=out, in_=in_), reads, writes)
        else:
            self.S.op(eng, lambda e: e.tensor_copy(out=out, in_=in_), reads, writes)

    def memset(self, eng, out, val, writes):
        self.S.op(eng, lambda e: e.memset(out, val), (), writes)

    def dma(self, out, in_, reads, writes):
        self.S.dma(out, in_, reads, writes)

    def load_bf16(self, dst, dst_ap, src, src_ap, shape):
        stg = self.stage[self.stage_i % len(self.stage)]
        self.stage_i += 1
        n = int(np.prod(shape[1:]))
        sv = stg.t[0:shape[0], 0:n]
        if len(shape) == 3:
            sv = sv.rearrange("p (a b) -> p a b", b=shape[2])
        self.dma(sv, src_ap, [src], [stg])
        eng = ["act", "dve", "pool"][self.stage_i % 3]
        self.cp(eng, dst_ap, sv, [stg], [dst])

    def rstd_from_ps(self, ps, ps_ap, n, nfeat, parts=128):
        r = self.rs_ring[self.rs_i % len(self.rs_ring)]
        self.rs_i += 1
        rv = r.t[0:parts, 0:n]
        self.act(rv, ps_ap, AF.Ln, [ps], [r], scale=1.0 / nfeat, bias=self.eps_t.t[0:parts, 0:1])
        self.act(rv, rv, AF.Exp, [r], [r], scale=-0.5)
        return r, rv

    def norm_mod(self, x, n, gmod, sh, h, inplace=False):
        sq = self.nm_sq
        self.act(sq.t[:, :, 0:n], x.t[:, :, 0:n], AF.Square, [x], [sq])
        ps = self.ps()
        for c in range(8):
            self.mm(ps.t[:, 0:n], self.ones_b.t[:, :], sq.t[:, c, 0:n], c == 0, c == 7, [self.ones_b, sq], [ps])
        r, rv = self.rstd_from_ps(ps, ps.t[:, 0:n], n, 1024.0)
        tmp = x if inplace else self.nm_tmp
        self.tt("dve", tmp.t[:, :, 0:n], x.t[:, :, 0:n], rv.unsqueeze(1).to_broadcast([128, 8, n]), ALU.mult,
                [x, r], [tmp])
        for c in range(8):
            if c % 2 == 0:
                self.act(h.t[:, c, 0:n], tmp.t[:, c, 0:n], AF.Identity, [tmp, self.modv], [h],
                         scale=gmod[:, c:c + 1], bias=sh[:, c:c + 1])
            else:
                self.ts("pool", h.t[:, c, 0:n], tmp.t[:, c, 0:n], gmod[:, c:c + 1], sh[:, c:c + 1], ALU.mult, ALU.add,
                        [tmp, self.modv], [h])


def seq_desc(kind):
    d = {}
    if kind == "p":
        S = S_P
        segs = [("ctx", 0, 4095), ("ctx", 4095, 8191), ("ctx", 8191, 12286), ("halo", 12286, 12287),
                ("own", 12287, 16383), ("halo", 16383, 16384)]
        links = {4095: "a", 8191: "b", 12287: "c", 16383: "d"}
        own0 = ROT0
    else:
        S = S_S
        segs = [("own", 0, 4096)]
        links = {0: "zero"}
        own0 = -1
    tiles = []
    for (role, a, b) in segs:
        s = a
        while s < b:
            n = min(TA, b - s)
            tiles.append(dict(s=s, n=n, role=role, own=(s - own0) if role != "ctx" else None))
            s += n
    d.update(S=S, tiles=tiles, links=links, own0=own0, kind=kind)
    return d


def crossed(links, S, c_from, c_to):
    keys = []
    for p in range(c_from + 1, c_to + 1):
        k = links.get(p % S)
        if k is not None:
            keys.append(k)
    return keys


def build(debug=None, limit=None):
    debug = debug or {}
    limit = limit or {}
    nc = bass.Bass("TRN2", target_bir_lowering=False)
    with contextlib.ExitStack() as st:
        K = KB(nc, st, debug)
        _program(K, limit)
        finals = [t.b.last_w for t in K.outputs.values() if t.b.last_w is not None]
        K.S.flush(finals)
        print("kernel instrs:", K.S.ninstr, "dma sems:", K.S.nsem_alloc, flush=True)
    return nc, K


def _program(K, limit):
    nc = K.nc
    xs = K.inp("xs", [1024, S_S])
    xp = K.inp("xp", [1024, S_P])
    cvec = K.inp("cvec", [128, 8, 2])
    links_d = K.inp("links", [128, 4])
    rope_s = K.inp("rope_s", [32, 2, S_S])
    rope_p = K.inp("rope_p", [32, 2, S_P])
    ident_d = K.inp("ident", [128, 128])
    p96_d = K.inp("p96", [96, 96])
    adaw = K.inp("adaw", [2, 128, 8, 6144])
    adab = K.inp("adab", [128, 2, 48])
    n1g = K.inp("n1g", [128, 2, 8])
    n2g = K.inp("n2g", [128, 2, 8])
    w_in = K.inp("w_in", [128, 8, 1440])
    wkr = K.inp("wkr", [128, 8, 96])
    rgdiag = K.inp("rgdiag", [128, 16, 128])
    rgcb = K.inp("rgcb", [128, 4])
    rgbd = K.inp("rgbd", [128, 16, 128])
    rgb = K.inp("rgb", [128, 16])
    rglam = K.inp("rglam", [128, 8])
    qnorm = K.inp("qnorm", [128, 2])
    w_uq = K.inp("w_uq", [128, 2, 768])
    kvnorm = K.inp("kvnorm", [128, 1])
    wk = K.inp("wk", [128, 512])
    wv = K.inp("wv", [128, 512])
    qng = K.inp("qng", [96, 2])
    wo_rg = K.inp("wo_rg", [128, 4, 1024])
    wo_at = K.inp("wo_at", [64, 8, 1024])
    c_w_in = K.inp("c_w_in", [128, 8, 3072])
    cdiag = K.inp("cdiag", [128, 24, 128])
    c_w_out = K.inp("c_w_out", [128, 8, 1024])
    wq_d = K.inp("wq", [2, 128, 8, 2048])
    k12T = K.inp("k12T", [128, 4, 128])
    ut_d = K.inp("ut", [2, 128, 128, 1024])
    v_d = K.inp("pv", [2, 128, 128, 1024])
    ys = K.outp("ys", [1024, S_S])
    yp = K.outp("yp", [1024, S_S])

    kt_s = K.scratch("kt_s", [8, 96, S_P], BF16)
    v_s = K.scratch("v_s", [S_P, 512], BF16)
    qt_s = K.scratch("qt_s", [8, 96, NOWN], BF16)
    x1_s = {k: K.scratch("x1_" + k, [1024, NOWN], F32) for k in "sp"}
    x2_s = {k: K.scratch("x2_" + k, [1024, NOWN], F32) for k in "sp"}
    x3_s = {k: K.scratch("x3_" + k, [1024, S_S], F32) for k in "sp"}
    wq_b = K.scratch("wq_b", [2, 16, 128, 8, 128], BF16)
    uv_b = K.scratch("uv_b", [2, 128, 128, 2048], BF16)

    K.psf = [T(st_enter(K, nc.psum_tensor("psf%d" % i, [128, 512], F32)), Buf("psf%d" % i)) for i in range(6)]
    K.psb = [T(st_enter(K, nc.psum_tensor("psb%d" % i, [128, 1024], BF16)), Buf("psb%d" % i)) for i in range(2)]

    ident_f = K.sb([128, 128], F32, "identf")
    ident_b = K.sb([128, 128], BF16, "identb")
    K.ones_b = K.sb([128, 128], BF16, "onesb")
    ones_f = K.sb([128, 128], F32, "onesf")
    K.eps_t = K.sb([128, 1], F32, "eps")
    K.modv = K.sb([128, 2, 2, 6, 8], F32, "modv")
    gm = K.sb([128, 2, 2, 2, 8], F32, "gm")
    links = K.sb([128, 4], F32, "links")
    K.rs_ring = K.ring(3, [128, 264], F32, "rstd")
    K.rs_i = 0
    K.stage_i = 0
    K.one_t = K.sb([128, 1], F32, "one")
    K.memset("pool", K.one_t.t[:], 1.0, [K.one_t])
    modv = K.modv
    K.memset("pool", K.ones_b.t[:], 1.0, [K.ones_b])
    K.memset("pool", ones_f.t[:], 1.0, [ones_f])
    K.memset("pool", K.eps_t.t[:], EPS, [K.eps_t])
    K.dma(ident_f.t[:], ident_d.t[:, :], [ident_d], [ident_f])
    K.cp("dve", ident_b.t[:], ident_f.t[:], [ident_f], [ident_b])
    K.dma(links.t[:], links_d.t[:, :], [links_d], [links])

    LK = {"a": links.t[:, 0:1], "b": links.t[:, 1:2], "c": links.t[:, 2:3], "d": links.t[:, 3:4]}

    def apply_links(eng, ap, keys, t, parts=128):
        for k in keys:
            if k == "zero":
                K.ts(eng, ap, ap, 0.0, None, ALU.mult, None, [t], [t])
            else:
                K.ts(eng, ap, ap, LK[k][0:parts], None, ALU.mult, None, [t, links], [t])

    with K.phase():
        K.stage = K.ring(2, [128, 4096], F32, "stage")
        K.castb = K.ring(2, [128, 4096], BF16, "castb")
        zt = K.sb([128, 8, 1], F32, "zt")
        K.memset("pool", zt.t[:], 0.0, [zt])
        for cc in (0, NOWN - 1):
            K.dma(x2_s["s"].t[:, cc:cc + 1].rearrange("(c p) t -> p c t", p=128), zt.t[:], [zt], [x2_s["s"]])
        cv = K.sb([128, 8, 2], F32, "cv")
        K.dma(cv.t[:], cvec.t[:, :, :], [cvec], [cv])
        scv = K.sb([128, 8, 2], F32, "scv")
        K.act(scv.t[:], cv.t[:], AF.Silu, [cv], [scv])
        adb = K.sb([128, 2, 48], F32, "adb")
        K.dma(adb.t[:], adab.t[:, :, :], [adab], [adb])
        g12 = K.sb([128, 2, 2, 8], F32, "g12")
        K.dma(g12.t[:, 0], n1g.t[:, :, :], [n1g], [g12])
        K.dma(g12.t[:, 1], n2g.t[:, :, :], [n2g], [g12])
        awr = K.ring(2, [128, 8, 512], F32, "adaw")
        for l in range(2):
            psm = K.ps()
            for g in range(12):
                aw = awr[g % 2]
                K.dma(aw.t[:], adaw.t[l, :, :, g * 512:(g + 1) * 512], [adaw], [aw])
                for jj in range(4):
                    j = g * 4 + jj
                    for kc in range(8):
                        K.mm(psm.t[:, 2 * j:2 * j + 2], aw.t[:, kc, jj * 128:(jj + 1) * 128], scv.t[:, kc, :],
                             kc == 0, kc == 7, [aw, scv], [psm])
            for s in range(2):
                K.tt("dve", modv.t[:, l, s].rearrange("p w c -> p (w c)"),
                     psm.t[:, 0:96].rearrange("p (j s) -> p j s", s=2)[:, :, s], adb.t[:, l, :], ALU.add,
                     [psm, adb], [modv])
        for l in range(2):
            for s in range(2):
                for k in range(2):
                    K.stt(gm.t[:, l, s, k, :], modv.t[:, l, s, 1 + 3 * k, :], 1.0, g12.t[:, k, l, :], ALU.add, ALU.mult,
                          [modv, g12], [gm])
        if not limit.get("skip_prep"):
            cast_i = 0
            for l in range(2):
                jobs = [(wq_d.t[l], None, wq_d, wq_b, 4)]
                for i0 in range(0, 128, 4):
                    jobs.append((ut_d.t[l, i0:i0 + 4].rearrange("i p f -> p i f"),
                                 uv_b.t[l, i0:i0 + 4, :, 0:1024].rearrange("i p f -> p i f"), ut_d, uv_b, 1))
                    jobs.append((v_d.t[l, i0:i0 + 4].rearrange("i p f -> p i f"),
                                 uv_b.t[l, i0:i0 + 4, :, 1024:2048].rearrange("i p f -> p i f"), v_d, uv_b, 1))
                for (src, dst, srcT, dstT, nsplit) in jobs:
                    for sp_ in range(nsplit):
                        if nsplit == 1:
                            s_ap, d_ap = src, dst
                            shp = [128, 4, 1024]
                        else:
                            s_ap = src[:, 2 * sp_:2 * sp_ + 2, :]
                            d_ap = wq_b.t[l, :, :, 2 * sp_:2 * sp_ + 2, :].rearrange("c p k j -> p k c j")
                            shp = [128, 2, 2048]
                        stg = K.stage[cast_i % 2]
                        sv = stg.t[:, :].rearrange("p (a b) -> p a b", b=shp[2])
                        K.dma(sv, s_ap, [srcT], [stg])
                        cb = K.castb[cast_i % 2]
                        cbv = cb.t[:, :].rearrange("p (a b) -> p a b", b=shp[2])
                        K.cp(["act", "dve"][cast_i % 2], cbv, sv, [stg], [cb])
                        if nsplit == 1:
                            K.dma(d_ap, cbv, [cb], [dstT])
                        else:
                            for kk in range(2):
                                K.dma(wq_b.t[l, :, :, 2 * sp_ + kk, :].rearrange("c p j -> p c j"),
                                      cbv[:, kk, :].rearrange("p (c j) -> p c j", j=128), [cb], [dstT])
                        cast_i += 1

    def MV(l, s, which):
        return modv.t[:, l, s, which, :]

    def GM(l, s, k):
        return gm.t[:, l, s, k, :]

    seqs = [("s", 0, xs, rope_s, ys), ("p", 1, xp, rope_p, yp)]
    if limit.get("seqs"):
        seqs = [q for q in seqs if q[0] in limit["seqs"]]

    for (kind, si, xT, rope_d, yout) in seqs:
        sd = seq_desc(kind)
        S = sd["S"]
        with K.phase():
            _layer0_mixer(K, sd, si, xT, rope_d, dict(
                w_in=w_in, wkr=wkr, rgdiag=rgdiag, rgcb=rgcb, rgbd=rgbd, rgb=rgb, rglam=rglam, qnorm=qnorm, w_uq=w_uq,
                kvnorm=kvnorm, wk=wk, wv=wv, qng=qng, wo_rg=wo_rg, wo_at=wo_at, p96=p96_d, ident_b=ident_b,
                ones_f=ones_f, kt_s=kt_s, v_s=v_s, qt_s=qt_s, x1=x1_s[kind], MV=MV, GM=GM, apply_links=apply_links,
                links=links, LK=LK), limit)
        if limit.get("stop") in ("stageA", "mixer0"):
            continue
        with K.phase():
            tl = [[(1 + NT * k, NT)] for k in range(S_S // NT)]
            if kind == "p":
                tl.append([(0, 1), (NOWN - 1, 1)])
            if limit.get("ptiles"):
                tl = tl[:limit["ptiles"]]
            _peer(K, 0, si, x1_s[kind], x2_s[kind], tl, dict(wq_b=wq_b, uv_b=uv_b, k12T=k12T, ident_b=ident_b,
                                                             ident_f=ident_f, MV=MV, GM=GM), limit)
        if limit.get("stop") == "peer0":
            continue
        with K.phase():
            _mixer_c(K, si, kind, x2_s[kind], x3_s[kind], dict(c_w_in=c_w_in, cdiag=cdiag, c_w_out=c_w_out, MV=MV, GM=GM,
                                                                LK=LK, links=links), limit)
        if limit.get("stop") == "mixer1":
            continue
        with K.phase():
            tl = [[(NT * k, NT)] for k in range(S_S // NT)]
            if limit.get("ptiles"):
                tl = tl[:limit["ptiles"]]
            _peer(K, 1, si, x3_s[kind], yout, tl, dict(wq_b=wq_b, uv_b=uv_b, k12T=k12T, ident_b=ident_b,
                                                       ident_f=ident_f, MV=MV, GM=GM), limit)


def st_enter(K, cm):
    return K.st.enter_context(cm)


def _layer0_mixer(K, sd, si, xT, rope_d, W, limit):
    S = sd["S"]
    kind = sd["kind"]
    tiles = sd["tiles"]
    MV, GM = W["MV"], W["GM"]
    apply_links = W["apply_links"]
    LK = W["LK"]
    links = W["links"]
    NE = TA + 3
    GY = K.sb([128, 4, NOWN], BF16, "GY")
    _stageA(K, sd, si, xT, rope_d, W, limit, GY)
    if limit.get("stop") == "stageA":
        dbg = K.outp("dbg_rg_" + kind, [128, 4, NOWN], BF16)
        K.dma(dbg.t[:, :, :], GY.t[:], [GY], [dbg])
        K.S.barrier()
        return
    with K.phase():
        _stageB(K, sd, si, xT, W, limit, GY)


def _stageA(K, sd, si, xT, rope_d, W, limit, GY):
  with K.phase():
    S = sd["S"]
    kind = sd["kind"]
    tiles = sd["tiles"]
    MV, GM = W["MV"], W["GM"]
    apply_links = W["apply_links"]
    LK = W["LK"]
    links = W["links"]
    NE = TA + 3
    K.stage = K.ring(1, [128, 2048], F32, "stage")
    w_in_b = K.sb([128, 8, 1440], BF16, "w_in_b")
    for kc in range(8):
        K.load_bf16(w_in_b, w_in_b.t[:, kc, :], W["w_in"], W["w_in"].t[:, kc, :], [128, 1440])
    wkr_b = K.sb([128, 8, 96], BF16, "wkr_b")
    K.load_bf16(wkr_b, wkr_b.t[:], W["wkr"], W["wkr"].t[:, :, :], [128, 8, 96])
    dg_b = K.sb([128, 16, 128], BF16, "dg_b")
    K.load_bf16(dg_b, dg_b.t[:], W["rgdiag"], W["rgdiag"].t[:, :, :], [128, 16, 128])
    bd_b = K.sb([128, 16, 128], BF16, "bd_b")
    K.load_bf16(bd_b, bd_b.t[:], W["rgbd"], W["rgbd"].t[:, :, :], [128, 16, 128])
    wuq_b = K.sb([128, 2, 768], BF16, "wuq_b")
    K.load_bf16(wuq_b, wuq_b.t[:], W["w_uq"], W["w_uq"].t[:, :, :], [128, 2, 768])
    wk_b = K.sb([128, 512], BF16, "wk_b")
    K.load_bf16(wk_b, wk_b.t[:], W["wk"], W["wk"].t[:, :], [128, 512])
    wv_b = K.sb([128, 512], BF16, "wv_b")
    K.load_bf16(wv_b, wv_b.t[:], W["wv"], W["wv"].t[:, :], [128, 512])
    p96_b = K.sb([96, 96], BF16, "p96_b")
    K.load_bf16(p96_b, p96_b.t[:], W["p96"], W["p96"].t[:, :], [96, 96])
    small = K.sb([128, 64], F32, "small")
    K.dma(small.t[:, 0:4], W["rgcb"].t[:, :], [W["rgcb"]], [small])
    K.dma(small.t[:, 4:20], W["rgb"].t[:, :], [W["rgb"]], [small])
    K.dma(small.t[:, 20:28], W["rglam"].t[:, :], [W["rglam"]], [small])
    K.dma(small.t[:, 28:30], W["qnorm"].t[:, :], [W["qnorm"]], [small])
    K.dma(small.t[:, 30:31], W["kvnorm"].t[:, :], [W["kvnorm"]], [small])
    K.dma(small.t[0:96, 32:34], W["qng"].t[:, :], [W["qng"]], [small])
    K.act(small.t[:, 52:60], small.t[:, 20:28], AF.Exp, [small], [small], scale=-1.0)
    K.act(small.t[:, 52:60], small.t[:, 52:60], AF.Ln, [small], [small], bias=K.one_t.t[:, 0:1])
    K.ts("dve", small.t[:, 36:44], small.t[:, 52:60], -8.0, None, ALU.mult, None, [small], [small])
    K.ts("dve", small.t[:, 44:52], small.t[:, 52:60], -16.0, None, ALU.mult, None, [small], [small])

    def SA(d, ch):
        return small.t[:, 36 + d * 4 + ch:37 + d * 4 + ch]

    def SA2(d, ch):
        return small.t[:, 44 + d * 4 + ch:45 + d * 4 + ch]

    def GB(d, which, ch):
        j = 4 + (d * 2 + which) * 4 + ch
        return small.t[:, j:j + 1]

    gmod1, sh1, g1 = GM(0, si, 0), MV(0, si, 0), MV(0, si, 2)

    XC = K.sb([128, 4, NOWN], BF16, "XC")
    HF = K.sb([128, 4, NOWN], BF16, "HF")
    nctx = sum(1 for t in tiles if t["role"] != "own")
    AB = K.sb([128, 2, max(nctx, 1), 2, 4], F32, "AB")
    carry = K.sb([128, 2, 4], F32, "carry")
    K.memset("pool", carry.t[:], 0.0, [carry])

    xe_r = K.ring(1, [128, 8, NE], F32, "xe")
    he_r = K.ring(2, [128, 8, NE], BF16, "he")
    K.nm_sq = K.sb([128, 8, NE], BF16, "nmsq")
    xr_b = K.sb([128, 4, NE], BF16, "xr_b")
    xc_t = K.ring(2, [128, 4, TA], BF16, "xc_t")
    g_r = K.ring(2, [128, TA], F32, "g_r")
    g_i = K.ring(2, [128, TA], F32, "g_i")
    g_a = K.ring(2, [128, TA], F32, "g_a")
    g_q = K.ring(2, [128, TA], F32, "g_q")
    g_b = K.ring(2, [128, TA], F32, "g_b")
    g_h = K.ring(2, [128, TA], F32, "g_h")
    sumr = K.sb([128, 64], F32, "sumr")
    sqb = K.ring(2, [128, TA], BF16, "sqb")
    ckv = K.ring(2, [128, TA], BF16, "ckv")
    qn = K.ring(2, [128, 2, TA], BF16, "qn")
    vt = K.ring(2, [128, 512], BF16, "vt")
    krope = K.sb([96, TA], F32, "krope")
    kh = K.ring(2, [96, TA], F32, "kh")
    khn = K.ring(2, [96, TA], F32, "khn")
    khb = K.ring(3, [96, TA], BF16, "khb")
    rt1 = K.ring(2, [96, TA], F32, "rt1")
    rope_t = K.ring(2, [96, 2, TA], F32, "rope")
    gi = [0]

    def rnext(r):
        gi[0] += 1
        return r[gi[0] % len(r)]

    W_p96 = p96_b
    ctx_i = [0]

    def load_ext(t):
        s, n = t["s"], t["n"]
        xe = rnext(xe_r)
        lo, hi = s - 2, s + n + 1
        pieces = []
        c = lo
        while c < hi:
            cm = c % S
            ln = min(hi - c, S - cm)
            pieces.append((c - lo, cm, ln))
            c += ln
        for (o, cm, ln) in pieces:
            K.dma(xe.t[:, :, o:o + ln], xT.t[:, cm:cm + ln].rearrange("(c p) t -> p c t", p=128), [xT], [xe])
        return xe

    def load_rope(t):
        s, n = t["s"], t["n"]
        rp = rnext(rope_t)
        K.dma(rp.t[64:96, :, 0:n], rope_d.t[:, :, s:s + n], [rope_d], [rp])
        return rp

    def tileA(t, mode):
        s, n = t["s"], t["n"]
        ne = n + 3
        if mode in ("ctx", "ownF"):
            xe = load_ext(t)
            he = rnext(he_r)
            K.norm_mod(xe, ne, gmod1, sh1, he, inplace=True)
            for ch in range(4):
                ps = K.ps()
                for kc in range(8):
                    K.mm(ps.t[:, 0:ne], w_in_b.t[:, kc, ch * 128:(ch + 1) * 128], he.t[:, kc, 0:ne], kc == 0, kc == 7,
                         [w_in_b, he], [ps])
                K.cp("act", xr_b.t[:, ch, 0:ne], ps.t[:, 0:ne], [ps], [xr_b])
            for (col, cf, ct) in ((0, s - 2, s), (1, s - 1, s), (ne - 1, s + n - 1, s + n)):
                ks = crossed(sd["links"], S, cf, ct)
                if ks:
                    apply_links("pool", xr_b.t[:, :, col:col + 1], ks, xr_b)
            if mode == "ctx":
                xc = rnext(xc_t)
                xcv = lambda ch: xc.t[:, ch, 0:n]
            else:
                xc = XC
                xcv = lambda ch: XC.t[:, ch, t["own"]:t["own"] + n]
            for ch in range(4):
                ps = K.ps()
                for k in range(4):
                    K.mm(ps.t[:, 0:n], dg_b.t[:, ch * 4 + k, :], xr_b.t[:, ch, k:k + n], k == 0, k == 3, [dg_b, xr_b],
                         [ps])
                K.act(xcv(ch), ps.t[:, 0:n], AF.Identity, [ps, small], [xc], bias=small.t[:, ch:ch + 1])
            rp = load_rope(t)
            ps_kv = K.ps()
            for kc in range(8):
                K.mm(ps_kv.t[:, 0:n], w_in_b.t[:, kc, 1280:1408], he.t[:, kc, 2:2 + n], kc == 0, kc == 7, [w_in_b, he],
                     [ps_kv])
            sq = rnext(sqb)
            K.act(sq.t[:, 0:n], ps_kv.t[:, 0:n], AF.Square, [ps_kv], [sq])
            ps2 = K.ps()
            K.mm(ps2.t[:, 0:n], K.ones_b.t[:, :], sq.t[:, 0:n], True, True, [K.ones_b, sq], [ps2])
            r, rv = K.rstd_from_ps(ps2, ps2.t[:, 0:n], n, 128.0)
            ck = rnext(ckv)
            K.stt(ck.t[:, 0:n], ps_kv.t[:, 0:n], small.t[:, 30:31], rv, ALU.mult, ALU.mult, [ps_kv, small, r], [ck])
            for sub in range(0, n, 128):
                ns = min(128, n - sub)
                psv = K.ps()
                K.mm(psv.t[0:ns, :], ck.t[:, sub:sub + ns], wv_b.t[:, :], True, True, [ck, wv_b], [psv])
                v1 = rnext(vt)
                K.cp("act", v1.t[0:ns, :], psv.t[0:ns, :], [psv], [v1])
                K.dma(W["v_s"].t[s + sub:s + sub + ns, :], v1.t[0:ns, :], [v1], [W["v_s"]])
            pskr = K.ps()
            for kc in range(8):
                K.mm(pskr.t[0:96, 0:n], wkr_b.t[:, kc, :], he.t[:, kc, 2:2 + n], kc == 0, kc == 7, [wkr_b, he], [pskr])
            K.cp("act", krope.t[64:96, 0:n], pskr.t[64:96, 0:n], [pskr], [krope])
            for h in range(8):
                psk = K.ps()
                K.mm(psk.t[0:64, 0:n], wk_b.t[:, h * 64:(h + 1) * 64], ck.t[:, 0:n], True, True, [wk_b, ck], [psk])
                khf = rnext(kh)
                K.cp("act", khf.t[0:64, 0:n], psk.t[0:64, 0:n], [psk], [khf])
                K.cp("pool", khf.t[64:96, 0:n], krope.t[64:96, 0:n], [krope], [khf])
                _norm_rope_from_sb(K, khf, n, small, 33, rp, W_p96, sqb, khn, khb, rt1, rnext,
                                   W["kt_s"], W["kt_s"].t[h, :, s:s + n])
        if mode == "ownF":
            o = t["own"]
            for ch in range(4):
                ps = K.ps()
                for kc in range(8):
                    K.mm(ps.t[:, 0:n], w_in_b.t[:, kc, 512 + ch * 128:512 + (ch + 1) * 128], he.t[:, kc, 2:2 + n],
                         kc == 0, kc == 7, [w_in_b, he], [ps])
                K.act(GY.t[:, ch, o:o + n], ps.t[:, 0:n], AF.Gelu_apprx_tanh, [ps], [GY])
            psq = [K.ps(), K.ps()]
            sq2 = [rnext(sqb), rnext(sqb)]
            for c in range(2):
                for kc in range(8):
                    K.mm(psq[c].t[:, 0:n], w_in_b.t[:, kc, 1024 + c * 128:1024 + (c + 1) * 128], he.t[:, kc, 2:2 + n],
                         kc == 0, kc == 7, [w_in_b, he], [psq[c]])
                K.act(sq2[c].t[:, 0:n], psq[c].t[:, 0:n], AF.Square, [psq[c]], [sq2[c]])
            ps2 = K.ps()
            for c in range(2):
                K.mm(ps2.t[:, 0:n], K.ones_b.t[:, :], sq2[c].t[:, 0:n], c == 0, c == 1, [K.ones_b, sq2[c]], [ps2])
            r, rv = K.rstd_from_ps(ps2, ps2.t[:, 0:n], n, 256.0)
            qq = rnext(qn)
            for c in range(2):
                K.stt(qq.t[:, c, 0:n], psq[c].t[:, 0:n], small.t[:, 28 + c:29 + c], rv, ALU.mult, ALU.mult,
                      [psq[c], small, r], [qq])
            for h in range(8):
                psh = K.ps()
                for c in range(2):
                    K.mm(psh.t[0:96, 0:n], wuq_b.t[:, c, h * 96:(h + 1) * 96], qq.t[:, c, 0:n], c == 0, c == 1,
                         [wuq_b, qq], [psh])
                khf = rnext(kh)
                K.cp("act", khf.t[:, 0:n], psh.t[0:96, 0:n], [psh], [khf])
                _norm_rope_from_sb(K, khf, n, small, 32, rp, W_p96, sqb, khn, khb, rt1, rnext,
                                   W["qt_s"], W["qt_s"].t[h, :, o:o + n])
        dirs = (0, 1) if mode == "ctx" else ((0,) if mode == "ownF" else (1,))
        if mode == "ctx":
            ti = ctx_i[0]
            ctx_i[0] += 1
            t["ctx_idx"] = ti
        for d in dirs:
            for ch in range(4):
                if mode == "ctx":
                    xcs, xca = xc, xc.t[:, ch, 0:n]
                else:
                    xcs, xca = XC, XC.t[:, ch, t["own"]:t["own"] + n]
                psr_ = K.ps()
                K.mm(psr_.t[:, 0:n], bd_b.t[:, (d * 2 + 0) * 4 + ch, :], xca, True, True, [bd_b, xcs], [psr_])
                psi_ = K.ps()
                K.mm(psi_.t[:, 0:n], bd_b.t[:, (d * 2 + 1) * 4 + ch, :], xca, True, True, [bd_b, xcs], [psi_])
                rr = rnext(g_r)
                sc = sumr.t[:, (d * 4 + ch):(d * 4 + ch) + 1]
                K.act(rr.t[:, 0:n], psr_.t[:, 0:n], AF.Sigmoid, [psr_, small], [rr, sumr], bias=GB(d, 0, ch),
                      accum=sc if mode == "ctx" else None)
                ii = rnext(g_i)
                K.act(ii.t[:, 0:n], psi_.t[:, 0:n], AF.Sigmoid, [psi_, small], [ii], bias=GB(d, 1, ch))
                aa = rnext(g_a)
                K.act(aa.t[:, 0:n], rr.t[:, 0:n], AF.Exp, [rr, small], [aa], scale=SA(d, ch))
                qq_ = rnext(g_q)
                K.act(qq_.t[:, 0:n], rr.t[:, 0:n], AF.Exp, [rr, small], [qq_], scale=SA2(d, ch))
                K.act(qq_.t[:, 0:n], qq_.t[:, 0:n], AF.Ln, [qq_], [qq_], scale=-1.0, bias=K.one_t.t[:, 0:1])
                K.act(qq_.t[:, 0:n], qq_.t[:, 0:n], AF.Exp, [qq_], [qq_], scale=0.5)
                K.tt("dve", ii.t[:, 0:n], ii.t[:, 0:n], xca, ALU.mult, [ii, xcs], [ii])
                bb = rnext(g_b)
                K.tt("pool", bb.t[:, 0:n], qq_.t[:, 0:n], ii.t[:, 0:n], ALU.mult, [qq_, ii], [bb])
                hh = rnext(g_h)
                if mode == "ctx":
                    init = 0.0
                    rd = [aa, bb]
                else:
                    init = carry.t[:, d, ch:ch + 1]
                    rd = [aa, bb, carry]
                if d == 0:
                    K.S.op("dve", lambda e, hh=hh, aa=aa, bb=bb, init=init, n=n: e.tensor_tensor_scan(
                        out=hh.t[:, 0:n], data0=aa.t[:, 0:n], data1=bb.t[:, 0:n], initial=init, op0=ALU.mult,
                        op1=ALU.add), rd, [hh])
                    last = hh.t[:, n - 1:n]
                else:
                    K.S.op("dve", lambda e, hh=hh, aa=aa, bb=bb, init=init, n=n: e.tensor_tensor_scan(
                        out=hh.t[:, 0:n][:, ::-1], data0=aa.t[:, 0:n][:, ::-1], data1=bb.t[:, 0:n][:, ::-1],
                        initial=init, op0=ALU.mult, op1=ALU.add), rd, [hh])
                    last = hh.t[:, 0:1]
                if mode == "ctx":
                    K.cp("pool", AB.t[:, d, ti, 1, ch:ch + 1], last, [hh], [AB])
                    K.act(AB.t[:, d, ti, 0, ch:ch + 1], sc, AF.Exp, [sumr, small], [AB], scale=SA(d, ch))
                else:
                    o = t["own"]
                    K.cp("pool", carry.t[:, d, ch:ch + 1], last, [hh], [carry])
                    if d == 0:
                        K.cp("act", HF.t[:, ch, o:o + n], hh.t[:, 0:n], [hh], [HF])
                    else:
                        K.tt("dve", hh.t[:, 0:n], hh.t[:, 0:n], HF.t[:, ch, o:o + n], ALU.add, [hh, HF], [hh])
                        K.tt("pool", GY.t[:, ch, o:o + n], hh.t[:, 0:n], GY.t[:, ch, o:o + n], ALU.mult, [hh, GY], [GY])

    ctx_tiles = [t for t in tiles if t["role"] != "own"]
    own_tiles = [t for t in tiles if t["role"] != "ctx"]
    if limit.get("atiles"):
        own_tiles = own_tiles[:limit["atiles"]]
        ctx_tiles = ctx_tiles[:limit["atiles"]] if ctx_tiles else ctx_tiles
    for t in ctx_tiles:
        tileA(t, "ctx")

    def link_between(prev, t, d):
        if prev is None:
            return
        if d == 0:
            a = prev["s"] + prev["n"] - 1
            b_ = t["s"]
            if b_ <= a:
                b_ += S
            ks = crossed(sd["links"], S, a, b_)
        else:
            a = t["s"] + t["n"] - 1
            b_ = prev["s"]
            if b_ <= a:
                b_ += S
            ks = crossed(sd["links"], S, a, b_)
        apply_links("dve", carry.t[:, d, :], ks, carry)

    def chain(order, d):
        K.memset("pool", carry.t[:, d, :], 0.0, [carry])
        prev = None
        for t in order:
            link_between(prev, t, d)
            ti = t["ctx_idx"]
            K.tt("dve", carry.t[:, d, :], carry.t[:, d, :], AB.t[:, d, ti, 0, :], ALU.mult, [carry, AB], [carry])
            K.tt("dve", carry.t[:, d, :], carry.t[:, d, :], AB.t[:, d, ti, 1, :], ALU.add, [carry, AB], [carry])
            prev = t
        return prev

    halo = [t for t in tiles if t["role"] == "halo"]
    ctxo = [t for t in tiles if t["role"] == "ctx"]
    if kind == "p" and not limit.get("atiles"):
        fwd_order = [halo[1]] + ctxo
        prev = chain(fwd_order, 0)
    else:
        prev = None
    first = True
    for t in own_tiles:
        if kind == "p" or not first:
            link_between(prev, t, 0)
        first = False
        tileA(t, "ownF")
        prev = t
    if kind == "p" and not limit.get("atiles"):
        bwd_order = [halo[0]] + ctxo[::-1]
        prev = chain(bwd_order, 1)
    else:
        prev = None
        K.memset("pool", carry.t[:, 1, :], 0.0, [carry])
    first = True
    for t in own_tiles[::-1]:
        if kind == "p" or not first:
            link_between(prev, t, 1)
        first = False
        tileA(t, "ownB")
        prev = t


def _stageB(K, sd, si, xT, W, limit, GY):
    S = sd["S"]
    kind = sd["kind"]
    MV, GM = W["MV"], W["GM"]
    g1 = MV(0, si, 2)
    K.stage = K.ring(1, [128, 2048], F32, "stage")
    wo_rg_b = K.sb([128, 4, 1024], BF16, "wo_rg_b")
    for c in range(4):
        K.load_bf16(wo_rg_b, wo_rg_b.t[:, c, :], W["wo_rg"], W["wo_rg"].t[:, c, :], [128, 1024])
    wo_at_b = K.sb([64, 8, 1024], BF16, "wo_at_b")
    for c in range(0, 8, 2):
        K.load_bf16(wo_at_b, wo_at_b.t[:, c:c + 2, :], W["wo_at"], W["wo_at"].t[:, c:c + 2, :], [64, 2, 1024])
    QB = 512
    KBLK = 2048
    nkb = S // KBLK
    qtile = K.ring(2, [96, QB], BF16, "qtile")
    kblk = K.ring(2, [96, KBLK], BF16, "kblk")
    vblk = K.ring(2, [128, KBLK // 128, 65], BF16, "vblk")
    for vb in vblk:
        K.memset("pool", vb.t[:, :, 64:65], 1.0, [vb])
    pbuf = K.ring(3, [128, QB], BF16, "pbuf")
    at = [K.sb([64, QB], BF16, "at%d" % h) for h in range(8)]
    rd_t = K.sb([128, QB], F32, "rd")
    bcs = K.sb([64, QB], F32, "bcs")
    xq = K.ring(2, [128, 8, QB], F32, "xq")
    x1t = K.ring(2, [128, 8, QB], F32, "x1t")
    ps_s = [T(K.psf[i].t, K.psf[i].b) for i in range(3)]
    ps_o = [K.psf[3], K.psf[4]]
    ps_m = K.psf[5]
    scale = 96.0 ** -0.5
    own0 = sd["own0"]
    qtl = [[(1 + QB * k, QB)] for k in range(S_S // QB)]
    if kind == "p":
        qtl.append([(0, 1), (NOWN - 1, 1)])
    if limit.get("qtiles"):
        qtl = qtl[:limit["qtiles"]]
    it = 0
    for pieces in qtl:
        nq = sum(p[1] for p in pieces)
        xt_ = xq[it % 2]
        o = 0
        for (c0, ln) in pieces:
            cm = (own0 + c0) % S
            K.dma(xt_.t[:, :, o:o + ln], xT.t[:, cm:cm + ln].rearrange("(c p) t -> p c t", p=128), [xT], [xt_])
            o += ln
        for h in range(8):
            qt_ = qtile[(it * 8 + h) % 2]
            o = 0
            for (c0, ln) in pieces:
                K.dma(qt_.t[:, o:o + ln], W["qt_s"].t[h, :, c0:c0 + ln], [W["qt_s"]], [qt_])
                o += ln
            pso = ps_o[h % 2]
            nkt = S // 128
            jobs = []
            for kb_ in range(nkb):
                jobs.append(kb_)
            kcur = None
            vcur = None
            pend = None
            bi = 0
            for kt in range(nkt):
                if kt % (KBLK // 128) == 0:
                    kb_ = kt // (KBLK // 128)
                    kcur = kblk[(it * 8 * nkb + h * nkb + kb_) % 2]
                    vcur = vblk[(it * 8 * nkb + h * nkb + kb_) % 2]
                    K.dma(kcur.t[:, :], W["kt_s"].t[h, :, kb_ * KBLK:(kb_ + 1) * KBLK], [W["kt_s"]], [kcur])
                    K.dma(vcur.t[:, :, 0:64],
                          W["v_s"].t[kb_ * KBLK:(kb_ + 1) * KBLK, h * 64:(h + 1) * 64].rearrange("(k p) d -> p k d", p=128),
                          [W["v_s"]], [vcur])
                kk = kt % (KBLK // 128)
                pss = ps_s[kt % 3]
                K.mm(pss.t[:, 0:nq], kcur.t[:, kk * 128:(kk + 1) * 128], qt_.t[:, 0:nq], True, True, [kcur, qt_], [pss])
                if pend is not None:
                    (pb, pkt, pv, pkk) = pend
                    K.mm(pso.t[0:65, 0:nq], pv.t[:, pkk, :], pb.t[:, 0:nq], pkt == 0, False, [pv, pb], [pso])
                pb = pbuf[kt % 3]
                K.act(pb.t[:, 0:nq], pss.t[:, 0:nq], AF.Exp, [pss], [pb], scale=scale)
                pend = (pb, kt, vcur, kk)
            (pb, pkt, pv, pkk) = pend
            K.mm(pso.t[0:65, 0:nq], pv.t[:, pkk, :], pb.t[:, 0:nq], pkt == 0, True, [pv, pb], [pso])
            K.S.op("dve", lambda e, pso=pso, nq=nq: e.reciprocal(out=rd_t.t[64:65, 0:nq], in_=pso.t[64:65, 0:nq]),
                   [pso], [rd_t])
            psb_ = ps_m
            K.mm(psb_.t[0:64, 0:nq], W["ones_f"].t[64:65, 0:64], rd_t.t[64:65, 0:nq], True, True, [W["ones_f"], rd_t],
                 [psb_])
            K.cp("act", bcs.t[:, 0:nq], psb_.t[0:64, 0:nq], [psb_], [bcs])
            K.tt("dve", at[h].t[:, 0:nq], pso.t[0:64, 0:nq], bcs.t[:, 0:nq], ALU.mult, [pso, bcs], [at[h]])
        x1 = x1t[it % 2]
        for dc in range(8):
            psm = K.ps() if False else ps_m
            o = 0
            for (c0, ln) in pieces:
                for c in range(4):
                    K.mm(psm.t[:, o:o + ln], wo_rg_b.t[:, c, dc * 128:(dc + 1) * 128], GY.t[:, c, c0:c0 + ln],
                         c == 0, False, [wo_rg_b, GY], [psm])
                for h in range(8):
                    K.mm(psm.t[:, o:o + ln], wo_at_b.t[:, h, dc * 128:(dc + 1) * 128], at[h].t[:, o:o + ln],
                         False, h == 7, [wo_at_b, at[h]], [psm])
                o += ln
            K.stt(x1.t[:, dc, 0:nq], psm.t[:, 0:nq], g1[:, dc:dc + 1], xt_.t[:, dc, 0:nq], ALU.mult, ALU.add,
                  [psm, K.modv, xt_], [x1])
        o = 0
        for (c0, ln) in pieces:
            K.dma(W["x1"].t[:, c0:c0 + ln].rearrange("(c p) t -> p c t", p=128), x1.t[:, :, o:o + ln], [x1], [W["x1"]])
            o += ln
        it += 1


def _norm_rope_from_sb(K, khf, n, small, gcol, rp, p96_b, sqb, khn, khb, rt1, rnext, dst, dst_ap):
    sq = rnext(sqb)
    K.act(sq.t[0:96, 0:n], khf.t[:, 0:n], AF.Square, [khf], [sq])
    ps2 = K.ps()
    K.mm(ps2.t[0:96, 0:n], K.ones_b.t[0:96, 0:96], sq.t[0:96, 0:n], True, True, [K.ones_b, sq], [ps2])
    r, rv = K.rstd_from_ps(ps2, ps2.t[0:96, 0:n], n, 96.0, parts=96)
    kb = rnext(khb)
    K.stt(kb.t[:, 0:n], khf.t[:, 0:n], small.t[0:96, gcol:gcol + 1], rv, ALU.mult, ALU.mult, [khf, small, r], [kb])
    ps3 = K.ps()
    K.mm(ps3.t[0:96, 0:n], p96_b.t[:, :], kb.t[:, 0:n], True, True, [p96_b, kb], [ps3])
    t1 = rnext(rt1)
    K.tt("pool", t1.t[64:96, 0:n], kb.t[64:96, 0:n], rp.t[64:96, 0, 0:n], ALU.mult, [kb, rp], [t1])
    t2 = rnext(rt1)
    K.tt("dve", t2.t[64:96, 0:n], ps3.t[64:96, 0:n], rp.t[64:96, 1, 0:n], ALU.mult, [ps3, rp], [t2])
    K.tt("dve", kb.t[64:96, 0:n], t1.t[64:96, 0:n], t2.t[64:96, 0:n], ALU.add, [t1, t2, kb], [kb])
    K.dma(dst_ap, kb.t[:, 0:n], [kb], [dst])


def _mixer_c(K, si, kind, x2, x3, W, limit):
    MV, GM = W["MV"], W["GM"]
    LK, links = W["LK"], W["links"]
    gmod, sh, g1 = GM(1, si, 0), MV(1, si, 0), MV(1, si, 2)
    K.stage = K.ring(2, [128, 2048], F32, "stage")
    cw_b = K.sb([128, 8, 3072], BF16, "cw_b")
    for kc in range(8):
        for hf in range(2):
            K.load_bf16(cw_b, cw_b.t[:, kc, hf * 1536:(hf + 1) * 1536], W["c_w_in"],
                        W["c_w_in"].t[:, kc, hf * 1536:(hf + 1) * 1536], [128, 1536])
    cd_b = K.sb([128, 24, 128], BF16, "cd_b")
    for hf in range(2):
        K.load_bf16(cd_b, cd_b.t[:, hf * 12:(hf + 1) * 12, :], W["cdiag"], W["cdiag"].t[:, hf * 12:(hf + 1) * 12, :],
                    [128, 12, 128])
    co_b = K.sb([128, 8, 1024], BF16, "co_b")
    for kc in range(0, 8, 2):
        K.load_bf16(co_b, co_b.t[:, kc:kc + 2, :], W["c_w_out"], W["c_w_out"].t[:, kc:kc + 2, :], [128, 2, 1024])
    NE = NT + 2
    xe_r = K.ring(2, [128, 8, NE], F32, "cxe")
    he_r = K.ring(2, [128, 8, NE], BF16, "che")
    K.nm_sq = K.sb([128, 8, NE], BF16, "cnmsq")
    K.nm_tmp = K.sb([128, 8, NE], F32, "cnmtmp")
    cgs = K.ring(2, [128, NE], F32, "cgs")
    u_b = K.sb([128, 8, NE], BF16, "u_b")
    bg = K.sb([128, 8, NT], F32, "bg")
    y_b = K.ring(2, [128, 8, NT], BF16, "y_b")
    x3t = K.ring(2, [128, 8, NT], F32, "x3t")
    ntl = S_S // NT
    if limit.get("ctiles"):
        ntl = limit["ctiles"]
    for k in range(ntl):
        xe = xe_r[k % 2]
        K.dma(xe.t[:, :, :], x2.t[:, NT * k:NT * k + NE].rearrange("(c p) t -> p c t", p=128), [x2], [xe])
        he = he_r[k % 2]
        K.norm_mod(xe, NE, gmod, sh, he)
        for ch in range(8):
            psc = K.ps()
            for kc in range(8):
                K.mm(psc.t[:, 0:NE], cw_b.t[:, kc, 1024 + ch * 128:1024 + (ch + 1) * 128], he.t[:, kc, :], kc == 0,
                     kc == 7, [cw_b, he], [psc])
            psx = K.ps()
            for kc in range(8):
                K.mm(psx.t[:, 0:NE], cw_b.t[:, kc, 2048 + ch * 128:2048 + (ch + 1) * 128], he.t[:, kc, :], kc == 0,
                     kc == 7, [cw_b, he], [psx])
            cg = cgs[ch % 2]
            K.cp("act", cg.t[:, :], psc.t[:, 0:NE], [psc], [cg])
            K.tt("dve", u_b.t[:, ch, :], psx.t[:, 0:NE], cg.t[:, :], ALU.mult, [psx, cg], [u_b])
            psb_ = K.ps()
            for kc in range(8):
                K.mm(psb_.t[:, 0:NT], cw_b.t[:, kc, ch * 128:(ch + 1) * 128], he.t[:, kc, 1:1 + NT], kc == 0, kc == 7,
                     [cw_b, he], [psb_])
            K.cp("act", bg.t[:, ch, :], psb_.t[:, 0:NT], [psb_], [bg])
        if k == 0:
            key = "c" if kind == "p" else "zero"
            _lk(K, u_b, u_b.t[:, :, 0:1], key, LK, links)
        if k == S_S // NT - 1:
            key = "d" if kind == "p" else "zero"
            _lk(K, u_b, u_b.t[:, :, NE - 1:NE], key, LK, links)
        yb = y_b[k % 2]
        for ch in range(8):
            psv = K.ps()
            for tp in range(3):
                K.mm(psv.t[:, 0:NT], cd_b.t[:, ch * 3 + tp, :], u_b.t[:, ch, tp:tp + NT], tp == 0, tp == 2, [cd_b, u_b],
                     [psv])
            K.tt("dve", yb.t[:, ch, :], psv.t[:, 0:NT], bg.t[:, ch, :], ALU.mult, [psv, bg], [yb])
        x3_ = x3t[k % 2]
        for dc in range(8):
            psm = K.ps()
            for c in range(8):
                K.mm(psm.t[:, 0:NT], co_b.t[:, c, dc * 128:(dc + 1) * 128], yb.t[:, c, :], c == 0, c == 7, [co_b, yb],
                     [psm])
            K.stt(x3_.t[:, dc, :], psm.t[:, 0:NT], g1[:, dc:dc + 1], xe.t[:, dc, 1:1 + NT], ALU.mult, ALU.add,
                  [psm, K.modv, xe], [x3_])
        K.dma(x3.t[:, NT * k:NT * (k + 1)].rearrange("(c p) t -> p c t", p=128), x3_.t[:, :, :], [x3_], [x3])


def _lk(K, t, ap, key, LK, links):
    if key == "zero":
        K.ts("pool", ap, ap, 0.0, None, ALU.mult, None, [t], [t])
    else:
        K.ts("pool", ap, ap, LK[key], None, ALU.mult, None, [t, links], [t])


def _peer(K, l, si, xsrc, xdst, tlist, W, limit):
    MV, GM = W["MV"], W["GM"]
    gmod, sh, g2 = GM(l, si, 1), MV(l, si, 3), MV(l, si, 5)
    ident_b, ident_f = W["ident_b"], W["ident_f"]
    wq_b, uv_b = W["wq_b"], W["uv_b"]
    K.stage = K.ring(1, [128, 256], F32, "pstage")
    kT = K.sb([128, 2, 128], BF16, "kT")
    K.load_bf16(kT, kT.t[:], W["k12T"], W["k12T"].t[:, 2 * l:2 * l + 2, :], [128, 2, 128])
    GT = K.sb([128, NT, 128], BF16, "GT")
    RT = K.sb([128, 128, 128], BF16, "RT")
    OHT = K.sb([128, 128, 128], BF16, "OHT")
    ROr = K.ring(2, [128, 128, 64], BF16, "RO")
    s12 = K.sb([128, 16, 128], F32, "s12")
    e2 = K.sb([128, 8, 128], BF16, "e2")
    vv = K.sb([128, 16, 16], F32, "vv")
    vs = K.sb([128, 8, 16], F32, "vs")
    sm = K.sb([128, 8, 64], F32, "sm")
    h2 = K.sb([128, 8, NT], BF16, "h2")
    wqs = K.ring(2, [128, 8, 128], BF16, "wqs")
    NSB = 4
    strm = K.sb([128, NSB, 2048], BF16, "strm")
    sbuf_ = [Buf("strm%d" % i) for i in range(NSB)]
    uvs = [T(strm.t[:, i, :], sbuf_[i]) for i in range(NSB)]
    qTv = strm.t[:, :, :].rearrange("p a b -> p (a b)")[:, 0:16 * NT].rearrange("p (c t) -> p c t", t=NT)
    qTb = sbuf_[0:2]
    gel = K.ring(3, [128, NT], BF16, "gel")
    atb = K.ring(3, [128, NT], BF16, "atb")
    rtb = RT.t[:, :, :].rearrange("p a b -> p (a b)")
    rtf = rtb.bitcast(F32)
    ohf = OHT.t[:, :, :].rearrange("p a b -> p (a b)").bitcast(F32)
    K.nm_sq = T(rtb[:, 0:8 * NT].rearrange("p (c t) -> p c t", t=NT), RT.b)
    K.nm_tmp = T(ohf[:, 0:8 * NT].rearrange("p (c t) -> p c t", t=NT), OHT.b)
    GTB = [GT.b, Buf("GTb")]
    gtf = GT.t[:, 128:256, :].rearrange("p a b -> p (a b)").bitcast(F32)
    tmpT = T(gtf[:, 0:2048].rearrange("p (c j) -> p c j", j=128), GTB[1])
    e2fT = T(gtf[:, 2048:3072].rearrange("p (h j) -> p h j", j=128), GTB[1])
    tmp = tmpT.t
    e2f = e2fT.t
    xt = T(rtf[:, 4096:4096 + 8 * NT].rearrange("p (c t) -> p c t", t=NT), RT.b)
    cand = ohf[:, 0:2048].rearrange("p (h a b) -> p h a b", a=16, b=16)
    ctmp = ohf[:, 2048:4096].rearrange("p (h a b) -> p h a b", a=16, b=16)
    sel = ohf[:, 4096:6144].rearrange("p (h a b) -> p h a b", a=16, b=16)
    t1v = ohf[:, 6144:8192].rearrange("p (h a b) -> p h a b", a=16, b=16)
    osb = T(ohf[:, 0:1024], OHT.b)
    nchunk = limit.get("pchunks", 128)

    def load_x(pieces):
        o = 0
        for (c0, ln) in pieces:
            K.dma(xt.t[:, :, o:o + ln], xsrc.t[:, c0:c0 + ln].rearrange("(c p) t -> p c t", p=128), [xsrc], [xt])
            o += ln

    for pieces in tlist:
        nt = sum(p[1] for p in pieces)
        load_x(pieces)
        K.norm_mod(xt, nt, gmod, sh, h2)
        for c in range(16):
            wq_ = wqs[c % 2]
            K.dma(wq_.t[:, :, :], wq_b.t[l, c], [wq_b], [wq_])
            ps = K.ps()
            for kc in range(8):
                K.mm(ps.t[:, 0:nt], wq_.t[:, kc, :], h2.t[:, kc, 0:nt], kc == 0, kc == 7, [wq_, h2], [ps])
            K.cp("act" if c % 2 == 0 else "dve", qTv[:, c, 0:nt], ps.t[:, 0:nt], [ps], qTb)
        for t0 in range(0, nt, 128):
            ns = min(128, nt - t0)
            for g in range(4):
                ps = K.ps()
                for m in range(4):
                    c = g * 4 + m
                    K.mm(ps.t[0:ns, m * 128:(m + 1) * 128], qTv[:, c, t0:t0 + ns], kT.t[:, c % 2, :], True, True,
                         qTb + [kT], [ps])
                K.cp("act", s12.t[0:ns, g * 4:(g + 1) * 4, :], ps.t[0:ns, :].rearrange("p (m j) -> p m j", j=128),
                     [ps], [s12])
            for c in range(16):
                K.S.op("dve", lambda e, c=c, ns=ns: e.max(out=vv.t[0:ns, c, 0:8], in_=s12.t[0:ns, c, :]), [s12], [vv])
            for c in range(16):
                K.S.op("dve", lambda e, c=c, ns=ns: e.match_replace(out=tmp[0:ns, c, :], in_to_replace=vv.t[0:ns, c, 0:8],
                                                                    in_values=s12.t[0:ns, c, :], imm_value=-BIG),
                       [s12, vv], [tmpT])
            for c in range(16):
                K.S.op("dve", lambda e, c=c, ns=ns: e.max(out=vv.t[0:ns, c, 8:16], in_=tmp[0:ns, c, :]), [tmpT], [vv])
            v4 = vv.t[:, :, :].rearrange("p (h w) a -> p h w a", w=2)
            v1 = v4[0:ns, :, 0, :]
            v2 = v4[0:ns, :, 1, :]
            s4 = s12.t[:, :, :].rearrange("p (h w) j -> p h w j", w=2)
            s1 = s4[0:ns, :, 0, :]
            s2 = s4[0:ns, :, 1, :]
            K.tt("pool", cand[0:ns], v1.unsqueeze(3).to_broadcast([ns, 8, 16, 16]),
                 v2.unsqueeze(2).to_broadcast([ns, 8, 16, 16]), ALU.add, [vv], [OHT])
            for h in range(8):
                K.S.op("dve", lambda e, h=h, ns=ns: e.max(out=vs.t[0:ns, h, 0:8], in_=cand[0:ns, h]), [OHT], [vs])
            for h in range(8):
                K.S.op("dve", lambda e, h=h, ns=ns: e.match_replace(out=ctmp[0:ns, h], in_to_replace=vs.t[0:ns, h, 0:8],
                                                                    in_values=cand[0:ns, h], imm_value=-BIG),
                       [OHT, vs], [OHT])
            for h in range(8):
                K.S.op("dve", lambda e, h=h, ns=ns: e.max(out=vs.t[0:ns, h, 8:16], in_=ctmp[0:ns, h]), [OHT], [vs])
            ev = sm.t[0:ns, :, 0:16]
            thr = sm.t[0:ns, :, 16:32]
            Z = sm.t[0:ns, :, 32]
            rZ = sm.t[0:ns, :, 33]
            w1 = sm.t[0:ns, :, 40:56]
            K.tt("pool", ev, vs.t[0:ns], vs.t[0:ns, :, 0:1].to_broadcast([ns, 8, 16]), ALU.subtract, [vs], [sm])
            K.act(ev, ev, AF.Exp, [sm], [sm])
            K.S.op("dve", lambda e, ev=ev, Z=Z: e.tensor_reduce(out=Z, in_=ev, op=ALU.add, axis=AX.X), [sm], [sm])
            K.S.op("dve", lambda e, Z=Z, rZ=rZ: e.reciprocal(out=rZ, in_=Z), [sm], [sm])
            K.tt("pool", w1, v1, v1[:, :, 0:1].to_broadcast([ns, 8, 16]), ALU.subtract, [vv], [sm])
            K.act(w1, w1, AF.Exp, [sm], [sm])
            K.tt("pool", w1, w1, rZ.unsqueeze(2).to_broadcast([ns, 8, 16]), ALU.mult, [sm], [sm])
            K.tt("pool", e2f[0:ns], s2, v2[:, :, 0:1].to_broadcast([ns, 8, 128]), ALU.subtract, [s12, vv], [e2fT])
            K.act(e2.t[0:ns], e2f[0:ns], AF.Exp, [e2fT], [e2])
            K.tt("dve", sel[0:ns], cand[0:ns], vs.t[0:ns, :, 15:16].unsqueeze(3).to_broadcast([ns, 8, 16, 16]),
                 ALU.is_ge, [OHT, vs], [OHT])
            K.tt("pool", t1v[0:ns], sel[0:ns], v2.unsqueeze(2).to_broadcast([ns, 8, 16, 16]), ALU.mult, [OHT, vv], [OHT])
            K.ts("pool", sel[0:ns], sel[0:ns], -BIG, BIG, ALU.mult, ALU.add, [OHT], [OHT])
            K.tt("pool", t1v[0:ns], t1v[0:ns], sel[0:ns], ALU.add, [OHT], [OHT])
            K.S.op("dve", lambda e, ns=ns, thr=thr: e.tensor_reduce(out=thr, in_=t1v[0:ns], op=ALU.min, axis=AX.X),
                   [OHT], [sm])
            rnd = 0
            for ih in range(2):
                RO = ROr[rnd % 2]
                rnd += 1
                is_ = slice(ih * 64, (ih + 1) * 64)
                for h in range(8):
                    for a_ in range(16):
                        K.ts("dve", RO.t[0:ns, h * 16 + a_, :], s1[:, h, is_], v1[:, h, a_:a_ + 1], w1[:, h, a_:a_ + 1],
                             ALU.is_equal, ALU.mult, [s12, vv, sm], [RO])
                _transposes(K, RO, OHT, ih, ns, ident_b)
            for jh in range(2):
                RO = ROr[rnd % 2]
                rnd += 1
                js = slice(jh * 64, (jh + 1) * 64)
                for h in range(8):
                    for a_ in range(16):
                        K.stt(RO.t[0:ns, h * 16 + a_, :], s2[:, h, js], thr[:, h, a_:a_ + 1], e2.t[0:ns, h, js],
                              ALU.is_ge, ALU.mult, [s12, sm, e2], [RO])
                _transposes(K, RO, RT, jh, ns, ident_b)
            for tb in range(0, ns, 4):
                nb = min(4, ns - tb)
                ps = K.ps()
                for u in range(nb):
                    K.mm(ps.t[:, u * 128:(u + 1) * 128], RT.t[:, :, tb + u], OHT.t[:, :, tb + u], True, True, [RT, OHT],
                         [ps])
                K.cp("act", GT.t[:, t0 + tb:t0 + tb + nb, :].rearrange("p t i -> p (t i)"), ps.t[:, 0:nb * 128],
                     [ps], [GTB[t0 // 128]])
        nsub = (nt + 127) // 128
        pso = [[K.psf[2 + 2 * s_ + dh] for dh in range(2)] for s_ in range(nsub)]
        pss = [K.psf[0], K.psf[1]]
        PF = NSB - 2

        def issue_dma(i):
            uv_ = uvs[i % NSB]
            K.dma(uv_.t, uv_b.t[l, i], [uv_b], [uv_])

        def front(i):
            uv_ = uvs[i % NSB]
            ps = pss[i % 2]
            for kc in range(8):
                K.mm(ps.t[:, 0:nt], uv_.t[:, kc * 128:(kc + 1) * 128], h2.t[:, kc, 0:nt], kc == 0, kc == 7,
                     [uv_, h2], [ps])
            gl = gel[i % 3]
            K.act(gl.t[:, 0:nt], ps.t[:, 0:nt], AF.Gelu_apprx_tanh, [ps], [gl])
            ab = atb[i % 3]
            K.tt("dve" if i % 3 else "pool", ab.t[:, 0:nt], gl.t[:, 0:nt], GT.t[:, 0:nt, i], ALU.mult, [gl] + GTB, [ab])

        def back(i):
            uv_ = uvs[i % NSB]
            ab = atb[i % 3]
            for s_ in range(nsub):
                ns = min(128, nt - s_ * 128)
                for dh in range(2):
                    K.mm(pso[s_][dh].t[0:ns, :], ab.t[:, s_ * 128:s_ * 128 + ns],
                         uv_.t[:, 1024 + dh * 512:1024 + (dh + 1) * 512], i == 0, i == nchunk - 1, [ab, uv_],
                         [pso[s_][dh]])

        for i in range(min(PF, nchunk)):
            issue_dma(i)
        for i in range(nchunk + 1):
            if i < nchunk:
                front(i)
            if i >= 1:
                back(i - 1)
            if i + PF < nchunk:
                issue_dma(i + PF)
        load_x(pieces)
        for s_ in range(nsub):
            ns = min(128, nt - s_ * 128)
            for dh in range(2):
                K.cp("act", osb.t[0:ns, dh * 512:(dh + 1) * 512], pso[s_][dh].t[0:ns, :], [pso[s_][dh]], [osb])
            for dc in range(8):
                ps = pss[dc % 2]
                K.tr(ps.t[:, 0:ns], osb.t[0:ns, dc * 128:(dc + 1) * 128], ident_f.t[0:ns, 0:ns], [osb, ident_f], [ps])
                K.stt(xt.t[:, dc, s_ * 128:s_ * 128 + ns], ps.t[:, 0:ns], g2[:, dc:dc + 1],
                      xt.t[:, dc, s_ * 128:s_ * 128 + ns], ALU.mult, ALU.add, [ps, K.modv, xt], [xt])
        o = 0
        for (c0, ln) in pieces:
            K.dma(xdst.t[:, c0:c0 + ln].rearrange("(c p) t -> p c t", p=128), xt.t[:, :, o:o + ln], [xt], [xdst])
            o += ln


def _transposes(K, RO, DST, half, ns, ident_b):
    for g in range(8):
        pb = K.psb[g % 2]
        for u in range(8):
            jj = g * 8 + u
            K.tr(pb.t[:, u * 128:u * 128 + ns], RO.t[0:ns, :, jj], ident_b.t[0:ns, 0:ns], [RO, ident_b], [pb])
        j0 = half * 64 + g * 8
        if ns == 128:
            K.cp("act", DST.t[:, j0:j0 + 8, :].rearrange("p j t -> p (j t)"), pb.t[:, :], [pb], [DST])
        else:
            K.cp("act", DST.t[:, j0:j0 + 8, 0:ns], pb.t[:, :].rearrange("p (j t) -> p j t", t=128)[:, :, 0:ns], [pb], [DST])


def _fm(v, n=8):
    return np.ascontiguousarray(np.asarray(v, np.float32).reshape(n, 128).T)


def _wmat(w):
    K_, N = w.shape
    return np.ascontiguousarray(np.asarray(w, np.float32).reshape(K_ // 128, 128, N).transpose(1, 0, 2))


def _rope_tables(pos):
    inv = (1.0 / (10000.0 ** (np.arange(0, 32, 2, dtype=np.float32) / np.float32(32)))).astype(np.float32)
    ang = pos.astype(np.float32)[:, None] * inv[None, :]
    c = np.cos(ang).astype(np.float32).T
    s = np.sin(ang).astype(np.float32).T
    out = np.empty((32, 2, pos.shape[0]), np.float32)
    out[0:16, 0] = c
    out[16:32, 0] = c
    out[0:16, 1] = s
    out[16:32, 1] = s
    return out


def host_inputs(I):
    f = np.float32
    shared = {}
    shared["ident"] = np.eye(128, dtype=f)
    p96 = np.zeros((96, 96), f)
    for m in range(16):
        p96[64 + m + 16, 64 + m] = -1.0
        p96[64 + m, 64 + m + 16] = 1.0
    shared["p96"] = p96
    shared["adaw"] = np.ascontiguousarray(I["ada_w"].reshape(2, 8, 128, 6144).transpose(0, 2, 1, 3))
    shared["adab"] = np.ascontiguousarray(I["ada_b"].reshape(2, 48, 128).transpose(2, 0, 1))
    shared["n1g"] = np.ascontiguousarray(I["norm1_g"].reshape(2, 8, 128).transpose(2, 0, 1))
    shared["n2g"] = np.ascontiguousarray(I["norm2_g"].reshape(2, 8, 128).transpose(2, 0, 1))
    shared["w_in"] = _wmat(I["ab_w_in"][0])
    wkr = np.zeros((128, 8, 96), f)
    wkr[:, :, 64:96] = shared["w_in"][:, :, 1408:1440]
    shared["wkr"] = wkr
    cw = I["rg_conv_w"][0]
    dg = np.zeros((128, 16, 128), f)
    for ch in range(4):
        for k in range(4):
            dg[np.arange(128), ch * 4 + k, np.arange(128)] = cw[k, ch * 128:(ch + 1) * 128]
    shared["rgdiag"] = dg
    shared["rgcb"] = _fm(I["rg_conv_b"][0], 4)
    bd = np.zeros((128, 16, 128), f)
    rgb = np.zeros((128, 16), f)
    for d in range(2):
        for wi, (wn, bn) in enumerate((("rg_wa", "rg_ba"), ("rg_wx", "rg_bx"))):
            for ch in range(4):
                idx = (d * 2 + wi) * 4 + ch
                for hh in range(2):
                    bd[hh * 64:(hh + 1) * 64, idx, hh * 64:(hh + 1) * 64] = I[wn][0, d, ch * 2 + hh]
                rgb[:, idx] = I[bn][0, d, ch * 128:(ch + 1) * 128]
    shared["rgbd"] = bd
    shared["rgb"] = rgb
    lam = np.zeros((128, 8), f)
    for d in range(2):
        lam[:, d * 4:(d + 1) * 4] = _fm(I["rg_lambda"][0, d], 4)
    shared["rglam"] = lam
    shared["qnorm"] = _fm(I["mla_q_norm"][0], 2)
    shared["w_uq"] = _wmat(I["mla_w_uq"][0])
    shared["kvnorm"] = _fm(I["mla_kv_norm"][0], 1)
    wukv = I["mla_w_ukv"][0].reshape(128, 8, 128)
    shared["wk"] = np.ascontiguousarray(wukv[:, :, 0:64].reshape(128, 512))
    shared["wv"] = np.ascontiguousarray(wukv[:, :, 64:128].reshape(128, 512))
    shared["qng"] = np.ascontiguousarray(np.stack([I["mla_qn_q"][0], I["mla_qn_k"][0]], axis=1).astype(f))
    wo = I["ab_w_out"][0]
    shared["wo_rg"] = _wmat(wo[0:512])
    shared["wo_at"] = np.ascontiguousarray(wo[512:1024].reshape(8, 64, 1024).transpose(1, 0, 2))
    shared["c_w_in"] = _wmat(I["c_w_in"][0])
    ccw = I["c_conv_w"][0]
    cd = np.zeros((128, 24, 128), f)
    for ch in range(8):
        for k in range(3):
            cd[np.arange(128), ch * 3 + k, np.arange(128)] = ccw[k, ch * 128:(ch + 1) * 128]
    shared["cdiag"] = cd
    shared["c_w_out"] = _wmat(I["c_w_out"][0])
    shared["wq"] = np.ascontiguousarray(I["peer_wq"].reshape(2, 8, 128, 2048).transpose(0, 2, 1, 3))
    k12 = np.zeros((128, 4, 128), f)
    for l in range(2):
        k12[:, 2 * l + 0, :] = I["peer_k1"][l].T
        k12[:, 2 * l + 1, :] = I["peer_k2"][l].T
    shared["k12T"] = k12
    U = I["peer_u"].reshape(2, 128, 128, 8, 128)
    shared["ut"] = np.ascontiguousarray(U.transpose(0, 1, 4, 3, 2)).reshape(2, 128, 128, 1024)
    shared["pv"] = np.ascontiguousarray(I["peer_v"].reshape(2, 128, 128, 1024))
    shared["rope_s"] = _rope_tables(np.arange(S_S))
    maps = []
    for c in range(8):
        b, q = c // 4, c % 4
        m = dict(shared)
        m["xs"] = np.ascontiguousarray(I["x_sample"][c].T)
        start = ((q + 1) * 4096 + 1) % S_P
        pos = (start + np.arange(S_P)) % S_P
        m["xp"] = np.ascontiguousarray(I["x_prompt"][b][pos].T)
        m["rope_p"] = _rope_tables(pos)
        cv = np.zeros((128, 8, 2), f)
        cv[:, :, 0] = _fm(I["c_sample"][c])
        cv[:, :, 1] = _fm(I["c_prompt"][b])
        m["cvec"] = cv
        lk = np.ones((128, 4), f)
        lk[:, {2: 0, 1: 1, 0: 2, 3: 3}[q]] = 0.0
        m["links"] = lk
        maps.append(m)
    return maps


_CACHE = {}


def kernel(**inputs):
    I = {k: np.asarray(v) for k, v in inputs.items()}
    maps = host_inputs(I)
    if "nc" not in _CACHE:
        _CACHE["nc"] = build()
    nc, K = _CACHE["nc"]
    res = run_bass_kernel_spmd(nc, maps, core_ids=list(range(8)))
    y_prompt = np.empty((2, S_P, 1024), np.float32)
    y_sample = np.empty((8, S_S, 1024), np.float32)
    for c in range(8):
        r = res.results[c]
        b, q = c // 4, c % 4
        y_sample[c] = r["ys"].T
        y_prompt[b, q * 4096:(q + 1) * 4096] = r["yp"].T
    return (y_prompt, y_sample)
```

```python
import os
import contextlib
import numpy as np
import concourse.bass as bass
import concourse.mybir as mybir
from concourse.bass_utils import run_bass_kernel_spmd

F32 = mybir.dt.float32
BF16 = mybir.dt.bfloat16
AF = mybir.ActivationFunctionType
ALU = mybir.AluOpType
AX = mybir.AxisListType
ENGS = ["pe", "act", "dve", "pool", "sp"]
EPS = 1e-6
BIG = 1.0e30

S_S = 4096
S_P = 16384
NOWN = 4098
ROT0 = 12286
TA = 256
NT = 256


class Buf:
    __slots__ = ("name", "last_w", "readers", "dsem", "dval")

    def __init__(self, name):
        self.name = name
        self.last_w = None
        self.readers = []
        self.dsem = None
        self.dval = 0


class T:
    __slots__ = ("t", "b")

    def __init__(self, t, b):
        self.t = t
        self.b = b

    def __getitem__(self, k):
        return self.t[k]


def _b(x):
    return x.b if isinstance(x, T) else x


class Sched:
    def __init__(self, nc, stack):
        self.nc = nc
        self.stack = stack
        self.ops = {e: [] for e in ENGS}
        self.cnt = {e: 0 for e in ENGS}
        self.sem = {e: stack.enter_context(nc.semaphore("sem_" + e)) for e in ENGS if e != "sp"}
        self.seen = {e: {} for e in ENGS}
        self.dma_bufs = []
        self.free_sems = []
        self.nsem_alloc = 0
        self.ninstr = 0

    def _deps(self, eng, reads, writes):
        deps = []
        own = self.sem.get(eng)
        for b in reads:
            if b.last_w is not None:
                deps.append(b.last_w)
        for b in writes:
            if b.last_w is not None and b.last_w[0] is not own:
                deps.append(b.last_w)
            deps.extend(r for r in b.readers if r[0] is not own)
        seen = self.seen[eng]
        best = {}
        pe_sem = self.sem["pe"]
        for (s, v) in deps:
            if eng == "pe" and s is pe_sem:
                continue
            k = id(s)
            if seen.get(k, 0) >= v:
                continue
            if k not in best or best[k][1] < v:
                best[k] = (s, v)
        for k, (s, v) in best.items():
            seen[k] = v
        return list(best.values())

    def op(self, eng, fn, reads=(), writes=()):
        reads = [_b(x) for x in reads]
        writes = [_b(x) for x in writes]
        deps = self._deps(eng, reads, writes)
        self.cnt[eng] += 1
        s = self.sem[eng]
        tok = (s, self.cnt[eng])
        self.ops[eng].append((deps, fn, s, 1))
        for b in writes:
            b.last_w = tok
            b.readers = []
        for b in reads:
            if b in writes:
                continue
            b.readers = [r for r in b.readers if r[0] is not s] + [tok]
        self.ninstr += 1

    def dma(self, out_ap, in_ap, reads=(), writes=(), eng="sp"):
        reads = [_b(x) for x in reads]
        writes = [_b(x) for x in writes]
        deps = self._deps(eng, reads, writes)
        tb = writes[0] if writes else reads[0]
        if tb.dsem is None:
            if self.free_sems:
                tb.dsem, tb.dval = self.free_sems.pop()
            else:
                self.nsem_alloc += 1
                tb.dsem = self.stack.enter_context(self.nc.semaphore("dsem%d" % self.nsem_alloc))
                tb.dval = 0
            self.dma_bufs.append(tb)
        tb.dval += 16
        tok = (tb.dsem, tb.dval)
        self.ops[eng].append((deps, lambda e: e.dma_start(out=out_ap, in_=in_ap, allow_slow_non_contiguous=True), tb.dsem, 16))
        for b in writes:
            b.last_w = tok
            b.readers = []
        for b in reads:
            b.readers = b.readers + [tok]
        self.ninstr += 1
        return tok

    def barrier(self):
        toks = [(self.sem[e], self.cnt[e]) for e in self.sem if self.cnt[e] > 0]
        toks += [(b.dsem, b.dval) for b in self.dma_bufs]
        for e in ENGS:
            deps = []
            for (s, v) in toks:
                if e in self.sem and s is self.sem[e]:
                    continue
                if self.seen[e].get(id(s), 0) >= v:
                    continue
                self.seen[e][id(s)] = v
                deps.append((s, v))
            self.ops[e].append((deps, None, None, 0))
        for b in self.dma_bufs:
            self.free_sems.append((b.dsem, b.dval))
            b.dsem = None
        self.dma_bufs = []

    def flush(self, final_waits=()):
        nc = self.nc
        engmap = {"pe": "tensor", "act": "scalar", "dve": "vector", "pool": "gpsimd", "sp": "sync"}
        with nc.Block() as block:
            for e in ENGS:
                ops = self.ops[e]
                fw = list(final_waits) if e == "sp" else []

                def body(eng, ops=ops, fw=fw):
                    for (deps, fn, s, inc) in ops:
                        for (ds, dv) in deps:
                            eng.wait_ge(ds, dv)
                        if fn is not None:
                            fn(eng).then_inc(s, inc)
                    for (ds, dv) in fw:
                        eng.wait_ge(ds, dv)
                getattr(block, engmap[e])(body)
        self.ops = {e: [] for e in ENGS}


class KB:
    def __init__(self, nc, st, debug):
        self.nc = nc
        self.st = st
        self.S = Sched(nc, st)
        self.ph = None
        self.uid = 0
        self.debug = debug
        self.inputs = {}
        self.outputs = {}
        self.psr = 0

    def sb(self, shape, dt=F32, name="t", buf=None):
        self.uid += 1
        nm = "%s_%d" % (name, self.uid)
        t = (self.ph or self.st).enter_context(self.nc.sbuf_tensor(nm, list(shape), dt))
        return T(t, buf if buf is not None else Buf(nm))

    def ring(self, n, shape, dt=F32, name="r"):
        return [self.sb(shape, dt, name) for _ in range(n)]

    def inp(self, name, shape, dt=F32):
        t = T(self.nc.dram_tensor(name, list(shape), dt, kind="ExternalInput").ap(), Buf(name))
        self.inputs[name] = t
        return t

    def outp(self, name, shape, dt=F32):
        t = T(self.nc.dram_tensor(name, list(shape), dt, kind="ExternalOutput").ap(), Buf(name))
        self.outputs[name] = t
        return t

    def scratch(self, name, shape, dt):
        kind = "ExternalOutput" if (self.debug and name in self.debug) else "Internal"
        t = T(self.nc.dram_tensor(name, list(shape), dt, kind=kind).ap(), Buf(name))
        if kind == "ExternalOutput":
            self.outputs[name] = t
        return t

    @contextlib.contextmanager
    def phase(self):
        with contextlib.ExitStack() as ph:
            old = self.ph
            self.ph = ph
            yield
            self.S.barrier()
            self.S.flush()
            self.ph = old

    def ps(self):
        b = self.psf[self.psr % len(self.psf)]
        self.psr += 1
        return b

    def mm(self, out, lhsT, rhs, start, stop, reads, writes):
        self.S.op("pe", lambda e: e.matmul(out, lhsT=lhsT, rhs=rhs, start=start, stop=stop), reads, writes)

    def tr(self, out, in_, ident, reads, writes):
        self.S.op("pe", lambda e: e.transpose(out=out, in_=in_, identity=ident), reads, writes)

    def act(self, out, in_, func, reads, writes, scale=None, bias=None, accum=None):
        kw = {}
        if scale is not None:
            kw["scale"] = scale
        if bias is not None:
            kw["bias"] = bias
        if accum is not None:
            kw["accum_out"] = accum
        self.S.op("act", lambda e: e.activation(out=out, in_=in_, func=func, **kw), reads, writes)

    def ts(self, eng, out, in0, s1, s2, op0, op1, reads, writes):
        if op1 is None:
            self.S.op(eng, lambda e: e.tensor_scalar(out=out, in0=in0, scalar1=s1, scalar2=None, op0=op0), reads, writes)
        else:
            self.S.op(eng, lambda e: e.tensor_scalar(out=out, in0=in0, scalar1=s1, scalar2=s2, op0=op0, op1=op1),
                      reads, writes)

    def tt(self, eng, out, in0, in1, op, reads, writes):
        self.S.op(eng, lambda e: e.tensor_tensor(out=out, in0=in0, in1=in1, op=op), reads, writes)

    def stt(self, out, in0, scalar, in1, op0, op1, reads, writes):
        self.S.op("dve", lambda e: e.scalar_tensor_tensor(out=out, in0=in0, scalar=scalar, in1=in1, op0=op0, op1=op1),
                  reads, writes)

    def cp(self, eng, out, in_, reads, writes):
        if eng == "act":
            self.S.op("act", lambda e: e.copy(out=out, in_=in_), reads, writes)
        else:
            self.S.op(eng, lambda e: e.tensor_copy(out=out, in_=in_), reads, writes)

    def memset(self, eng, out, val, writes):
        self.S.op(eng, lambda e: e.memset(out, val), (), writes)

    def dma(self, out, in_, reads, writes):
        self.S.dma(out, in_, reads, writes)

    def load_bf16(self, dst, dst_ap, src, src_ap, shape):
        stg = self.stage[self.stage_i % len(self.stage)]
        self.stage_i += 1
        n = int(np.prod(shape[1:]))
        sv = stg.t[0:shape[0], 0:n]
        if len(shape) == 3:
            sv = sv.rearrange("p (a b) -> p a b", b=shape[2])
        self.dma(sv, src_ap, [src], [stg])
        eng = ["act", "dve", "pool"][self.stage_i % 3]
        self.cp(eng, dst_ap, sv, [stg], [dst])

    def rstd_from_ps(self, ps, ps_ap, n, nfeat, parts=128):
        r = self.rs_ring[self.rs_i % len(self.rs_ring)]
        self.rs_i += 1
        rv = r.t[0:parts, 0:n]
        self.act(rv, ps_ap, AF.Ln, [ps], [r], scale=1.0 / nfeat, bias=self.eps_t.t[0:parts, 0:1])
        self.act(rv, rv, AF.Exp, [r], [r], scale=-0.5)
        return r, rv

    def norm_mod(self, x, n, gmod, sh, h, inplace=False):
        sq = self.nm_sq
        self.act(sq.t[:, :, 0:n], x.t[:, :, 0:n], AF.Square, [x], [sq])
        ps = self.ps()
        for c in range(8):
            self.mm(ps.t[:, 0:n], self.ones_b.t[:, :], sq.t[:, c, 0:n], c == 0, c == 7, [self.ones_b, sq], [ps])
        r, rv = self.rstd_from_ps(ps, ps.t[:, 0:n], n, 1024.0)
        tmp = x if inplace else self.nm_tmp
        self.tt("dve", tmp.t[:, :, 0:n], x.t[:, :, 0:n], rv.unsqueeze(1).to_broadcast([128, 8, n]), ALU.mult,
                [x, r], [tmp])
        for c in range(8):
            if c % 2 == 0:
                self.act(h.t[:, c, 0:n], tmp.t[:, c, 0:n], AF.Identity, [tmp, self.modv], [h],
                         scale=gmod[:, c:c + 1], bias=sh[:, c:c + 1])
            else:
                self.ts("pool", h.t[:, c, 0:n], tmp.t[:, c, 0:n], gmod[:, c:c + 1], sh[:, c:c + 1], ALU.mult, ALU.add,
                        [tmp, self.modv], [h])


def seq_desc(kind):
    d = {}
    if kind == "p":
        S = S_P
        segs = [("ctx", 0, 4095), ("ctx", 4095, 8191), ("ctx", 8191, 12286), ("halo", 12286, 12287),
                ("own", 12287, 16383), ("halo", 16383, 16384)]
        links = {4095: "a", 8191: "b", 12287: "c", 16383: "d"}
        own0 = ROT0
    else:
        S = S_S
        segs = [("own", 0, 4096)]
        links = {0: "zero"}
        own0 = -1
    tiles = []
    for (role, a, b) in segs:
        s = a
        while s < b:
            n = min(TA, b - s)
            tiles.append(dict(s=s, n=n, role=role, own=(s - own0) if role != "ctx" else None))
            s += n
    d.update(S=S, tiles=tiles, links=links, own0=own0, kind=kind)
    return d


def crossed(links, S, c_from, c_to):
    keys = []
    for p in range(c_from + 1, c_to + 1):
        k = links.get(p % S)
        if k is not None:
            keys.append(k)
    return keys


def build(debug=None, limit=None):
    debug = debug or {}
    limit = limit or {}
    nc = bass.Bass("TRN2", target_bir_lowering=False)
    with contextlib.ExitStack() as st:
        K = KB(nc, st, debug)
        _program(K, limit)
        finals = [t.b.last_w for t in K.outputs.values() if t.b.last_w is not None]
        K.S.flush(finals)
        print("kernel instrs:", K.S.ninstr, "dma sems:", K.S.nsem_alloc, flush=True)
    return nc, K


def _program(K, limit):
    nc = K.nc
    xs = K.inp("xs", [1024, S_S])
    xp = K.inp("xp", [1024, S_P])
    cvec = K.inp("cvec", [128, 8, 2])
    links_d = K.inp("links", [128, 4])
    rope_s = K.inp("rope_s", [32, 2, S_S])
    rope_p = K.inp("rope_p", [32, 2, S_P])
    ident_d = K.inp("ident", [128, 128])
    p96_d = K.inp("p96", [96, 96])
    adaw = K.inp("adaw", [2, 128, 8, 6144])
    adab = K.inp("adab", [128, 2, 48])
    n1g = K.inp("n1g", [128, 2, 8])
    n2g = K.inp("n2g", [128, 2, 8])
    w_in = K.inp("w_in", [128, 8, 1440])
    wkr = K.inp("wkr", [128, 8, 96])
    rgdiag = K.inp("rgdiag", [128, 16, 128])
    rgcb = K.inp("rgcb", [128, 4])
    rgbd = K.inp("rgbd", [128, 16, 128])
    rgb = K.inp("rgb", [128, 16])
    rglam = K.inp("rglam", [128, 8])
    qnorm = K.inp("qnorm", [128, 2])
    w_uq = K.inp("w_uq", [128, 2, 768])
    kvnorm = K.inp("kvnorm", [128, 1])
    wk = K.inp("wk", [128, 512])
    wv = K.inp("wv", [128, 512])
    qng = K.inp("qng", [96, 2])
    wo_rg = K.inp("wo_rg", [128, 4, 1024])
    wo_at = K.inp("wo_at", [64, 8, 1024])
    c_w_in = K.inp("c_w_in", [128, 8, 3072])
    cdiag = K.inp("cdiag", [128, 24, 128])
    c_w_out = K.inp("c_w_out", [128, 8, 1024])
    wq_d = K.inp("wq", [2, 128, 8, 2048])
    k12T = K.inp("k12T", [128, 4, 128])
    ut_d = K.inp("ut", [2, 128, 128, 1024])
    v_d = K.inp("pv", [2, 128, 128, 1024])
    ys = K.outp("ys", [1024, S_S])
    yp = K.outp("yp", [1024, S_S])

    kt_s = K.scratch("kt_s", [8, 96, S_P], BF16)
    v_s = K.scratch("v_s", [S_P, 512], BF16)
    qt_s = K.scratch("qt_s", [8, 96, NOWN], BF16)
    x1_s = {k: K.scratch("x1_" + k, [1024, NOWN], F32) for k in "sp"}
    x2_s = {k: K.scratch("x2_" + k, [1024, NOWN], F32) for k in "sp"}
    x3_s = {k: K.scratch("x3_" + k, [1024, S_S], F32) for k in "sp"}
    wq_b = K.scratch("wq_b", [2, 16, 128, 8, 128], BF16)
    uv_b = K.scratch("uv_b", [2, 128, 128, 2048], BF16)

    K.psf = [T(st_enter(K, nc.psum_tensor("psf%d" % i, [128, 512], F32)), Buf("psf%d" % i)) for i in range(6)]
    K.psb = [T(st_enter(K, nc.psum_tensor("psb%d" % i, [128, 1024], BF16)), Buf("psb%d" % i)) for i in range(2)]

    ident_f = K.sb([128, 128], F32, "identf")
    ident_b = K.sb([128, 128], BF16, "identb")
    K.ones_b = K.sb([128, 128], BF16, "onesb")
    ones_f = K.sb([128, 128], F32, "onesf")
    K.eps_t = K.sb([128, 1], F32, "eps")
    K.modv = K.sb([128, 2, 2, 6, 8], F32, "modv")
    gm = K.sb([128, 2, 2, 2, 8], F32, "gm")
    links = K.sb([128, 4], F32, "links")
    K.rs_ring = K.ring(3, [128, 264], F32, "rstd")
    K.rs_i = 0
    K.stage_i = 0
    K.one_t = K.sb([128, 1], F32, "one")
    K.memset("pool", K.one_t.t[:], 1.0, [K.one_t])
    modv = K.modv
    K.memset("pool", K.ones_b.t[:], 1.0, [K.ones_b])
    K.memset("pool", ones_f.t[:], 1.0, [ones_f])
    K.memset("pool", K.eps_t.t[:], EPS, [K.eps_t])
    K.dma(ident_f.t[:], ident_d.t[:, :], [ident_d], [ident_f])
    K.cp("dve", ident_b.t[:], ident_f.t[:], [ident_f], [ident_b])
    K.dma(links.t[:], links_d.t[:, :], [links_d], [links])

    LK = {"a": links.t[:, 0:1], "b": links.t[:, 1:2], "c": links.t[:, 2:3], "d": links.t[:, 3:4]}

    def apply_links(eng, ap, keys, t, parts=128):
        for k in keys:
            if k == "zero":
                K.ts(eng, ap, ap, 0.0, None, ALU.mult, None, [t], [t])
            else:
                K.ts(eng, ap, ap, LK[k][0:parts], None, ALU.mult, None, [t, links], [t])

    with K.phase():
        K.stage = K.ring(2, [128, 4096], F32, "stage")
        K.castb = K.ring(2, [128, 4096], BF16, "castb")
        zt = K.sb([128, 8, 1], F32, "zt")
        K.memset("pool", zt.t[:], 0.0, [zt])
        for cc in (0, NOWN - 1):
            K.dma(x2_s["s"].t[:, cc:cc + 1].rearrange("(c p) t -> p c t", p=128), zt.t[:], [zt], [x2_s["s"]])
        cv = K.sb([128, 8, 2], F32, "cv")
        K.dma(cv.t[:], cvec.t[:, :, :], [cvec], [cv])
        scv = K.sb([128, 8, 2], F32, "scv")
        K.act(scv.t[:], cv.t[:], AF.Silu, [cv], [scv])
        adb = K.sb([128, 2, 48], F32, "adb")
        K.dma(adb.t[:], adab.t[:, :, :], [adab], [adb])
        g12 = K.sb([128, 2, 2, 8], F32, "g12")
        K.dma(g12.t[:, 0], n1g.t[:, :, :], [n1g], [g12])
        K.dma(g12.t[:, 1], n2g.t[:, :, :], [n2g], [g12])
        awr = K.ring(2, [128, 8, 512], F32, "adaw")
        for l in range(2):
            psm = K.ps()
            for g in range(12):
                aw = awr[g % 2]
                K.dma(aw.t[:], adaw.t[l, :, :, g * 512:(g + 1) * 512], [adaw], [aw])
                for jj in range(4):
                    j = g * 4 + jj
                    for kc in range(8):
                        K.mm(psm.t[:, 2 * j:2 * j + 2], aw.t[:, kc, jj * 128:(jj + 1) * 128], scv.t[:, kc, :],
                             kc == 0, kc == 7, [aw, scv], [psm])
            for s in range(2):
                K.tt("dve", modv.t[:, l, s].rearrange("p w c -> p (w c)"),
                     psm.t[:, 0:96].rearrange("p (j s) -> p j s", s=2)[:, :, s], adb.t[:, l, :], ALU.add,
                     [psm, adb], [modv])
        for l in range(2):
            for s in range(2):
                for k in range(2):
                    K.stt(gm.t[:, l, s, k, :], modv.t[:, l, s, 1 + 3 * k, :], 1.0, g12.t[:, k, l, :], ALU.add, ALU.mult,
                          [modv, g12], [gm])
        if not limit.get("skip_prep"):
            cast_i = 0
            for l in range(2):
                jobs = [(wq_d.t[l], None, wq_d, wq_b, 4)]
                for i0 in range(0, 128, 4):
                    jobs.append((ut_d.t[l, i0:i0 + 4].rearrange("i p f -> p i f"),
                                 uv_b.t[l, i0:i0 + 4, :, 0:1024].rearrange("i p f -> p i f"), ut_d, uv_b, 1))
                    jobs.append((v_d.t[l, i0:i0 + 4].rearrange("i p f -> p i f"),
                                 uv_b.t[l, i0:i0 + 4, :, 1024:2048].rearrange("i p f -> p i f"), v_d, uv_b, 1))
                for (src, dst, srcT, dstT, nsplit) in jobs:
                    for sp_ in range(nsplit):
                        if nsplit == 1:
                            s_ap, d_ap = src, dst
                            shp = [128, 4, 1024]
                        else:
                            s_ap = src[:, 2 * sp_:2 * sp_ + 2, :]
                            d_ap = wq_b.t[l, :, :, 2 * sp_:2 * sp_ + 2, :].rearrange("c p k j -> p k c j")
                            shp = [128, 2, 2048]
                        stg = K.stage[cast_i % 2]
                        sv = stg.t[:, :].rearrange("p (a b) -> p a b", b=shp[2])
                        K.dma(sv, s_ap, [srcT], [stg])
                        cb = K.castb[cast_i % 2]
                        cbv = cb.t[:, :].rearrange("p (a b) -> p a b", b=shp[2])
                        K.cp(["act", "dve"][cast_i % 2], cbv, sv, [stg], [cb])
                        if nsplit == 1:
                            K.dma(d_ap, cbv, [cb], [dstT])
                        else:
                            for kk in range(2):
                                K.dma(wq_b.t[l, :, :, 2 * sp_ + kk, :].rearrange("c p j -> p c j"),
                                      cbv[:, kk, :].rearrange("p (c j) -> p c j", j=128), [cb], [dstT])
                        cast_i += 1

    def MV(l, s, which):
        return modv.t[:, l, s, which, :]

    def GM(l, s, k):
        return gm.t[:, l, s, k, :]

    seqs = [("s", 0, xs, rope_s, ys), ("p", 1, xp, rope_p, yp)]
    if limit.get("seqs"):
        seqs = [q for q in seqs if q[0] in limit["seqs"]]

    for (kind, si, xT, rope_d, yout) in seqs:
        sd = seq_desc(kind)
        S = sd["S"]
        with K.phase():
            _layer0_mixer(K, sd, si, xT, rope_d, dict(
                w_in=w_in, wkr=wkr, rgdiag=rgdiag, rgcb=rgcb, rgbd=rgbd, rgb=rgb, rglam=rglam, qnorm=qnorm, w_uq=w_uq,
                kvnorm=kvnorm, wk=wk, wv=wv, qng=qng, wo_rg=wo_rg, wo_at=wo_at, p96=p96_d, ident_b=ident_b,
                ones_f=ones_f, kt_s=kt_s, v_s=v_s, qt_s=qt_s, x1=x1_s[kind], MV=MV, GM=GM, apply_links=apply_links,
                links=links, LK=LK), limit)
        if limit.get("stop") in ("stageA", "mixer0"):
            continue
        with K.phase():
            tl = [[(1 + NT * k, NT)] for k in range(S_S // NT)]
            if kind == "p":
                tl.append([(0, 1), (NOWN - 1, 1)])
            if limit.get("ptiles"):
                tl = tl[:limit["ptiles"]]
            _peer(K, 0, si, x1_s[kind], x2_s[kind], tl, dict(wq_b=wq_b, uv_b=uv_b, k12T=k12T, ident_b=ident_b,
                                                             ident_f=ident_f, MV=MV, GM=GM), limit)
        if limit.get("stop") == "peer0":
            continue
        with K.phase():
            _mixer_c(K, si, kind, x2_s[kind], x3_s[kind], dict(c_w_in=c_w_in, cdiag=cdiag, c_w_out=c_w_out, MV=MV, GM=GM,
                                                                LK=LK, links=links), limit)
        if limit.get("stop") == "mixer1":
            continue
        with K.phase():
            tl = [[(NT * k, NT)] for k in range(S_S // NT)]
            if limit.get("ptiles"):
                tl = tl[:limit["ptiles"]]
            _peer(K, 1, si, x3_s[kind], yout, tl, dict(wq_b=wq_b, uv_b=uv_b, k12T=k12T, ident_b=ident_b,
                                                       ident_f=ident_f, MV=MV, GM=GM), limit)


def st_enter(K, cm):
    return K.st.enter_context(cm)


def _layer0_mixer(K, sd, si, xT, rope_d, W, limit):
    S = sd["S"]
    kind = sd["kind"]
    tiles = sd["tiles"]
    MV, GM = W["MV"], W["GM"]
    apply_links = W["apply_links"]
    LK = W["LK"]
    links = W["links"]
    NE = TA + 3
    GY = K.sb([128, 4, NOWN], BF16, "GY")
    _stageA(K, sd, si, xT, rope_d, W, limit, GY)
    if limit.get("stop") == "stageA":
        dbg = K.outp("dbg_rg_" + kind, [128, 4, NOWN], BF16)
        K.dma(dbg.t[:, :, :], GY.t[:], [GY], [dbg])
        K.S.barrier()
        return
    with K.phase():
        _stageB(K, sd, si, xT, W, limit, GY)


def _stageA(K, sd, si, xT, rope_d, W, limit, GY):
  with K.phase():
    S = sd["S"]
    kind = sd["kind"]
    tiles = sd["tiles"]
    MV, GM = W["MV"], W["GM"]
    apply_links = W["apply_links"]
    LK = W["LK"]
    links = W["links"]
    NE = TA + 3
    K.stage = K.ring(1, [128, 2048], F32, "stage")
    w_in_b = K.sb([128, 8, 1440], BF16, "w_in_b")
    for kc in range(8):
        K.load_bf16(w_in_b, w_in_b.t[:, kc, :], W["w_in"], W["w_in"].t[:, kc, :], [128, 1440])
    wkr_b = K.sb([128, 8, 96], BF16, "wkr_b")
    K.load_bf16(wkr_b, wkr_b.t[:], W["wkr"], W["wkr"].t[:, :, :], [128, 8, 96])
    dg_b = K.sb([128, 16, 128], BF16, "dg_b")
    K.load_bf16(dg_b, dg_b.t[:], W["rgdiag"], W["rgdiag"].t[:, :, :], [128, 16, 128])
    bd_b = K.sb([128, 16, 128], BF16, "bd_b")
    K.load_bf16(bd_b, bd_b.t[:], W["rgbd"], W["rgbd"].t[:, :, :], [128, 16, 128])
    wuq_b = K.sb([128, 2, 768], BF16, "wuq_b")
    K.load_bf16(wuq_b, wuq_b.t[:], W["w_uq"], W["w_uq"].t[:, :, :], [128, 2, 768])
    wk_b = K.sb([128, 512], BF16, "wk_b")
    K.load_bf16(wk_b, wk_b.t[:], W["wk"], W["wk"].t[:, :], [128, 512])
    wv_b = K.sb([128, 512], BF16, "wv_b")
    K.load_bf16(wv_b, wv_b.t[:], W["wv"], W["wv"].t[:, :], [128, 512])
    p96_b = K.sb([96, 96], BF16, "p96_b")
    K.load_bf16(p96_b, p96_b.t[:], W["p96"], W["p96"].t[:, :], [96, 96])
    small = K.sb([128, 64], F32, "small")
    K.dma(small.t[:, 0:4], W["rgcb"].t[:, :], [W["rgcb"]], [small])
    K.dma(small.t[:, 4:20], W["rgb"].t[:, :], [W["rgb"]], [small])
    K.dma(small.t[:, 20:28], W["rglam"].t[:, :], [W["rglam"]], [small])
    K.dma(small.t[:, 28:30], W["qnorm"].t[:, :], [W["qnorm"]], [small])
    K.dma(small.t[:, 30:31], W["kvnorm"].t[:, :], [W["kvnorm"]], [small])
    K.dma(small.t[0:96, 32:34], W["qng"].t[:, :], [W["qng"]], [small])
    K.act(small.t[:, 52:60], small.t[:, 20:28], AF.Exp, [small], [small], scale=-1.0)
    K.act(small.t[:, 52:60], small.t[:, 52:60], AF.Ln, [small], [small], bias=K.one_t.t[:, 0:1])
    K.ts("dve", small.t[:, 36:44], small.t[:, 52:60], -8.0, None, ALU.mult, None, [small], [small])
    K.ts("dve", small.t[:, 44:52], small.t[:, 52:60], -16.0, None, ALU.mult, None, [small], [small])

    def SA(d, ch):
        return small.t[:, 36 + d * 4 + ch:37 + d * 4 + ch]

    def SA2(d, ch):
        return small.t[:, 44 + d * 4 + ch:45 + d * 4 + ch]

    def GB(d, which, ch):
        j = 4 + (d * 2 + which) * 4 + ch
        return small.t[:, j:j + 1]

    gmod1, sh1, g1 = GM(0, si, 0), MV(0, si, 0), MV(0, si, 2)

    XC = K.sb([128, 4, NOWN], BF16, "XC")
    HF = K.sb([128, 4, NOWN], BF16, "HF")
    nctx = sum(1 for t in tiles if t["role"] != "own")
    AB = K.sb([128, 2, max(nctx, 1), 2, 4], F32, "AB")
    carry = K.sb([128, 2, 4], F32, "carry")
    K.memset("pool", carry.t[:], 0.0, [carry])

    xe_r = K.ring(1, [128, 8, NE], F32, "xe")
    he_r = K.ring(2, [128, 8, NE], BF16, "he")
    K.nm_sq = K.sb([128, 8, NE], BF16, "nmsq")
    xr_b = K.sb([128, 4, NE], BF16, "xr_b")
    xc_t = K.ring(2, [128, 4, TA], BF16, "xc_t")
    g_r = K.ring(2, [128, TA], F32, "g_r")
    g_i = K.ring(2, [128, TA], F32, "g_i")
    g_a = K.ring(2, [128, TA], F32, "g_a")
    g_q = K.ring(2, [128, TA], F32, "g_q")
    g_b = K.ring(2, [128, TA], F32, "g_b")
    g_h = K.ring(2, [128, TA], F32, "g_h")
    sumr = K.sb([128, 64], F32, "sumr")
    sqb = K.ring(2, [128, TA], BF16, "sqb")
    ckv = K.ring(2, [128, TA], BF16, "ckv")
    qn = K.ring(2, [128, 2, TA], BF16, "qn")
    vt = K.ring(2, [128, 512], BF16, "vt")
    krope = K.sb([96, TA], F32, "krope")
    kh = K.ring(2, [96, TA], F32, "kh")
    khn = K.ring(2, [96, TA], F32, "khn")
    khb = K.ring(3, [96, TA], BF16, "khb")
    rt1 = K.ring(2, [96, TA], F32, "rt1")
    rope_t = K.ring(2, [96, 2, TA], F32, "rope")
    gi = [0]

    def rnext(r):
        gi[0] += 1
        return r[gi[0] % len(r)]

    W_p96 = p96_b
    ctx_i = [0]

    def load_ext(t):
        s, n = t["s"], t["n"]
        xe = rnext(xe_r)
        lo, hi = s - 2, s + n + 1
        pieces = []
        c = lo
        while c < hi:
            cm = c % S
            ln = min(hi - c, S - cm)
            pieces.append((c - lo, cm, ln))
            c += ln
        for (o, cm, ln) in pieces:
            K.dma(xe.t[:, :, o:o + ln], xT.t[:, cm:cm + ln].rearrange("(c p) t -> p c t", p=128), [xT], [xe])
        return xe

    def load_rope(t):
        s, n = t["s"], t["n"]
        rp = rnext(rope_t)
        K.dma(rp.t[64:96, :, 0:n], rope_d.t[:, :, s:s + n], [rope_d], [rp])
        return rp

    def tileA(t, mode):
        s, n = t["s"], t["n"]
        ne = n + 3
        if mode in ("ctx", "ownF"):
            xe = load_ext(t)
            he = rnext(he_r)
            K.norm_mod(xe, ne, gmod1, sh1, he, inplace=True)
            for ch in range(4):
                ps = K.ps()
                for kc in range(8):
                    K.mm(ps.t[:, 0:ne], w_in_b.t[:, kc, ch * 128:(ch + 1) * 128], he.t[:, kc, 0:ne], kc == 0, kc == 7,
                         [w_in_b, he], [ps])
                K.cp("act", xr_b.t[:, ch, 0:ne], ps.t[:, 0:ne], [ps], [xr_b])
            for (col, cf, ct) in ((0, s - 2, s), (1, s - 1, s), (ne - 1, s + n - 1, s + n)):
                ks = crossed(sd["links"], S, cf, ct)
                if ks:
                    apply_links("pool", xr_b.t[:, :, col:col + 1], ks, xr_b)
            if mode == "ctx":
                xc = rnext(xc_t)
                xcv = lambda ch: xc.t[:, ch, 0:n]
            else:
                xc = XC
                xcv = lambda ch: XC.t[:, ch, t["own"]:t["own"] + n]
            for ch in range(4):
                ps = K.ps()
                for k in range(4):
                    K.mm(ps.t[:, 0:n], dg_b.t[:, ch * 4 + k, :], xr_b.t[:, ch, k:k + n], k == 0, k == 3, [dg_b, xr_b],
                         [ps])
                K.act(xcv(ch), ps.t[:, 0:n], AF.Identity, [ps, small], [xc], bias=small.t[:, ch:ch + 1])
            rp = load_rope(t)
            ps_kv = K.ps()
            for kc in range(8):
                K.mm(ps_kv.t[:, 0:n], w_in_b.t[:, kc, 1280:1408], he.t[:, kc, 2:2 + n], kc == 0, kc == 7, [w_in_b, he],
                     [ps_kv])
            sq = rnext(sqb)
            K.act(sq.t[:, 0:n], ps_kv.t[:, 0:n], AF.Square, [ps_kv], [sq])
            ps2 = K.ps()
            K.mm(ps2.t[:, 0:n], K.ones_b.t[:, :], sq.t[:, 0:n], True, True, [K.ones_b, sq], [ps2])
            r, rv = K.rstd_from_ps(ps2, ps2.t[:, 0:n], n, 128.0)
            ck = rnext(ckv)
            K.stt(ck.t[:, 0:n], ps_kv.t[:, 0:n], small.t[:, 30:31], rv, ALU.mult, ALU.mult, [ps_kv, small, r], [ck])
            for sub in range(0, n, 128):
                ns = min(128, n - sub)
                psv = K.ps()
                K.mm(psv.t[0:ns, :], ck.t[:, sub:sub + ns], wv_b.t[:, :], True, True, [ck, wv_b], [psv])
                v1 = rnext(vt)
                K.cp("act", v1.t[0:ns, :], psv.t[0:ns, :], [psv], [v1])
                K.dma(W["v_s"].t[s + sub:s + sub + ns, :], v1.t[0:ns, :], [v1], [W["v_s"]])
            pskr = K.ps()
            for kc in range(8):
                K.mm(pskr.t[0:96, 0:n], wkr_b.t[:, kc, :], he.t[:, kc, 2:2 + n], kc == 0, kc == 7, [wkr_b, he], [pskr])
            K.cp("act", krope.t[64:96, 0:n], pskr.t[64:96, 0:n], [pskr], [krope])
            for h in range(8):
                psk = K.ps()
                K.mm(psk.t[0:64, 0:n], wk_b.t[:, h * 64:(h + 1) * 64], ck.t[:, 0:n], True, True, [wk_b, ck], [psk])
                khf = rnext(kh)
                K.cp("act", khf.t[0:64, 0:n], psk.t[0:64, 0:n], [psk], [khf])
                K.cp("pool", khf.t[64:96, 0:n], krope.t[64:96, 0:n], [krope], [khf])
                _norm_rope_from_sb(K, khf, n, small, 33, rp, W_p96, sqb, khn, khb, rt1, rnext,
                                   W["kt_s"], W["kt_s"].t[h, :, s:s + n])
        if mode == "ownF":
            o = t["own"]
            for ch in range(4):
                ps = K.ps()
                for kc in range(8):
                    K.mm(ps.t[:, 0:n], w_in_b.t[:, kc, 512 + ch * 128:512 + (ch + 1) * 128], he.t[:, kc, 2:2 + n],
                         kc == 0, kc == 7, [w_in_b, he], [ps])
                K.act(GY.t[:, ch, o:o + n], ps.t[:, 0:n], AF.Gelu_apprx_tanh, [ps], [GY])
            psq = [K.ps(), K.ps()]
            sq2 = [rnext(sqb), rnext(sqb)]
            for c in range(2):
                for kc in range(8):
                    K.mm(psq[c].t[:, 0:n], w_in_b.t[:, kc, 1024 + c * 128:1024 + (c + 1) * 128], he.t[:, kc, 2:2 + n],
                         kc == 0, kc == 7, [w_in_b, he], [psq[c]])
                K.act(sq2[c].t[:, 0:n], psq[c].t[:, 0:n], AF.Square, [psq[c]], [sq2[c]])
            ps2 = K.ps()
            for c in range(2):
                K.mm(ps2.t[:, 0:n], K.ones_b.t[:, :], sq2[c].t[:, 0:n], c == 0, c == 1, [K.ones_b, sq2[c]], [ps2])
            r, rv = K.rstd_from_ps(ps2, ps2.t[:, 0:n], n, 256.0)
            qq = rnext(qn)
            for c in range(2):
                K.stt(qq.t[:, c, 0:n], psq[c].t[:, 0:n], small.t[:, 28 + c:29 + c], rv, ALU.mult, ALU.mult,
                      [psq[c], small, r], [qq])
            for h in range(8):
                psh = K.ps()
                for c in range(2):
                    K.mm(psh.t[0:96, 0:n], wuq_b.t[:, c, h * 96:(h + 1) * 96], qq.t[:, c, 0:n], c == 0, c == 1,
                         [wuq_b, qq], [psh])
                khf = rnext(kh)
                K.cp("act", khf.t[:, 0:n], psh.t[0:96, 0:n], [psh], [khf])
                _norm_rope_from_sb(K, khf, n, small, 32, rp, W_p96, sqb, khn, khb, rt1, rnext,
                                   W["qt_s"], W["qt_s"].t[h, :, o:o + n])
        dirs = (0, 1) if mode == "ctx" else ((0,) if mode == "ownF" else (1,))
        if mode == "ctx":
            ti = ctx_i[0]
            ctx_i[0] += 1
            t["ctx_idx"] = ti
        for d in dirs:
            for ch in range(4):
                if mode == "ctx":
                    xcs, xca = xc, xc.t[:, ch, 0:n]
                else:
                    xcs, xca = XC, XC.t[:, ch, t["own"]:t["own"] + n]
                psr_ = K.ps()
                K.mm(psr_.t[:, 0:n], bd_b.t[:, (d * 2 + 0) * 4 + ch, :], xca, True, True, [bd_b, xcs], [psr_])
                psi_ = K.ps()
                K.mm(psi_.t[:, 0:n], bd_b.t[:, (d * 2 + 1) * 4 + ch, :], xca, True, True, [bd_b, xcs], [psi_])
                rr = rnext(g_r)
                sc = sumr.t[:, (d * 4 + ch):(d * 4 + ch) + 1]
                K.act(rr.t[:, 0:n], psr_.t[:, 0:n], AF.Sigmoid, [psr_, small], [rr, sumr], bias=GB(d, 0, ch),
                      accum=sc if mode == "ctx" else None)
                ii = rnext(g_i)
                K.act(ii.t[:, 0:n], psi_.t[:, 0:n], AF.Sigmoid, [psi_, small], [ii], bias=GB(d, 1, ch))
                aa = rnext(g_a)
                K.act(aa.t[:, 0:n], rr.t[:, 0:n], AF.Exp, [rr, small], [aa], scale=SA(d, ch))
                qq_ = rnext(g_q)
                K.act(qq_.t[:, 0:n], rr.t[:, 0:n], AF.Exp, [rr, small], [qq_], scale=SA2(d, ch))
                K.act(qq_.t[:, 0:n], qq_.t[:, 0:n], AF.Ln, [qq_], [qq_], scale=-1.0, bias=K.one_t.t[:, 0:1])
                K.act(qq_.t[:, 0:n], qq_.t[:, 0:n], AF.Exp, [qq_], [qq_], scale=0.5)
                K.tt("dve", ii.t[:, 0:n], ii.t[:, 0:n], xca, ALU.mult, [ii, xcs], [ii])
                bb = rnext(g_b)
                K.tt("pool", bb.t[:, 0:n], qq_.t[:, 0:n], ii.t[:, 0:n], ALU.mult, [qq_, ii], [bb])
                hh = rnext(g_h)
                if mode == "ctx":
                    init = 0.0
                    rd = [aa, bb]
                else:
                    init = carry.t[:, d, ch:ch + 1]
                    rd = [aa, bb, carry]
                if d == 0:
                    K.S.op("dve", lambda e, hh=hh, aa=aa, bb=bb, init=init, n=n: e.tensor_tensor_scan(
                        out=hh.t[:, 0:n], data0=aa.t[:, 0:n], data1=bb.t[:, 0:n], initial=init, op0=ALU.mult,
                        op1=ALU.add), rd, [hh])
                    last = hh.t[:, n - 1:n]
                else:
                    K.S.op("dve", lambda e, hh=hh, aa=aa, bb=bb, init=init, n=n: e.tensor_tensor_scan(
                        out=hh.t[:, 0:n][:, ::-1], data0=aa.t[:, 0:n][:, ::-1], data1=bb.t[:, 0:n][:, ::-1],
                        initial=init, op0=ALU.mult, op1=ALU.add), rd, [hh])
                    last = hh.t[:, 0:1]
                if mode == "ctx":
                    K.cp("pool", AB.t[:, d, ti, 1, ch:ch + 1], last, [hh], [AB])
                    K.act(AB.t[:, d, ti, 0, ch:ch + 1], sc, AF.Exp, [sumr, small], [AB], scale=SA(d, ch))
                else:
                    o = t["own"]
                    K.cp("pool", carry.t[:, d, ch:ch + 1], last, [hh], [carry])
                    if d == 0:
                        K.cp("act", HF.t[:, ch, o:o + n], hh.t[:, 0:n], [hh], [HF])
                    else:
                        K.tt("dve", hh.t[:, 0:n], hh.t[:, 0:n], HF.t[:, ch, o:o + n], ALU.add, [hh, HF], [hh])
                        K.tt("pool", GY.t[:, ch, o:o + n], hh.t[:, 0:n], GY.t[:, ch, o:o + n], ALU.mult, [hh, GY], [GY])

    ctx_tiles = [t for t in tiles if t["role"] != "own"]
    own_tiles = [t for t in tiles if t["role"] != "ctx"]
    if limit.get("atiles"):
        own_tiles = own_tiles[:limit["atiles"]]
        ctx_tiles = ctx_tiles[:limit["atiles"]] if ctx_tiles else ctx_tiles
    for t in ctx_tiles:
        tileA(t, "ctx")

    def link_between(prev, t, d):
        if prev is None:
            return
        if d == 0:
            a = prev["s"] + prev["n"] - 1
            b_ = t["s"]
            if b_ <= a:
                b_ += S
            ks = crossed(sd["links"], S, a, b_)
        else:
            a = t["s"] + t["n"] - 1
            b_ = prev["s"]
            if b_ <= a:
                b_ += S
            ks = crossed(sd["links"], S, a, b_)
        apply_links("dve", carry.t[:, d, :], ks, carry)

    def chain(order, d):
        K.memset("pool", carry.t[:, d, :], 0.0, [carry])
        prev = None
        for t in order:
            link_between(prev, t, d)
            ti = t["ctx_idx"]
            K.tt("dve", carry.t[:, d, :], carry.t[:, d, :], AB.t[:, d, ti, 0, :], ALU.mult, [carry, AB], [carry])
            K.tt("dve", carry.t[:, d, :], carry.t[:, d, :], AB.t[:, d, ti, 1, :], ALU.add, [carry, AB], [carry])
            prev = t
        return prev

    halo = [t for t in tiles if t["role"] == "halo"]
    ctxo = [t for t in tiles if t["role"] == "ctx"]
    if kind == "p" and not limit.get("atiles"):
        fwd_order = [halo[1]] + ctxo
        prev = chain(fwd_order, 0)
    else:
        prev = None
    first = True
    for t in own_tiles:
        if kind == "p" or not first:
            link_between(prev, t, 0)
        first = False
        tileA(t, "ownF")
        prev = t
    if kind == "p" and not limit.get("atiles"):
        bwd_order = [halo[0]] + ctxo[::-1]
        prev = chain(bwd_order, 1)
    else:
        prev = None
        K.memset("pool", carry.t[:, 1, :], 0.0, [carry])
    first = True
    for t in own_tiles[::-1]:
        if kind == "p" or not first:
            link_between(prev, t, 1)
        first = False
        tileA(t, "ownB")
        prev = t


def _stageB(K, sd, si, xT, W, limit, GY):
    S = sd["S"]
    kind = sd["kind"]
    MV, GM = W["MV"], W["GM"]
    g1 = MV(0, si, 2)
    K.stage = K.ring(1, [128, 2048], F32, "stage")
    wo_rg_b = K.sb([128, 4, 1024], BF16, "wo_rg_b")
    for c in range(4):
        K.load_bf16(wo_rg_b, wo_rg_b.t[:, c, :], W["wo_rg"], W["wo_rg"].t[:, c, :], [128, 1024])
    wo_at_b = K.sb([64, 8, 1024], BF16, "wo_at_b")
    for c in range(0, 8, 2):
        K.load_bf16(wo_at_b, wo_at_b.t[:, c:c + 2, :], W["wo_at"], W["wo_at"].t[:, c:c + 2, :], [64, 2, 1024])
    QB = 512
    KBLK = 2048
    nkb = S // KBLK
    qtile = K.ring(2, [96, QB], BF16, "qtile")
    kblk = K.ring(2, [96, KBLK], BF16, "kblk")
    vblk = K.ring(2, [128, KBLK // 128, 65], BF16, "vblk")
    for vb in vblk:
        K.memset("pool", vb.t[:, :, 64:65], 1.0, [vb])
    pbuf = K.ring(3, [128, QB], BF16, "pbuf")
    at = [K.sb([64, QB], BF16, "at%d" % h) for h in range(8)]
    rd_t = K.sb([128, QB], F32, "rd")
    bcs = K.sb([64, QB], F32, "bcs")
    xq = K.ring(2, [128, 8, QB], F32, "xq")
    x1t = K.ring(2, [128, 8, QB], F32, "x1t")
    ps_s = [T(K.psf[i].t, K.psf[i].b) for i in range(3)]
    ps_o = [K.psf[3], K.psf[4]]
    ps_m = K.psf[5]
    scale = 96.0 ** -0.5
    own0 = sd["own0"]
    qtl = [[(1 + QB * k, QB)] for k in range(S_S // QB)]
    if kind == "p":
        qtl.append([(0, 1), (NOWN - 1, 1)])
    if limit.get("qtiles"):
        qtl = qtl[:limit["qtiles"]]
    it = 0
    for pieces in qtl:
        nq = sum(p[1] for p in pieces)
        xt_ = xq[it % 2]
        o = 0
        for (c0, ln) in pieces:
            cm = (own0 + c0) % S
            K.dma(xt_.t[:, :, o:o + ln], xT.t[:, cm:cm + ln].rearrange("(c p) t -> p c t", p=128), [xT], [xt_])
            o += ln
        for h in range(8):
            qt_ = qtile[(it * 8 + h) % 2]
            o = 0
            for (c0, ln) in pieces:
                K.dma(qt_.t[:, o:o + ln], W["qt_s"].t[h, :, c0:c0 + ln], [W["qt_s"]], [qt_])
                o += ln
            pso = ps_o[h % 2]
            nkt = S // 128
            jobs = []
            for kb_ in range(nkb):
                jobs.append(kb_)
            kcur = None
            vcur = None
            pend = None
            bi = 0
            for kt in range(nkt):
                if kt % (KBLK // 128) == 0:
                    kb_ = kt // (KBLK // 128)
                    kcur = kblk[(it * 8 * nkb + h * nkb + kb_) % 2]
                    vcur = vblk[(it * 8 * nkb + h * nkb + kb_) % 2]
                    K.dma(kcur.t[:, :], W["kt_s"].t[h, :, kb_ * KBLK:(kb_ + 1) * KBLK], [W["kt_s"]], [kcur])
                    K.dma(vcur.t[:, :, 0:64],
                          W["v_s"].t[kb_ * KBLK:(kb_ + 1) * KBLK, h * 64:(h + 1) * 64].rearrange("(k p) d -> p k d", p=128),
                          [W["v_s"]], [vcur])
                kk = kt % (KBLK // 128)
                pss = ps_s[kt % 3]
                K.mm(pss.t[:, 0:nq], kcur.t[:, kk * 128:(kk + 1) * 128], qt_.t[:, 0:nq], True, True, [kcur, qt_], [pss])
                if pend is not None:
                    (pb, pkt, pv, pkk) = pend
                    K.mm(pso.t[0:65, 0:nq], pv.t[:, pkk, :], pb.t[:, 0:nq], pkt == 0, False, [pv, pb], [pso])
                pb = pbuf[kt % 3]
                K.act(pb.t[:, 0:nq], pss.t[:, 0:nq], AF.Exp, [pss], [pb], scale=scale)
                pend = (pb, kt, vcur, kk)
            (pb, pkt, pv, pkk) = pend
            K.mm(pso.t[0:65, 0:nq], pv.t[:, pkk, :], pb.t[:, 0:nq], pkt == 0, True, [pv, pb], [pso])
            K.S.op("dve", lambda e, pso=pso, nq=nq: e.reciprocal(out=rd_t.t[64:65, 0:nq], in_=pso.t[64:65, 0:nq]),
                   [pso], [rd_t])
            psb_ = ps_m
            K.mm(psb_.t[0:64, 0:nq], W["ones_f"].t[64:65, 0:64], rd_t.t[64:65, 0:nq], True, True, [W["ones_f"], rd_t],
                 [psb_])
            K.cp("act", bcs.t[:, 0:nq], psb_.t[0:64, 0:nq], [psb_], [bcs])
            K.tt("dve", at[h].t[:, 0:nq], pso.t[0:64, 0:nq], bcs.t[:, 0:nq], ALU.mult, [pso, bcs], [at[h]])
        x1 = x1t[it % 2]
        for dc in range(8):
            psm = K.ps() if False else ps_m
            o = 0
            for (c0, ln) in pieces:
                for c in range(4):
                    K.mm(psm.t[:, o:o + ln], wo_rg_b.t[:, c, dc * 128:(dc + 1) * 128], GY.t[:, c, c0:c0 + ln],
                         c == 0, False, [wo_rg_b, GY], [psm])
                for h in range(8):
                    K.mm(psm.t[:, o:o + ln], wo_at_b.t[:, h, dc * 128:(dc + 1) * 128], at[h].t[:, o:o + ln],
                         False, h == 7, [wo_at_b, at[h]], [psm])
                o += ln
            K.stt(x1.t[:, dc, 0:nq], psm.t[:, 0:nq], g1[:, dc:dc + 1], xt_.t[:, dc, 0:nq], ALU.mult, ALU.add,
                  [psm, K.modv, xt_], [x1])
        o = 0
        for (c0, ln) in pieces:
            K.dma(W["x1"].t[:, c0:c0 + ln].rearrange("(c p) t -> p c t", p=128), x1.t[:, :, o:o + ln], [x1], [W["x1"]])
            o += ln
        it += 1


def _norm_rope_from_sb(K, khf, n, small, gcol, rp, p96_b, sqb, khn, khb, rt1, rnext, dst, dst_ap):
    sq = rnext(sqb)
    K.act(sq.t[0:96, 0:n], khf.t[:, 0:n], AF.Square, [khf], [sq])
    ps2 = K.ps()
    K.mm(ps2.t[0:96, 0:n], K.ones_b.t[0:96, 0:96], sq.t[0:96, 0:n], True, True, [K.ones_b, sq], [ps2])
    r, rv = K.rstd_from_ps(ps2, ps2.t[0:96, 0:n], n, 96.0, parts=96)
    kb = rnext(khb)
    K.stt(kb.t[:, 0:n], khf.t[:, 0:n], small.t[0:96, gcol:gcol + 1], rv, ALU.mult, ALU.mult, [khf, small, r], [kb])
    ps3 = K.ps()
    K.mm(ps3.t[0:96, 0:n], p96_b.t[:, :], kb.t[:, 0:n], True, True, [p96_b, kb], [ps3])
    t1 = rnext(rt1)
    K.tt("pool", t1.t[64:96, 0:n], kb.t[64:96, 0:n], rp.t[64:96, 0, 0:n], ALU.mult, [kb, rp], [t1])
    t2 = rnext(rt1)
    K.tt("dve", t2.t[64:96, 0:n], ps3.t[64:96, 0:n], rp.t[64:96, 1, 0:n], ALU.mult, [ps3, rp], [t2])
    K.tt("dve", kb.t[64:96, 0:n], t1.t[64:96, 0:n], t2.t[64:96, 0:n], ALU.add, [t1, t2, kb], [kb])
    K.dma(dst_ap, kb.t[:, 0:n], [kb], [dst])


def _mixer_c(K, si, kind, x2, x3, W, limit):
    MV, GM = W["MV"], W["GM"]
    LK, links = W["LK"], W["links"]
    gmod, sh, g1 = GM(1, si, 0), MV(1, si, 0), MV(1, si, 2)
    K.stage = K.ring(2, [128, 2048], F32, "stage")
    cw_b = K.sb([128, 8, 3072], BF16, "cw_b")
    for kc in range(8):
        for hf in range(2):
            K.load_bf16(cw_b, cw_b.t[:, kc, hf * 1536:(hf + 1) * 1536], W["c_w_in"],
                        W["c_w_in"].t[:, kc, hf * 1536:(hf + 1) * 1536], [128, 1536])
    cd_b = K.sb([128, 24, 128], BF16, "cd_b")
    for hf in range(2):
        K.load_bf16(cd_b, cd_b.t[:, hf * 12:(hf + 1) * 12, :], W["cdiag"], W["cdiag"].t[:, hf * 12:(hf + 1) * 12, :],
                    [128, 12, 128])
    co_b = K.sb([128, 8, 1024], BF16, "co_b")
    for kc in range(0, 8, 2):
        K.load_bf16(co_b, co_b.t[:, kc:kc + 2, :], W["c_w_out"], W["c_w_out"].t[:, kc:kc + 2, :], [128, 2, 1024])
    NE = NT + 2
    xe_r = K.ring(2, [128, 8, NE], F32, "cxe")
    he_r = K.ring(2, [128, 8, NE], BF16, "che")
    K.nm_sq = K.sb([128, 8, NE], BF16, "cnmsq")
    K.nm_tmp = K.sb([128, 8, NE], F32, "cnmtmp")
    cgs = K.ring(2, [128, NE], F32, "cgs")
    u_b = K.sb([128, 8, NE], BF16, "u_b")
    bg = K.sb([128, 8, NT], F32, "bg")
    y_b = K.ring(2, [128, 8, NT], BF16, "y_b")
    x3t = K.ring(2, [128, 8, NT], F32, "x3t")
    ntl = S_S // NT
    if limit.get("ctiles"):
        ntl = limit["ctiles"]
    for k in range(ntl):
        xe = xe_r[k % 2]
        K.dma(xe.t[:, :, :], x2.t[:, NT * k:NT * k + NE].rearrange("(c p) t -> p c t", p=128), [x2], [xe])
        he = he_r[k % 2]
        K.norm_mod(xe, NE, gmod, sh, he)
        for ch in range(8):
            psc = K.ps()
            for kc in range(8):
                K.mm(psc.t[:, 0:NE], cw_b.t[:, kc, 1024 + ch * 128:1024 + (ch + 1) * 128], he.t[:, kc, :], kc == 0,
                     kc == 7, [cw_b, he], [psc])
            psx = K.ps()
            for kc in range(8):
                K.mm(psx.t[:, 0:NE], cw_b.t[:, kc, 2048 + ch * 128:2048 + (ch + 1) * 128], he.t[:, kc, :], kc == 0,
                     kc == 7, [cw_b, he], [psx])
            cg = cgs[ch % 2]
            K.cp("act", cg.t[:, :], psc.t[:, 0:NE], [psc], [cg])
            K.tt("dve", u_b.t[:, ch, :], psx.t[:, 0:NE], cg.t[:, :], ALU.mult, [psx, cg], [u_b])
            psb_ = K.ps()
            for kc in range(8):
                K.mm(psb_.t[:, 0:NT], cw_b.t[:, kc, ch * 128:(ch + 1) * 128], he.t[:, kc, 1:1 + NT], kc == 0, kc == 7,
                     [cw_b, he], [psb_])
            K.cp("act", bg.t[:, ch, :], psb_.t[:, 0:NT], [psb_], [bg])
        if k == 0:
            key = "c" if kind == "p" else "zero"
            _lk(K, u_b, u_b.t[:, :, 0:1], key, LK, links)
        if k == S_S // NT - 1:
            key = "d" if kind == "p" else "zero"
            _lk(K, u_b, u_b.t[:, :, NE - 1:NE], key, LK, links)
        yb = y_b[k % 2]
        for ch in range(8):
            psv = K.ps()
            for tp in range(3):
                K.mm(psv.t[:, 0:NT], cd_b.t[:, ch * 3 + tp, :], u_b.t[:, ch, tp:tp + NT], tp == 0, tp == 2, [cd_b, u_b],
                     [psv])
            K.tt("dve", yb.t[:, ch, :], psv.t[:, 0:NT], bg.t[:, ch, :], ALU.mult, [psv, bg], [yb])
        x3_ = x3t[k % 2]
        for dc in range(8):
            psm = K.ps()
            for c in range(8):
                K.mm(psm.t[:, 0:NT], co_b.t[:, c, dc * 128:(dc + 1) * 128], yb.t[:, c, :], c == 0, c == 7, [co_b, yb],
                     [psm])
            K.stt(x3_.t[:, dc, :], psm.t[:, 0:NT], g1[:, dc:dc + 1], xe.t[:, dc, 1:1 + NT], ALU.mult, ALU.add,
                  [psm, K.modv, xe], [x3_])
        K.dma(x3.t[:, NT * k:NT * (k + 1)].rearrange("(c p) t -> p c t", p=128), x3_.t[:, :, :], [x3_], [x3])


def _lk(K, t, ap, key, LK, links):
    if key == "zero":
        K.ts("pool", ap, ap, 0.0, None, ALU.mult, None, [t], [t])
    else:
        K.ts("pool", ap, ap, LK[key], None, ALU.mult, None, [t, links], [t])


def _peer(K, l, si, xsrc, xdst, tlist, W, limit):
    MV, GM = W["MV"], W["GM"]
    gmod, sh, g2 = GM(l, si, 1), MV(l, si, 3), MV(l, si, 5)
    ident_b, ident_f = W["ident_b"], W["ident_f"]
    wq_b, uv_b = W["wq_b"], W["uv_b"]
    K.stage = K.ring(1, [128, 256], F32, "pstage")
    kT = K.sb([128, 2, 128], BF16, "kT")
    K.load_bf16(kT, kT.t[:], W["k12T"], W["k12T"].t[:, 2 * l:2 * l + 2, :], [128, 2, 128])
    GT = K.sb([128, NT, 128], BF16, "GT")
    RT = K.sb([128, 128, 128], BF16, "RT")
    OHT = K.sb([128, 128, 128], BF16, "OHT")
    ROr = K.ring(2, [128, 128, 64], BF16, "RO")
    s12 = K.sb([128, 16, 128], F32, "s12")
    e2 = K.sb([128, 8, 128], BF16, "e2")
    vv = K.sb([128, 16, 16], F32, "vv")
    vs = K.sb([128, 8, 16], F32, "vs")
    sm = K.sb([128, 8, 64], F32, "sm")
    h2 = K.sb([128, 8, NT], BF16, "h2")
    wqs = K.ring(2, [128, 8, 128], BF16, "wqs")
    NSB = 4
    strm = K.sb([128, NSB, 2048], BF16, "strm")
    sbuf_ = [Buf("strm%d" % i) for i in range(NSB)]
    uvs = [T(strm.t[:, i, :], sbuf_[i]) for i in range(NSB)]
    qTv = strm.t[:, :, :].rearrange("p a b -> p (a b)")[:, 0:16 * NT].rearrange("p (c t) -> p c t", t=NT)
    qTb = sbuf_[0:2]
    gel = K.ring(3, [128, NT], BF16, "gel")
    atb = K.ring(3, [128, NT], BF16, "atb")
    rtb = RT.t[:, :, :].rearrange("p a b -> p (a b)")
    rtf = rtb.bitcast(F32)
    ohf = OHT.t[:, :, :].rearrange("p a b -> p (a b)").bitcast(F32)
    K.nm_sq = T(rtb[:, 0:8 * NT].rearrange("p (c t) -> p c t", t=NT), RT.b)
    K.nm_tmp = T(ohf[:, 0:8 * NT].rearrange("p (c t) -> p c t", t=NT), OHT.b)
    GTB = [GT.b, Buf("GTb")]
    gtf = GT.t[:, 128:256, :].rearrange("p a b -> p (a b)").bitcast(F32)
    tmpT = T(gtf[:, 0:2048].rearrange("p (c j) -> p c j", j=128), GTB[1])
    e2fT = T(gtf[:, 2048:3072].rearrange("p (h j) -> p h j", j=128), GTB[1])
    tmp = tmpT.t
    e2f = e2fT.t
    xt = T(rtf[:, 4096:4096 + 8 * NT].rearrange("p (c t) -> p c t", t=NT), RT.b)
    rof = ROr[1].t[:, :, :].rearrange("p a b -> p (a b)").bitcast(F32)
    RKB = ROr[1].b
    cand = rof[:, 0:2048].rearrange("p (h a b) -> p h a b", a=16, b=16)
    ctmp = rof[:, 2048:4096].rearrange("p (h a b) -> p h a b", a=16, b=16)
    sel = ctmp
    t1v = cand
    osb = T(ohf[:, 0:1024], OHT.b)
    nchunk = limit.get("pchunks", 128)

    def load_x(pieces):
        o = 0
        for (c0, ln) in pieces:
            K.dma(xt.t[:, :, o:o + ln], xsrc.t[:, c0:c0 + ln].rearrange("(c p) t -> p c t", p=128), [xsrc], [xt])
            o += ln

    for pieces in tlist:
        nt = sum(p[1] for p in pieces)
        load_x(pieces)
        K.norm_mod(xt, nt, gmod, sh, h2)
        for c in range(16):
            wq_ = wqs[c % 2]
            K.dma(wq_.t[:, :, :], wq_b.t[l, c], [wq_b], [wq_])
            ps = K.ps()
            for kc in range(8):
                K.mm(ps.t[:, 0:nt], wq_.t[:, kc, :], h2.t[:, kc, 0:nt], kc == 0, kc == 7, [wq_, h2], [ps])
            K.cp("act" if c % 2 == 0 else "dve", qTv[:, c, 0:nt], ps.t[:, 0:nt], [ps], qTb)
        st_ = {}

        def t_pre(t0):
            ns = min(128, nt - t0)
            for g in range(4):
                ps = K.ps()
                for m in range(4):
                    c = g * 4 + m
                    K.mm(ps.t[0:ns, m * 128:(m + 1) * 128], qTv[:, c, t0:t0 + ns], kT.t[:, c % 2, :], True, True,
                         qTb + [kT], [ps])
                K.cp("act", s12.t[0:ns, g * 4:(g + 1) * 4, :], ps.t[0:ns, :].rearrange("p (m j) -> p m j", j=128),
                     [ps], [s12])
            for c in range(16):
                K.S.op("dve", lambda e, c=c, ns=ns: e.max(out=vv.t[0:ns, c, 0:8], in_=s12.t[0:ns, c, :]), [s12], [vv])
            for c in range(16):
                K.S.op("dve", lambda e, c=c, ns=ns: e.match_replace(out=tmp[0:ns, c, :], in_to_replace=vv.t[0:ns, c, 0:8],
                                                                    in_values=s12.t[0:ns, c, :], imm_value=-BIG),
                       [s12, vv], [tmpT])
            for c in range(16):
                K.S.op("dve", lambda e, c=c, ns=ns: e.max(out=vv.t[0:ns, c, 8:16], in_=tmp[0:ns, c, :]), [tmpT], [vv])
            v4 = vv.t[:, :, :].rearrange("p (h w) a -> p h w a", w=2)
            v1 = v4[0:ns, :, 0, :]
            v2 = v4[0:ns, :, 1, :]
            s4 = s12.t[:, :, :].rearrange("p (h w) j -> p h w j", w=2)
            s1 = s4[0:ns, :, 0, :]
            s2 = s4[0:ns, :, 1, :]
            K.tt("pool", cand[0:ns], v1.unsqueeze(3).to_broadcast([ns, 8, 16, 16]),
                 v2.unsqueeze(2).to_broadcast([ns, 8, 16, 16]), ALU.add, [vv], [RKB])
            for h in range(8):
                K.S.op("dve", lambda e, h=h, ns=ns: e.max(out=vs.t[0:ns, h, 0:8], in_=cand[0:ns, h]), [RKB], [vs])
            for h in range(8):
                K.S.op("dve", lambda e, h=h, ns=ns: e.match_replace(out=ctmp[0:ns, h], in_to_replace=vs.t[0:ns, h, 0:8],
                                                                    in_values=cand[0:ns, h], imm_value=-BIG),
                       [RKB, vs], [RKB])
            for h in range(8):
                K.S.op("dve", lambda e, h=h, ns=ns: e.max(out=vs.t[0:ns, h, 8:16], in_=ctmp[0:ns, h]), [RKB], [vs])
            ev = sm.t[0:ns, :, 0:16]
            thr = sm.t[0:ns, :, 16:32]
            Z = sm.t[0:ns, :, 32]
            rZ = sm.t[0:ns, :, 33]
            w1 = sm.t[0:ns, :, 40:56]
            K.tt("pool", ev, vs.t[0:ns], vs.t[0:ns, :, 0:1].to_broadcast([ns, 8, 16]), ALU.subtract, [vs], [sm])
            K.act(ev, ev, AF.Exp, [sm], [sm])
            K.S.op("dve", lambda e, ev=ev, Z=Z: e.tensor_reduce(out=Z, in_=ev, op=ALU.add, axis=AX.X), [sm], [sm])
            K.S.op("dve", lambda e, Z=Z, rZ=rZ: e.reciprocal(out=rZ, in_=Z), [sm], [sm])
            K.tt("pool", w1, v1, v1[:, :, 0:1].to_broadcast([ns, 8, 16]), ALU.subtract, [vv], [sm])
            K.act(w1, w1, AF.Exp, [sm], [sm])
            K.tt("pool", w1, w1, rZ.unsqueeze(2).to_broadcast([ns, 8, 16]), ALU.mult, [sm], [sm])
            K.tt("pool", e2f[0:ns], s2, v2[:, :, 0:1].to_broadcast([ns, 8, 128]), ALU.subtract, [s12, vv], [e2fT])
            K.act(e2.t[0:ns], e2f[0:ns], AF.Exp, [e2fT], [e2])
            K.tt("dve", sel[0:ns], cand[0:ns], vs.t[0:ns, :, 15:16].unsqueeze(3).to_broadcast([ns, 8, 16, 16]),
                 ALU.is_ge, [RKB, vs], [RKB])
            K.tt("pool", t1v[0:ns], sel[0:ns], v2.unsqueeze(2).to_broadcast([ns, 8, 16, 16]), ALU.mult, [RKB, vv], [RKB])
            K.ts("pool", sel[0:ns], sel[0:ns], -BIG, BIG, ALU.mult, ALU.add, [RKB], [RKB])
            K.tt("pool", t1v[0:ns], t1v[0:ns], sel[0:ns], ALU.add, [RKB], [RKB])
            K.S.op("dve", lambda e, ns=ns, thr=thr: e.tensor_reduce(out=thr, in_=t1v[0:ns], op=ALU.min, axis=AX.X),
                   [RKB], [sm])
            st_[t0] = dict(ns=ns, s1=s1, s2=s2, v1=v1, v2=v2, thr=thr, w1=w1)

        def t_build(t0):
            d_ = st_[t0]
            ns, s1, s2, v1, v2, thr, w1 = d_['ns'], d_['s1'], d_['s2'], d_['v1'], d_['v2'], d_['thr'], d_['w1']
            rnd = 0
            for ih in range(2):
                RO = ROr[rnd % 2]
                rnd += 1
                is_ = slice(ih * 64, (ih + 1) * 64)
                for h in range(8):
                    for a_ in range(16):
                        K.ts("dve", RO.t[0:ns, h * 16 + a_, :], s1[:, h, is_], v1[:, h, a_:a_ + 1], w1[:, h, a_:a_ + 1],
                             ALU.is_equal, ALU.mult, [s12, vv, sm], [RO])
                _transposes(K, RO, OHT, ih, ns, ident_b)
            for jh in range(2):
                RO = ROr[rnd % 2]
                rnd += 1
                js = slice(jh * 64, (jh + 1) * 64)
                for h in range(8):
                    for a_ in range(16):
                        K.stt(RO.t[0:ns, h * 16 + a_, :], s2[:, h, js], thr[:, h, a_:a_ + 1], e2.t[0:ns, h, js],
                              ALU.is_ge, ALU.mult, [s12, sm, e2], [RO])
                _transposes(K, RO, RT, jh, ns, ident_b)

        def t_gmm(t0):
            ns = st_[t0]['ns']
            for tb in range(0, ns, 4):
                nb = min(4, ns - tb)
                ps = K.ps()
                for u in range(nb):
                    K.mm(ps.t[:, u * 128:(u + 1) * 128], RT.t[:, :, tb + u], OHT.t[:, :, tb + u], True, True, [RT, OHT],
                         [ps])
                K.cp("act", GT.t[:, t0 + tb:t0 + tb + nb, :].rearrange("p t i -> p (t i)"), ps.t[:, 0:nb * 128],
                     [ps], [GTB[t0 // 128]])

        subs = list(range(0, nt, 128))
        t_pre(subs[0])
        t_build(subs[0])
        for k_ in range(1, len(subs)):
            t_pre(subs[k_])
            t_gmm(subs[k_ - 1])
            t_build(subs[k_])
        t_gmm(subs[-1])
        nsub = (nt + 127) // 128
        pso = [[K.psf[2 + 2 * s_ + dh] for dh in range(2)] for s_ in range(nsub)]
        pss = [K.psf[0], K.psf[1]]
        PF = NSB - 2

        def issue_dma(i):
            uv_ = uvs[i % NSB]
            K.dma(uv_.t, uv_b.t[l, i], [uv_b], [uv_])

        def front(i):
            uv_ = uvs[i % NSB]
            ps = pss[i % 2]
            for kc in range(8):
                K.mm(ps.t[:, 0:nt], uv_.t[:, kc * 128:(kc + 1) * 128], h2.t[:, kc, 0:nt], kc == 0, kc == 7,
                     [uv_, h2], [ps])
            gl = gel[i % 3]
            K.act(gl.t[:, 0:nt], ps.t[:, 0:nt], AF.Gelu_apprx_tanh, [ps], [gl])
            ab = atb[i % 3]
            K.tt("dve" if i % 3 else "pool", ab.t[:, 0:nt], gl.t[:, 0:nt], GT.t[:, 0:nt, i], ALU.mult, [gl] + GTB, [ab])

        def back(i):
            uv_ = uvs[i % NSB]
            ab = atb[i % 3]
            for s_ in range(nsub):
                ns = min(128, nt - s_ * 128)
                for dh in range(2):
                    K.mm(pso[s_][dh].t[0:ns, :], ab.t[:, s_ * 128:s_ * 128 + ns],
                         uv_.t[:, 1024 + dh * 512:1024 + (dh + 1) * 512], i == 0, i == nchunk - 1, [ab, uv_],
                         [pso[s_][dh]])

        for i in range(min(PF, nchunk)):
            issue_dma(i)
        for i in range(nchunk + 1):
            if i < nchunk:
                front(i)
            if i >= 1:
                back(i - 1)
            if i + PF < nchunk:
                issue_dma(i + PF)
        load_x(pieces)
        for s_ in range(nsub):
            ns = min(128, nt - s_ * 128)
            for dh in range(2):
                K.cp("act", osb.t[0:ns, dh * 512:(dh + 1) * 512], pso[s_][dh].t[0:ns, :], [pso[s_][dh]], [osb])
            for dc in range(8):
                ps = pss[dc % 2]
                K.tr(ps.t[:, 0:ns], osb.t[0:ns, dc * 128:(dc + 1) * 128], ident_f.t[0:ns, 0:ns], [osb, ident_f], [ps])
                K.stt(xt.t[:, dc, s_ * 128:s_ * 128 + ns], ps.t[:, 0:ns], g2[:, dc:dc + 1],
                      xt.t[:, dc, s_ * 128:s_ * 128 + ns], ALU.mult, ALU.add, [ps, K.modv, xt], [xt])
        o = 0
        for (c0, ln) in pieces:
            K.dma(xdst.t[:, c0:c0 + ln].rearrange("(c p) t -> p c t", p=128), xt.t[:, :, o:o + ln], [xt], [xdst])
            o += ln


def _transposes(K, RO, DST, half, ns, ident_b):
    for g in range(8):
        pb = K.psb[g % 2]
        for u in range(8):
            jj = g * 8 + u
            K.tr(pb.t[:, u * 128:u * 128 + ns], RO.t[0:ns, :, jj], ident_b.t[0:ns, 0:ns], [RO, ident_b], [pb])
        j0 = half * 64 + g * 8
        if ns == 128:
            K.cp("act", DST.t[:, j0:j0 + 8, :].rearrange("p j t -> p (j t)"), pb.t[:, :], [pb], [DST])
        else:
            K.cp("act", DST.t[:, j0:j0 + 8, 0:ns], pb.t[:, :].rearrange("p (j t) -> p j t", t=128)[:, :, 0:ns], [pb], [DST])


def _fm(v, n=8):
    return np.ascontiguousarray(np.asarray(v, np.float32).reshape(n, 128).T)


def _wmat(w):
    K_, N = w.shape
    return np.ascontiguousarray(np.asarray(w, np.float32).reshape(K_ // 128, 128, N).transpose(1, 0, 2))


def _rope_tables(pos):
    inv = (1.0 / (10000.0 ** (np.arange(0, 32, 2, dtype=np.float32) / np.float32(32)))).astype(np.float32)
    ang = pos.astype(np.float32)[:, None] * inv[None, :]
    c = np.cos(ang).astype(np.float32).T
    s = np.sin(ang).astype(np.float32).T
    out = np.empty((32, 2, pos.shape[0]), np.float32)
    out[0:16, 0] = c
    out[16:32, 0] = c
    out[0:16, 1] = s
    out[16:32, 1] = s
    return out


def host_inputs(I):
    f = np.float32
    shared = {}
    shared["ident"] = np.eye(128, dtype=f)
    p96 = np.zeros((96, 96), f)
    for m in range(16):
        p96[64 + m + 16, 64 + m] = -1.0
        p96[64 + m, 64 + m + 16] = 1.0
    shared["p96"] = p96
    shared["adaw"] = np.ascontiguousarray(I["ada_w"].reshape(2, 8, 128, 6144).transpose(0, 2, 1, 3))
    shared["adab"] = np.ascontiguousarray(I["ada_b"].reshape(2, 48, 128).transpose(2, 0, 1))
    shared["n1g"] = np.ascontiguousarray(I["norm1_g"].reshape(2, 8, 128).transpose(2, 0, 1))
    shared["n2g"] = np.ascontiguousarray(I["norm2_g"].reshape(2, 8, 128).transpose(2, 0, 1))
    shared["w_in"] = _wmat(I["ab_w_in"][0])
    wkr = np.zeros((128, 8, 96), f)
    wkr[:, :, 64:96] = shared["w_in"][:, :, 1408:1440]
    shared["wkr"] = wkr
    cw = I["rg_conv_w"][0]
    dg = np.zeros((128, 16, 128), f)
    for ch in range(4):
        for k in range(4):
            dg[np.arange(128), ch * 4 + k, np.arange(128)] = cw[k, ch * 128:(ch + 1) * 128]
    shared["rgdiag"] = dg
    shared["rgcb"] = _fm(I["rg_conv_b"][0], 4)
    bd = np.zeros((128, 16, 128), f)
    rgb = np.zeros((128, 16), f)
    for d in range(2):
        for wi, (wn, bn) in enumerate((("rg_wa", "rg_ba"), ("rg_wx", "rg_bx"))):
            for ch in range(4):
                idx = (d * 2 + wi) * 4 + ch
                for hh in range(2):
                    bd[hh * 64:(hh + 1) * 64, idx, hh * 64:(hh + 1) * 64] = I[wn][0, d, ch * 2 + hh]
                rgb[:, idx] = I[bn][0, d, ch * 128:(ch + 1) * 128]
    shared["rgbd"] = bd
    shared["rgb"] = rgb
    lam = np.zeros((128, 8), f)
    for d in range(2):
        lam[:, d * 4:(d + 1) * 4] = _fm(I["rg_lambda"][0, d], 4)
    shared["rglam"] = lam
    shared["qnorm"] = _fm(I["mla_q_norm"][0], 2)
    shared["w_uq"] = _wmat(I["mla_w_uq"][0])
    shared["kvnorm"] = _fm(I["mla_kv_norm"][0], 1)
    wukv = I["mla_w_ukv"][0].reshape(128, 8, 128)
    shared["wk"] = np.ascontiguousarray(wukv[:, :, 0:64].reshape(128, 512))
    shared["wv"] = np.ascontiguousarray(wukv[:, :, 64:128].reshape(128, 512))
    shared["qng"] = np.ascontiguousarray(np.stack([I["mla_qn_q"][0], I["mla_qn_k"][0]], axis=1).astype(f))
    wo = I["ab_w_out"][0]
    shared["wo_rg"] = _wmat(wo[0:512])
    shared["wo_at"] = np.ascontiguousarray(wo[512:1024].reshape(8, 64, 1024).transpose(1, 0, 2))
    shared["c_w_in"] = _wmat(I["c_w_in"][0])
    ccw = I["c_conv_w"][0]
    cd = np.zeros((128, 24, 128), f)
    for ch in range(8):
        for k in range(3):
            cd[np.arange(128), ch * 3 + k, np.arange(128)] = ccw[k, ch * 128:(ch + 1) * 128]
    shared["cdiag"] = cd
    shared["c_w_out"] = _wmat(I["c_w_out"][0])
    shared["wq"] = np.ascontiguousarray(I["peer_wq"].reshape(2, 8, 128, 2048).transpose(0, 2, 1, 3))
    k12 = np.zeros((128, 4, 128), f)
    for l in range(2):
        k12[:, 2 * l + 0, :] = I["peer_k1"][l].T
        k12[:, 2 * l + 1, :] = I["peer_k2"][l].T
    shared["k12T"] = k12
    U = I["peer_u"].reshape(2, 128, 128, 8, 128)
    shared["ut"] = np.ascontiguousarray(U.transpose(0, 1, 4, 3, 2)).reshape(2, 128, 128, 1024)
    shared["pv"] = np.ascontiguousarray(I["peer_v"].reshape(2, 128, 128, 1024))
    shared["rope_s"] = _rope_tables(np.arange(S_S))
    maps = []
    for c in range(8):
        b, q = c // 4, c % 4
        m = dict(shared)
        m["xs"] = np.ascontiguousarray(I["x_sample"][c].T)
        start = ((q + 1) * 4096 + 1) % S_P
        pos = (start + np.arange(S_P)) % S_P
        m["xp"] = np.ascontiguousarray(I["x_prompt"][b][pos].T)
        m["rope_p"] = _rope_tables(pos)
        cv = np.zeros((128, 8, 2), f)
        cv[:, :, 0] = _fm(I["c_sample"][c])
        cv[:, :, 1] = _fm(I["c_prompt"][b])
        m["cvec"] = cv
        lk = np.ones((128, 4), f)
        lk[:, {2: 0, 1: 1, 0: 2, 3: 3}[q]] = 0.0
        m["links"] = lk
        maps.append(m)
    return maps


_CACHE = {}


def kernel(**inputs):
    I = {k: np.asarray(v) for k, v in inputs.items()}
    maps = host_inputs(I)
    if "nc" not in _CACHE:
        _CACHE["nc"] = build()
    nc, K = _CACHE["nc"]
    res = run_bass_kernel_spmd(nc, maps, core_ids=list(range(8)))
    y_prompt = np.empty((2, S_P, 1024), np.float32)
    y_sample = np.empty((8, S_S, 1024), np.float32)
    for c in range(8):
        r = res.results[c]
        b, q = c // 4, c % 4
        y_sample[c] = r["ys"].T
        y_prompt[b, q * 4096:(q + 1) * 4096] = r["yp"].T
    return (y_prompt, y_sample)
```

```python
import os
import contextlib
import numpy as np
import concourse.bass as bass
import concourse.mybir as mybir
from concourse.bass_utils import run_bass_kernel_spmd

F32 = mybir.dt.float32
BF16 = mybir.dt.bfloat16
AF = mybir.ActivationFunctionType
ALU = mybir.AluOpType
AX = mybir.AxisListType
ENGS = ["pe", "act", "dve", "pool", "sp"]
EPS = 1e-6
BIG = 1.0e30

S_S = 4096
S_P = 16384
NOWN = 4098
ROT0 = 12286
TA = 256
NT = 256


class Buf:
    __slots__ = ("name", "last_w", "readers", "dsem", "dval")

    def __init__(self, name):
        self.name = name
        self.last_w = None
        self.readers = []
        self.dsem = None
        self.dval = 0


class T:
    __slots__ = ("t", "b")

    def __init__(self, t, b):
        self.t = t
        self.b = b

    def __getitem__(self, k):
        return self.t[k]


def _b(x):
    return x.b if isinstance(x, T) else x


class Sched:
    def __init__(self, nc, stack):
        self.nc = nc
        self.stack = stack
        self.ops = {e: [] for e in ENGS}
        self.cnt = {e: 0 for e in ENGS}
        self.sem = {e: stack.enter_context(nc.semaphore("sem_" + e)) for e in ENGS if e != "sp"}
        self.seen = {e: {} for e in ENGS}
        self.dma_bufs = []
        self.free_sems = []
        self.nsem_alloc = 0
        self.ninstr = 0

    def _deps(self, eng, reads, writes):
        deps = []
        own = self.sem.get(eng)
        for b in reads:
            if b.last_w is not None:
                deps.append(b.last_w)
        for b in writes:
            if b.last_w is not None and b.last_w[0] is not own:
                deps.append(b.last_w)
            deps.extend(r for r in b.readers if r[0] is not own)
        seen = self.seen[eng]
        best = {}
        pe_sem = self.sem["pe"]
        for (s, v) in deps:
            if eng == "pe" and s is pe_sem:
                continue
            k = id(s)
            if seen.get(k, 0) >= v:
                continue
            if k not in best or best[k][1] < v:
                best[k] = (s, v)
        for k, (s, v) in best.items():
            seen[k] = v
        return list(best.values())

    def op(self, eng, fn, reads=(), writes=()):
        reads = [_b(x) for x in reads]
        writes = [_b(x) for x in writes]
        deps = self._deps(eng, reads, writes)
        self.cnt[eng] += 1
        s = self.sem[eng]
        tok = (s, self.cnt[eng])
        self.ops[eng].append((deps, fn, s, 1))
        for b in writes:
            b.last_w = tok
            b.readers = []
        for b in reads:
            if b in writes:
                continue
            b.readers = [r for r in b.readers if r[0] is not s] + [tok]
        self.ninstr += 1

    def dma(self, out_ap, in_ap, reads=(), writes=(), eng="sp"):
        reads = [_b(x) for x in reads]
        writes = [_b(x) for x in writes]
        deps = self._deps(eng, reads, writes)
        tb = writes[0] if writes else reads[0]
        if tb.dsem is None:
            if self.free_sems:
                tb.dsem, tb.dval = self.free_sems.pop()
            else:
                self.nsem_alloc += 1
                tb.dsem = self.stack.enter_context(self.nc.semaphore("dsem%d" % self.nsem_alloc))
                tb.dval = 0
            self.dma_bufs.append(tb)
        tb.dval += 16
        tok = (tb.dsem, tb.dval)
        self.ops[eng].append((deps, lambda e: e.dma_start(out=out_ap, in_=in_ap, allow_slow_non_contiguous=True), tb.dsem, 16))
        for b in writes:
            b.last_w = tok
            b.readers = []
        for b in reads:
            b.readers = b.readers + [tok]
        self.ninstr += 1
        return tok

    def barrier(self):
        toks = [(self.sem[e], self.cnt[e]) for e in self.sem if self.cnt[e] > 0]
        toks += [(b.dsem, b.dval) for b in self.dma_bufs]
        for e in ENGS:
            deps = []
            for (s, v) in toks:
                if e in self.sem and s is self.sem[e]:
                    continue
                if self.seen[e].get(id(s), 0) >= v:
                    continue
                self.seen[e][id(s)] = v
                deps.append((s, v))
            self.ops[e].append((deps, None, None, 0))
        for b in self.dma_bufs:
            self.free_sems.append((b.dsem, b.dval))
            b.dsem = None
        self.dma_bufs = []

    def flush(self, final_waits=()):
        nc = self.nc
        engmap = {"pe": "tensor", "act": "scalar", "dve": "vector", "pool": "gpsimd", "sp": "sync"}
        with nc.Block() as block:
            for e in ENGS:
                ops = self.ops[e]
                fw = list(final_waits) if e == "sp" else []

                def body(eng, ops=ops, fw=fw):
                    for (deps, fn, s, inc) in ops:
                        for (ds, dv) in deps:
                            eng.wait_ge(ds, dv)
                        if fn is not None:
                            fn(eng).then_inc(s, inc)
                    for (ds, dv) in fw:
                        eng.wait_ge(ds, dv)
                getattr(block, engmap[e])(body)
        self.ops = {e: [] for e in ENGS}


class KB:
    def __init__(self, nc, st, debug):
        self.nc = nc
        self.st = st
        self.S = Sched(nc, st)
        self.ph = None
        self.uid = 0
        self.debug = debug
        self.inputs = {}
        self.outputs = {}
        self.psr = 0

    def sb(self, shape, dt=F32, name="t", buf=None):
        self.uid += 1
        nm = "%s_%d" % (name, self.uid)
        t = (self.ph or self.st).enter_context(self.nc.sbuf_tensor(nm, list(shape), dt))
        return T(t, buf if buf is not None else Buf(nm))

    def ring(self, n, shape, dt=F32, name="r"):
        return [self.sb(shape, dt, name) for _ in range(n)]

    def inp(self, name, shape, dt=F32):
        t = T(self.nc.dram_tensor(name, list(shape), dt, kind="ExternalInput").ap(), Buf(name))
        self.inputs[name] = t
        return t

    def outp(self, name, shape, dt=F32):
        t = T(self.nc.dram_tensor(name, list(shape), dt, kind="ExternalOutput").ap(), Buf(name))
        self.outputs[name] = t
        return t

    def scratch(self, name, shape, dt):
        kind = "ExternalOutput" if (self.debug and name in self.debug) else "Internal"
        t = T(self.nc.dram_tensor(name, list(shape), dt, kind=kind).ap(), Buf(name))
        if kind == "ExternalOutput":
            self.outputs[name] = t
        return t

    @contextlib.contextmanager
    def phase(self):
        with contextlib.ExitStack() as ph:
            old = self.ph
            self.ph = ph
            yield
            self.S.barrier()
            self.S.flush()
            self.ph = old

    def ps(self):
        b = self.psf[self.psr % len(self.psf)]
        self.psr += 1
        return b

    def mm(self, out, lhsT, rhs, start, stop, reads, writes):
        self.S.op("pe", lambda e: e.matmul(out, lhsT=lhsT, rhs=rhs, start=start, stop=stop), reads, writes)

    def tr(self, out, in_, ident, reads, writes):
        self.S.op("pe", lambda e: e.transpose(out=out, in_=in_, identity=ident), reads, writes)

    def act(self, out, in_, func, reads, writes, scale=None, bias=None, accum=None):
        kw = {}
        if scale is not None:
            kw["scale"] = scale
        if bias is not None:
            kw["bias"] = bias
        if accum is not None:
            kw["accum_out"] = accum
        self.S.op("act", lambda e: e.activation(out=out, in_=in_, func=func, **kw), reads, writes)

    def ts(self, eng, out, in0, s1, s2, op0, op1, reads, writes):
        if op1 is None:
            self.S.op(eng, lambda e: e.tensor_scalar(out=out, in0=in0, scalar1=s1, scalar2=None, op0=op0), reads, writes)
        else:
            self.S.op(eng, lambda e: e.tensor_scalar(out=out, in0=in0, scalar1=s1, scalar2=s2, op0=op0, op1=op1),
                      reads, writes)

    def tt(self, eng, out, in0, in1, op, reads, writes):
        self.S.op(eng, lambda e: e.tensor_tensor(out=out, in0=in0, in1=in1, op=op), reads, writes)

    def stt(self, out, in0, scalar, in1, op0, op1, reads, writes):
        self.S.op("dve", lambda e: e.scalar_tensor_tensor(out=out, in0=in0, scalar=scalar, in1=in1, op0=op0, op1=op1),
                  reads, writes)

    def cp(self, eng, out, in_, reads, writes):
        if eng == "act":
            self.S.op("act", lambda e: e.copy(out=out, in_=in_), reads, writes)
        else:
            self.S.op(eng, lambda e: e.tensor_copy(out=out, in_=in_), reads, writes)

    def memset(self, eng, out, val, writes):
        self.S.op(eng, lambda e: e.memset(out, val), (), writes)

    def dma(self, out, in_, reads, writes):
        self.S.dma(out, in_, reads, writes)

    def load_bf16(self, dst, dst_ap, src, src_ap, shape):
        stg = self.stage[self.stage_i % len(self.stage)]
        self.stage_i += 1
        n = int(np.prod(shape[1:]))
        sv = stg.t[0:shape[0], 0:n]
        if len(shape) == 3:
            sv = sv.rearrange("p (a b) -> p a b", b=shape[2])
        self.dma(sv, src_ap, [src], [stg])
        eng = ["act", "dve", "pool"][self.stage_i % 3]
        self.cp(eng, dst_ap, sv, [stg], [dst])

    def rstd_from_ps(self, ps, ps_ap, n, nfeat, parts=128):
        r = self.rs_ring[self.rs_i % len(self.rs_ring)]
        self.rs_i += 1
        rv = r.t[0:parts, 0:n]
        self.act(rv, ps_ap, AF.Ln, [ps], [r], scale=1.0 / nfeat, bias=self.eps_t.t[0:parts, 0:1])
        self.act(rv, rv, AF.Exp, [r], [r], scale=-0.5)
        return r, rv

    def norm_mod(self, x, n, gmod, sh, h, inplace=False):
        sq = self.nm_sq
        self.act(sq.t[:, :, 0:n], x.t[:, :, 0:n], AF.Square, [x], [sq])
        ps = self.ps()
        for c in range(8):
            self.mm(ps.t[:, 0:n], self.ones_b.t[:, :], sq.t[:, c, 0:n], c == 0, c == 7, [self.ones_b, sq], [ps])
        r, rv = self.rstd_from_ps(ps, ps.t[:, 0:n], n, 1024.0)
        tmp = x if inplace else self.nm_tmp
        self.tt("dve", tmp.t[:, :, 0:n], x.t[:, :, 0:n], rv.unsqueeze(1).to_broadcast([128, 8, n]), ALU.mult,
                [x, r], [tmp])
        for c in range(8):
            if c % 2 == 0:
                self.act(h.t[:, c, 0:n], tmp.t[:, c, 0:n], AF.Identity, [tmp, self.modv], [h],
                         scale=gmod[:, c:c + 1], bias=sh[:, c:c + 1])
            else:
                self.ts("pool", h.t[:, c, 0:n], tmp.t[:, c, 0:n], gmod[:, c:c + 1], sh[:, c:c + 1], ALU.mult, ALU.add,
                        [tmp, self.modv], [h])


def seq_desc(kind):
    d = {}
    if kind == "p":
        S = S_P
        segs = [("ctx", 0, 4095), ("ctx", 4095, 8191), ("ctx", 8191, 12286), ("halo", 12286, 12287),
                ("own", 12287, 16383), ("halo", 16383, 16384)]
        links = {4095: "a", 8191: "b", 12287: "c", 16383: "d"}
        own0 = ROT0
    else:
        S = S_S
        segs = [("own", 0, 4096)]
        links = {0: "zero"}
        own0 = -1
    tiles = []
    for (role, a, b) in segs:
        s = a
        while s < b:
            n = min(TA, b - s)
            tiles.append(dict(s=s, n=n, role=role, own=(s - own0) if role != "ctx" else None))
            s += n
    d.update(S=S, tiles=tiles, links=links, own0=own0, kind=kind)
    return d


def crossed(links, S, c_from, c_to):
    keys = []
    for p in range(c_from + 1, c_to + 1):
        k = links.get(p % S)
        if k is not None:
            keys.append(k)
    return keys


def build(debug=None, limit=None):
    debug = debug or {}
    limit = limit or {}
    nc = bass.Bass("TRN2", target_bir_lowering=False)
    with contextlib.ExitStack() as st:
        K = KB(nc, st, debug)
        _program(K, limit)
        finals = [t.b.last_w for t in K.outputs.values() if t.b.last_w is not None]
        K.S.flush(finals)
        print("kernel instrs:", K.S.ninstr, "dma sems:", K.S.nsem_alloc, flush=True)
    return nc, K


def _program(K, limit):
    nc = K.nc
    xs = K.inp("xs", [1024, S_S])
    xp = K.inp("xp", [1024, S_P])
    cvec = K.inp("cvec", [128, 8, 2])
    links_d = K.inp("links", [128, 4])
    rope_s = K.inp("rope_s", [32, 2, S_S])
    rope_p = K.inp("rope_p", [32, 2, S_P])
    ident_d = K.inp("ident", [128, 128])
    p96_d = K.inp("p96", [96, 96])
    adaw = K.inp("adaw", [2, 128, 8, 6144])
    adab = K.inp("adab", [128, 2, 48])
    n1g = K.inp("n1g", [128, 2, 8])
    n2g = K.inp("n2g", [128, 2, 8])
    w_in = K.inp("w_in", [128, 8, 1440])
    wkr = K.inp("wkr", [128, 8, 96])
    rgdiag = K.inp("rgdiag", [128, 16, 128])
    rgcb = K.inp("rgcb", [128, 4])
    rgbd = K.inp("rgbd", [128, 16, 128])
    rgb = K.inp("rgb", [128, 16])
    rglam = K.inp("rglam", [128, 8])
    qnorm = K.inp("qnorm", [128, 2])
    w_uq = K.inp("w_uq", [128, 2, 768])
    kvnorm = K.inp("kvnorm", [128, 1])
    wk = K.inp("wk", [128, 512])
    wv = K.inp("wv", [128, 512])
    qng = K.inp("qng", [96, 2])
    wo_rg = K.inp("wo_rg", [128, 4, 1024])
    wo_at = K.inp("wo_at", [64, 8, 1024])
    c_w_in = K.inp("c_w_in", [128, 8, 3072])
    cdiag = K.inp("cdiag", [128, 24, 128])
    c_w_out = K.inp("c_w_out", [128, 8, 1024])
    wq_d = K.inp("wq", [2, 128, 8, 2048])
    k12T = K.inp("k12T", [128, 4, 128])
    ut_d = K.inp("ut", [2, 128, 128, 1024])
    v_d = K.inp("pv", [2, 128, 128, 1024])
    ys = K.outp("ys", [1024, S_S])
    yp = K.outp("yp", [1024, S_S])

    kt_s = K.scratch("kt_s", [8, 96, S_P], BF16)
    v_s = K.scratch("v_s", [S_P, 512], BF16)
    qt_s = K.scratch("qt_s", [8, 96, NOWN], BF16)
    x1_s = {k: K.scratch("x1_" + k, [1024, NOWN], F32) for k in "sp"}
    x2_s = {k: K.scratch("x2_" + k, [1024, NOWN], F32) for k in "sp"}
    x3_s = {k: K.scratch("x3_" + k, [1024, S_S], F32) for k in "sp"}
    wq_b = K.scratch("wq_b", [2, 16, 128, 8, 128], BF16)
    uv_b = K.scratch("uv_b", [2, 128, 128, 2048], BF16)

    K.psf = [T(st_enter(K, nc.psum_tensor("psf%d" % i, [128, 512], F32)), Buf("psf%d" % i)) for i in range(6)]
    K.psb = [T(st_enter(K, nc.psum_tensor("psb%d" % i, [128, 1024], BF16)), Buf("psb%d" % i)) for i in range(2)]

    ident_f = K.sb([128, 128], F32, "identf")
    ident_b = K.sb([128, 128], BF16, "identb")
    K.ones_b = K.sb([128, 128], BF16, "onesb")
    ones_f = K.sb([128, 128], F32, "onesf")
    K.eps_t = K.sb([128, 1], F32, "eps")
    K.modv = K.sb([128, 2, 2, 6, 8], F32, "modv")
    gm = K.sb([128, 2, 2, 2, 8], F32, "gm")
    links = K.sb([128, 4], F32, "links")
    K.rs_ring = K.ring(3, [128, 264], F32, "rstd")
    K.rs_i = 0
    K.stage_i = 0
    K.one_t = K.sb([128, 1], F32, "one")
    K.memset("pool", K.one_t.t[:], 1.0, [K.one_t])
    modv = K.modv
    K.memset("pool", K.ones_b.t[:], 1.0, [K.ones_b])
    K.memset("pool", ones_f.t[:], 1.0, [ones_f])
    K.memset("pool", K.eps_t.t[:], EPS, [K.eps_t])
    K.dma(ident_f.t[:], ident_d.t[:, :], [ident_d], [ident_f])
    K.cp("dve", ident_b.t[:], ident_f.t[:], [ident_f], [ident_b])
    K.dma(links.t[:], links_d.t[:, :], [links_d], [links])

    LK = {"a": links.t[:, 0:1], "b": links.t[:, 1:2], "c": links.t[:, 2:3], "d": links.t[:, 3:4]}

    def apply_links(eng, ap, keys, t, parts=128):
        for k in keys:
            if k == "zero":
                K.ts(eng, ap, ap, 0.0, None, ALU.mult, None, [t], [t])
            else:
                K.ts(eng, ap, ap, LK[k][0:parts], None, ALU.mult, None, [t, links], [t])

    with K.phase():
        K.stage = K.ring(4, [128, 4096], F32, "stage")
        K.castb = K.ring(4, [128, 4096], BF16, "castb")
        zt = K.sb([128, 8, 1], F32, "zt")
        K.memset("pool", zt.t[:], 0.0, [zt])
        for cc in (0, NOWN - 1):
            K.dma(x2_s["s"].t[:, cc:cc + 1].rearrange("(c p) t -> p c t", p=128), zt.t[:], [zt], [x2_s["s"]])
        cv = K.sb([128, 8, 2], F32, "cv")
        K.dma(cv.t[:], cvec.t[:, :, :], [cvec], [cv])
        scv = K.sb([128, 8, 2], F32, "scv")
        K.act(scv.t[:], cv.t[:], AF.Silu, [cv], [scv])
        adb = K.sb([128, 2, 48], F32, "adb")
        K.dma(adb.t[:], adab.t[:, :, :], [adab], [adb])
        g12 = K.sb([128, 2, 2, 8], F32, "g12")
        K.dma(g12.t[:, 0], n1g.t[:, :, :], [n1g], [g12])
        K.dma(g12.t[:, 1], n2g.t[:, :, :], [n2g], [g12])
        awr = K.ring(2, [128, 8, 512], F32, "adaw")
        for l in range(2):
            psm = K.ps()
            for g in range(12):
                aw = awr[g % 2]
                K.dma(aw.t[:], adaw.t[l, :, :, g * 512:(g + 1) * 512], [adaw], [aw])
                for jj in range(4):
                    j = g * 4 + jj
                    for kc in range(8):
                        K.mm(psm.t[:, 2 * j:2 * j + 2], aw.t[:, kc, jj * 128:(jj + 1) * 128], scv.t[:, kc, :],
                             kc == 0, kc == 7, [aw, scv], [psm])
            for s in range(2):
                K.tt("dve", modv.t[:, l, s].rearrange("p w c -> p (w c)"),
                     psm.t[:, 0:96].rearrange("p (j s) -> p j s", s=2)[:, :, s], adb.t[:, l, :], ALU.add,
                     [psm, adb], [modv])
        for l in range(2):
            for s in range(2):
                for k in range(2):
                    K.stt(gm.t[:, l, s, k, :], modv.t[:, l, s, 1 + 3 * k, :], 1.0, g12.t[:, k, l, :], ALU.add, ALU.mult,
                          [modv, g12], [gm])
        if not limit.get("skip_prep"):
            cast_i = 0
            for l in range(2):
                jobs = [(wq_d.t[l], None, wq_d, wq_b, 4)]
                for i0 in range(0, 128, 4):
                    jobs.append((ut_d.t[l, i0:i0 + 4].rearrange("i p f -> p i f"),
                                 uv_b.t[l, i0:i0 + 4, :, 0:1024].rearrange("i p f -> p i f"), ut_d, uv_b, 1))
                    jobs.append((v_d.t[l, i0:i0 + 4].rearrange("i p f -> p i f"),
                                 uv_b.t[l, i0:i0 + 4, :, 1024:2048].rearrange("i p f -> p i f"), v_d, uv_b, 1))
                for (src, dst, srcT, dstT, nsplit) in jobs:
                    for sp_ in range(nsplit):
                        if nsplit == 1:
                            s_ap, d_ap = src, dst
                            shp = [128, 4, 1024]
                        else:
                            s_ap = src[:, 2 * sp_:2 * sp_ + 2, :]
                            d_ap = wq_b.t[l, :, :, 2 * sp_:2 * sp_ + 2, :].rearrange("c p k j -> p k c j")
                            shp = [128, 2, 2048]
                        stg = K.stage[cast_i % 4]
                        sv = stg.t[:, :].rearrange("p (a b) -> p a b", b=shp[2])
                        K.dma(sv, s_ap, [srcT], [stg])
                        cb = K.castb[cast_i % 4]
                        cbv = cb.t[:, :].rearrange("p (a b) -> p a b", b=shp[2])
                        K.cp(["act", "dve"][cast_i % 2], cbv, sv, [stg], [cb])
                        if nsplit == 1:
                            K.dma(d_ap, cbv, [cb], [dstT])
                        else:
                            for kk in range(2):
                                K.dma(wq_b.t[l, :, :, 2 * sp_ + kk, :].rearrange("c p j -> p c j"),
                                      cbv[:, kk, :].rearrange("p (c j) -> p c j", j=128), [cb], [dstT])
                        cast_i += 1

    def MV(l, s, which):
        return modv.t[:, l, s, which, :]

    def GM(l, s, k):
        return gm.t[:, l, s, k, :]

    seqs = [("s", 0, xs, rope_s, ys), ("p", 1, xp, rope_p, yp)]
    if limit.get("seqs"):
        seqs = [q for q in seqs if q[0] in limit["seqs"]]

    for (kind, si, xT, rope_d, yout) in seqs:
        sd = seq_desc(kind)
        S = sd["S"]
        with K.phase():
            _layer0_mixer(K, sd, si, xT, rope_d, dict(
                w_in=w_in, wkr=wkr, rgdiag=rgdiag, rgcb=rgcb, rgbd=rgbd, rgb=rgb, rglam=rglam, qnorm=qnorm, w_uq=w_uq,
                kvnorm=kvnorm, wk=wk, wv=wv, qng=qng, wo_rg=wo_rg, wo_at=wo_at, p96=p96_d, ident_b=ident_b,
                ones_f=ones_f, kt_s=kt_s, v_s=v_s, qt_s=qt_s, x1=x1_s[kind], MV=MV, GM=GM, apply_links=apply_links,
                links=links, LK=LK), limit)
        if limit.get("stop") in ("stageA", "mixer0"):
            continue
        with K.phase():
            tl = [[(1 + NT * k, NT)] for k in range(S_S // NT)]
            if kind == "p":
                tl.append([(0, 1), (NOWN - 1, 1)])
            if limit.get("ptiles"):
                tl = tl[:limit["ptiles"]]
            _peer(K, 0, si, x1_s[kind], x2_s[kind], tl, dict(wq_b=wq_b, uv_b=uv_b, k12T=k12T, ident_b=ident_b,
                                                             ident_f=ident_f, MV=MV, GM=GM), limit)
        if limit.get("stop") == "peer0":
            continue
        with K.phase():
            _mixer_c(K, si, kind, x2_s[kind], x3_s[kind], dict(c_w_in=c_w_in, cdiag=cdiag, c_w_out=c_w_out, MV=MV, GM=GM,
                                                                LK=LK, links=links), limit)
        if limit.get("stop") == "mixer1":
            continue
        with K.phase():
            tl = [[(NT * k, NT)] for k in range(S_S // NT)]
            if limit.get("ptiles"):
                tl = tl[:limit["ptiles"]]
            _peer(K, 1, si, x3_s[kind], yout, tl, dict(wq_b=wq_b, uv_b=uv_b, k12T=k12T, ident_b=ident_b,
                                                       ident_f=ident_f, MV=MV, GM=GM), limit)


def st_enter(K, cm):
    return K.st.enter_context(cm)


def _layer0_mixer(K, sd, si, xT, rope_d, W, limit):
    S = sd["S"]
    kind = sd["kind"]
    tiles = sd["tiles"]
    MV, GM = W["MV"], W["GM"]
    apply_links = W["apply_links"]
    LK = W["LK"]
    links = W["links"]
    NE = TA + 3
    GY = K.sb([128, 4, NOWN], BF16, "GY")
    _stageA(K, sd, si, xT, rope_d, W, limit, GY)
    if limit.get("stop") == "stageA":
        dbg = K.outp("dbg_rg_" + kind, [128, 4, NOWN], BF16)
        K.dma(dbg.t[:, :, :], GY.t[:], [GY], [dbg])
        K.S.barrier()
        return
    with K.phase():
        _stageB(K, sd, si, xT, W, limit, GY)


def _stageA(K, sd, si, xT, rope_d, W, limit, GY):
  with K.phase():
    S = sd["S"]
    kind = sd["kind"]
    tiles = sd["tiles"]
    MV, GM = W["MV"], W["GM"]
    apply_links = W["apply_links"]
    LK = W["LK"]
    links = W["links"]
    NE = TA + 3
    K.stage = K.ring(1, [128, 2048], F32, "stage")
    w_in_b = K.sb([128, 8, 1440], BF16, "w_in_b")
    for kc in range(8):
        K.load_bf16(w_in_b, w_in_b.t[:, kc, :], W["w_in"], W["w_in"].t[:, kc, :], [128, 1440])
    wkr_b = K.sb([128, 8, 96], BF16, "wkr_b")
    K.load_bf16(wkr_b, wkr_b.t[:], W["wkr"], W["wkr"].t[:, :, :], [128, 8, 96])
    dg_b = K.sb([128, 16, 128], BF16, "dg_b")
    K.load_bf16(dg_b, dg_b.t[:], W["rgdiag"], W["rgdiag"].t[:, :, :], [128, 16, 128])
    bd_b = K.sb([128, 16, 128], BF16, "bd_b")
    K.load_bf16(bd_b, bd_b.t[:], W["rgbd"], W["rgbd"].t[:, :, :], [128, 16, 128])
    wuq_b = K.sb([128, 2, 768], BF16, "wuq_b")
    K.load_bf16(wuq_b, wuq_b.t[:], W["w_uq"], W["w_uq"].t[:, :, :], [128, 2, 768])
    wk_b = K.sb([128, 512], BF16, "wk_b")
    K.load_bf16(wk_b, wk_b.t[:], W["wk"], W["wk"].t[:, :], [128, 512])
    wv_b = K.sb([128, 512], BF16, "wv_b")
    K.load_bf16(wv_b, wv_b.t[:], W["wv"], W["wv"].t[:, :], [128, 512])
    p96_b = K.sb([96, 96], BF16, "p96_b")
    K.load_bf16(p96_b, p96_b.t[:], W["p96"], W["p96"].t[:, :], [96, 96])
    small = K.sb([128, 64], F32, "small")
    K.dma(small.t[:, 0:4], W["rgcb"].t[:, :], [W["rgcb"]], [small])
    K.dma(small.t[:, 4:20], W["rgb"].t[:, :], [W["rgb"]], [small])
    K.dma(small.t[:, 20:28], W["rglam"].t[:, :], [W["rglam"]], [small])
    K.dma(small.t[:, 28:30], W["qnorm"].t[:, :], [W["qnorm"]], [small])
    K.dma(small.t[:, 30:31], W["kvnorm"].t[:, :], [W["kvnorm"]], [small])
    K.dma(small.t[0:96, 32:34], W["qng"].t[:, :], [W["qng"]], [small])
    K.act(small.t[:, 52:60], small.t[:, 20:28], AF.Exp, [small], [small], scale=-1.0)
    K.act(small.t[:, 52:60], small.t[:, 52:60], AF.Ln, [small], [small], bias=K.one_t.t[:, 0:1])
    K.ts("dve", small.t[:, 36:44], small.t[:, 52:60], -8.0, None, ALU.mult, None, [small], [small])
    K.ts("dve", small.t[:, 44:52], small.t[:, 52:60], -16.0, None, ALU.mult, None, [small], [small])

    def SA(d, ch):
        return small.t[:, 36 + d * 4 + ch:37 + d * 4 + ch]

    def SA2(d, ch):
        return small.t[:, 44 + d * 4 + ch:45 + d * 4 + ch]

    def GB(d, which, ch):
        j = 4 + (d * 2 + which) * 4 + ch
        return small.t[:, j:j + 1]

    gmod1, sh1, g1 = GM(0, si, 0), MV(0, si, 0), MV(0, si, 2)

    XC = K.sb([128, 4, NOWN], BF16, "XC")
    HF = K.sb([128, 4, NOWN], BF16, "HF")
    nctx = sum(1 for t in tiles if t["role"] != "own")
    AB = K.sb([128, 2, max(nctx, 1), 2, 4], F32, "AB")
    carry = K.sb([128, 2, 4], F32, "carry")
    K.memset("pool", carry.t[:], 0.0, [carry])

    xe_r = K.ring(1, [128, 8, NE], F32, "xe")
    he_r = K.ring(2, [128, 8, NE], BF16, "he")
    K.nm_sq = K.sb([128, 8, NE], BF16, "nmsq")
    xr_b = K.sb([128, 4, NE], BF16, "xr_b")
    xc_t = K.ring(2, [128, 4, TA], BF16, "xc_t")
    g_r = K.ring(2, [128, TA], F32, "g_r")
    g_i = K.ring(2, [128, TA], F32, "g_i")
    g_a = K.ring(2, [128, TA], F32, "g_a")
    g_q = K.ring(2, [128, TA], F32, "g_q")
    g_b = K.ring(2, [128, TA], F32, "g_b")
    g_h = K.ring(2, [128, TA], F32, "g_h")
    sumr = K.sb([128, 64], F32, "sumr")
    sqb = K.ring(2, [128, TA], BF16, "sqb")
    ckv = K.ring(2, [128, TA], BF16, "ckv")
    qn = K.ring(2, [128, 2, TA], BF16, "qn")
    vt = K.ring(2, [128, 512], BF16, "vt")
    krope = K.sb([96, TA], F32, "krope")
    kh = K.ring(2, [96, TA], F32, "kh")
    khn = K.ring(2, [96, TA], F32, "khn")
    khb = K.ring(3, [96, TA], BF16, "khb")
    rt1 = K.ring(2, [96, TA], F32, "rt1")
    rope_t = K.ring(2, [96, 2, TA], F32, "rope")
    gi = [0]

    def rnext(r):
        gi[0] += 1
        return r[gi[0] % len(r)]

    W_p96 = p96_b
    ctx_i = [0]

    def load_ext(t):
        s, n = t["s"], t["n"]
        xe = rnext(xe_r)
        lo, hi = s - 2, s + n + 1
        pieces = []
        c = lo
        while c < hi:
            cm = c % S
            ln = min(hi - c, S - cm)
            pieces.append((c - lo, cm, ln))
            c += ln
        for (o, cm, ln) in pieces:
            K.dma(xe.t[:, :, o:o + ln], xT.t[:, cm:cm + ln].rearrange("(c p) t -> p c t", p=128), [xT], [xe])
        return xe

    def load_rope(t):
        s, n = t["s"], t["n"]
        rp = rnext(rope_t)
        K.dma(rp.t[64:96, :, 0:n], rope_d.t[:, :, s:s + n], [rope_d], [rp])
        return rp

    def tileA(t, mode):
        s, n = t["s"], t["n"]
        ne = n + 3
        if mode in ("ctx", "ownF"):
            xe = load_ext(t)
            he = rnext(he_r)
            K.norm_mod(xe, ne, gmod1, sh1, he, inplace=True)
            for ch in range(4):
                ps = K.ps()
                for kc in range(8):
                    K.mm(ps.t[:, 0:ne], w_in_b.t[:, kc, ch * 128:(ch + 1) * 128], he.t[:, kc, 0:ne], kc == 0, kc == 7,
                         [w_in_b, he], [ps])
                K.cp("act", xr_b.t[:, ch, 0:ne], ps.t[:, 0:ne], [ps], [xr_b])
            for (col, cf, ct) in ((0, s - 2, s), (1, s - 1, s), (ne - 1, s + n - 1, s + n)):
                ks = crossed(sd["links"], S, cf, ct)
                if ks:
                    apply_links("pool", xr_b.t[:, :, col:col + 1], ks, xr_b)
            if mode == "ctx":
                xc = rnext(xc_t)
                xcv = lambda ch: xc.t[:, ch, 0:n]
            else:
                xc = XC
                xcv = lambda ch: XC.t[:, ch, t["own"]:t["own"] + n]
            for ch in range(4):
                ps = K.ps()
                for k in range(4):
                    K.mm(ps.t[:, 0:n], dg_b.t[:, ch * 4 + k, :], xr_b.t[:, ch, k:k + n], k == 0, k == 3, [dg_b, xr_b],
                         [ps])
                K.act(xcv(ch), ps.t[:, 0:n], AF.Identity, [ps, small], [xc], bias=small.t[:, ch:ch + 1])
            rp = load_rope(t)
            ps_kv = K.ps()
            for kc in range(8):
                K.mm(ps_kv.t[:, 0:n], w_in_b.t[:, kc, 1280:1408], he.t[:, kc, 2:2 + n], kc == 0, kc == 7, [w_in_b, he],
                     [ps_kv])
            sq = rnext(sqb)
            K.act(sq.t[:, 0:n], ps_kv.t[:, 0:n], AF.Square, [ps_kv], [sq])
            ps2 = K.ps()
            K.mm(ps2.t[:, 0:n], K.ones_b.t[:, :], sq.t[:, 0:n], True, True, [K.ones_b, sq], [ps2])
            r, rv = K.rstd_from_ps(ps2, ps2.t[:, 0:n], n, 128.0)
            ck = rnext(ckv)
            K.stt(ck.t[:, 0:n], ps_kv.t[:, 0:n], small.t[:, 30:31], rv, ALU.mult, ALU.mult, [ps_kv, small, r], [ck])
            for sub in range(0, n, 128):
                ns = min(128, n - sub)
                psv = K.ps()
                K.mm(psv.t[0:ns, :], ck.t[:, sub:sub + ns], wv_b.t[:, :], True, True, [ck, wv_b], [psv])
                v1 = rnext(vt)
                K.cp("act", v1.t[0:ns, :], psv.t[0:ns, :], [psv], [v1])
                K.dma(W["v_s"].t[s + sub:s + sub + ns, :], v1.t[0:ns, :], [v1], [W["v_s"]])
            pskr = K.ps()
            for kc in range(8):
                K.mm(pskr.t[0:96, 0:n], wkr_b.t[:, kc, :], he.t[:, kc, 2:2 + n], kc == 0, kc == 7, [wkr_b, he], [pskr])
            K.cp("act", krope.t[64:96, 0:n], pskr.t[64:96, 0:n], [pskr], [krope])
            for h in range(8):
                psk = K.ps()
                K.mm(psk.t[0:64, 0:n], wk_b.t[:, h * 64:(h + 1) * 64], ck.t[:, 0:n], True, True, [wk_b, ck], [psk])
                khf = rnext(kh)
                K.cp("act", khf.t[0:64, 0:n], psk.t[0:64, 0:n], [psk], [khf])
                K.cp("pool", khf.t[64:96, 0:n], krope.t[64:96, 0:n], [krope], [khf])
                _norm_rope_from_sb(K, khf, n, small, 33, rp, W_p96, sqb, khn, khb, rt1, rnext,
                                   W["kt_s"], W["kt_s"].t[h, :, s:s + n])
        if mode == "ownF":
            o = t["own"]
            for ch in range(4):
                ps = K.ps()
                for kc in range(8):
                    K.mm(ps.t[:, 0:n], w_in_b.t[:, kc, 512 + ch * 128:512 + (ch + 1) * 128], he.t[:, kc, 2:2 + n],
                         kc == 0, kc == 7, [w_in_b, he], [ps])
                K.act(GY.t[:, ch, o:o + n], ps.t[:, 0:n], AF.Gelu_apprx_tanh, [ps], [GY])
            psq = [K.ps(), K.ps()]
            sq2 = [rnext(sqb), rnext(sqb)]
            for c in range(2):
                for kc in range(8):
                    K.mm(psq[c].t[:, 0:n], w_in_b.t[:, kc, 1024 + c * 128:1024 + (c + 1) * 128], he.t[:, kc, 2:2 + n],
                         kc == 0, kc == 7, [w_in_b, he], [psq[c]])
                K.act(sq2[c].t[:, 0:n], psq[c].t[:, 0:n], AF.Square, [psq[c]], [sq2[c]])
            ps2 = K.ps()
            for c in range(2):
                K.mm(ps2.t[:, 0:n], K.ones_b.t[:, :], sq2[c].t[:, 0:n], c == 0, c == 1, [K.ones_b, sq2[c]], [ps2])
            r, rv = K.rstd_from_ps(ps2, ps2.t[:, 0:n], n, 256.0)
            qq = rnext(qn)
            for c in range(2):
                K.stt(qq.t[:, c, 0:n], psq[c].t[:, 0:n], small.t[:, 28 + c:29 + c], rv, ALU.mult, ALU.mult,
                      [psq[c], small, r], [qq])
            for h in range(8):
                psh = K.ps()
                for c in range(2):
                    K.mm(psh.t[0:96, 0:n], wuq_b.t[:, c, h * 96:(h + 1) * 96], qq.t[:, c, 0:n], c == 0, c == 1,
                         [wuq_b, qq], [psh])
                khf = rnext(kh)
                K.cp("act", khf.t[:, 0:n], psh.t[0:96, 0:n], [psh], [khf])
                _norm_rope_from_sb(K, khf, n, small, 32, rp, W_p96, sqb, khn, khb, rt1, rnext,
                                   W["qt_s"], W["qt_s"].t[h, :, o:o + n])
        dirs = (0, 1) if mode == "ctx" else ((0,) if mode == "ownF" else (1,))
        if mode == "ctx":
            ti = ctx_i[0]
            ctx_i[0] += 1
            t["ctx_idx"] = ti
        for d in dirs:
            for ch in range(4):
                if mode == "ctx":
                    xcs, xca = xc, xc.t[:, ch, 0:n]
                else:
                    xcs, xca = XC, XC.t[:, ch, t["own"]:t["own"] + n]
                psr_ = K.ps()
                K.mm(psr_.t[:, 0:n], bd_b.t[:, (d * 2 + 0) * 4 + ch, :], xca, True, True, [bd_b, xcs], [psr_])
                psi_ = K.ps()
                K.mm(psi_.t[:, 0:n], bd_b.t[:, (d * 2 + 1) * 4 + ch, :], xca, True, True, [bd_b, xcs], [psi_])
                rr = rnext(g_r)
                sc = sumr.t[:, (d * 4 + ch):(d * 4 + ch) + 1]
                K.act(rr.t[:, 0:n], psr_.t[:, 0:n], AF.Sigmoid, [psr_, small], [rr, sumr], bias=GB(d, 0, ch),
                      accum=sc if mode == "ctx" else None)
                ii = rnext(g_i)
                K.act(ii.t[:, 0:n], psi_.t[:, 0:n], AF.Sigmoid, [psi_, small], [ii], bias=GB(d, 1, ch))
                aa = rnext(g_a)
                K.act(aa.t[:, 0:n], rr.t[:, 0:n], AF.Exp, [rr, small], [aa], scale=SA(d, ch))
                qq_ = rnext(g_q)
                K.act(qq_.t[:, 0:n], rr.t[:, 0:n], AF.Exp, [rr, small], [qq_], scale=SA2(d, ch))
                K.act(qq_.t[:, 0:n], qq_.t[:, 0:n], AF.Ln, [qq_], [qq_], scale=-1.0, bias=K.one_t.t[:, 0:1])
                K.act(qq_.t[:, 0:n], qq_.t[:, 0:n], AF.Exp, [qq_], [qq_], scale=0.5)
                K.tt("dve", ii.t[:, 0:n], ii.t[:, 0:n], xca, ALU.mult, [ii, xcs], [ii])
                bb = rnext(g_b)
                K.tt("pool", bb.t[:, 0:n], qq_.t[:, 0:n], ii.t[:, 0:n], ALU.mult, [qq_, ii], [bb])
                hh = rnext(g_h)
                if mode == "ctx":
                    init = 0.0
                    rd = [aa, bb]
                else:
                    init = carry.t[:, d, ch:ch + 1]
                    rd = [aa, bb, carry]
                if d == 0:
                    K.S.op("dve", lambda e, hh=hh, aa=aa, bb=bb, init=init, n=n: e.tensor_tensor_scan(
                        out=hh.t[:, 0:n], data0=aa.t[:, 0:n], data1=bb.t[:, 0:n], initial=init, op0=ALU.mult,
                        op1=ALU.add), rd, [hh])
                    last = hh.t[:, n - 1:n]
                else:
                    K.S.op("dve", lambda e, hh=hh, aa=aa, bb=bb, init=init, n=n: e.tensor_tensor_scan(
                        out=hh.t[:, 0:n][:, ::-1], data0=aa.t[:, 0:n][:, ::-1], data1=bb.t[:, 0:n][:, ::-1],
                        initial=init, op0=ALU.mult, op1=ALU.add), rd, [hh])
                    last = hh.t[:, 0:1]
                if mode == "ctx":
                    K.cp("pool", AB.t[:, d, ti, 1, ch:ch + 1], last, [hh], [AB])
                    K.act(AB.t[:, d, ti, 0, ch:ch + 1], sc, AF.Exp, [sumr, small], [AB], scale=SA(d, ch))
                else:
                    o = t["own"]
                    K.cp("pool", carry.t[:, d, ch:ch + 1], last, [hh], [carry])
                    if d == 0:
                        K.cp("act", HF.t[:, ch, o:o + n], hh.t[:, 0:n], [hh], [HF])
                    else:
                        K.tt("dve", hh.t[:, 0:n], hh.t[:, 0:n], HF.t[:, ch, o:o + n], ALU.add, [hh, HF], [hh])
                        K.tt("pool", GY.t[:, ch, o:o + n], hh.t[:, 0:n], GY.t[:, ch, o:o + n], ALU.mult, [hh, GY], [GY])

    ctx_tiles = [t for t in tiles if t["role"] != "own"]
    own_tiles = [t for t in tiles if t["role"] != "ctx"]
    if limit.get("atiles"):
        own_tiles = own_tiles[:limit["atiles"]]
        ctx_tiles = ctx_tiles[:limit["atiles"]] if ctx_tiles else ctx_tiles
    for t in ctx_tiles:
        tileA(t, "ctx")

    def link_between(prev, t, d):
        if prev is None:
            return
        if d == 0:
            a = prev["s"] + prev["n"] - 1
            b_ = t["s"]
            if b_ <= a:
                b_ += S
            ks = crossed(sd["links"], S, a, b_)
        else:
            a = t["s"] + t["n"] - 1
            b_ = prev["s"]
            if b_ <= a:
                b_ += S
            ks = crossed(sd["links"], S, a, b_)
        apply_links("dve", carry.t[:, d, :], ks, carry)

    def chain(order, d):
        K.memset("pool", carry.t[:, d, :], 0.0, [carry])
        prev = None
        for t in order:
            link_between(prev, t, d)
            ti = t["ctx_idx"]
            K.tt("dve", carry.t[:, d, :], carry.t[:, d, :], AB.t[:, d, ti, 0, :], ALU.mult, [carry, AB], [carry])
            K.tt("dve", carry.t[:, d, :], carry.t[:, d, :], AB.t[:, d, ti, 1, :], ALU.add, [carry, AB], [carry])
            prev = t
        return prev

    halo = [t for t in tiles if t["role"] == "halo"]
    ctxo = [t for t in tiles if t["role"] == "ctx"]
    if kind == "p" and not limit.get("atiles"):
        fwd_order = [halo[1]] + ctxo
        prev = chain(fwd_order, 0)
    else:
        prev = None
    first = True
    for t in own_tiles:
        if kind == "p" or not first:
            link_between(prev, t, 0)
        first = False
        tileA(t, "ownF")
        prev = t
    if kind == "p" and not limit.get("atiles"):
        bwd_order = [halo[0]] + ctxo[::-1]
        prev = chain(bwd_order, 1)
    else:
        prev = None
        K.memset("pool", carry.t[:, 1, :], 0.0, [carry])
    first = True
    for t in own_tiles[::-1]:
        if kind == "p" or not first:
            link_between(prev, t, 1)
        first = False
        tileA(t, "ownB")
        prev = t


def _stageB(K, sd, si, xT, W, limit, GY):
    S = sd["S"]
    kind = sd["kind"]
    MV, GM = W["MV"], W["GM"]
    g1 = MV(0, si, 2)
    K.stage = K.ring(1, [128, 2048], F32, "stage")
    wo_rg_b = K.sb([128, 4, 1024], BF16, "wo_rg_b")
    for c in range(4):
        K.load_bf16(wo_rg_b, wo_rg_b.t[:, c, :], W["wo_rg"], W["wo_rg"].t[:, c, :], [128, 1024])
    wo_at_b = K.sb([64, 8, 1024], BF16, "wo_at_b")
    for c in range(0, 8, 2):
        K.load_bf16(wo_at_b, wo_at_b.t[:, c:c + 2, :], W["wo_at"], W["wo_at"].t[:, c:c + 2, :], [64, 2, 1024])
    QB = 512
    KBLK = 2048
    nkb = S // KBLK
    qtile = K.ring(2, [96, QB], BF16, "qtile")
    kblk = K.ring(2, [96, KBLK], BF16, "kblk")
    vblk = K.ring(2, [128, KBLK // 128, 65], BF16, "vblk")
    for vb in vblk:
        K.memset("pool", vb.t[:, :, 64:65], 1.0, [vb])
    pbuf = K.ring(3, [128, QB], BF16, "pbuf")
    at = [K.sb([64, QB], BF16, "at%d" % h) for h in range(8)]
    rd_t = K.sb([128, QB], F32, "rd")
    bcs = K.sb([64, QB], F32, "bcs")
    xq = K.ring(2, [128, 8, QB], F32, "xq")
    x1t = K.ring(2, [128, 8, QB], F32, "x1t")
    ps_s = [T(K.psf[i].t, K.psf[i].b) for i in range(3)]
    ps_o = [K.psf[3], K.psf[4]]
    ps_m = K.psf[5]
    scale = 96.0 ** -0.5
    own0 = sd["own0"]
    qtl = [[(1 + QB * k, QB)] for k in range(S_S // QB)]
    if kind == "p":
        qtl.append([(0, 1), (NOWN - 1, 1)])
    if limit.get("qtiles"):
        qtl = qtl[:limit["qtiles"]]
    it = 0
    for pieces in qtl:
        nq = sum(p[1] for p in pieces)
        xt_ = xq[it % 2]
        o = 0
        for (c0, ln) in pieces:
            cm = (own0 + c0) % S
            K.dma(xt_.t[:, :, o:o + ln], xT.t[:, cm:cm + ln].rearrange("(c p) t -> p c t", p=128), [xT], [xt_])
            o += ln
        for h in range(8):
            qt_ = qtile[(it * 8 + h) % 2]
            o = 0
            for (c0, ln) in pieces:
                K.dma(qt_.t[:, o:o + ln], W["qt_s"].t[h, :, c0:c0 + ln], [W["qt_s"]], [qt_])
                o += ln
            pso = ps_o[h % 2]
            nkt = S // 128
            jobs = []
            for kb_ in range(nkb):
                jobs.append(kb_)
            kcur = None
            vcur = None
            pend = None
            bi = 0
            for kt in range(nkt):
                if kt % (KBLK // 128) == 0:
                    kb_ = kt // (KBLK // 128)
                    kcur = kblk[(it * 8 * nkb + h * nkb + kb_) % 2]
                    vcur = vblk[(it * 8 * nkb + h * nkb + kb_) % 2]
                    K.dma(kcur.t[:, :], W["kt_s"].t[h, :, kb_ * KBLK:(kb_ + 1) * KBLK], [W["kt_s"]], [kcur])
                    K.dma(vcur.t[:, :, 0:64],
                          W["v_s"].t[kb_ * KBLK:(kb_ + 1) * KBLK, h * 64:(h + 1) * 64].rearrange("(k p) d -> p k d", p=128),
                          [W["v_s"]], [vcur])
                kk = kt % (KBLK // 128)
                pss = ps_s[kt % 3]
                K.mm(pss.t[:, 0:nq], kcur.t[:, kk * 128:(kk + 1) * 128], qt_.t[:, 0:nq], True, True, [kcur, qt_], [pss])
                if pend is not None:
                    (pb, pkt, pv, pkk) = pend
                    K.mm(pso.t[0:65, 0:nq], pv.t[:, pkk, :], pb.t[:, 0:nq], pkt == 0, False, [pv, pb], [pso])
                pb = pbuf[kt % 3]
                K.act(pb.t[:, 0:nq], pss.t[:, 0:nq], AF.Exp, [pss], [pb], scale=scale)
                pend = (pb, kt, vcur, kk)
            (pb, pkt, pv, pkk) = pend
            K.mm(pso.t[0:65, 0:nq], pv.t[:, pkk, :], pb.t[:, 0:nq], pkt == 0, True, [pv, pb], [pso])
            K.S.op("dve", lambda e, pso=pso, nq=nq: e.reciprocal(out=rd_t.t[64:65, 0:nq], in_=pso.t[64:65, 0:nq]),
                   [pso], [rd_t])
            psb_ = ps_m
            K.mm(psb_.t[0:64, 0:nq], W["ones_f"].t[64:65, 0:64], rd_t.t[64:65, 0:nq], True, True, [W["ones_f"], rd_t],
                 [psb_])
            K.cp("act", bcs.t[:, 0:nq], psb_.t[0:64, 0:nq], [psb_], [bcs])
            K.tt("dve", at[h].t[:, 0:nq], pso.t[0:64, 0:nq], bcs.t[:, 0:nq], ALU.mult, [pso, bcs], [at[h]])
        x1 = x1t[it % 2]
        for dc in range(8):
            psm = K.ps() if False else ps_m
            o = 0
            for (c0, ln) in pieces:
                for c in range(4):
                    K.mm(psm.t[:, o:o + ln], wo_rg_b.t[:, c, dc * 128:(dc + 1) * 128], GY.t[:, c, c0:c0 + ln],
                         c == 0, False, [wo_rg_b, GY], [psm])
                for h in range(8):
                    K.mm(psm.t[:, o:o + ln], wo_at_b.t[:, h, dc * 128:(dc + 1) * 128], at[h].t[:, o:o + ln],
                         False, h == 7, [wo_at_b, at[h]], [psm])
                o += ln
            K.stt(x1.t[:, dc, 0:nq], psm.t[:, 0:nq], g1[:, dc:dc + 1], xt_.t[:, dc, 0:nq], ALU.mult, ALU.add,
                  [psm, K.modv, xt_], [x1])
        o = 0
        for (c0, ln) in pieces:
            K.dma(W["x1"].t[:, c0:c0 + ln].rearrange("(c p) t -> p c t", p=128), x1.t[:, :, o:o + ln], [x1], [W["x1"]])
            o += ln
        it += 1


def _norm_rope_from_sb(K, khf, n, small, gcol, rp, p96_b, sqb, khn, khb, rt1, rnext, dst, dst_ap):
    sq = rnext(sqb)
    K.act(sq.t[0:96, 0:n], khf.t[:, 0:n], AF.Square, [khf], [sq])
    ps2 = K.ps()
    K.mm(ps2.t[0:96, 0:n], K.ones_b.t[0:96, 0:96], sq.t[0:96, 0:n], True, True, [K.ones_b, sq], [ps2])
    r, rv = K.rstd_from_ps(ps2, ps2.t[0:96, 0:n], n, 96.0, parts=96)
    kb = rnext(khb)
    K.stt(kb.t[:, 0:n], khf.t[:, 0:n], small.t[0:96, gcol:gcol + 1], rv, ALU.mult, ALU.mult, [khf, small, r], [kb])
    ps3 = K.ps()
    K.mm(ps3.t[0:96, 0:n], p96_b.t[:, :], kb.t[:, 0:n], True, True, [p96_b, kb], [ps3])
    t1 = rnext(rt1)
    K.tt("pool", t1.t[64:96, 0:n], kb.t[64:96, 0:n], rp.t[64:96, 0, 0:n], ALU.mult, [kb, rp], [t1])
    t2 = rnext(rt1)
    K.tt("dve", t2.t[64:96, 0:n], ps3.t[64:96, 0:n], rp.t[64:96, 1, 0:n], ALU.mult, [ps3, rp], [t2])
    K.tt("dve", kb.t[64:96, 0:n], t1.t[64:96, 0:n], t2.t[64:96, 0:n], ALU.add, [t1, t2, kb], [kb])
    K.dma(dst_ap, kb.t[:, 0:n], [kb], [dst])


def _mixer_c(K, si, kind, x2, x3, W, limit):
    MV, GM = W["MV"], W["GM"]
    LK, links = W["LK"], W["links"]
    gmod, sh, g1 = GM(1, si, 0), MV(1, si, 0), MV(1, si, 2)
    K.stage = K.ring(2, [128, 2048], F32, "stage")
    cw_b = K.sb([128, 8, 3072], BF16, "cw_b")
    for kc in range(8):
        for hf in range(2):
            K.load_bf16(cw_b, cw_b.t[:, kc, hf * 1536:(hf + 1) * 1536], W["c_w_in"],
                        W["c_w_in"].t[:, kc, hf * 1536:(hf + 1) * 1536], [128, 1536])
    cd_b = K.sb([128, 24, 128], BF16, "cd_b")
    for hf in range(2):
        K.load_bf16(cd_b, cd_b.t[:, hf * 12:(hf + 1) * 12, :], W["cdiag"], W["cdiag"].t[:, hf * 12:(hf + 1) * 12, :],
                    [128, 12, 128])
    co_b = K.sb([128, 8, 1024], BF16, "co_b")
    for kc in range(0, 8, 2):
        K.load_bf16(co_b, co_b.t[:, kc:kc + 2, :], W["c_w_out"], W["c_w_out"].t[:, kc:kc + 2, :], [128, 2, 1024])
    NE = NT + 2
    xe_r = K.ring(2, [128, 8, NE], F32, "cxe")
    he_r = K.ring(2, [128, 8, NE], BF16, "che")
    K.nm_sq = K.sb([128, 8, NE], BF16, "cnmsq")
    K.nm_tmp = K.sb([128, 8, NE], F32, "cnmtmp")
    cgs = K.ring(2, [128, NE], F32, "cgs")
    u_b = K.sb([128, 8, NE], BF16, "u_b")
    bg = K.sb([128, 8, NT], F32, "bg")
    y_b = K.ring(2, [128, 8, NT], BF16, "y_b")
    x3t = K.ring(2, [128, 8, NT], F32, "x3t")
    ntl = S_S // NT
    if limit.get("ctiles"):
        ntl = limit["ctiles"]
    for k in range(ntl):
        xe = xe_r[k % 2]
        K.dma(xe.t[:, :, :], x2.t[:, NT * k:NT * k + NE].rearrange("(c p) t -> p c t", p=128), [x2], [xe])
        he = he_r[k % 2]
        K.norm_mod(xe, NE, gmod, sh, he)
        for ch in range(8):
            psc = K.ps()
            for kc in range(8):
                K.mm(psc.t[:, 0:NE], cw_b.t[:, kc, 1024 + ch * 128:1024 + (ch + 1) * 128], he.t[:, kc, :], kc == 0,
                     kc == 7, [cw_b, he], [psc])
            psx = K.ps()
            for kc in range(8):
                K.mm(psx.t[:, 0:NE], cw_b.t[:, kc, 2048 + ch * 128:2048 + (ch + 1) * 128], he.t[:, kc, :], kc == 0,
                     kc == 7, [cw_b, he], [psx])
            cg = cgs[ch % 2]
            K.cp("act", cg.t[:, :], psc.t[:, 0:NE], [psc], [cg])
            K.tt("dve", u_b.t[:, ch, :], psx.t[:, 0:NE], cg.t[:, :], ALU.mult, [psx, cg], [u_b])
            psb_ = K.ps()
            for kc in range(8):
                K.mm(psb_.t[:, 0:NT], cw_b.t[:, kc, ch * 128:(ch + 1) * 128], he.t[:, kc, 1:1 + NT], kc == 0, kc == 7,
                     [cw_b, he], [psb_])
            K.cp("act", bg.t[:, ch, :], psb_.t[:, 0:NT], [psb_], [bg])
        if k == 0:
            key = "c" if kind == "p" else "zero"
            _lk(K, u_b, u_b.t[:, :, 0:1], key, LK, links)
        if k == S_S // NT - 1:
            key = "d" if kind == "p" else "zero"
            _lk(K, u_b, u_b.t[:, :, NE - 1:NE], key, LK, links)
        yb = y_b[k % 2]
        for ch in range(8):
            psv = K.ps()
            for tp in range(3):
                K.mm(psv.t[:, 0:NT], cd_b.t[:, ch * 3 + tp, :], u_b.t[:, ch, tp:tp + NT], tp == 0, tp == 2, [cd_b, u_b],
                     [psv])
            K.tt("dve", yb.t[:, ch, :], psv.t[:, 0:NT], bg.t[:, ch, :], ALU.mult, [psv, bg], [yb])
        x3_ = x3t[k % 2]
        for dc in range(8):
            psm = K.ps()
            for c in range(8):
                K.mm(psm.t[:, 0:NT], co_b.t[:, c, dc * 128:(dc + 1) * 128], yb.t[:, c, :], c == 0, c == 7, [co_b, yb],
                     [psm])
            K.stt(x3_.t[:, dc, :], psm.t[:, 0:NT], g1[:, dc:dc + 1], xe.t[:, dc, 1:1 + NT], ALU.mult, ALU.add,
                  [psm, K.modv, xe], [x3_])
        K.dma(x3.t[:, NT * k:NT * (k + 1)].rearrange("(c p) t -> p c t", p=128), x3_.t[:, :, :], [x3_], [x3])


def _lk(K, t, ap, key, LK, links):
    if key == "zero":
        K.ts("pool", ap, ap, 0.0, None, ALU.mult, None, [t], [t])
    else:
        K.ts("pool", ap, ap, LK[key], None, ALU.mult, None, [t, links], [t])


def _peer(K, l, si, xsrc, xdst, tlist, W, limit):
    MV, GM = W["MV"], W["GM"]
    gmod, sh, g2 = GM(l, si, 1), MV(l, si, 3), MV(l, si, 5)
    ident_b, ident_f = W["ident_b"], W["ident_f"]
    wq_b, uv_b = W["wq_b"], W["uv_b"]
    K.stage = K.ring(1, [128, 256], F32, "pstage")
    kT = K.sb([128, 2, 128], BF16, "kT")
    K.load_bf16(kT, kT.t[:], W["k12T"], W["k12T"].t[:, 2 * l:2 * l + 2, :], [128, 2, 128])
    GT = K.sb([128, NT, 128], BF16, "GT")
    RT = K.sb([128, 128, 128], BF16, "RT")
    OHT = K.sb([128, 128, 128], BF16, "OHT")
    ROr = K.ring(2, [128, 128, 64], BF16, "RO")
    s12 = K.sb([128, 16, 128], F32, "s12")
    e2 = K.sb([128, 8, 128], BF16, "e2")
    vv = K.sb([128, 16, 16], F32, "vv")
    vs = K.sb([128, 8, 16], F32, "vs")
    sm = K.sb([128, 8, 64], F32, "sm")
    h2 = K.sb([128, 8, NT], BF16, "h2")
    wqs = K.ring(2, [128, 8, 128], BF16, "wqs")
    NSB = 4
    strm = K.sb([128, NSB, 2048], BF16, "strm")
    sbuf_ = [Buf("strm%d" % i) for i in range(NSB)]
    uvs = [T(strm.t[:, i, :], sbuf_[i]) for i in range(NSB)]
    qTv = strm.t[:, :, :].rearrange("p a b -> p (a b)")[:, 0:16 * NT].rearrange("p (c t) -> p c t", t=NT)
    qTb = sbuf_[0:2]
    gel = K.ring(3, [128, NT], BF16, "gel")
    atb = K.ring(3, [128, NT], BF16, "atb")
    rtb = RT.t[:, :, :].rearrange("p a b -> p (a b)")
    rtf = rtb.bitcast(F32)
    ohf = OHT.t[:, :, :].rearrange("p a b -> p (a b)").bitcast(F32)
    K.nm_sq = T(rtb[:, 0:8 * NT].rearrange("p (c t) -> p c t", t=NT), RT.b)
    K.nm_tmp = T(ohf[:, 0:8 * NT].rearrange("p (c t) -> p c t", t=NT), OHT.b)
    GTB = [GT.b, Buf("GTb")]
    gtf = GT.t[:, 128:256, :].rearrange("p a b -> p (a b)").bitcast(F32)
    tmpT = T(gtf[:, 0:2048].rearrange("p (c j) -> p c j", j=128), GTB[1])
    e2fT = T(gtf[:, 2048:3072].rearrange("p (h j) -> p h j", j=128), GTB[1])
    tmp = tmpT.t
    e2f = e2fT.t
    xt = T(rtf[:, 4096:4096 + 8 * NT].rearrange("p (c t) -> p c t", t=NT), RT.b)
    rof = ROr[1].t[:, :, :].rearrange("p a b -> p (a b)").bitcast(F32)
    RKB = ROr[1].b
    cand = rof[:, 0:2048].rearrange("p (h a b) -> p h a b", a=16, b=16)
    ctmp = rof[:, 2048:4096].rearrange("p (h a b) -> p h a b", a=16, b=16)
    sel = ctmp
    t1v = cand
    osb = T(ohf[:, 0:1024], OHT.b)
    nchunk = limit.get("pchunks", 128)

    def load_x(pieces):
        o = 0
        for (c0, ln) in pieces:
            K.dma(xt.t[:, :, o:o + ln], xsrc.t[:, c0:c0 + ln].rearrange("(c p) t -> p c t", p=128), [xsrc], [xt])
            o += ln

    for pieces in tlist:
        nt = sum(p[1] for p in pieces)
        load_x(pieces)
        K.norm_mod(xt, nt, gmod, sh, h2)
        for c in range(16):
            wq_ = wqs[c % 2]
            K.dma(wq_.t[:, :, :], wq_b.t[l, c], [wq_b], [wq_])
            ps = K.ps()
            for kc in range(8):
                K.mm(ps.t[:, 0:nt], wq_.t[:, kc, :], h2.t[:, kc, 0:nt], kc == 0, kc == 7, [wq_, h2], [ps])
            K.cp("act" if c % 2 == 0 else "dve", qTv[:, c, 0:nt], ps.t[:, 0:nt], [ps], qTb)
        st_ = {}

        def t_pre(t0):
            ns = min(128, nt - t0)
            for g in range(4):
                ps = K.ps()
                for m in range(4):
                    c = g * 4 + m
                    K.mm(ps.t[0:ns, m * 128:(m + 1) * 128], qTv[:, c, t0:t0 + ns], kT.t[:, c % 2, :], True, True,
                         qTb + [kT], [ps])
                K.cp("act", s12.t[0:ns, g * 4:(g + 1) * 4, :], ps.t[0:ns, :].rearrange("p (m j) -> p m j", j=128),
                     [ps], [s12])
            for c in range(16):
                K.S.op("dve", lambda e, c=c, ns=ns: e.max(out=vv.t[0:ns, c, 0:8], in_=s12.t[0:ns, c, :]), [s12], [vv])
            for c in range(16):
                K.S.op("dve", lambda e, c=c, ns=ns: e.match_replace(out=tmp[0:ns, c, :], in_to_replace=vv.t[0:ns, c, 0:8],
                                                                    in_values=s12.t[0:ns, c, :], imm_value=-BIG),
                       [s12, vv], [tmpT])
            for c in range(16):
                K.S.op("dve", lambda e, c=c, ns=ns: e.max(out=vv.t[0:ns, c, 8:16], in_=tmp[0:ns, c, :]), [tmpT], [vv])
            v4 = vv.t[:, :, :].rearrange("p (h w) a -> p h w a", w=2)
            v1 = v4[0:ns, :, 0, :]
            v2 = v4[0:ns, :, 1, :]
            s4 = s12.t[:, :, :].rearrange("p (h w) j -> p h w j", w=2)
            s1 = s4[0:ns, :, 0, :]
            s2 = s4[0:ns, :, 1, :]
            K.tt("pool", cand[0:ns], v1.unsqueeze(3).to_broadcast([ns, 8, 16, 16]),
                 v2.unsqueeze(2).to_broadcast([ns, 8, 16, 16]), ALU.add, [vv], [RKB])
            for h in range(8):
                K.S.op("dve", lambda e, h=h, ns=ns: e.max(out=vs.t[0:ns, h, 0:8], in_=cand[0:ns, h]), [RKB], [vs])
            for h in range(8):
                K.S.op("dve", lambda e, h=h, ns=ns: e.match_replace(out=ctmp[0:ns, h], in_to_replace=vs.t[0:ns, h, 0:8],
                                                                    in_values=cand[0:ns, h], imm_value=-BIG),
                       [RKB, vs], [RKB])
            for h in range(8):
                K.S.op("dve", lambda e, h=h, ns=ns: e.max(out=vs.t[0:ns, h, 8:16], in_=ctmp[0:ns, h]), [RKB], [vs])
            ev = sm.t[0:ns, :, 0:16]
            thr = sm.t[0:ns, :, 16:32]
            Z = sm.t[0:ns, :, 32]
            rZ = sm.t[0:ns, :, 33]
            w1 = sm.t[0:ns, :, 40:56]
            K.tt("pool", ev, vs.t[0:ns], vs.t[0:ns, :, 0:1].to_broadcast([ns, 8, 16]), ALU.subtract, [vs], [sm])
            K.act(ev, ev, AF.Exp, [sm], [sm])
            K.S.op("dve", lambda e, ev=ev, Z=Z: e.tensor_reduce(out=Z, in_=ev, op=ALU.add, axis=AX.X), [sm], [sm])
            K.S.op("dve", lambda e, Z=Z, rZ=rZ: e.reciprocal(out=rZ, in_=Z), [sm], [sm])
            K.tt("pool", w1, v1, v1[:, :, 0:1].to_broadcast([ns, 8, 16]), ALU.subtract, [vv], [sm])
            K.act(w1, w1, AF.Exp, [sm], [sm])
            K.tt("pool", w1, w1, rZ.unsqueeze(2).to_broadcast([ns, 8, 16]), ALU.mult, [sm], [sm])
            K.tt("pool", e2f[0:ns], s2, v2[:, :, 0:1].to_broadcast([ns, 8, 128]), ALU.subtract, [s12, vv], [e2fT])
            K.act(e2.t[0:ns], e2f[0:ns], AF.Exp, [e2fT], [e2])
            K.tt("dve", sel[0:ns], cand[0:ns], vs.t[0:ns, :, 15:16].unsqueeze(3).to_broadcast([ns, 8, 16, 16]),
                 ALU.is_ge, [RKB, vs], [RKB])
            K.tt("pool", t1v[0:ns], sel[0:ns], v2.unsqueeze(2).to_broadcast([ns, 8, 16, 16]), ALU.mult, [RKB, vv], [RKB])
            K.ts("pool", sel[0:ns], sel[0:ns], -BIG, BIG, ALU.mult, ALU.add, [RKB], [RKB])
            K.tt("pool", t1v[0:ns], t1v[0:ns], sel[0:ns], ALU.add, [RKB], [RKB])
            K.S.op("dve", lambda e, ns=ns, thr=thr: e.tensor_reduce(out=thr, in_=t1v[0:ns], op=ALU.min, axis=AX.X),
                   [RKB], [sm])
            st_[t0] = dict(ns=ns, s1=s1, s2=s2, v1=v1, v2=v2, thr=thr, w1=w1)

        def t_build(t0):
            d_ = st_[t0]
            ns, s1, s2, v1, v2, thr, w1 = d_['ns'], d_['s1'], d_['s2'], d_['v1'], d_['v2'], d_['thr'], d_['w1']
            rnd = 0
            for ih in range(2):
                RO = ROr[rnd % 2]
                rnd += 1
                is_ = slice(ih * 64, (ih + 1) * 64)
                for h in range(8):
                    for a_ in range(16):
                        K.ts("dve", RO.t[0:ns, h * 16 + a_, :], s1[:, h, is_], v1[:, h, a_:a_ + 1], w1[:, h, a_:a_ + 1],
                             ALU.is_equal, ALU.mult, [s12, vv, sm], [RO])
                _transposes(K, RO, OHT, ih, ns, ident_b)
            for jh in range(2):
                RO = ROr[rnd % 2]
                rnd += 1
                js = slice(jh * 64, (jh + 1) * 64)
                for h in range(8):
                    for a_ in range(16):
                        K.stt(RO.t[0:ns, h * 16 + a_, :], s2[:, h, js], thr[:, h, a_:a_ + 1], e2.t[0:ns, h, js],
                              ALU.is_ge, ALU.mult, [s12, sm, e2], [RO])
                _transposes(K, RO, RT, jh, ns, ident_b)

        def t_gmm(t0):
            ns = st_[t0]['ns']
            for tb in range(0, ns, 4):
                nb = min(4, ns - tb)
                ps = K.ps()
                for u in range(nb):
                    K.mm(ps.t[:, u * 128:(u + 1) * 128], RT.t[:, :, tb + u], OHT.t[:, :, tb + u], True, True, [RT, OHT],
                         [ps])
                K.cp("act", GT.t[:, t0 + tb:t0 + tb + nb, :].rearrange("p t i -> p (t i)"), ps.t[:, 0:nb * 128],
                     [ps], [GTB[t0 // 128]])

        subs = list(range(0, nt, 128))
        t_pre(subs[0])
        t_build(subs[0])
        for k_ in range(1, len(subs)):
            t_pre(subs[k_])
            t_gmm(subs[k_ - 1])
            t_build(subs[k_])
        t_gmm(subs[-1])
        nsub = (nt + 127) // 128
        pso = [[K.psf[2 + 2 * s_ + dh] for dh in range(2)] for s_ in range(nsub)]
        pss = [K.psf[0], K.psf[1]]
        PF = NSB - 2

        def issue_dma(i):
            uv_ = uvs[i % NSB]
            K.dma(uv_.t, uv_b.t[l, i], [uv_b], [uv_])

        def front(i):
            uv_ = uvs[i % NSB]
            ps = pss[i % 2]
            for kc in range(8):
                K.mm(ps.t[:, 0:nt], uv_.t[:, kc * 128:(kc + 1) * 128], h2.t[:, kc, 0:nt], kc == 0, kc == 7,
                     [uv_, h2], [ps])
            gl = gel[i % 3]
            K.act(gl.t[:, 0:nt], ps.t[:, 0:nt], AF.Gelu_apprx_tanh, [ps], [gl])
            ab = atb[i % 3]
            K.tt("dve" if i % 3 else "pool", ab.t[:, 0:nt], gl.t[:, 0:nt], GT.t[:, 0:nt, i], ALU.mult, [gl] + GTB, [ab])

        def back(i):
            uv_ = uvs[i % NSB]
            ab = atb[i % 3]
            for s_ in range(nsub):
                ns = min(128, nt - s_ * 128)
                for dh in range(2):
                    K.mm(pso[s_][dh].t[0:ns, :], ab.t[:, s_ * 128:s_ * 128 + ns],
                         uv_.t[:, 1024 + dh * 512:1024 + (dh + 1) * 512], i == 0, i == nchunk - 1, [ab, uv_],
                         [pso[s_][dh]])

        for i in range(min(PF, nchunk)):
            issue_dma(i)
        for i in range(nchunk + 1):
            if i < nchunk:
                front(i)
            if i >= 1:
                back(i - 1)
            if i + PF < nchunk:
                issue_dma(i + PF)
        load_x(pieces)
        for s_ in range(nsub):
            ns = min(128, nt - s_ * 128)
            for dh in range(2):
                K.cp("act", osb.t[0:ns, dh * 512:(dh + 1) * 512], pso[s_][dh].t[0:ns, :], [pso[s_][dh]], [osb])
            for dc in range(8):
                ps = pss[dc % 2]
                K.tr(ps.t[:, 0:ns], osb.t[0:ns, dc * 128:(dc + 1) * 128], ident_f.t[0:ns, 0:ns], [osb, ident_f], [ps])
                K.stt(xt.t[:, dc, s_ * 128:s_ * 128 + ns], ps.t[:, 0:ns], g2[:, dc:dc + 1],
                      xt.t[:, dc, s_ * 128:s_ * 128 + ns], ALU.mult, ALU.add, [ps, K.modv, xt], [xt])
        o = 0
        for (c0, ln) in pieces:
            K.dma(xdst.t[:, c0:c0 + ln].rearrange("(c p) t -> p c t", p=128), xt.t[:, :, o:o + ln], [xt], [xdst])
            o += ln


def _transposes(K, RO, DST, half, ns, ident_b):
    for g in range(8):
        pb = K.psb[g % 2]
        for u in range(8):
            jj = g * 8 + u
            K.tr(pb.t[:, u * 128:u * 128 + ns], RO.t[0:ns, :, jj], ident_b.t[0:ns, 0:ns], [RO, ident_b], [pb])
        j0 = half * 64 + g * 8
        if ns == 128:
            K.cp("act", DST.t[:, j0:j0 + 8, :].rearrange("p j t -> p (j t)"), pb.t[:, :], [pb], [DST])
        else:
            K.cp("act", DST.t[:, j0:j0 + 8, 0:ns], pb.t[:, :].rearrange("p (j t) -> p j t", t=128)[:, :, 0:ns], [pb], [DST])


def _fm(v, n=8):
    return np.ascontiguousarray(np.asarray(v, np.float32).reshape(n, 128).T)


def _wmat(w):
    K_, N = w.shape
    return np.ascontiguousarray(np.asarray(w, np.float32).reshape(K_ // 128, 128, N).transpose(1, 0, 2))


def _rope_tables(pos):
    inv = (1.0 / (10000.0 ** (np.arange(0, 32, 2, dtype=np.float32) / np.float32(32)))).astype(np.float32)
    ang = pos.astype(np.float32)[:, None] * inv[None, :]
    c = np.cos(ang).astype(np.float32).T
    s = np.sin(ang).astype(np.float32).T
    out = np.empty((32, 2, pos.shape[0]), np.float32)
    out[0:16, 0] = c
    out[16:32, 0] = c
    out[0:16, 1] = s
    out[16:32, 1] = s
    return out


def host_inputs(I):
    f = np.float32
    shared = {}
    shared["ident"] = np.eye(128, dtype=f)
    p96 = np.zeros((96, 96), f)
    for m in range(16):
        p96[64 + m + 16, 64 + m] = -1.0
        p96[64 + m, 64 + m + 16] = 1.0
    shared["p96"] = p96
    shared["adaw"] = np.ascontiguousarray(I["ada_w"].reshape(2, 8, 128, 6144).transpose(0, 2, 1, 3))
    shared["adab"] = np.ascontiguousarray(I["ada_b"].reshape(2, 48, 128).transpose(2, 0, 1))
    shared["n1g"] = np.ascontiguousarray(I["norm1_g"].reshape(2, 8, 128).transpose(2, 0, 1))
    shared["n2g"] = np.ascontiguousarray(I["norm2_g"].reshape(2, 8, 128).transpose(2, 0, 1))
    shared["w_in"] = _wmat(I["ab_w_in"][0])
    wkr = np.zeros((128, 8, 96), f)
    wkr[:, :, 64:96] = shared["w_in"][:, :, 1408:1440]
    shared["wkr"] = wkr
    cw = I["rg_conv_w"][0]
    dg = np.zeros((128, 16, 128), f)
    for ch in range(4):
        for k in range(4):
            dg[np.arange(128), ch * 4 + k, np.arange(128)] = cw[k, ch * 128:(ch + 1) * 128]
    shared["rgdiag"] = dg
    shared["rgcb"] = _fm(I["rg_conv_b"][0], 4)
    bd = np.zeros((128, 16, 128), f)
    rgb = np.zeros((128, 16), f)
    for d in range(2):
        for wi, (wn, bn) in enumerate((("rg_wa", "rg_ba"), ("rg_wx", "rg_bx"))):
            for ch in range(4):
                idx = (d * 2 + wi) * 4 + ch
                for hh in range(2):
                    bd[hh * 64:(hh + 1) * 64, idx, hh * 64:(hh + 1) * 64] = I[wn][0, d, ch * 2 + hh]
                rgb[:, idx] = I[bn][0, d, ch * 128:(ch + 1) * 128]
    shared["rgbd"] = bd
    shared["rgb"] = rgb
    lam = np.zeros((128, 8), f)
    for d in range(2):
        lam[:, d * 4:(d + 1) * 4] = _fm(I["rg_lambda"][0, d], 4)
    shared["rglam"] = lam
    shared["qnorm"] = _fm(I["mla_q_norm"][0], 2)
    shared["w_uq"] = _wmat(I["mla_w_uq"][0])
    shared["kvnorm"] = _fm(I["mla_kv_norm"][0], 1)
    wukv = I["mla_w_ukv"][0].reshape(128, 8, 128)
    shared["wk"] = np.ascontiguousarray(wukv[:, :, 0:64].reshape(128, 512))
    shared["wv"] = np.ascontiguousarray(wukv[:, :, 64:128].reshape(128, 512))
    shared["qng"] = np.ascontiguousarray(np.stack([I["mla_qn_q"][0], I["mla_qn_k"][0]], axis=1).astype(f))
    wo = I["ab_w_out"][0]
    shared["wo_rg"] = _wmat(wo[0:512])
    shared["wo_at"] = np.ascontiguousarray(wo[512:1024].reshape(8, 64, 1024).transpose(1, 0, 2))
    shared["c_w_in"] = _wmat(I["c_w_in"][0])
    ccw = I["c_conv_w"][0]
    cd = np.zeros((128, 24, 128), f)
    for ch in range(8):
        for k in range(3):
            cd[np.arange(128), ch * 3 + k, np.arange(128)] = ccw[k, ch * 128:(ch + 1) * 128]
    shared["cdiag"] = cd
    shared["c_w_out"] = _wmat(I["c_w_out"][0])
    shared["wq"] = np.ascontiguousarray(I["peer_wq"].reshape(2, 8, 128, 2048).transpose(0, 2, 1, 3))
    k12 = np.zeros((128, 4, 128), f)
    for l in range(2):
        k12[:, 2 * l + 0, :] = I["peer_k1"][l].T
        k12[:, 2 * l + 1, :] = I["peer_k2"][l].T
    shared["k12T"] = k12
    U = I["peer_u"].reshape(2, 128, 128, 8, 128)
    shared["ut"] = np.ascontiguousarray(U.transpose(0, 1, 4, 3, 2)).reshape(2, 128, 128, 1024)
    shared["pv"] = np.ascontiguousarray(I["peer_v"].reshape(2, 128, 128, 1024))
    shared["rope_s"] = _rope_tables(np.arange(S_S))
    maps = []
    for c in range(8):
        b, q = c // 4, c % 4
        m = dict(shared)
        m["xs"] = np.ascontiguousarray(I["x_sample"][c].T)
        start = ((q + 1) * 4096 + 1) % S_P
        pos = (start + np.arange(S_P)) % S_P
        m["xp"] = np.ascontiguousarray(I["x_prompt"][b][pos].T)
        m["rope_p"] = _rope_tables(pos)
        cv = np.zeros((128, 8, 2), f)
        cv[:, :, 0] = _fm(I["c_sample"][c])
        cv[:, :, 1] = _fm(I["c_prompt"][b])
        m["cvec"] = cv
        lk = np.ones((128, 4), f)
        lk[:, {2: 0, 1: 1, 0: 2, 3: 3}[q]] = 0.0
        m["links"] = lk
        maps.append(m)
    return maps


_CACHE = {}


def kernel(**inputs):
    I = {k: np.asarray(v) for k, v in inputs.items()}
    maps = host_inputs(I)
    if "nc" not in _CACHE:
        _CACHE["nc"] = build()
    nc, K = _CACHE["nc"]
    res = run_bass_kernel_spmd(nc, maps, core_ids=list(range(8)))
    y_prompt = np.empty((2, S_P, 1024), np.float32)
    y_sample = np.empty((8, S_S, 1024), np.float32)
    for c in range(8):
        r = res.results[c]
        b, q = c // 4, c % 4
        y_sample[c] = r["ys"].T
        y_prompt[b, q * 4096:(q + 1) * 4096] = r["yp"].T
    return (y_prompt, y_sample)
```
